# Optimizing a Trainium2 kernel written in Bass

```python
import math
import jax
import jax.numpy as jnp
from jax import lax
import numpy as np

D_MODEL = 4096
BATCH = 4
SEQ = 4096
DEPTH = 4

N_MEM = 256
MIX_WIDTH = D_MODEL
GROUP_W = MIX_WIDTH // 4
S5_CH = GROUP_W
S5_GROUP = 16
S5_NGROUPS = S5_CH // S5_GROUP
S5_STATE = 64
MLA_NOPE = 128
MLA_ROPE = 64
MLA_V = 128
MLA_QK = MLA_NOPE + MLA_ROPE
MLA_HEADS = GROUP_W // MLA_V
Q_LORA = max(128, int(round(1536 * D_MODEL / 7168 / 128)) * 128)
KV_LORA = max(128, int(round(512 * D_MODEL / 7168 / 128)) * 128)
ROPE_THETA = 10000.0
Q_BLOCK = 128
RWKV_HEAD = 64
RWKV_HEADS = GROUP_W // RWKV_HEAD
DECAY_LORA = 64
AAA_LORA = 64
RWKV_COLS = 3 * GROUP_W + 2 * DECAY_LORA + 2 * AAA_LORA
RWKV_SPLITS = (GROUP_W, 2 * GROUP_W, 3 * GROUP_W, 3 * GROUP_W + DECAY_LORA,
               3 * GROUP_W + 2 * DECAY_LORA, 3 * GROUP_W + 2 * DECAY_LORA + AAA_LORA)
RWKV_LN_EPS = 64e-5
MEM_HEADS = 4
MEM_HEAD_DIM = GROUP_W // MEM_HEADS
COL_WIDTHS = (S5_CH, S5_CH, Q_LORA, KV_LORA, MLA_ROPE, GROUP_W, RWKV_COLS, GROUP_W, GROUP_W, GROUP_W)
N_COLS = sum(COL_WIDTHS)
EPS = 1e-6

kernel_name = 'hybrid_s5_mla_rwkv7_memory_encoder'


def rms_norm(x, g, eps=EPS):
    xf = x.astype(jnp.float32)
    y = xf * lax.rsqrt(jnp.mean(xf * xf, axis=-1, keepdims=True) + eps)
    return (y * g.astype(jnp.float32)).astype(x.dtype)


def rope_tables(positions):
    inv_freq = 1.0 / (ROPE_THETA ** (jnp.arange(0, MLA_ROPE, 2, dtype=jnp.float32) / MLA_ROPE))
    ang = positions.astype(jnp.float32)[..., None] * inv_freq
    return jnp.cos(ang)[:, :, None, :], jnp.sin(ang)[:, :, None, :]


def apply_rope(t, cos, sin):
    half = t.shape[-1] // 2
    t1, t2 = t[..., :half], t[..., half:]
    return jnp.concatenate([t1 * cos - t2 * sin, t2 * cos + t1 * sin], axis=-1).astype(t.dtype)


def _s5_combine(e1, e2):
    a1r, a1i, b1r, b1i = e1
    a2r, a2i, b2r, b2i = e2
    return (a2r * a1r - a2i * a1i, a2r * a1i + a2i * a1r,
            a2r * b1r - a2i * b1i + b2r, a2r * b1i + a2i * b1r + b2i)


def s5_direction(ug, lam_re, lam_im, b_re, b_im, c_re, c_im, log_dt, reverse):
    f32 = jnp.float32
    s = ug.shape[1]
    dt = jnp.exp(log_dt.astype(f32))[:, None]
    lr, li = lam_re.astype(f32), lam_im.astype(f32)
    mag = jnp.exp(lr * dt)
    ab_re, ab_im = mag * jnp.cos(li * dt), mag * jnp.sin(li * dt)
    den = lr * lr + li * li
    nr, ni = ab_re - 1.0, ab_im
    coef_re = (nr * lr + ni * li) / den
    coef_im = (ni * lr - nr * li) / den
    bb_re = coef_re[..., None] * b_re - coef_im[..., None] * b_im
    bb_im = coef_re[..., None] * b_im + coef_im[..., None] * b_re
    uf = ug.astype(f32)
    bu_re = jnp.einsum('bsgc,gpc->bsgp', uf, bb_re)
    bu_im = jnp.einsum('bsgc,gpc->bsgp', uf, bb_im)
    shape = (1, s, S5_NGROUPS, S5_STATE)
    a_re = jnp.broadcast_to(ab_re, shape)
    a_im = jnp.broadcast_to(ab_im, shape)
    _, _, h_re, h_im = lax.associative_scan(_s5_combine, (a_re, a_im, bu_re, bu_im), reverse=reverse, axis=1)
    return jnp.einsum('bsgp,gcp->bsgc', h_re, c_re) - jnp.einsum('bsgp,gcp->bsgc', h_im, c_im)


def s5_branch(u, lam_re, lam_im, b_re, b_im, c_re, c_im, log_dt, d_skip, glu_w, glu_b):
    b, s = u.shape[:2]
    ug = u.reshape(b, s, S5_NGROUPS, S5_GROUP)
    y = (s5_direction(ug, lam_re, lam_im, b_re, b_im, c_re[0], c_im[0], log_dt[0], False)
         + s5_direction(ug, lam_re, lam_im, b_re, b_im, c_re[1], c_im[1], log_dt[1], True))
    y = y.reshape(b, s, S5_CH) + d_skip * u.astype(jnp.float32)
    y = jax.nn.gelu(y)
    y = y * jax.nn.sigmoid(y @ glu_w + glu_b)
    return y.astype(u.dtype)


def block_attention(q, k, v, scale):
    b, s, h, dk = q.shape
    nb = s // Q_BLOCK
    qb = q.reshape(b, nb, Q_BLOCK, h, dk).transpose(1, 0, 2, 3, 4)

    def one_block(qi):
        sc = jnp.einsum('bqhd,bkhd->bhqk', qi, k).astype(jnp.float32) * scale
        p = jax.nn.softmax(sc, axis=-1).astype(v.dtype)
        return jnp.einsum('bhqk,bkhd->bqhd', p, v)

    out = lax.map(one_block, qb)
    return out.transpose(1, 0, 2, 3, 4).reshape(b, s, h, v.shape[-1])


def mla_branch(cq, ckv, kpe, q_a_norm, kv_a_norm, w_uq, w_ukv, q_norm, k_norm, cos, sin):
    b, s = cq.shape[:2]
    q = (rms_norm(cq, q_a_norm) @ w_uq).reshape(b, s, MLA_HEADS, MLA_QK)
    kv = (rms_norm(ckv, kv_a_norm) @ w_ukv).reshape(b, s, MLA_HEADS, MLA_NOPE + MLA_V)
    k_nope, v = kv[..., :MLA_NOPE], kv[..., MLA_NOPE:]
    k = jnp.concatenate([k_nope, jnp.broadcast_to(kpe[:, :, None, :], (b, s, MLA_HEADS, MLA_ROPE))], axis=-1)
    q = rms_norm(q, q_norm)
    k = rms_norm(k, k_norm)
    q = jnp.concatenate([q[..., :MLA_NOPE], apply_rope(q[..., MLA_NOPE:], cos, sin)], axis=-1)
    k = jnp.concatenate([k[..., :MLA_NOPE], apply_rope(k[..., MLA_NOPE:], cos, sin)], axis=-1)
    o = block_attention(q, k, v, MLA_QK ** -0.5)
    return o.reshape(b, s, GROUP_W)


def centred_shift(t, mu_prev, mu_next):
    prev = jnp.pad(t[:, :-1], ((0, 0), (1, 0), (0, 0)))
    nxt = jnp.pad(t[:, 1:], ((0, 0), (0, 1), (0, 0)))
    return t + mu_prev * (prev - t) + mu_next * (nxt - t)


def wkv7_scan(r, w, k, v, kk, a, reverse):
    b, s, h, n = r.shape

    def step(state, inp):
        r_t, w_t, k_t, v_t, kk_t, a_t = inp
        sa = jnp.einsum('bhij,bhj->bhi', state, -kk_t)
        state = (state * w_t[:, :, None, :] + sa[..., None] * (kk_t * a_t)[:, :, None, :]
                 + v_t[..., None] * k_t[:, :, None, :])
        return state, jnp.einsum('bhij,bhj->bhi', state, r_t)

    seq_first = lambda t: jnp.swapaxes(t, 0, 1)
    init = jnp.zeros((b, h, n, n), jnp.float32)
    _, ys = lax.scan(step, init, (seq_first(r), seq_first(w), seq_first(k), seq_first(v),
                                   seq_first(kk), seq_first(a)), reverse=reverse)
    return jnp.swapaxes(ys, 0, 1)


def rwkv_branch(c, mu, w0, w2, a0, a2, k_k, k_a, r_k, ln_w, ln_b):
    f32 = jnp.float32
    b, s = c.shape[:2]
    c = centred_shift(c.astype(f32), mu[0], mu[1])
    r, k, v, wf, wb, af, ab = jnp.split(c, RWKV_SPLITS, axis=-1)
    heads = lambda t: t.reshape(b, s, RWKV_HEADS, RWKV_HEAD).astype(f32)
    kk = heads(k * k_k)
    kk = kk / jnp.maximum(jnp.sqrt(jnp.sum(kk * kk, axis=-1, keepdims=True)), 1e-12)
    rh, vh = heads(r), heads(v)

    def direction(w_in, a_in, d, reverse):
        w = -jax.nn.softplus(-(w0[d] + jnp.tanh(w_in) @ w2[d])) - 0.5
        decay = jnp.exp(-jnp.exp(w.astype(f32)))
        a = jax.nn.sigmoid(a0[d] + a_in @ a2[d])
        kd = heads(k * (1.0 + (a - 1.0) * k_a))
        return wkv7_scan(rh, heads(decay), kd, vh, kk, heads(a), reverse), kd

    yf, kf = direction(wf, af, 0, False)
    yb, kb = direction(wb, ab, 1, True)
    y = yf + yb
    mean = jnp.mean(y, axis=-1, keepdims=True)
    var = jnp.mean(jnp.square(y - mean), axis=-1, keepdims=True)
    y = ((y - mean) * lax.rsqrt(var + RWKV_LN_EPS)).reshape(b, s, GROUP_W) * ln_w + ln_b
    bonus = jnp.sum(rh * (kf + kb) * r_k, axis=-1, keepdims=True) * vh
    return y + bonus.reshape(b, s, GROUP_W)


def memory_branch(mq, mem_n, w_k, w_v, q_norm, k_norm):
    b, s = mq.shape[:2]
    m = mem_n.shape[1]
    q = rms_norm(mq.reshape(b, s, MEM_HEADS, MEM_HEAD_DIM), q_norm)
    k = rms_norm((mem_n @ w_k).reshape(b, m, MEM_HEADS, MEM_HEAD_DIM), k_norm)
    v = (mem_n @ w_v).reshape(b, m, MEM_HEADS, MEM_HEAD_DIM)
    sc = jnp.einsum('bshd,bmhd->bhsm', q, k).astype(jnp.float32) * MEM_HEAD_DIM ** -0.5
    p = jax.nn.softmax(sc, axis=-1).astype(v.dtype)
    return jnp.einsum('bhsm,bmhd->bshd', p, v).reshape(b, s, GROUP_W)


def setup_inputs(seed: int = 0) -> dict:
    key = jax.random.key(seed)
    keys = jax.random.split(key, 64)
    counter = [0]
    f32 = jnp.float32
    L = DEPTH

    def nk():
        kk_ = keys[counter[0]]
        counter[0] += 1
        return kk_

    def nrm(shape, scale):
        return jax.random.normal(nk(), shape, f32) * scale

    def gain(shape):
        return 1.0 + nrm(shape, 0.02)

    x = nrm((BATCH, SEQ, D_MODEL), 1.0)
    mem = nrm((BATCH, N_MEM, D_MODEL), 1.0)
    offs = jax.random.randint(nk(), (BATCH, 1), 0, 2048, dtype=jnp.int32)
    positions = (offs + jnp.arange(SEQ, dtype=jnp.int32)[None, :]).astype(jnp.int32)

    ln_g = gain((L, D_MODEL))
    w_in = nrm((L, D_MODEL, N_COLS), D_MODEL ** -0.5)
    w_out = nrm((L, MIX_WIDTH, D_MODEL), 0.5 * MIX_WIDTH ** -0.5)
    branch_g = gain((L, 3, GROUP_W))

    s5_lam_re = -0.5 + nrm((L, S5_NGROUPS, S5_STATE), 0.01)
    s5_lam_im = math.pi * jnp.arange(S5_STATE, dtype=f32) + nrm((L, S5_NGROUPS, S5_STATE), 0.01)
    s5_b_re = nrm((L, S5_NGROUPS, S5_STATE, S5_GROUP), (2 * S5_GROUP) ** -0.5)
    s5_b_im = nrm((L, S5_NGROUPS, S5_STATE, S5_GROUP), (2 * S5_GROUP) ** -0.5)
    s5_c_re = nrm((L, 2, S5_NGROUPS, S5_GROUP, S5_STATE), (2 * S5_STATE) ** -0.5)
    s5_c_im = nrm((L, 2, S5_NGROUPS, S5_GROUP, S5_STATE), (2 * S5_STATE) ** -0.5)
    s5_log_dt = jax.random.uniform(nk(), (L, 2, S5_NGROUPS), f32, math.log(0.001), math.log(0.1))
    s5_d = nrm((L, S5_CH), 1.0)
    s5_glu_w = nrm((L, S5_CH, S5_CH), S5_CH ** -0.5)
    s5_glu_b = nrm((L, S5_CH), 0.02)

    mla_q_a_norm = gain((L, Q_LORA))
    mla_kv_a_norm = gain((L, KV_LORA))
    mla_w_uq = nrm((L, Q_LORA, MLA_HEADS * MLA_QK), Q_LORA ** -0.5)
    mla_w_ukv = nrm((L, KV_LORA, MLA_HEADS * (MLA_NOPE + MLA_V)), KV_LORA ** -0.5)
    mla_q_norm = gain((L, MLA_QK))
    mla_k_norm = gain((L, MLA_QK))

    rwkv_mu = jax.random.uniform(nk(), (L, 2, RWKV_COLS), f32, 0.0, 0.5)
    ramp = (jnp.arange(GROUP_W, dtype=f32) / (GROUP_W - 1)) ** 0.85
    rwkv_w0 = -7.0 + 5.0 * ramp + 0.5 + nrm((L, 2, GROUP_W), 0.1)
    rwkv_w2 = nrm((L, 2, DECAY_LORA, GROUP_W), 0.1 * DECAY_LORA ** -0.5)
    rwkv_a0 = nrm((L, 2, GROUP_W), 0.1)
    rwkv_a2 = nrm((L, 2, AAA_LORA, GROUP_W), 0.5 * AAA_LORA ** -0.5)
    rwkv_k_k = 0.85 + nrm((L, GROUP_W), 0.02)
    rwkv_k_a = 1.0 + nrm((L, GROUP_W), 0.02)
    rwkv_r_k = nrm((L, RWKV_HEADS, RWKV_HEAD), 0.1)
    rwkv_ln_w = gain((L, GROUP_W))
    rwkv_ln_b = nrm((L, GROUP_W), 0.02)

    mem_norm_g = gain((L, D_MODEL))
    mem_w_k = nrm((L, D_MODEL, GROUP_W), D_MODEL ** -0.5)
    mem_w_v = nrm((L, D_MODEL, GROUP_W), D_MODEL ** -0.5)
    mem_q_norm = gain((L, MEM_HEAD_DIM))
    mem_k_norm = gain((L, MEM_HEAD_DIM))

    return {'x': x, 'mem': mem, 'positions': positions, 'ln_g': ln_g, 'w_in': w_in, 'w_out': w_out,
            'branch_g': branch_g, 's5_lam_re': s5_lam_re, 's5_lam_im': s5_lam_im, 's5_b_re': s5_b_re,
            's5_b_im': s5_b_im, 's5_c_re': s5_c_re, 's5_c_im': s5_c_im, 's5_log_dt': s5_log_dt,
            's5_d': s5_d, 's5_glu_w': s5_glu_w, 's5_glu_b': s5_glu_b, 'mla_q_a_norm': mla_q_a_norm,
            'mla_kv_a_norm': mla_kv_a_norm, 'mla_w_uq': mla_w_uq, 'mla_w_ukv': mla_w_ukv,
            'mla_q_norm': mla_q_norm, 'mla_k_norm': mla_k_norm, 'rwkv_mu': rwkv_mu, 'rwkv_w0': rwkv_w0,
            'rwkv_w2': rwkv_w2, 'rwkv_a0': rwkv_a0, 'rwkv_a2': rwkv_a2, 'rwkv_k_k': rwkv_k_k,
            'rwkv_k_a': rwkv_k_a, 'rwkv_r_k': rwkv_r_k, 'rwkv_ln_w': rwkv_ln_w, 'rwkv_ln_b': rwkv_ln_b,
            'mem_norm_g': mem_norm_g, 'mem_w_k': mem_w_k, 'mem_w_v': mem_w_v,
            'mem_q_norm': mem_q_norm, 'mem_k_norm': mem_k_norm}


def reference(x, mem, positions, ln_g, w_in, w_out, branch_g, s5_lam_re, s5_lam_im, s5_b_re, s5_b_im,
              s5_c_re, s5_c_im, s5_log_dt, s5_d, s5_glu_w, s5_glu_b, mla_q_a_norm, mla_kv_a_norm,
              mla_w_uq, mla_w_ukv, mla_q_norm, mla_k_norm, rwkv_mu, rwkv_w0, rwkv_w2, rwkv_a0, rwkv_a2,
              rwkv_k_k, rwkv_k_a, rwkv_r_k, rwkv_ln_w, rwkv_ln_b, mem_norm_g, mem_w_k, mem_w_v,
              mem_q_norm, mem_k_norm):
    cos, sin = rope_tables(positions)
    splits = [int(c) for c in np.cumsum(COL_WIDTHS)[:-1]]
    for l in range(DEPTH):
        h = rms_norm(x, ln_g[l])
        (a_u, a_gate, b_cq, b_ckv, b_kpe, b_gate, c_cols, c_gate, m_q, m_gate) = jnp.split(
            h @ w_in[l], splits, axis=-1)
        ya = s5_branch(a_u, s5_lam_re[l], s5_lam_im[l], s5_b_re[l], s5_b_im[l], s5_c_re[l], s5_c_im[l],
                       s5_log_dt[l], s5_d[l], s5_glu_w[l], s5_glu_b[l])
        yb = mla_branch(b_cq, b_ckv, b_kpe, mla_q_a_norm[l], mla_kv_a_norm[l], mla_w_uq[l], mla_w_ukv[l],
                        mla_q_norm[l], mla_k_norm[l], cos, sin)
        yc = rwkv_branch(c_cols, rwkv_mu[l], rwkv_w0[l], rwkv_w2[l], rwkv_a0[l], rwkv_a2[l], rwkv_k_k[l],
                         rwkv_k_a[l], rwkv_r_k[l], rwkv_ln_w[l], rwkv_ln_b[l]).astype(x.dtype)
        mem_n = rms_norm(mem, mem_norm_g[l])
        ym = memory_branch(m_q, mem_n, mem_w_k[l], mem_w_v[l], mem_q_norm[l], mem_k_norm[l])
        merged = jnp.concatenate([
            rms_norm(ya, branch_g[l, 0]) * jax.nn.silu(a_gate),
            rms_norm(yb, branch_g[l, 1]) * jax.nn.silu(b_gate),
            yc * jax.nn.silu(c_gate),
            rms_norm(ym, branch_g[l, 2]) * jax.nn.silu(m_gate)], axis=-1)
        x = x + (merged @ w_out[l]).astype(x.dtype)
    return x
```

```python
import numpy as np
from contextlib import ExitStack
import concourse.bass as bass
import concourse.mybir as mybir
from concourse.bass_utils import run_bass_kernel_spmd

F32 = mybir.dt.float32
BF16 = mybir.dt.bfloat16
I32 = mybir.dt.int32
ALU = mybir.AluOpType
AF = mybir.ActivationFunctionType
AX = mybir.AxisListType

D = 4096
S = 4096
L = 4
NCOLS = 10688
NT = S // 128
EPS = 1e-6
C_AU, C_AG, C_CQ, C_CKV, C_KPE, C_BG, C_RW, C_CG, C_MQ, C_MG = 0, 1024, 2048, 2944, 3200, 3264, 4288, 7616, 8640, 9664
NDMA = 8


class Prog:
    def __init__(self, nc, es):
        self.nc = nc
        self.E = {'pe': nc.tensor, 'act': nc.scalar, 'dve': nc.vector, 'pool': nc.gpsimd, 'sp': nc.sync}
        self.sems = {}
        for e in ['pe', 'act', 'dve', 'pool']:
            self.sems[e] = es.enter_context(nc.semaphore('s_' + e))
        for q in ['sp', 'pool']:
            for i in range(NDMA):
                self.sems[('d', q, i)] = es.enter_context(nc.semaphore(f'd_{q}_{i}'))
        self.cnt = {e: 0 for e in ['pe', 'act', 'dve', 'pool']}
        self.dma_n = {'sp': 0, 'pool': 0}
        self.waited = {}
        self.bufs = {}
        self.nins = 0

    def _wait(self, eng, key, val):
        if self.waited.get((eng, key), 0) >= val:
            return
        self.E[eng].wait_ge(self.sems[key], val)
        self.waited[(eng, key)] = val

    def _deps(self, reads, writes):
        deps = {}
        for b in reads:
            st = self.bufs.get(b)
            if st and st[0] is not None:
                k, v = st[0]
                if deps.get(k, 0) < v:
                    deps[k] = v
        for b in writes:
            st = self.bufs.get(b)
            if st:
                if st[0] is not None:
                    k, v = st[0]
                    if deps.get(k, 0) < v:
                        deps[k] = v
                for k, v in st[1].items():
                    if deps.get(k, 0) < v:
                        deps[k] = v
        return deps

    def _commit(self, tk, reads, writes):
        k, v = tk
        for b in reads:
            st = self.bufs.get(b)
            if st is None:
                st = self.bufs[b] = [None, {}]
            if st[1].get(k, 0) < v:
                st[1][k] = v
        for b in writes:
            self.bufs[b] = [tk, {}]

    def op(self, eng, fn, reads=(), writes=()):
        deps = self._deps(reads, writes)
        for k, v in deps.items():
            if k == 'pe' and eng == 'pe':
                continue
            self._wait(eng, k, v)
        ins = fn(self.E[eng])
        self.cnt[eng] += 1
        ins.then_inc(self.sems[eng], 1)
        self._commit((eng, self.cnt[eng]), reads, writes)
        self.nins += 1

    def dma(self, q, out, in_, reads=(), writes=(), **kw):
        deps = self._deps(reads, writes)
        n = self.dma_n[q]
        self.dma_n[q] += 1
        key = ('d', q, n % NDMA)
        val = 16 * (n // NDMA + 1)
        if n >= NDMA:
            deps[key] = max(deps.get(key, 0), val - 16)
        for k, v in deps.items():
            self._wait(q, k, v)
        ins = self.E[q].dma_start(out=out, in_=in_, **kw)
        ins.then_inc(self.sems[key], 16)
        self._commit((key, val), reads, writes)
        self.nins += 1

    def barrier(self):
        for e in ['pe', 'act', 'dve', 'pool', 'sp']:
            for k in self.sems:
                if isinstance(k, tuple):
                    n = self.dma_n[k[1]]
                    v = 16 * ((n - k[2] + NDMA - 1) // NDMA) if n > k[2] else 0
                else:
                    v = self.cnt[k]
                if v > 0:
                    self._wait(e, k, v)
        self.bufs.clear()


def build(n_layers=L, phases='AMBCDE', dbg=False, RWDT=F32):
    nc = bass.Bass("TRN2", target_bir_lowering=False)
    es = ExitStack()

    in_names = []

    def din(name, shape, dt=F32):
        in_names.append(name)
        return nc.dram_tensor(name, list(shape), dt, kind="ExternalInput").ap()

    def dscr(name, shape, dt=F32):
        return nc.dram_tensor(name, list(shape), dt, kind="ExternalOutput" if dbg else "Internal").ap()

    x_in = din("x", [S, D])
    mem_in = din("mem", [256, D])
    pos_in = din("positions", [1, S], I32)
    ln_g = din("ln_g", [L, D])
    w_in = din("w_in", [L, D, NCOLS]) if ('A' in phases or not dbg) else None
    w_out = din("w_out", [L, D, D]) if ('E' in phases or not dbg) else None
    cst_ident = din("c_ident", [128, 128])
    c_masks_t = din("c_masks", [4, 128, 128])
    c_masks = [c_masks_t[i, :, :] for i in range(4)]
    c_iota = din("c_iota", [1, 512])
    c_invfreq = din("c_invfreq", [1, 32])
    branch_g = din("branch_g", [L, 3072])
    mla_q_a_norm = din("mla_q_a_norm", [L, 896]); mla_kv_a_norm = din("mla_kv_a_norm", [L, 256])
    mla_w_uq = din("mla_w_uq", [L, 896, 1536]); mla_w_ukv = din("mla_w_ukv", [L, 256, 2048])
    mla_q_norm = din("mla_q_norm", [L, 192]); mla_k_norm = din("mla_k_norm", [L, 192])
    mem_norm_g = din("mem_norm_g", [L, D]); mem_w_k = din("mem_w_k", [L, D, 1024]) if ('M' in phases or not dbg) else None
    mem_w_v = din("mem_w_v", [L, D, 1024]) if ('M' in phases or not dbg) else None
    mem_q_norm = din("mem_q_norm", [L, 256]); mem_k_norm = din("mem_k_norm", [L, 256])
    s5_lam_re = din("s5_lam_re", [L, 64, 64]); s5_lam_im = din("s5_lam_im", [L, 64, 64])
    s5_b_re = din("s5_b_re", [L, 64, 64, 16]); s5_b_im = din("s5_b_im", [L, 64, 64, 16])
    s5_c_re = din("s5_c_re", [L, 2, 64, 16, 64]); s5_c_im = din("s5_c_im", [L, 2, 64, 16, 64])
    s5_log_dt = din("s5_log_dt", [L, 2, 64]); s5_d = din("s5_d", [L, 1024]); s5_glu_w = din("s5_glu_w", [L, 1024, 1024])
    s5_glu_b = din("s5_glu_b", [L, 1024])
    rwkv_mu = din("rwkv_mu", [L, 2, 3328]); rwkv_w0 = din("rwkv_w0", [L, 2, 1024]); rwkv_w2 = din("rwkv_w2", [L, 2, 64, 1024])
    rwkv_a0 = din("rwkv_a0", [L, 2, 1024]); rwkv_a2 = din("rwkv_a2", [L, 2, 64, 1024]); rwkv_k_k = din("rwkv_k_k", [L, 1024])
    rwkv_k_a = din("rwkv_k_a", [L, 1024]); rwkv_r_k = din("rwkv_r_k", [L, 1024]); rwkv_ln_w = din("rwkv_ln_w", [L, 1024])
    rwkv_ln_b = din("rwkv_ln_b", [L, 1024])
    y_out = nc.dram_tensor("y", [S, D], F32, kind="ExternalOutput").ap()
    proj = dscr("proj", [S, NCOLS])
    wbf_in = nc.dram_tensor("wbf_in", [D, NCOLS], BF16, kind="Internal").ap()
    rwc = dscr("rwc", [S, 3328])
    ysc = dscr("ysc", [2, S, 1024])
    bon = dscr("bon", [2, S, 16])
    br = dscr("br", [S, 4096])
    ygd = dscr("ygd", [S, 1024])
    wbf_out = nc.dram_tensor("wbf_out", [D, D], BF16, kind="Internal").ap()
    qT_d = nc.dram_tensor("qT_d", [8, 192, S], BF16, kind="Internal").ap()
    kT_d = nc.dram_tensor("kT_d", [8, 192, S], BF16, kind="Internal").ap()
    v_d = nc.dram_tensor("v_d", [S, 1024], BF16, kind="Internal").ap()

    p = Prog(nc, es)

    uniq = [0]

    def sb(stack, name, shape, dt=F32):
        uniq[0] += 1
        return stack.enter_context(nc.sbuf_tensor(f"{name}_{uniq[0]}", list(shape), dt))

    def ps(stack, name, shape, dt=F32):
        uniq[0] += 1
        return stack.enter_context(nc.psum_tensor(f"{name}_{uniq[0]}", list(shape), dt))

    ident_f = sb(es, "ident_f", [128, 128], F32)
    ident_b = sb(es, "ident_b", [128, 128], BF16)
    p.dma('sp', ident_f[:], cst_ident[:, :], writes=['ident_f'])
    p.op('dve', lambda e: e.tensor_copy(ident_b[:], ident_f[:]), reads=['ident_f'], writes=['ident_b'])

    eps_t = sb(es, "eps_t", [128, 1], F32)
    p.op('dve', lambda e: e.memset(eps_t[:], EPS), writes=['eps_t'])

    def phase_A(l, xsrc):
        for r in range(0, D, 512):
            p.dma('pool', wbf_in[r:r + 512, :], w_in[l, r:r + 512, :], writes=[('wbf_in', r)])
        with ExitStack() as st:
            gt = sb(st, "A_g", [128, D], F32)
            xt = sb(st, "A_x", [128, D], F32)
            hb = sb(st, "A_hb", [128, D], BF16)
            hT = sb(st, "A_hT", [128, 32, 1024], BF16)
            W = [sb(st, f"A_W{i}", [128, 32, 512], BF16) for i in range(2)]
            ob = [sb(st, f"A_ob{i}", [128, 512], F32) for i in range(4)]
            ss = sb(st, "A_ss", [128, 1], F32)
            rstd = sb(st, "A_rstd", [128, 1], F32)
            ptr = [ps(st, f"A_pt{i}", [128, 8, 128], BF16) for i in range(2)]
            pmm = [ps(st, f"A_pm{i}", [128, 512], F32) for i in range(4)]
            p.dma('sp', gt[:], ln_g[l:l + 1, :].partition_broadcast(128), writes=['A_g'])
            nch = (NCOLS + 511) // 512
            wi = 0
            for g in range(S // 1024):
                for ti in range(8):
                    t0 = g * 1024 + ti * 128
                    p.dma('sp', xt[:], xsrc[t0:t0 + 128, :], writes=['A_x'])
                    p.op('act', lambda e: e.activation(hb[:], xt[:], AF.Square, accum_out=ss[:]),
                         reads=['A_x'], writes=['A_hb', 'A_ss'])
                    p.op('act', lambda e: e.activation(rstd[:], ss[:], AF.Sqrt, bias=eps_t[:], scale=1.0 / D),
                         reads=['A_ss', 'eps_t'], writes=['A_rstd'])
                    p.op('dve', lambda e: e.reciprocal(rstd[:], rstd[:]), reads=['A_rstd'], writes=['A_rstd'])
                    p.op('dve', lambda e: e.scalar_tensor_tensor(hb[:], xt[:], rstd[:], gt[:], ALU.mult, ALU.mult),
                         reads=['A_x', 'A_rstd', 'A_g'], writes=['A_hb'])
                    for k8 in range(4):
                        pt = ptr[k8 % 2]
                        for kk in range(8):
                            k = k8 * 8 + kk
                            p.op('pe', lambda e: e.transpose(pt[:, kk, :], hb[:, k * 128:(k + 1) * 128], ident_b[:]),
                                 reads=['A_hb', 'ident_b'], writes=[('A_pt', k8 % 2)])
                        eng = 'act' if k8 % 2 == 0 else 'dve'
                        dst = hT[:, k8 * 8:(k8 + 1) * 8, ti * 128:(ti + 1) * 128]
                        if eng == 'act':
                            p.op('act', lambda e: e.copy(dst, pt[:]), reads=[('A_pt', k8 % 2)], writes=[('A_hT', ti)])
                        else:
                            p.op('dve', lambda e: e.tensor_copy(dst, pt[:]), reads=[('A_pt', k8 % 2)], writes=[('A_hT', ti)])
                for ci in range(nch):
                    n0 = ci * 512
                    nw = min(512, NCOLS - n0)
                    Wt = W[wi % 2]
                    for k4 in range(4):
                        p.dma('sp', Wt[:, k4 * 8:(k4 + 1) * 8, 0:nw],
                              wbf_in[k4 * 1024:(k4 + 1) * 1024, n0:n0 + nw].rearrange("(k p) n -> p k n", p=128),
                              reads=[('wbf_in', (k4 * 1024) // 512 * 512), ('wbf_in', (k4 * 1024) // 512 * 512 + 512)],
                              writes=[('A_W', wi % 2)])
                    for ti in range(8):
                        t0 = g * 1024 + ti * 128
                        j = (ci * 8 + ti) % 4
                        pm = pmm[j]
                        for k in range(32):
                            p.op('pe', lambda e: e.matmul(pm[:, 0:nw], hT[:, k, ti * 128:(ti + 1) * 128], Wt[:, k, 0:nw],
                                                         start=(k == 0), stop=(k == 31)),
                                 reads=[('A_hT', ti), ('A_W', wi % 2)], writes=[('A_pm', j)])
                        if j % 2 == 0:
                            p.op('act', lambda e: e.copy(ob[j][:, 0:nw], pm[:, 0:nw]), reads=[('A_pm', j)], writes=[('A_ob', j)])
                        else:
                            p.op('dve', lambda e: e.tensor_copy(ob[j][:, 0:nw], pm[:, 0:nw]), reads=[('A_pm', j)], writes=[('A_ob', j)])
                        p.dma('sp', proj[t0:t0 + 128, n0:n0 + nw], ob[j][:, 0:nw], reads=[('A_ob', j)],
                              writes=[('proj', t0 // 128, ci)])
                    wi += 1
        p.barrier()


    RW = 3328
    NEG_E = -float(np.exp(-0.5))
    RW_DT = RWDT

    def phase_D(l):
        with ExitStack() as st:
            mup = sb(st, "D0_mup", [128, RW]); mun = sb(st, "D0_mun", [128, RW]); m0 = sb(st, "D0_m0", [128, RW])
            ct = sb(st, "D0_c", [128, RW]); pt_ = sb(st, "D0_p", [128, RW]); nt = sb(st, "D0_n", [128, RW])
            p.dma('sp', mup[:], rwkv_mu[l, 0:1, :].partition_broadcast(128), writes=['D0_mup'])
            p.dma('sp', mun[:], rwkv_mu[l, 1:2, :].partition_broadcast(128), writes=['D0_mun'])
            p.op('dve', lambda e: e.tensor_tensor(m0[:], mup[:], mun[:], ALU.add), reads=['D0_mup', 'D0_mun'], writes=['D0_m0'])
            p.op('dve', lambda e: e.tensor_scalar(m0[:], m0[:], -1.0, 1.0, ALU.mult, ALU.add), reads=['D0_m0'], writes=['D0_m0'])
            for i in range(NT):
                t0 = i * 128
                p.dma('sp', ct[:], proj[t0:t0 + 128, C_RW:C_RW + RW], reads=[('proj', i, 'all')], writes=['D0_c'])
                if i == 0:
                    p.op('pool', lambda e: e.memset(pt_[:], 0.0), writes=['D0_p'])
                    p.dma('sp', pt_[1:128, :], proj[0:127, C_RW:C_RW + RW], reads=[('proj', 0, 'all')], writes=['D0_p'])
                else:
                    p.dma('sp', pt_[:], proj[t0 - 1:t0 + 127, C_RW:C_RW + RW], reads=[('proj', i, 'all'), ('proj', i - 1, 'all')], writes=['D0_p'])
                if i == NT - 1:
                    p.op('pool', lambda e: e.memset(nt[:], 0.0), writes=['D0_n'])
                    p.dma('sp', nt[0:127, :], proj[t0 + 1:t0 + 128, C_RW:C_RW + RW], reads=[('proj', i, 'all')], writes=['D0_n'])
                else:
                    p.dma('sp', nt[:], proj[t0 + 1:t0 + 129, C_RW:C_RW + RW], reads=[('proj', i, 'all'), ('proj', i + 1, 'all')], writes=['D0_n'])
                p.op('dve', lambda e: e.tensor_tensor(ct[:], ct[:], m0[:], ALU.mult), reads=['D0_c', 'D0_m0'], writes=['D0_c'])
                p.op('pool', lambda e: e.tensor_tensor(pt_[:], pt_[:], mup[:], ALU.mult), reads=['D0_p', 'D0_mup'], writes=['D0_p'])
                p.op('pool', lambda e: e.tensor_tensor(nt[:], nt[:], mun[:], ALU.mult), reads=['D0_n', 'D0_mun'], writes=['D0_n'])
                p.op('dve', lambda e: e.tensor_tensor(ct[:], ct[:], pt_[:], ALU.add), reads=['D0_c', 'D0_p'], writes=['D0_c'])
                p.op('dve', lambda e: e.tensor_tensor(ct[:], ct[:], nt[:], ALU.add), reads=['D0_c', 'D0_n'], writes=['D0_c'])
                p.dma('sp', rwc[t0:t0 + 128, :], ct[:], reads=['D0_c'], writes=[('rwc', i)])
        p.barrier()
        with ExitStack() as st:
            def bc(name, src):
                t = sb(st, name, [128, 1024])
                p.dma('sp', t[:], src.partition_broadcast(128), writes=[name])
                return t
            kk_c = bc("D_kk_c", rwkv_k_k[l:l + 1, :]); ka_c = bc("D_ka_c", rwkv_k_a[l:l + 1, :])
            rk_c = bc("D_rk_c", rwkv_r_k[l:l + 1, :])
            c1 = sb(st, "D_c1", [128, 1024])
            p.op('dve', lambda e: e.tensor_scalar(c1[:], ka_c[:], -1.0, 1.0, ALU.mult, ALU.add), reads=['D_ka_c'], writes=['D_c1'])
            w0_c = sb(st, "D_w0", [128, 1024]); a0_c = sb(st, "D_a0", [128, 1024])
            w2_t = sb(st, "D_w2", [64, 1024]); a2_t = sb(st, "D_a2", [64, 1024])
            mS = sb(st, "D_mS", [128, 128]); mI = sb(st, "D_mI", [128, 128]); mST = sb(st, "D_mST", [128, 128])
            imask = sb(st, "D_imask", [64, 1024])
            b4 = lambda t: t[:].unsqueeze(1).broadcast_to([128, 4, 128])
            v4 = lambda a: a.rearrange("p (a b) -> p a b", a=4)
            triI = sb(st, "D_triI", [128, 128]); triC = sb(st, "D_triC", [128, 128])
            identr = sb(st, "D_identr", [128, 128], RW_DT)
            p.op('dve', lambda e: e.tensor_copy(identr[:], ident_f[:]), reads=['ident_f'], writes=['D_identr'])
            for h in range(16):
                p.op('pool', lambda e: e.tensor_copy(imask[:, h * 64:(h + 1) * 64], ident_f[0:64, 0:64]), reads=['ident_f'], writes=['D_imask'])
            rw = sb(st, "D_rw", [128, RW])
            kk = sb(st, "D_kk", [128, 1024]); ld = sb(st, "D_ld", [128, 1024]); a_t = sb(st, "D_a", [128, 1024])
            kd = sb(st, "D_kd", [128, 1024]); ba = sb(st, "D_ba", [128, 1024]); tmp = sb(st, "D_tmp", [128, 1024])
            Ab = sb(st, "D_Ab", [128, 1024]); Rb = sb(st, "D_Rb", [128, 1024]); Bb = sb(st, "D_Bb", [128, 1024]); Kb = sb(st, "D_Kb", [128, 1024])
            Abr = sb(st, "D_Abr", [128, 1024], RW_DT)
            Bt = sb(st, "D_Bt", [128, 1024], RW_DT); Kt = sb(st, "D_Kt", [128, 1024], RW_DT); Vr = sb(st, "D_Vr", [128, 1024], RW_DT)
            ydg = sb(st, "D_ydg", [64, 1024], RW_DT)
            sm = sb(st, "D_sm", [128, 64]); smT = sb(st, "D_smT", [64, 2, 128])
            hs = sb(st, "D_hs", [128, 16]); hs2 = sb(st, "D_hs2", [128, 16])
            AbT = sb(st, "D_AbT", [64, 16, 128], RW_DT); RbT = sb(st, "D_RbT", [64, 16, 128], RW_DT)
            BbT = sb(st, "D_BbT", [64, 16, 128], RW_DT); KbT = sb(st, "D_KbT", [64, 16, 128], RW_DT)
            Q = [sb(st, f"D_Q{i}", [128, 512], RW_DT) for i in range(2)]
            QT = [sb(st, f"D_QT{i}", [128, 512], RW_DT) for i in range(2)]
            P = sb(st, "D_P", [128, 512]); Pr = sb(st, "D_Pr", [128, 512], RW_DT)
            MrbT = sb(st, "D_MrbT", [128, 512], RW_DT); LakT = sb(st, "D_LakT", [128, 512], RW_DT); MrkT = sb(st, "D_MrkT", [128, 512], RW_DT)
            AXt = sb(st, "D_AX", [128, 4, 128], RW_DT); AU = sb(st, "D_AU", [128, 4, 128], RW_DT)
            RhT = sb(st, "D_RhT", [64, 16, 128], RW_DT); GT = sb(st, "D_GT", [64, 1024], RW_DT)
            Hh = sb(st, "D_H", [64, 1024]); Yh = sb(st, "D_Yh", [128, 1024])
            ST = sb(st, "D_ST", [64, 1024]); STr = sb(st, "D_STr", [64, 1024], RW_DT)
            pb = [ps(st, f"D_pb{i}", [128, 512]) for i in range(8)]
            pbi = [0]

            def nb():
                i = pbi[0] % 8
                pbi[0] += 1
                return i

            for d in range(2):
                p.dma('sp', w0_c[:], rwkv_w0[l, d:d + 1, :].partition_broadcast(128), writes=['D_w0'])
                p.dma('sp', a0_c[:], rwkv_a0[l, d:d + 1, :].partition_broadcast(128), writes=['D_a0'])
                p.dma('sp', w2_t[:], rwkv_w2[l, d, :, :], writes=['D_w2'])
                p.dma('sp', a2_t[:], rwkv_a2[l, d, :, :], writes=['D_a2'])
                cm = c_masks
                p.dma('sp', mS[:], cm[0 if d == 0 else 1], writes=['D_mS'])
                p.dma('sp', mI[:], cm[2 if d == 0 else 3], writes=['D_mI'])
                p.dma('sp', mST[:], cm[1 if d == 0 else 0], writes=['D_mST'])
                p.dma('sp', triI[:], cm[2 if d == 0 else 3], writes=['D_triI'])
                p.dma('sp', triC[:], cm[1 if d == 0 else 0], writes=['D_triC'])
                p.op('dve', lambda e: e.memset(ST[:], 0.0), writes=['D_ST'])
                p.op('dve', lambda e: e.memset(STr[:], 0.0), writes=['D_STr'])
                order = range(NT) if d == 0 else range(NT - 1, -1, -1)
                for c in order:
                    t0 = c * 128
                    p.dma('sp', rw[:], rwc[t0:t0 + 128, :], reads=[('rwc', c)], writes=['D_rw'])
                    r_ = rw[:, 0:1024]; k_ = rw[:, 1024:2048]; v_ = rw[:, 2048:3072]
                    win = rw[:, 3072 + 64 * d:3136 + 64 * d]; ain = rw[:, 3200 + 64 * d:3264 + 64 * d]
                    p.op('dve', lambda e: e.tensor_tensor(kk[:], k_, kk_c[:], ALU.mult), reads=['D_rw', 'D_kk_c'], writes=['D_kk'])
                    p.op('pool', lambda e: e.tensor_tensor(tmp[:], kk[:], kk[:], ALU.mult), reads=['D_kk'], writes=['D_tmp'])
                    p.op('dve', lambda e: e.tensor_reduce(hs[:], tmp[:].rearrange("p (h j) -> p h j", h=16), AX.X, ALU.add), reads=['D_tmp'], writes=['D_hs'])
                    p.op('act', lambda e: e.activation(hs[:], hs[:], AF.Sqrt), reads=['D_hs'], writes=['D_hs'])
                    p.op('dve', lambda e: e.tensor_scalar(hs[:], hs[:], 1e-12, None, ALU.max), reads=['D_hs'], writes=['D_hs'])
                    p.op('dve', lambda e: e.reciprocal(hs[:], hs[:]), reads=['D_hs'], writes=['D_hs'])
                    p.op('dve', lambda e: e.tensor_tensor(kk[:].rearrange("p (h j) -> p h j", h=16), kk[:].rearrange("p (h j) -> p h j", h=16),
                                                         hs[:].unsqueeze(2).broadcast_to([128, 16, 64]), ALU.mult), reads=['D_kk', 'D_hs'], writes=['D_kk'])
                    p.op('act', lambda e: e.activation(sm[:], win, AF.Tanh), reads=['D_rw'], writes=['D_sm'])
                    b0 = nb()
                    p.op('pe', lambda e: e.transpose(pb[b0][0:64, 0:128], sm[:], ident_f[:]), reads=['D_sm', 'ident_f'], writes=[('D_pb', b0)])
                    p.op('pe', lambda e: e.transpose(pb[b0][0:64, 128:256], ain, ident_f[:]), reads=['D_rw', 'ident_f'], writes=[('D_pb', b0)])
                    p.op('act', lambda e: e.copy(smT[:].rearrange("p a b -> p (a b)"), pb[b0][0:64, 0:256]), reads=[('D_pb', b0)], writes=['D_smT'])
                    for half in range(2):
                        cs_ = slice(half * 512, (half + 1) * 512)
                        b1 = nb()
                        p.op('pe', lambda e: e.matmul(pb[b1][:, :], smT[:, 0, :], w2_t[:, cs_], start=True, stop=True),
                             reads=['D_smT', 'D_w2'], writes=[('D_pb', b1)])
                        p.op('dve', lambda e: e.tensor_tensor(ld[:, cs_], pb[b1][:, :], w0_c[:, cs_], ALU.add), reads=[('D_pb', b1), 'D_w0'], writes=['D_ld'])
                        b2 = nb()
                        p.op('pe', lambda e: e.matmul(pb[b2][:, :], smT[:, 1, :], a2_t[:, cs_], start=True, stop=True),
                             reads=['D_smT', 'D_a2'], writes=[('D_pb', b2)])
                        p.op('dve', lambda e: e.tensor_tensor(a_t[:, cs_], pb[b2][:, :], a0_c[:, cs_], ALU.add), reads=[('D_pb', b2), 'D_a0'], writes=['D_a'])
                    p.op('act', lambda e: e.activation(ld[:], ld[:], AF.Sigmoid), reads=['D_ld'], writes=['D_ld'])
                    p.op('act', lambda e: e.activation(a_t[:], a_t[:], AF.Sigmoid), reads=['D_a'], writes=['D_a'])
                    p.op('pool', lambda e: e.tensor_scalar(ld[:], ld[:], NEG_E, None, ALU.mult), reads=['D_ld'], writes=['D_ld'])
                    p.op('dve', lambda e: e.tensor_tensor(tmp[:], a_t[:], ka_c[:], ALU.mult), reads=['D_a', 'D_ka_c'], writes=['D_tmp'])
                    p.op('dve', lambda e: e.tensor_tensor(tmp[:], tmp[:], c1[:], ALU.add), reads=['D_tmp', 'D_c1'], writes=['D_tmp'])
                    p.op('dve', lambda e: e.tensor_tensor(kd[:], tmp[:], k_, ALU.mult), reads=['D_tmp', 'D_rw'], writes=['D_kd'])
                    p.op('pool', lambda e: e.tensor_tensor(ba[:], kk[:], a_t[:], ALU.mult), reads=['D_kk', 'D_a'], writes=['D_ba'])
                    p.op('pool', lambda e: e.tensor_tensor(tmp[:], kd[:], rk_c[:], ALU.mult), reads=['D_kd', 'D_rk_c'], writes=['D_tmp'])
                    p.op('pool', lambda e: e.tensor_tensor(tmp[:], tmp[:], r_, ALU.mult), reads=['D_tmp', 'D_rw'], writes=['D_tmp'])
                    p.op('dve', lambda e: e.tensor_reduce(hs2[:], tmp[:].rearrange("p (h j) -> p h j", h=16), AX.X, ALU.add), reads=['D_tmp'], writes=['D_hs2'])
                    p.dma('sp', bon[d, t0:t0 + 128, :], hs2[:], reads=['D_hs2'], writes=[('bon', d, c)])
                    for half in range(2):
                        cs_ = slice(half * 512, (half + 1) * 512)
                        bcs = nb()
                        p.op('pe', lambda e: e.matmul(pb[bcs][:, :], triI[:], ld[:, cs_], start=True, stop=True), reads=['D_triI', 'D_ld'], writes=[('D_pb', bcs)])
                        brm = nb()
                        p.op('pe', lambda e: e.matmul(pb[brm][:, :], triC[:], ld[:, cs_], start=True, stop=True), reads=['D_triC', 'D_ld'], writes=[('D_pb', brm)])
                        p.op('dve', lambda e: e.tensor_tensor(tmp[:, cs_], pb[bcs][:, :], ld[:, cs_], ALU.subtract), reads=[('D_pb', bcs), 'D_ld'], writes=['D_tmp'])
                        p.op('act', lambda e: e.activation(tmp[:, cs_], tmp[:, cs_], AF.Exp), reads=['D_tmp'], writes=['D_tmp'])
                        p.op('dve', lambda e: e.scalar_tensor_tensor(Ab[:, cs_], kk[:, cs_], -1.0, tmp[:, cs_], ALU.mult, ALU.mult), reads=['D_kk', 'D_tmp'], writes=['D_Ab'])
                        p.op('act', lambda e: e.activation(Rb[:, cs_], pb[bcs][:, :], AF.Exp), reads=[('D_pb', bcs)], writes=['D_Rb'])
                        p.op('act', lambda e: e.activation(Kt[0:64, cs_] if False else tmp[0:64, cs_], pb[brm][0:64, :], AF.Exp), reads=[('D_pb', brm), 'D_tmp'], writes=['D_tmp'])
                        p.op('dve', lambda e: e.tensor_tensor(tmp[0:64, cs_], tmp[0:64, cs_], Rb[0:64, cs_], ALU.mult), reads=['D_tmp', 'D_Rb'], writes=['D_tmp'])
                        p.op('dve', lambda e: e.tensor_tensor(ydg[:, cs_], tmp[0:64, cs_], imask[:, cs_], ALU.mult), reads=['D_tmp', 'D_imask'], writes=['D_ydg'])
                        p.op('pool', lambda e: e.tensor_tensor(Rb[:, cs_], Rb[:, cs_], r_[:, cs_] if False else rw[:, half * 512:(half + 1) * 512], ALU.mult), reads=['D_Rb', 'D_rw', 'D_tmp'], writes=['D_Rb'])
                        p.op('act', lambda e: e.activation(tmp[:, cs_], pb[bcs][:, :], AF.Exp, scale=-1.0), reads=[('D_pb', bcs), 'D_tmp', 'D_ydg'], writes=['D_tmp'])
                        p.op('dve', lambda e: e.tensor_tensor(Bb[:, cs_], ba[:, cs_], tmp[:, cs_], ALU.mult), reads=['D_ba', 'D_tmp'], writes=['D_Bb'])
                        p.op('pool', lambda e: e.tensor_tensor(Kb[:, cs_], kd[:, cs_], tmp[:, cs_], ALU.mult), reads=['D_kd', 'D_tmp'], writes=['D_Kb'])
                        p.op('act', lambda e: e.activation(tmp[:, cs_], pb[brm][:, :], AF.Exp), reads=[('D_pb', brm), 'D_tmp', 'D_Bb', 'D_Kb'], writes=['D_tmp'])
                        p.op('dve', lambda e: e.tensor_tensor(Bt[:, cs_], ba[:, cs_], tmp[:, cs_], ALU.mult), reads=['D_ba', 'D_tmp'], writes=['D_Bt'])
                        p.op('pool', lambda e: e.tensor_tensor(Kt[:, cs_], kd[:, cs_], tmp[:, cs_], ALU.mult), reads=['D_kd', 'D_tmp'], writes=['D_Kt'])
                    p.op('act', lambda e: e.copy(Vr[:], v_), reads=['D_rw'], writes=['D_Vr'])
                    p.op('act', lambda e: e.copy(Abr[:], Ab[:]), reads=['D_Ab'], writes=['D_Abr'])
                    for (src, dstT, nm) in ((Ab, AbT, 'D_AbT'), (Rb, RbT, 'D_RbT'), (Bb, BbT, 'D_BbT'), (Kb, KbT, 'D_KbT')):
                        srcn = {'D_AbT': 'D_Ab', 'D_RbT': 'D_Rb', 'D_BbT': 'D_Bb', 'D_KbT': 'D_Kb'}[nm]
                        for h4 in range(4):
                            bt = nb()
                            for hl in range(4):
                                h = h4 * 4 + hl
                                p.op('pe', lambda e: e.transpose(pb[bt][0:64, hl * 128:(hl + 1) * 128], src[:, h * 64:(h + 1) * 64], ident_f[:]),
                                     reads=[srcn, 'ident_f'], writes=[('D_pb', bt)])
                            dst = dstT[:, h4 * 4:(h4 + 1) * 4, :].rearrange("p a b -> p (a b)")
                            if h4 % 2 == 0:
                                p.op('act', lambda e: e.copy(dst, pb[bt][0:64, :]), reads=[('D_pb', bt)], writes=[nm])
                            else:
                                p.op('dve', lambda e: e.tensor_copy(dst, pb[bt][0:64, :]), reads=[('D_pb', bt)], writes=[nm])
                    for h4 in range(4):
                        bA, bB, bC, bD, bE = nb(), nb(), nb(), nb(), nb()
                        for hl in range(4):
                            h = h4 * 4 + hl
                            sl = slice(hl * 128, (hl + 1) * 128)
                            p.op('pe', lambda e: e.matmul(pb[bA][:, sl], BbT[:, h, :], AbT[:, h, :], start=True, stop=True), reads=['D_BbT', 'D_AbT'], writes=[('D_pb', bA)])
                            p.op('pe', lambda e: e.matmul(pb[bB][:, sl], BbT[:, h, :], RbT[:, h, :], start=True, stop=True), reads=['D_BbT', 'D_RbT'], writes=[('D_pb', bB)])
                            p.op('pe', lambda e: e.matmul(pb[bC][:, sl], KbT[:, h, :], AbT[:, h, :], start=True, stop=True), reads=['D_KbT', 'D_AbT'], writes=[('D_pb', bC)])
                            p.op('pe', lambda e: e.matmul(pb[bD][:, sl], KbT[:, h, :], RbT[:, h, :], start=True, stop=True), reads=['D_KbT', 'D_RbT'], writes=[('D_pb', bD)])
                            p.op('pe', lambda e: e.matmul(pb[bE][:, sl], AbT[:, h, :], BbT[:, h, :], start=True, stop=True), reads=['D_BbT', 'D_AbT'], writes=[('D_pb', bE)])
                        qi = 0
                        p.op('dve', lambda e: e.tensor_tensor(v4(Q[qi][:]), v4(pb[bA][:, :]), b4(mS), ALU.mult), reads=[('D_pb', bA), 'D_mS'], writes=[('D_Q', qi)])
                        p.op('dve', lambda e: e.tensor_tensor(v4(MrbT[:]), v4(pb[bB][:, :]), b4(mI), ALU.mult), reads=[('D_pb', bB), 'D_mI'], writes=['D_MrbT'])
                        p.op('dve', lambda e: e.tensor_tensor(v4(LakT[:]), v4(pb[bC][:, :]), b4(mS), ALU.mult), reads=[('D_pb', bC), 'D_mS'], writes=['D_LakT'])
                        p.op('dve', lambda e: e.tensor_tensor(v4(MrkT[:]), v4(pb[bD][:, :]), b4(mI), ALU.mult), reads=[('D_pb', bD), 'D_mI'], writes=['D_MrkT'])
                        p.op('dve', lambda e: e.tensor_tensor(v4(QT[qi][:]), v4(pb[bE][:, :]), b4(mST), ALU.mult), reads=[('D_pb', bE), 'D_mST'], writes=[('D_QT', qi)])
                        p.op('pool', lambda e: e.tensor_tensor(v4(P[:]), v4(Q[qi][:]), b4(ident_f), ALU.add), reads=[('D_Q', qi), 'ident_f'], writes=['D_P'])
                        p.op('pool', lambda e: e.tensor_copy(Pr[:], P[:]), reads=['D_P'], writes=['D_Pr'])
                        for lvl in range(6):
                            qn = 1 - qi
                            bqT = nb()
                            for hl in range(4):
                                sl = slice(hl * 128, (hl + 1) * 128)
                                p.op('pe', lambda e: e.matmul(pb[bqT][:, sl], Q[qi][:, sl], QT[qi][:, sl], start=True, stop=True),
                                     reads=[('D_Q', qi), ('D_QT', qi)], writes=[('D_pb', bqT)])
                            if lvl < 5:
                                bq = nb()
                                for hl in range(4):
                                    sl = slice(hl * 128, (hl + 1) * 128)
                                    p.op('pe', lambda e: e.matmul(pb[bq][:, sl], QT[qi][:, sl], Q[qi][:, sl], start=True, stop=True),
                                         reads=[('D_Q', qi), ('D_QT', qi)], writes=[('D_pb', bq)])
                            p.op('act', lambda e: e.copy(QT[qn][:], pb[bqT][:, :]), reads=[('D_pb', bqT)], writes=[('D_QT', qn)])
                            if lvl < 5:
                                p.op('dve', lambda e: e.tensor_copy(Q[qn][:], pb[bq][:, :]), reads=[('D_pb', bq)], writes=[('D_Q', qn)])
                            bp = nb()
                            for hl in range(4):
                                sl = slice(hl * 128, (hl + 1) * 128)
                                p.op('pe', lambda e: e.matmul(pb[bp][:, sl], QT[qn][:, sl], Pr[:, sl], start=True, stop=True),
                                     reads=[('D_QT', qn), 'D_Pr'], writes=[('D_pb', bp)])
                            p.op('dve', lambda e: e.tensor_tensor(P[:], P[:], pb[bp][:, :], ALU.add), reads=['D_P', ('D_pb', bp)], writes=['D_P'])
                            p.op('pool', lambda e: e.tensor_copy(Pr[:], P[:]), reads=['D_P'], writes=['D_Pr'])
                            qi = qn
                        bx = nb()
                        for hl in range(4):
                            h = h4 * 4 + hl
                            p.op('pe', lambda e: e.matmul(pb[bx][:, hl * 64:(hl + 1) * 64], LakT[:, hl * 128:(hl + 1) * 128], Vr[:, h * 64:(h + 1) * 64], start=True, stop=True),
                                 reads=['D_LakT', 'D_Vr'], writes=[('D_pb', bx)])
                        p.op('act', lambda e: e.copy(AXt[:, :, 64:128], pb[bx][:, 0:256].rearrange("p (a b) -> p a b", a=4)), reads=[('D_pb', bx)], writes=['D_AX'])
                        p.op('pool', lambda e: e.tensor_copy(AXt[:, :, 0:64], Abr[:, h4 * 256:(h4 + 1) * 256].rearrange("p (a b) -> p a b", a=4)), reads=['D_Abr'], writes=['D_AX'])
                        bu = nb()
                        for hl in range(4):
                            p.op('pe', lambda e: e.matmul(pb[bu][:, hl * 128:(hl + 1) * 128], Pr[:, hl * 128:(hl + 1) * 128], AXt[:, hl, :], start=True, stop=True),
                                 reads=['D_Pr', 'D_AX'], writes=[('D_pb', bu)])
                        p.op('act', lambda e: e.copy(AU[:].rearrange("p a b -> p (a b)"), pb[bu][:, :]), reads=[('D_pb', bu)], writes=['D_AU'])
                        br_, bg, bh, by = nb(), nb(), nb(), nb()
                        for hl in range(4):
                            h = h4 * 4 + hl
                            hc = slice(h * 64, (h + 1) * 64)
                            p.op('pe', lambda e: e.matmul(pb[br_][0:64, hl * 128:(hl + 1) * 128], AU[:, hl, 0:64], MrbT[:, hl * 128:(hl + 1) * 128], start=True, stop=True),
                                 reads=['D_AU', 'D_MrbT'], writes=[('D_pb', br_)])
                            p.op('pe', lambda e: e.matmul(pb[bg][0:64, hl * 64:(hl + 1) * 64], AU[:, hl, 0:64], Bt[:, hc], start=True, stop=False),
                                 reads=['D_AU', 'D_Bt'], writes=[('D_pb', bg)])
                            p.op('pe', lambda e: e.matmul(pb[bg][0:64, hl * 64:(hl + 1) * 64], identr[0:64, 0:64], ydg[:, hc], start=False, stop=True),
                                 reads=['D_identr', 'D_ydg'], writes=[('D_pb', bg)])
                            p.op('pe', lambda e: e.matmul(pb[bh][0:64, hl * 64:(hl + 1) * 64], Bt[:, hc], AU[:, hl, 64:128], start=True, stop=False),
                                 reads=['D_AU', 'D_Bt'], writes=[('D_pb', bh)])
                            p.op('pe', lambda e: e.matmul(pb[bh][0:64, hl * 64:(hl + 1) * 64], Kt[:, hc], Vr[:, hc], start=False, stop=True),
                                 reads=['D_Kt', 'D_Vr'], writes=[('D_pb', bh)])
                            p.op('pe', lambda e: e.matmul(pb[by][:, hl * 64:(hl + 1) * 64], MrbT[:, hl * 128:(hl + 1) * 128], AU[:, hl, 64:128], start=True, stop=False),
                                 reads=['D_AU', 'D_MrbT'], writes=[('D_pb', by)])
                            p.op('pe', lambda e: e.matmul(pb[by][:, hl * 64:(hl + 1) * 64], MrkT[:, hl * 128:(hl + 1) * 128], Vr[:, hc], start=False, stop=True),
                                 reads=['D_MrkT', 'D_Vr'], writes=[('D_pb', by)])
                        p.op('dve', lambda e: e.tensor_tensor(RhT[:, h4 * 4:(h4 + 1) * 4, :].rearrange("p a b -> p (a b)"), pb[br_][0:64, :],
                                                             RbT[:, h4 * 4:(h4 + 1) * 4, :].rearrange("p a b -> p (a b)"), ALU.add),
                             reads=[('D_pb', br_), 'D_RbT'], writes=['D_RhT'])
                        p.op('act', lambda e: e.copy(GT[:, h4 * 256:(h4 + 1) * 256], pb[bg][0:64, 0:256]), reads=[('D_pb', bg)], writes=['D_GT'])
                        p.op('act', lambda e: e.copy(Hh[:, h4 * 256:(h4 + 1) * 256], pb[bh][0:64, 0:256]), reads=[('D_pb', bh)], writes=['D_H'])
                        p.op('dve', lambda e: e.tensor_copy(Yh[:, h4 * 256:(h4 + 1) * 256], pb[by][:, 0:256]), reads=[('D_pb', by)], writes=['D_Yh'])
                    for half in range(2):
                        bY = nb()
                        bS = nb()
                        for hh in range(8):
                            h = half * 8 + hh
                            hc = slice(h * 64, (h + 1) * 64)
                            p.op('pe', lambda e: e.matmul(pb[bY][:, hh * 64:(hh + 1) * 64], RhT[:, h, :], STr[:, hc], start=True, stop=True),
                                 reads=['D_RhT', 'D_STr'], writes=[('D_pb', bY)])
                            p.op('pe', lambda e: e.matmul(pb[bS][0:64, hh * 64:(hh + 1) * 64], GT[:, hc], STr[:, hc], start=True, stop=True),
                                 reads=['D_GT', 'D_STr'], writes=[('D_pb', bS)])
                        cs_ = slice(half * 512, (half + 1) * 512)
                        p.op('dve', lambda e: e.tensor_tensor(Yh[:, cs_], pb[bY][:, :], Yh[:, cs_], ALU.add), reads=[('D_pb', bY), 'D_Yh'], writes=['D_Yh'])
                        p.op('dve', lambda e: e.tensor_tensor(ST[:, cs_], pb[bS][0:64, :], Hh[:, cs_], ALU.add), reads=[('D_pb', bS), 'D_H'], writes=[('D_ST', half)])
                    p.op('act', lambda e: e.copy(STr[:], ST[:]), reads=[('D_ST', 0), ('D_ST', 1)], writes=['D_STr'])
                    p.dma('sp', ysc[d, t0:t0 + 128, :], Yh[:], reads=['D_Yh'], writes=[('ysc', d, c)])
        p.barrier()
        with ExitStack() as st:
            lnw = sb(st, "D2_lnw", [128, 1024]); lnb = sb(st, "D2_lnb", [128, 1024])
            p.dma('sp', lnw[:], rwkv_ln_w[l:l + 1, :].partition_broadcast(128), writes=['D2_lnw'])
            p.dma('sp', lnb[:], rwkv_ln_b[l:l + 1, :].partition_broadcast(128), writes=['D2_lnb'])
            y0 = sb(st, "D2_y0", [128, 1024]); y1 = sb(st, "D2_y1", [128, 1024]); vt = sb(st, "D2_v", [128, 1024]); sq = sb(st, "D2_sq", [128, 1024])
            b0t = sb(st, "D2_b0", [128, 16]); b1t = sb(st, "D2_b1", [128, 16]); mean = sb(st, "D2_mean", [128, 16]); var = sb(st, "D2_var", [128, 16])
            eps2 = sb(st, "D2_eps", [128, 1])
            p.op('dve', lambda e: e.memset(eps2[:], 64e-5), writes=['D2_eps'])
            v3 = lambda t: t[:].rearrange("p (h j) -> p h j", h=16)
            bc3 = lambda t: t[:].unsqueeze(2).broadcast_to([128, 16, 64])
            for i in range(NT):
                t0 = i * 128
                p.dma('sp', y0[:], ysc[0, t0:t0 + 128, :], reads=[('ysc', 0, i)], writes=['D2_y0'])
                p.dma('sp', y1[:], ysc[1, t0:t0 + 128, :], reads=[('ysc', 1, i)], writes=['D2_y1'])
                p.dma('sp', vt[:], rwc[t0:t0 + 128, 2048:3072], reads=[('rwc', i)], writes=['D2_v'])
                p.dma('sp', b0t[:], bon[0, t0:t0 + 128, :], reads=[('bon', 0, i)], writes=['D2_b0'])
                p.dma('sp', b1t[:], bon[1, t0:t0 + 128, :], reads=[('bon', 1, i)], writes=['D2_b1'])
                p.op('dve', lambda e: e.tensor_tensor(y0[:], y0[:], y1[:], ALU.add), reads=['D2_y0', 'D2_y1'], writes=['D2_y0'])
                p.op('dve', lambda e: e.tensor_reduce(mean[:], v3(y0), AX.X, ALU.add), reads=['D2_y0'], writes=['D2_mean'])
                p.op('dve', lambda e: e.tensor_scalar(mean[:], mean[:], 1.0 / 64, None, ALU.mult), reads=['D2_mean'], writes=['D2_mean'])
                p.op('dve', lambda e: e.tensor_tensor(v3(y0), v3(y0), bc3(mean), ALU.subtract), reads=['D2_y0', 'D2_mean'], writes=['D2_y0'])
                p.op('pool', lambda e: e.tensor_tensor(sq[:], y0[:], y0[:], ALU.mult), reads=['D2_y0'], writes=['D2_sq'])
                p.op('dve', lambda e: e.tensor_reduce(var[:], v3(sq), AX.X, ALU.add), reads=['D2_sq'], writes=['D2_var'])
                p.op('act', lambda e: e.activation(var[:], var[:], AF.Sqrt, bias=eps2[:], scale=1.0 / 64), reads=['D2_var', 'D2_eps'], writes=['D2_var'])
                p.op('dve', lambda e: e.reciprocal(var[:], var[:]), reads=['D2_var'], writes=['D2_var'])
                p.op('dve', lambda e: e.tensor_tensor(v3(y0), v3(y0), bc3(var), ALU.mult), reads=['D2_y0', 'D2_var'], writes=['D2_y0'])
                p.op('pool', lambda e: e.tensor_tensor(y0[:], y0[:], lnw[:], ALU.mult), reads=['D2_y0', 'D2_lnw'], writes=['D2_y0'])
                p.op('pool', lambda e: e.tensor_tensor(y0[:], y0[:], lnb[:], ALU.add), reads=['D2_y0', 'D2_lnb'], writes=['D2_y0'])
                p.op('dve', lambda e: e.tensor_tensor(b0t[:], b0t[:], b1t[:], ALU.add), reads=['D2_b0', 'D2_b1'], writes=['D2_b0'])
                p.op('dve', lambda e: e.tensor_tensor(v3(vt), v3(vt), bc3(b0t), ALU.mult), reads=['D2_v', 'D2_b0'], writes=['D2_v'])
                p.op('dve', lambda e: e.tensor_tensor(y0[:], y0[:], vt[:], ALU.add), reads=['D2_y0', 'D2_v'], writes=['D2_y0'])
                p.dma('sp', br[t0:t0 + 128, 2048:3072], y0[:], reads=['D2_y0'], writes=[('br', i, 2)])
        p.barrier()


    TWO_PI = float(2 * np.pi)

    def phase_C(l):
        with ExitStack() as st:
            TC = 512
            lr = sb(st, "C_lr", [128, 32]); li = sb(st, "C_li", [128, 32])
            for two in range(2):
                p.dma('sp', lr[two * 64:(two + 1) * 64, :], s5_lam_re[l, two::2, :].rearrange("q p -> p q"), writes=['C_lr'], allow_slow_non_contiguous=True)
                p.dma('sp', li[two * 64:(two + 1) * 64, :], s5_lam_im[l, two::2, :].rearrange("q p -> p q"), writes=['C_li'], allow_slow_non_contiguous=True)
            den = sb(st, "C_den", [128, 32]); t_a = sb(st, "C_ta", [128, 32]); t_b = sb(st, "C_tb", [128, 32]); t_c = sb(st, "C_tc", [128, 32])
            t_i = sb(st, "C_ti", [128, 32], I32)
            p.op('dve', lambda e: e.tensor_tensor(den[:], lr[:], lr[:], ALU.mult), reads=['C_lr'], writes=['C_den'])
            p.op('dve', lambda e: e.tensor_tensor(t_a[:], li[:], li[:], ALU.mult), reads=['C_li'], writes=['C_ta'])
            p.op('dve', lambda e: e.tensor_tensor(den[:], den[:], t_a[:], ALU.add), reads=['C_den', 'C_ta'], writes=['C_den'])
            p.op('dve', lambda e: e.reciprocal(den[:], den[:]), reads=['C_den'], writes=['C_den'])
            mag = [sb(st, f"C_mag{d}", [128, 32]) for d in range(2)]
            th = [sb(st, f"C_th{d}", [128, 32]) for d in range(2)]
            cre = [sb(st, f"C_cre{d}", [128, 32]) for d in range(2)]
            cim = [sb(st, f"C_cim{d}", [128, 32]) for d in range(2)]
            dtt = sb(st, "C_dt", [128, 32]); sn = sb(st, "C_sn", [128, 32]); cs = sb(st, "C_cs", [128, 32])

            def emit_sin(out, ang, n, key_out, key_ang, ti_, tf_, kti, ktf):
                p.op('dve', lambda e: e.tensor_scalar(ti_, ang, 1.0 / TWO_PI, None, ALU.mult), reads=[key_ang], writes=[kti])
                p.op('dve', lambda e: e.tensor_copy(tf_, ti_), reads=[kti], writes=[ktf])
                p.op('dve', lambda e: e.scalar_tensor_tensor(tf_, tf_, -TWO_PI, ang, ALU.mult, ALU.add), reads=[ktf, key_ang], writes=[ktf])
                p.op('dve', lambda e: e.tensor_scalar(tf_, tf_, float(np.pi), float(-np.pi), ALU.min, ALU.max), reads=[ktf], writes=[ktf])
                p.op('act', lambda e: e.activation(out, tf_, AF.Sin), reads=[ktf], writes=[key_out])

            for d in range(2):
                for two in range(2):
                    p.dma('sp', dtt[two * 64:(two + 1) * 64, :], s5_log_dt[l, d:d + 1, two::2].partition_broadcast(64), writes=['C_dt'],
                          allow_slow_non_contiguous=True)
                p.op('act', lambda e: e.activation(dtt[:], dtt[:], AF.Exp), reads=['C_dt'], writes=['C_dt'])
                p.op('dve', lambda e: e.tensor_tensor(t_a[:], lr[:], dtt[:], ALU.mult), reads=['C_lr', 'C_dt'], writes=['C_ta'])
                p.op('act', lambda e: e.activation(mag[d][:], t_a[:], AF.Exp), reads=['C_ta'], writes=[f'C_mag{d}'])
                p.op('dve', lambda e: e.tensor_tensor(th[d][:], li[:], dtt[:], ALU.mult), reads=['C_li', 'C_dt'], writes=[f'C_th{d}'])
                emit_sin(sn[:], th[d][:], 32, 'C_sn', f'C_th{d}', t_i[:], t_b[:], 'C_ti', 'C_tb')
                p.op('dve', lambda e: e.tensor_scalar(t_c[:], th[d][:], float(np.pi / 2), None, ALU.add), reads=[f'C_th{d}'], writes=['C_tc'])
                emit_sin(cs[:], t_c[:], 32, 'C_cs', 'C_tc', t_i[:], t_b[:], 'C_ti', 'C_tb')
                p.op('dve', lambda e: e.tensor_tensor(cs[:], cs[:], mag[d][:], ALU.mult), reads=['C_cs', f'C_mag{d}'], writes=['C_cs'])
                p.op('dve', lambda e: e.tensor_scalar(cs[:], cs[:], -1.0, None, ALU.add), reads=['C_cs'], writes=['C_cs'])
                p.op('dve', lambda e: e.tensor_tensor(sn[:], sn[:], mag[d][:], ALU.mult), reads=['C_sn', f'C_mag{d}'], writes=['C_sn'])
                p.op('dve', lambda e: e.tensor_tensor(t_a[:], cs[:], lr[:], ALU.mult), reads=['C_cs', 'C_lr'], writes=['C_ta'])
                p.op('dve', lambda e: e.tensor_tensor(t_b[:], sn[:], li[:], ALU.mult), reads=['C_sn', 'C_li'], writes=['C_tb'])
                p.op('dve', lambda e: e.tensor_tensor(t_a[:], t_a[:], t_b[:], ALU.add), reads=['C_ta', 'C_tb'], writes=['C_ta'])
                p.op('dve', lambda e: e.tensor_tensor(cre[d][:], t_a[:], den[:], ALU.mult), reads=['C_ta', 'C_den'], writes=[f'C_cre{d}'])
                p.op('dve', lambda e: e.tensor_tensor(t_a[:], sn[:], lr[:], ALU.mult), reads=['C_sn', 'C_lr'], writes=['C_ta'])
                p.op('dve', lambda e: e.tensor_tensor(t_b[:], cs[:], li[:], ALU.mult), reads=['C_cs', 'C_li'], writes=['C_tb'])
                p.op('dve', lambda e: e.tensor_tensor(t_a[:], t_a[:], t_b[:], ALU.subtract), reads=['C_ta', 'C_tb'], writes=['C_ta'])
                p.op('dve', lambda e: e.tensor_tensor(cim[d][:], t_a[:], den[:], ALU.mult), reads=['C_ta', 'C_den'], writes=[f'C_cim{d}'])
            WB = [[sb(st, f"C_WB{d}{ri}", [128, 16, 128]) for ri in range(2)] for d in range(2)]
            WC = [[sb(st, f"C_WC{d}{ri}", [128, 32, 64]) for ri in range(2)] for d in range(2)]
            pbs = [ps(st, f"C_pb{i}", [128, 512]) for i in range(8)]
            st2 = ExitStack()
            Bm = [sb(st2, f"C_Bm{ri}", [128, 32, 64]) for ri in range(2)]
            for ri, src in enumerate((s5_b_re, s5_b_im)):
                p.op('pool', lambda e: e.memset(Bm[ri][:], 0.0), writes=[f'C_Bm{ri}'])
                for two in range(2):
                    for qpar in range(2):
                        off = qpar * 32 + two * 16
                        p.dma('sp', Bm[ri][two * 64:(two + 1) * 64, qpar::2, off:off + 16],
                              src[l, (2 * qpar + two)::4, :, :].rearrange("m p c -> p m c"),
                              writes=[f'C_Bm{ri}'], allow_slow_non_contiguous=True)
            bbt = sb(st2, "C_bbt", [128, 32, 64]); bbt2 = sb(st2, "C_bbt2", [128, 32, 64])
            pbi = [0]

            def nb():
                i = pbi[0] % 8
                pbi[0] += 1
                return i
            b3 = lambda t: t[:].unsqueeze(2).broadcast_to([128, 32, 64])
            for d in range(2):
                for ri in range(2):
                    if ri == 0:
                        p.op('dve', lambda e: e.tensor_tensor(bbt[:], Bm[0][:], b3(cre[d]), ALU.mult), reads=['C_Bm0', f'C_cre{d}'], writes=['C_bbt'])
                        p.op('pool', lambda e: e.tensor_tensor(bbt2[:], Bm[1][:], b3(cim[d]), ALU.mult), reads=['C_Bm1', f'C_cim{d}'], writes=['C_bbt2'])
                        p.op('dve', lambda e: e.tensor_tensor(bbt[:], bbt[:], bbt2[:], ALU.subtract), reads=['C_bbt', 'C_bbt2'], writes=['C_bbt'])
                    else:
                        p.op('dve', lambda e: e.tensor_tensor(bbt[:], Bm[1][:], b3(cre[d]), ALU.mult), reads=['C_Bm1', f'C_cre{d}'], writes=['C_bbt'])
                        p.op('pool', lambda e: e.tensor_tensor(bbt2[:], Bm[0][:], b3(cim[d]), ALU.mult), reads=['C_Bm0', f'C_cim{d}'], writes=['C_bbt2'])
                        p.op('dve', lambda e: e.tensor_tensor(bbt[:], bbt[:], bbt2[:], ALU.add), reads=['C_bbt', 'C_bbt2'], writes=['C_bbt'])
                    for q in range(32):
                        bt = nb()
                        hb = (q % 4) // 2
                        qi_ = (q // 4) * 2 + q % 2
                        p.op('pe', lambda e: e.matmul(pbs[bt][hb * 64:(hb + 1) * 64, 0:128], bbt[:, q, :], ident_f[:], start=True, stop=True),
                             reads=['C_bbt', 'ident_f'], writes=[('C_pb', bt)])
                        p.op('act', lambda e: e.copy(WB[d][ri][hb * 64:(hb + 1) * 64, qi_, :], pbs[bt][hb * 64:(hb + 1) * 64, 0:128]),
                             reads=[('C_pb', bt)], writes=[f'C_WB{d}{ri}'])
            Cn = sb(st2, "C_Cn", [64, 32, 128])
            for d in range(2):
                for ri, src in enumerate((s5_c_re, s5_c_im)):
                    p.op('pool', lambda e: e.memset(Cn[:], 0.0), writes=['C_Cn'])
                    for two in range(2):
                        for qpar in range(2):
                            off = qpar * 32 + two * 16
                            p.dma('sp', Cn[off:off + 16, qpar::2, two * 64:(two + 1) * 64],
                                  src[l, d, (2 * qpar + two)::4, :, :].rearrange("m c p -> c m p"),
                                  writes=['C_Cn'], allow_slow_non_contiguous=True)
                    for q4 in range(16):
                        bt = nb()
                        for qq in range(2):
                            q = q4 * 2 + qq
                            p.op('pe', lambda e: e.transpose(pbs[bt][:, qq * 64:(qq + 1) * 64], Cn[:, q, :], ident_f[0:64, 0:64]),
                                 reads=['C_Cn', 'ident_f'], writes=[('C_pb', bt)])
                        dst = WC[d][ri][:, q4 * 2:(q4 + 1) * 2, :].rearrange("p a b -> p (a b)")
                        if ri == 0:
                            p.op('act', lambda e: e.copy(dst, pbs[bt][:, 0:128]), reads=[('C_pb', bt)], writes=[f'C_WC{d}{ri}'])
                        else:
                            p.op('act', lambda e: e.mul(dst, pbs[bt][:, 0:128], -1.0), reads=[('C_pb', bt)], writes=[f'C_WC{d}{ri}'])
            p.barrier()
            st2.close()
            ut = sb(st, "C_ut", [128, 128]); uT = sb(st, "C_uT", [128, S]); yacc = sb(st, "C_yacc", [128, S])
            iota1 = sb(st, "C_iota", [128, TC])
            p.dma('sp', iota1[:], c_iota[0:1, 0:TC].partition_broadcast(128), writes=['C_iota'])
            ang = sb(st, "C_ang", [128, TC]); ang2 = sb(st, "C_ang2", [128, TC]); tfi = sb(st, "C_tfi", [128, TC], I32); tff = sb(st, "C_tff", [128, TC])
            cosT = sb(st, "C_cosT", [128, TC]); sinT = sb(st, "C_sinT", [128, TC]); rtab = sb(st, "C_rtab", [128, TC])
            gre = sb(st, "C_gre", [128, TC]); gim = sb(st, "C_gim", [128, TC]); w1 = sb(st, "C_w1", [128, TC]); w2_ = sb(st, "C_w2", [128, TC])
            hre = sb(st, "C_hre", [128, TC]); him = sb(st, "C_him", [128, TC]); carry = sb(st, "C_carry", [128, 2])
            dsk = sb(st, "C_dsk", [128, 8])
            p.dma('sp', dsk[:], s5_d[l, :].rearrange("(b c) -> c b", c=128), writes=['C_dsk'], allow_slow_non_contiguous=True)
            yo = sb(st, "C_yo", [128, 128]); y3 = sb(st, "C_y3", [128, 512]); y4 = sb(st, "C_y4", [128, 512])
            for cb in range(8):
                for i in range(NT):
                    p.dma('sp', ut[:], proj[i * 128:(i + 1) * 128, C_AU + cb * 128:C_AU + (cb + 1) * 128], reads=[('proj', i, 'all')], writes=['C_ut'])
                    bt = nb()
                    p.op('pe', lambda e: e.transpose(pbs[bt][:, 0:128], ut[:], ident_f[:]), reads=['C_ut', 'ident_f'], writes=[('C_pb', bt)])
                    p.op('act', lambda e: e.copy(uT[:, i * 128:(i + 1) * 128], pbs[bt][:, 0:128]), reads=[('C_pb', bt)], writes=[('C_uT', i // 4)])
                for d in range(2):
                    for qq in range(4):
                        q = cb * 4 + qq
                        ps32 = slice((qq // 2) * 64, (qq // 2) * 64 + 64)
                        p.op('dve', lambda e: e.tensor_scalar(ang[:], iota1[:], th[d][:, q:q + 1], None, ALU.mult), reads=['C_iota', f'C_th{d}'], writes=['C_ang'])
                        emit_sin(sinT[:], ang[:], TC, 'C_sinT', 'C_ang', tfi[:], tff[:], 'C_tfi', 'C_tff')
                        p.op('dve', lambda e: e.tensor_scalar(ang2[:], ang[:], float(np.pi / 2), None, ALU.add), reads=['C_ang'], writes=['C_ang2'])
                        emit_sin(cosT[:], ang2[:], TC, 'C_cosT', 'C_ang2', tfi[:], tff[:], 'C_tfi', 'C_tff')
                        p.op('act', lambda e: e.activation(rtab[:], iota1[:], AF.Copy, scale=0.0, bias=0.0) if False else e.mul(rtab[:], iota1[:], 0.0), reads=['C_iota'], writes=['C_rtab'])
                        p.op('dve', lambda e: e.tensor_scalar(rtab[:], rtab[:], mag[d][:, q:q + 1], None, ALU.add), reads=['C_rtab', f'C_mag{d}'], writes=['C_rtab'])
                        p.op('dve', lambda e: e.memset(carry[:], 0.0), writes=['C_carry'])
                        chunks = range(S // TC) if d == 0 else range(S // TC - 1, -1, -1)
                        for ch in chunks:
                            tsl = slice(ch * TC, (ch + 1) * TC)
                            bre, bim = nb(), nb()
                            p.op('pe', lambda e: e.matmul(pbs[bre][:, :], WB[d][0][ps32, (q // 4) * 2 + q % 2, :], uT[ps32, tsl], start=True, stop=True),
                                 reads=[f'C_WB{d}0', ('C_uT', ch)], writes=[('C_pb', bre)])
                            p.op('pe', lambda e: e.matmul(pbs[bim][:, :], WB[d][1][ps32, (q // 4) * 2 + q % 2, :], uT[ps32, tsl], start=True, stop=True),
                                 reads=[f'C_WB{d}1', ('C_uT', ch)], writes=[('C_pb', bim)])
                            Bre = pbs[bre][:, :] if d == 0 else pbs[bre][:, ::-1]
                            Bim = pbs[bim][:, :] if d == 0 else pbs[bim][:, ::-1]
                            p.op('dve', lambda e: e.tensor_tensor(w1[:], Bre, cosT[:], ALU.mult), reads=[('C_pb', bre), 'C_cosT'], writes=['C_w1'])
                            p.op('dve', lambda e: e.tensor_tensor(w2_[:], Bim, sinT[:], ALU.mult), reads=[('C_pb', bim), 'C_sinT'], writes=['C_w2'])
                            p.op('pool', lambda e: e.tensor_tensor(w1[:], w1[:], w2_[:], ALU.add), reads=['C_w1', 'C_w2'], writes=['C_w1'])
                            p.op('dve', lambda e: e.tensor_tensor_scan(gre[:], rtab[:], w1[:], carry[:, 0:1], ALU.mult, ALU.add),
                                 reads=['C_rtab', 'C_w1', 'C_carry'], writes=['C_gre'])
                            p.op('dve', lambda e: e.tensor_tensor(w2_[:], Bim, cosT[:], ALU.mult), reads=[('C_pb', bim), 'C_cosT', 'C_w1'], writes=['C_w2'])
                            p.op('dve', lambda e: e.tensor_tensor(w1[:], Bre, sinT[:], ALU.mult), reads=[('C_pb', bre), 'C_sinT', 'C_gre'], writes=['C_w1'])
                            p.op('pool', lambda e: e.tensor_tensor(w2_[:], w2_[:], w1[:], ALU.subtract), reads=['C_w1', 'C_w2'], writes=['C_w2'])
                            p.op('dve', lambda e: e.tensor_tensor_scan(gim[:], rtab[:], w2_[:], carry[:, 1:2], ALU.mult, ALU.add),
                                 reads=['C_rtab', 'C_w2', 'C_carry'], writes=['C_gim'])
                            Hre = hre[:] if d == 0 else hre[:, ::-1]
                            Him = him[:] if d == 0 else him[:, ::-1]
                            p.op('pool', lambda e: e.tensor_tensor(w1[:], gre[:], cosT[:], ALU.mult), reads=['C_gre', 'C_cosT', 'C_w2'], writes=['C_w1'])
                            p.op('pool', lambda e: e.tensor_tensor(w2_[:], gim[:], sinT[:], ALU.mult), reads=['C_gim', 'C_sinT'], writes=['C_w2'])
                            p.op('dve', lambda e: e.tensor_tensor(Hre, w1[:], w2_[:], ALU.subtract), reads=['C_w1', 'C_w2'], writes=['C_hre'])
                            p.op('pool', lambda e: e.tensor_tensor(w1[:], gre[:], sinT[:], ALU.mult), reads=['C_gre', 'C_sinT', 'C_hre'], writes=['C_w1'])
                            p.op('pool', lambda e: e.tensor_tensor(w2_[:], gim[:], cosT[:], ALU.mult), reads=['C_gim', 'C_cosT', 'C_hre'], writes=['C_w2'])
                            p.op('dve', lambda e: e.tensor_tensor(Him, w1[:], w2_[:], ALU.add), reads=['C_w1', 'C_w2'], writes=['C_him'])
                            last = TC - 1 if d == 0 else 0
                            p.op('act', lambda e: e.copy(carry[:, 0:1], hre[:, last:last + 1]), reads=['C_hre'], writes=['C_carry'])
                            p.op('act', lambda e: e.copy(carry[:, 1:2], him[:, last:last + 1]), reads=['C_him'], writes=['C_carry'])
                            by = nb()
                            p.op('pe', lambda e: e.matmul(pbs[by][ps32, :], WC[d][0][:, q, :], hre[:], start=True, stop=False),
                                 reads=[f'C_WC{d}0', 'C_hre'], writes=[('C_pb', by)])
                            p.op('pe', lambda e: e.matmul(pbs[by][ps32, :], WC[d][1][:, q, :], him[:], start=False, stop=True),
                                 reads=[f'C_WC{d}1', 'C_him'], writes=[('C_pb', by)])
                            if d == 0 and qq % 2 == 0:
                                p.op('act', lambda e: e.copy(yacc[ps32, tsl], pbs[by][ps32, :]), reads=[('C_pb', by)], writes=[('C_yacc', qq // 2, ch)])
                            else:
                                p.op('dve', lambda e: e.tensor_tensor(yacc[ps32, tsl], yacc[ps32, tsl], pbs[by][ps32, :], ALU.add),
                                     reads=[('C_pb', by), ('C_yacc', qq // 2, ch)], writes=[('C_yacc', qq // 2, ch)])
                for ch in range(S // TC):
                    tsl = slice(ch * TC, (ch + 1) * TC)
                    rk = [('C_yacc', qq, ch) for qq in range(2)]
                    p.op('dve', lambda e: e.scalar_tensor_tensor(y3[:], uT[:, tsl], dsk[:, cb:cb + 1], yacc[:, tsl], ALU.mult, ALU.add),
                         reads=rk + [('C_uT', ch), 'C_dsk'], writes=['C_y3'])
                    p.op('pool', lambda e: e.tensor_tensor(y4[:], y3[:], y3[:], ALU.mult), reads=['C_y3'], writes=['C_y4'])
                    p.op('dve', lambda e: e.tensor_scalar(y4[:], y4[:], 0.044715, 1.0, ALU.mult, ALU.add), reads=['C_y4'], writes=['C_y4'])
                    p.op('dve', lambda e: e.tensor_tensor(y4[:], y4[:], y3[:], ALU.mult), reads=['C_y4', 'C_y3'], writes=['C_y4'])
                    p.op('act', lambda e: e.activation(y4[:], y4[:], AF.Sigmoid, scale=1.5957691216057308), reads=['C_y4'], writes=['C_y4'])
                    p.op('dve', lambda e: e.tensor_tensor(y3[:], y3[:], y4[:], ALU.mult), reads=['C_y4', 'C_y3'], writes=['C_y3'])
                    for i4_ in range(TC // 128):
                        i = ch * (TC // 128) + i4_
                        bt = nb()
                        p.op('pe', lambda e: e.transpose(pbs[bt][:, 0:128], y3[:, i4_ * 128:(i4_ + 1) * 128], ident_f[:]), reads=['C_y3', 'ident_f'], writes=[('C_pb', bt)])
                        p.op('act', lambda e: e.copy(yo[:], pbs[bt][:, 0:128]), reads=[('C_pb', bt)], writes=['C_yo'])
                        p.dma('sp', ygd[i * 128:(i + 1) * 128, cb * 128:(cb + 1) * 128], yo[:], reads=['C_yo'], writes=[('ygd', i, cb)])
        p.barrier()
        with ExitStack() as st:
            gw = sb(st, "C2_gw", [128, 8, 1024], BF16)
            p.dma('pool', gw[:], s5_glu_w[l, :, :].rearrange("(k p) n -> p k n", p=128), writes=['C2_gw'])
            gb = sb(st, "C2_gb", [128, 1024])
            p.dma('sp', gb[:], s5_glu_b[l:l + 1, :].partition_broadcast(128), writes=['C2_gb'])
            yg = sb(st, "C2_yg", [128, 1024]); ygb = sb(st, "C2_ygb", [128, 1024], BF16); ygT = sb(st, "C2_ygT", [128, 8, 128], BF16)
            sg = sb(st, "C2_sg", [128, 1024])
            ptr = ps(st, "C2_pt", [128, 8, 128], BF16)
            pm = [ps(st, f"C2_pm{i}", [128, 512]) for i in range(2)]
            for i in range(NT):
                p.dma('sp', yg[:], ygd[i * 128:(i + 1) * 128, :], reads=[('ygd', i, cb) for cb in range(8)], writes=['C2_yg'])
                p.op('act', lambda e: e.copy(ygb[:], yg[:]), reads=['C2_yg'], writes=['C2_ygb'])
                for k in range(8):
                    p.op('pe', lambda e: e.transpose(ptr[:, k, :], ygb[:, k * 128:(k + 1) * 128], ident_b[:]), reads=['C2_ygb', 'ident_b'], writes=['C2_pt'])
                p.op('dve', lambda e: e.tensor_copy(ygT[:], ptr[:]), reads=['C2_pt'], writes=['C2_ygT'])
                for half in range(2):
                    cs_ = slice(half * 512, (half + 1) * 512)
                    for k in range(8):
                        p.op('pe', lambda e: e.matmul(pm[half][:, :], ygT[:, k, :], gw[:, k, cs_], start=(k == 0), stop=(k == 7)),
                             reads=['C2_ygT', 'C2_gw'], writes=[('C2_pm', half)])
                    p.op('dve', lambda e: e.tensor_tensor(sg[:, cs_], pm[half][:, :], gb[:, cs_], ALU.add), reads=[('C2_pm', half), 'C2_gb'], writes=['C2_sg'])
                p.op('act', lambda e: e.activation(sg[:], sg[:], AF.Sigmoid), reads=['C2_sg'], writes=['C2_sg'])
                p.op('dve', lambda e: e.tensor_tensor(sg[:], sg[:], yg[:], ALU.mult), reads=['C2_sg', 'C2_yg'], writes=['C2_sg'])
                p.dma('sp', br[i * 128:(i + 1) * 128, 0:1024], sg[:], reads=['C2_sg'], writes=[('br', i, 0)])
        p.barrier()


    ropec = sb(es, "rope_c", [128, NT, 32]); ropes = sb(es, "rope_s", [128, NT, 32])

    def prologue_rope():
        with ExitStack() as st:
            pi_ = sb(st, "R_pi", [128, NT], I32); pf = sb(st, "R_pf", [128, NT]); ivf = sb(st, "R_ivf", [128, 32])
            ang = sb(st, "R_ang", [128, NT, 32]); ti_ = sb(st, "R_ti", [128, NT, 32], I32); tf_ = sb(st, "R_tf", [128, NT, 32])
            p.dma('sp', pi_[:], pos_in[0, :].rearrange("(i p) -> p i", p=128), writes=['R_pi'], allow_slow_non_contiguous=True)
            p.dma('sp', ivf[:], c_invfreq[0:1, :].partition_broadcast(128), writes=['R_ivf'])
            p.op('dve', lambda e: e.tensor_copy(pf[:], pi_[:]), reads=['R_pi'], writes=['R_pf'])
            p.op('dve', lambda e: e.tensor_tensor(ang[:], pf[:].unsqueeze(2).broadcast_to([128, NT, 32]),
                                                 ivf[:].unsqueeze(1).broadcast_to([128, NT, 32]), ALU.mult), reads=['R_pf', 'R_ivf'], writes=['R_ang'])
            for which, dst, key in ((0, ropes, 'rope_s'), (1, ropec, 'rope_c')):
                if which == 1:
                    p.op('dve', lambda e: e.tensor_scalar(ang[:], ang[:], float(np.pi / 2), None, ALU.add), reads=['R_ang'], writes=['R_ang'])
                p.op('dve', lambda e: e.tensor_scalar(ti_[:], ang[:], 1.0 / TWO_PI, None, ALU.mult), reads=['R_ang'], writes=['R_ti'])
                p.op('dve', lambda e: e.tensor_copy(tf_[:], ti_[:]), reads=['R_ti'], writes=['R_tf'])
                p.op('dve', lambda e: e.scalar_tensor_tensor(tf_[:], tf_[:], -TWO_PI, ang[:], ALU.mult, ALU.add), reads=['R_tf', 'R_ang'], writes=['R_tf'])
                p.op('dve', lambda e: e.tensor_scalar(tf_[:], tf_[:], float(np.pi), float(-np.pi), ALU.min, ALU.max), reads=['R_tf'], writes=['R_tf'])
                p.op('act', lambda e: e.activation(dst[:], tf_[:], AF.Sin), reads=['R_tf'], writes=[key])
        p.barrier()

    def phase_B(l):
        with ExitStack() as st:
            wuq = sb(st, "B_wuq", [128, 7, 1536], BF16); wukv = sb(st, "B_wukv", [128, 2, 2048], BF16)
            p.dma('pool', wuq[:], mla_w_uq[l, :, :].rearrange("(k p) n -> p k n", p=128), writes=['B_wuq'])
            p.dma('pool', wukv[:], mla_w_ukv[l, :, :].rearrange("(k p) n -> p k n", p=128), writes=['B_wukv'])
            gqa = sb(st, "B_gqa", [128, 896]); gkva = sb(st, "B_gkva", [128, 256]); gq = sb(st, "B_gq", [128, 192]); gk = sb(st, "B_gk", [128, 192])
            p.dma('sp', gqa[:], mla_q_a_norm[l:l + 1, :].partition_broadcast(128), writes=['B_gqa'])
            p.dma('sp', gkva[:], mla_kv_a_norm[l:l + 1, :].partition_broadcast(128), writes=['B_gkva'])
            p.dma('sp', gq[:], mla_q_norm[l:l + 1, :].partition_broadcast(128), writes=['B_gq'])
            p.dma('sp', gk[:], mla_k_norm[l:l + 1, :].partition_broadcast(128), writes=['B_gk'])
            lat = sb(st, "B_lat", [128, 1216]); latb = sb(st, "B_latb", [128, 1152], BF16); latT = sb(st, "B_latT", [128, 9, 128], BF16)
            ss = sb(st, "B_ss", [128, 2]); junk = sb(st, "B_junk", [128, 896], BF16)
            qk = [sb(st, f"B_qk{i}", [128, 8, 192]) for i in range(2)]
            sq = sb(st, "B_sq", [128, 8, 192]); hs = sb(st, "B_hs", [128, 8])
            r1 = sb(st, "B_r1", [128, 8, 32]); r2 = sb(st, "B_r2", [128, 8, 32]); r3 = sb(st, "B_r3", [128, 8, 32])
            qkb = sb(st, "B_qkb", [128, 8, 192], BF16); vb = sb(st, "B_vb", [128, 1024], BF16)
            tT = sb(st, "B_tT", [128, 16, 128], BF16)
            ptr = [ps(st, f"B_pt{i}", [128, 8, 128], BF16) for i in range(2)]
            pm = [ps(st, f"B_pm{i}", [128, 512]) for i in range(4)]
            pmi = [0]
            import os
            for i in range(int(os.environ.get("KNTB", NT))):
                t0 = i * 128
                p.dma('sp', lat[:], proj[t0:t0 + 128, C_CQ:C_CQ + 1216], reads=[('proj', i, 'all')], writes=['B_lat'])
                p.op('act', lambda e: e.activation(junk[:], lat[:, 0:896], AF.Square, accum_out=ss[:, 0:1]), reads=['B_lat'], writes=['B_junk', 'B_ss0'])
                p.op('act', lambda e: e.activation(junk[:, 0:256], lat[:, 896:1152], AF.Square, accum_out=ss[:, 1:2]), reads=['B_lat'], writes=['B_junk', 'B_ss1'])
                p.op('act', lambda e: e.activation(ss[:, 0:1], ss[:, 0:1], AF.Sqrt, bias=eps_t[:], scale=1.0 / 896), reads=['B_ss0', 'eps_t'], writes=['B_ss0'])
                p.op('act', lambda e: e.activation(ss[:, 1:2], ss[:, 1:2], AF.Sqrt, bias=eps_t[:], scale=1.0 / 256), reads=['B_ss1', 'eps_t'], writes=['B_ss1'])
                p.op('dve', lambda e: e.reciprocal(ss[:], ss[:]), reads=['B_ss0', 'B_ss1'], writes=['B_ss0', 'B_ss1'])
                p.op('dve', lambda e: e.scalar_tensor_tensor(latb[:, 0:896], lat[:, 0:896], ss[:, 0:1], gqa[:], ALU.mult, ALU.mult),
                     reads=['B_lat', 'B_ss0', 'B_gqa'], writes=['B_latb'])
                p.op('dve', lambda e: e.scalar_tensor_tensor(latb[:, 896:1152], lat[:, 896:1152], ss[:, 1:2], gkva[:], ALU.mult, ALU.mult),
                     reads=['B_lat', 'B_ss1', 'B_gkva'], writes=['B_latb'])
                BSTOP = int(os.environ.get("BSTOP", 9))
                if BSTOP <= 1:
                    continue
                for k in range(9):
                    pt = ptr[0] if k < 8 else ptr[1]
                    p.op('pe', lambda e: e.transpose(pt[:, k % 8, :], latb[:, k * 128:(k + 1) * 128], ident_b[:]), reads=['B_latb', 'ident_b'],
                         writes=[('B_pt', 0 if k < 8 else 1)])
                p.op('act', lambda e: e.copy(latT[:, 0:8, :], ptr[0][:]), reads=[('B_pt', 0)], writes=['B_latT'])
                p.op('dve', lambda e: e.tensor_copy(latT[:, 8, :], ptr[1][:, 0, :]), reads=[('B_pt', 1)], writes=['B_latT'])
                for c3 in range(3):
                    j = pmi[0] % 4
                    pmi[0] += 1
                    for k in range(7):
                        p.op('pe', lambda e: e.matmul(pm[j][:, :], latT[:, k, :], wuq[:, k, c3 * 512:(c3 + 1) * 512], start=(k == 0), stop=(k == 6)),
                             reads=['B_latT', 'B_wuq'], writes=[('B_pm', j)])
                    p.op('act', lambda e: e.copy(qk[0][:].rearrange("p h d -> p (h d)")[:, c3 * 512:(c3 + 1) * 512], pm[j][:, :]),
                         reads=[('B_pm', j)], writes=['B_qk0'])
                if BSTOP <= 2:
                    continue
                for c4 in range(4):
                    j = pmi[0] % 4
                    pmi[0] += 1
                    for k in range(2):
                        p.op('pe', lambda e: e.matmul(pm[j][:, :], latT[:, 7 + k, :], wukv[:, k, c4 * 512:(c4 + 1) * 512], start=(k == 0), stop=(k == 1)),
                             reads=['B_latT', 'B_wukv'], writes=[('B_pm', j)])
                    pv = pm[j][:, :].rearrange("p (h d) -> p h d", h=2)
                    BSKIP = os.environ.get("BSKIP", "")
                    if 'a' not in BSKIP:
                        p.op('act', lambda e: e.copy(qk[1][:, c4 * 2:(c4 + 1) * 2, 0:128], pv[:, :, 0:128]), reads=[('B_pm', j)], writes=['B_qk1'])
                    for hh in range(2):
                        hcol = (c4 * 2 + hh) * 128
                        p.op('act', lambda e: e.copy(vb[:, hcol:hcol + 128], pm[j][:, hh * 256 + 128:hh * 256 + 256]),
                             reads=[('B_pm', j)], writes=['B_vb'])
                if 'p' not in BSKIP:
                    p.op('pool', lambda e: e.tensor_copy(qk[1][:, :, 128:192], lat[:, 1152:1216].unsqueeze(1).broadcast_to([128, 8, 64])),
                         reads=['B_lat'], writes=['B_qk1'])
                if 'v' not in BSKIP:
                    p.dma('sp', v_d[t0:t0 + 128, :], vb[:], reads=['B_vb'], writes=[('v_d', i)])
                if BSTOP <= 3:
                    continue
                for which in range(2):
                    X = qk[which]
                    xk = f'B_qk{which}'
                    g = gq if which == 0 else gk
                    gk_ = 'B_gq' if which == 0 else 'B_gk'
                    p.op('pool', lambda e: e.tensor_tensor(sq[:], X[:], X[:], ALU.mult), reads=[xk], writes=['B_sq'])
                    p.op('dve', lambda e: e.tensor_reduce(hs[:], sq[:], AX.X, ALU.add), reads=['B_sq'], writes=['B_hs'])
                    p.op('act', lambda e: e.activation(hs[:], hs[:], AF.Sqrt, bias=eps_t[:], scale=1.0 / 192), reads=['B_hs', 'eps_t'], writes=['B_hs'])
                    p.op('dve', lambda e: e.reciprocal(hs[:], hs[:]), reads=['B_hs'], writes=['B_hs'])
                    p.op('dve', lambda e: e.tensor_tensor(X[:], X[:], hs[:].unsqueeze(2).broadcast_to([128, 8, 192]), ALU.mult), reads=[xk, 'B_hs'], writes=[xk])
                    p.op('pool', lambda e: e.tensor_tensor(X[:], X[:], g[:].unsqueeze(1).broadcast_to([128, 8, 192]), ALU.mult), reads=[xk, gk_], writes=[xk])
                    cb_ = ropec[:, i, :].unsqueeze(1).broadcast_to([128, 8, 32]); sb_ = ropes[:, i, :].unsqueeze(1).broadcast_to([128, 8, 32])
                    T1 = X[:, :, 128:160]; T2 = X[:, :, 160:192]
                    p.op('dve', lambda e: e.tensor_tensor(r1[:], T1, sb_, ALU.mult), reads=[xk, 'rope_s'], writes=['B_r1'])
                    p.op('dve', lambda e: e.tensor_tensor(r2[:], T2, sb_, ALU.mult), reads=[xk, 'rope_s'], writes=['B_r2'])
                    p.op('dve', lambda e: e.tensor_tensor(r3[:], T1, cb_, ALU.mult), reads=[xk, 'rope_c'], writes=['B_r3'])
                    p.op('dve', lambda e: e.tensor_tensor(T1, r3[:], r2[:], ALU.subtract), reads=['B_r3', 'B_r2'], writes=[xk])
                    p.op('dve', lambda e: e.tensor_tensor(r3[:], T2, cb_, ALU.mult), reads=[xk, 'rope_c'], writes=['B_r3'])
                    p.op('dve', lambda e: e.tensor_tensor(T2, r3[:], r1[:], ALU.add), reads=['B_r3', 'B_r1'], writes=[xk])
                    p.op('act', lambda e: e.copy(qkb[:], X[:]), reads=[xk], writes=['B_qkb'])
                    if BSTOP <= 4:
                        continue
                    for h in range(8):
                        pt = ptr[h % 2]
                        p.op('pe', lambda e: e.transpose(pt[:, 0, :], qkb[:, h, 0:128], ident_b[:]), reads=['B_qkb', 'ident_b'], writes=[('B_pt', h % 2)])
                        p.op('pe', lambda e: e.transpose(pt[0:64, 1, :], qkb[:, h, 128:192], ident_b[:]), reads=['B_qkb', 'ident_b'], writes=[('B_pt', h % 2)])
                        p.op('act', lambda e: e.copy(tT[:, 2 * h, :], pt[:, 0, :]), reads=[('B_pt', h % 2)], writes=[('B_tT', h)])
                        p.op('dve', lambda e: e.tensor_copy(tT[0:64, 2 * h + 1, :], pt[0:64, 1, :]), reads=[('B_pt', h % 2)], writes=[('B_tT', h)])
                        dstT = qT_d if which == 0 else kT_d
                        p.dma('sp', dstT[h, 0:128, t0:t0 + 128], tT[:, 2 * h, :], reads=[('B_tT', h)], writes=[('qkT', which, h, i)])
                        p.dma('sp', dstT[h, 128:192, t0:t0 + 128], tT[0:64, 2 * h + 1, :], reads=[('B_tT', h)], writes=[('qkT', which, h, i)])
        p.barrier()
        if 'b' in phases:
            return
        with ExitStack() as st:
            qT = sb(st, "B2_qT", [128, S], BF16); qTr = sb(st, "B2_qTr", [64, S], BF16)
            kT = sb(st, "B2_kT", [128, S], BF16); kTr = sb(st, "B2_kTr", [64, S], BF16)
            Va = sb(st, "B2_Va", [128, NT, 132], BF16)
            PT = [sb(st, f"B2_PT{i}", [128, 512], BF16) for i in range(2)]
            ob = sb(st, "B2_ob", [128, 128]); rs = sb(st, "B2_rs", [128, 1])
            psc = [ps(st, f"B2_ps{i}", [128, 512]) for i in range(2)]
            pac = [ps(st, f"B2_pa{i}", [128, 512]) for i in range(4)]
            p.op('dve', lambda e: e.memset(Va[:], 1.0), writes=['B2_Va'])
            SCALE = float(192 ** -0.5)
            it = 0
            for h in range(8):
                p.dma('sp', qT[:], qT_d[h, 0:128, :], writes=['B2_qT'])
                p.dma('sp', qTr[:], qT_d[h, 128:192, :], writes=['B2_qTr'])
                p.dma('sp', kT[:], kT_d[h, 0:128, :], writes=['B2_kT'])
                p.dma('sp', kTr[:], kT_d[h, 128:192, :], writes=['B2_kTr'])
                p.dma('sp', Va[:, :, 0:128], v_d[:, h * 128:(h + 1) * 128].rearrange("(i p) d -> p i d", p=128), writes=['B2_Va'])
                for qb in range(S // 512):
                    qs = slice(qb * 512, (qb + 1) * 512)
                    for kt in range(NT):
                        ks = slice(kt * 128, (kt + 1) * 128)
                        j = it % 2
                        it += 1
                        p.op('pe', lambda e: e.matmul(psc[j][:, :], kT[:, ks], qT[:, qs], start=True, stop=False), reads=['B2_kT', 'B2_qT'], writes=[('B2_ps', j)])
                        p.op('pe', lambda e: e.matmul(psc[j][:, :], kTr[:, ks], qTr[:, qs], start=False, stop=True), reads=['B2_kTr', 'B2_qTr'], writes=[('B2_ps', j)])
                        p.op('act', lambda e: e.activation(PT[j][:], psc[j][:, :], AF.Exp, scale=SCALE), reads=[('B2_ps', j)], writes=[('B2_PT', j)])
                        for sub in range(4):
                            p.op('pe', lambda e: e.matmul(pac[sub][:, 0:129], PT[j][:, sub * 128:(sub + 1) * 128], Va[:, kt, 0:129],
                                                         start=(kt == 0), stop=(kt == NT - 1)), reads=[('B2_PT', j), 'B2_Va'], writes=[('B2_pa', sub)])
                    for sub in range(4):
                        t0 = qb * 512 + sub * 128
                        p.op('dve', lambda e: e.reciprocal(rs[:], pac[sub][:, 128:129]), reads=[('B2_pa', sub)], writes=['B2_rs'])
                        p.op('dve', lambda e: e.tensor_scalar(ob[:], pac[sub][:, 0:128], rs[:], None, ALU.mult), reads=[('B2_pa', sub), 'B2_rs'], writes=['B2_ob'])
                        p.dma('sp', br[t0:t0 + 128, 1024 + h * 128:1024 + (h + 1) * 128], ob[:], reads=['B2_ob'], writes=[('br', t0 // 128, 1, h)])
        p.barrier()

    def phase_M(l):
        with ExitStack() as st:
            wk = sb(st, "M_wk", [128, 32, 1024], BF16)
            gm = sb(st, "M_gm", [128, D]); mt_ = sb(st, "M_mt", [128, D]); mb = sb(st, "M_mb", [128, D], BF16)
            memT = sb(st, "M_memT", [128, 32, 256], BF16)
            ss = sb(st, "M_ss", [128, 1]); hs = sb(st, "M_hs", [128, 4]); gqn = sb(st, "M_gqn", [128, 256]); gkn = sb(st, "M_gkn", [128, 256])
            Kt = sb(st, "M_K", [128, 1024]); sq = sb(st, "M_sq", [128, 1024]); Kb = sb(st, "M_Kb", [128, 1024], BF16)
            KmT = sb(st, "M_KmT", [128, 8, 256], BF16)
            Vm = sb(st, "M_Vm", [128, 2, 4, 260], BF16)
            ptr = [ps(st, f"M_pt{i}", [128, 8, 128], BF16) for i in range(2)]
            pm = [ps(st, f"M_pm{i}", [128, 512]) for i in range(2)]
            psc = [ps(st, f"M_ps{i}", [128, 512]) for i in range(2)]
            pac = [ps(st, f"M_pa{i}", [128, 512]) for i in range(2)]
            p.dma('sp', gm[:], mem_norm_g[l:l + 1, :].partition_broadcast(128), writes=['M_gm'])
            p.dma('sp', gqn[:], mem_q_norm[l:l + 1, :].partition_broadcast(128), writes=['M_gqn'])
            p.dma('sp', gkn[:], mem_k_norm[l:l + 1, :].partition_broadcast(128), writes=['M_gkn'])
            p.op('dve', lambda e: e.memset(Vm[:], 1.0), writes=['M_Vm'])
            for mt in range(2):
                p.dma('sp', mt_[:], mem_in[mt * 128:(mt + 1) * 128, :], writes=['M_mt'])
                p.op('act', lambda e: e.activation(mb[:], mt_[:], AF.Square, accum_out=ss[:]), reads=['M_mt'], writes=['M_mb', 'M_ss'])
                p.op('act', lambda e: e.activation(ss[:], ss[:], AF.Sqrt, bias=eps_t[:], scale=1.0 / D), reads=['M_ss', 'eps_t'], writes=['M_ss'])
                p.op('dve', lambda e: e.reciprocal(ss[:], ss[:]), reads=['M_ss'], writes=['M_ss'])
                p.op('dve', lambda e: e.scalar_tensor_tensor(mb[:], mt_[:], ss[:], gm[:], ALU.mult, ALU.mult), reads=['M_mt', 'M_ss', 'M_gm'], writes=['M_mb'])
                for k8 in range(4):
                    pt = ptr[k8 % 2]
                    for kk in range(8):
                        k = k8 * 8 + kk
                        p.op('pe', lambda e: e.transpose(pt[:, kk, :], mb[:, k * 128:(k + 1) * 128], ident_b[:]), reads=['M_mb', 'ident_b'], writes=[('M_pt', k8 % 2)])
                    p.op('act', lambda e: e.copy(memT[:, k8 * 8:(k8 + 1) * 8, mt * 128:(mt + 1) * 128], pt[:]), reads=[('M_pt', k8 % 2)], writes=['M_memT'])
            for which, wsrc in ((0, mem_w_k), (1, mem_w_v)):
                for k4 in range(4):
                    p.dma('pool', wk[:, k4 * 8:(k4 + 1) * 8, :], wsrc[l, k4 * 1024:(k4 + 1) * 1024, :].rearrange("(k p) n -> p k n", p=128), writes=['M_wk'])
                for mt in range(2):
                    for half in range(2):
                        for k in range(32):
                            p.op('pe', lambda e: e.matmul(pm[half][:, :], memT[:, k, mt * 128:(mt + 1) * 128], wk[:, k, half * 512:(half + 1) * 512],
                                                         start=(k == 0), stop=(k == 31)), reads=['M_memT', 'M_wk'], writes=[('M_pm', half)])
                        if which == 0:
                            p.op('act', lambda e: e.copy(Kt[:, half * 512:(half + 1) * 512], pm[half][:, :]), reads=[('M_pm', half)], writes=['M_K'])
                        else:
                            p.op('act', lambda e: e.copy(Vm[:, mt, half * 2:(half + 1) * 2, 0:256], pm[half][:, :].rearrange("p (h d) -> p h d", h=2)),
                                 reads=[('M_pm', half)], writes=['M_Vm'])
                    if which == 0:
                        K3 = Kt[:].rearrange("p (h d) -> p h d", h=4)
                        p.op('pool', lambda e: e.tensor_tensor(sq[:], Kt[:], Kt[:], ALU.mult), reads=['M_K'], writes=['M_sq'])
                        p.op('dve', lambda e: e.tensor_reduce(hs[:], sq[:].rearrange("p (h d) -> p h d", h=4), AX.X, ALU.add), reads=['M_sq'], writes=['M_hs'])
                        p.op('act', lambda e: e.activation(hs[:], hs[:], AF.Sqrt, bias=eps_t[:], scale=1.0 / 256), reads=['M_hs', 'eps_t'], writes=['M_hs'])
                        p.op('dve', lambda e: e.reciprocal(hs[:], hs[:]), reads=['M_hs'], writes=['M_hs'])
                        p.op('dve', lambda e: e.tensor_tensor(K3, K3, hs[:].unsqueeze(2).broadcast_to([128, 4, 256]), ALU.mult), reads=['M_K', 'M_hs'], writes=['M_K'])
                        p.op('dve', lambda e: e.tensor_tensor(Kb[:].rearrange("p (h d) -> p h d", h=4), K3, gkn[:].unsqueeze(1).broadcast_to([128, 4, 256]), ALU.mult),
                             reads=['M_K', 'M_gkn'], writes=['M_Kb'])
                        for k in range(8):
                            p.op('pe', lambda e: e.transpose(ptr[0][:, k, :], Kb[:, k * 128:(k + 1) * 128], ident_b[:]), reads=['M_Kb', 'ident_b'], writes=[('M_pt', 0)])
                        p.op('act', lambda e: e.copy(KmT[:, :, mt * 128:(mt + 1) * 128], ptr[0][:]), reads=[('M_pt', 0)], writes=['M_KmT'])
            qt = sb(st, "M_q", [128, 1024]); qb_ = sb(st, "M_qb", [128, 1024], BF16); qT = sb(st, "M_qT", [128, 8, 512], BF16)
            PT = sb(st, "M_PT", [128, 2, 512], BF16); ob = sb(st, "M_ob", [128, 1024]); rs = sb(st, "M_rs", [128, 1])
            for g4 in range(NT // 4):
                for ti in range(4):
                    i = g4 * 4 + ti
                    p.dma('sp', qt[:], proj[i * 128:(i + 1) * 128, C_MQ:C_MQ + 1024], reads=[('proj', i, 'all')], writes=['M_q'])
                    Q3 = qt[:].rearrange("p (h d) -> p h d", h=4)
                    p.op('pool', lambda e: e.tensor_tensor(sq[:], qt[:], qt[:], ALU.mult), reads=['M_q'], writes=['M_sq'])
                    p.op('dve', lambda e: e.tensor_reduce(hs[:], sq[:].rearrange("p (h d) -> p h d", h=4), AX.X, ALU.add), reads=['M_sq'], writes=['M_hs'])
                    p.op('act', lambda e: e.activation(hs[:], hs[:], AF.Sqrt, bias=eps_t[:], scale=1.0 / 256), reads=['M_hs', 'eps_t'], writes=['M_hs'])
                    p.op('dve', lambda e: e.reciprocal(hs[:], hs[:]), reads=['M_hs'], writes=['M_hs'])
                    p.op('dve', lambda e: e.tensor_tensor(Q3, Q3, hs[:].unsqueeze(2).broadcast_to([128, 4, 256]), ALU.mult), reads=['M_q', 'M_hs'], writes=['M_q'])
                    p.op('dve', lambda e: e.tensor_tensor(qb_[:].rearrange("p (h d) -> p h d", h=4), Q3, gqn[:].unsqueeze(1).broadcast_to([128, 4, 256]), ALU.mult),
                         reads=['M_q', 'M_gqn'], writes=['M_qb'])
                    for k in range(8):
                        p.op('pe', lambda e: e.transpose(ptr[ti % 2][:, k, :], qb_[:, k * 128:(k + 1) * 128], ident_b[:]), reads=['M_qb', 'ident_b'], writes=[('M_pt', ti % 2)])
                    p.op('act', lambda e: e.copy(qT[:, :, ti * 128:(ti + 1) * 128], ptr[ti % 2][:]), reads=[('M_pt', ti % 2)], writes=['M_qT'])
                for h in range(4):
                    for mt in range(2):
                        for dc in range(2):
                            p.op('pe', lambda e: e.matmul(psc[mt][:, :], KmT[:, h * 2 + dc, mt * 128:(mt + 1) * 128], qT[:, h * 2 + dc, :],
                                                         start=(dc == 0), stop=(dc == 1)), reads=['M_KmT', 'M_qT'], writes=[('M_ps', mt)])
                        p.op('act', lambda e: e.activation(PT[:, mt, :], psc[mt][:, :], AF.Exp, scale=1.0 / 16), reads=[('M_ps', mt)], writes=['M_PT'])
                    for ti in range(4):
                        i = g4 * 4 + ti
                        j = ti % 2
                        for mt in range(2):
                            p.op('pe', lambda e: e.matmul(pac[j][:, 0:257], PT[:, mt, ti * 128:(ti + 1) * 128], Vm[:, mt, h, 0:257],
                                                         start=(mt == 0), stop=(mt == 1)), reads=['M_PT', 'M_Vm'], writes=[('M_pa', j)])
                        p.op('dve', lambda e: e.reciprocal(rs[:], pac[j][:, 256:257]), reads=[('M_pa', j)], writes=['M_rs'])
                        p.op('dve', lambda e: e.tensor_scalar(ob[:, 0:256], pac[j][:, 0:256], rs[:], None, ALU.mult), reads=[('M_pa', j), 'M_rs'], writes=['M_ob'])
                        p.dma('sp', br[i * 128:(i + 1) * 128, 3072 + h * 256:3072 + (h + 1) * 256], ob[:, 0:256], reads=['M_ob'], writes=[('br', i, 3, h)])
        p.barrier()

    def phase_E(l, xsrc):
        for r in range(0, D, 512):
            p.dma('pool', wbf_out[r:r + 512, :], w_out[l, r:r + 512, :], writes=[('wbf_out', r)])
        with ExitStack() as st:
            bt = sb(st, "E_b", [128, D]); G = sb(st, "E_G", [128, D]); mg = sb(st, "E_mg", [128, D], BF16)
            bg = sb(st, "E_bg", [128, 3, 1024]); ss = sb(st, "E_ss", [128, 1]); junk = sb(st, "E_junk", [128, 1024], BF16)
            mT = sb(st, "E_mT", [128, 32, 1024], BF16)
            W = [sb(st, f"E_W{i}", [128, 32, 512], BF16) for i in range(2)]
            xt = [sb(st, f"E_x{i}", [128, 512]) for i in range(2)]
            ptr = [ps(st, f"E_pt{i}", [128, 8, 128], BF16) for i in range(2)]
            pmm = [ps(st, f"E_pm{i}", [128, 512]) for i in range(4)]
            p.dma('sp', bg[:].rearrange("p a b -> p (a b)"), branch_g[l:l + 1, :].partition_broadcast(128), writes=['E_bg'])
            gates = (C_AG, C_BG, C_CG, C_MG)
            wi = 0
            for g in range(S // 1024):
                for ti in range(8):
                    i = g * 8 + ti
                    t0 = i * 128
                    p.dma('sp', bt[:], br[t0:t0 + 128, :], reads=[('br', i, 'all')], writes=['E_b'])
                    for bi in range(4):
                        p.dma('sp', G[:, bi * 1024:(bi + 1) * 1024], proj[t0:t0 + 128, gates[bi]:gates[bi] + 1024], reads=[('proj', i, 'all')], writes=['E_G'])
                    p.op('act', lambda e: e.activation(G[:], G[:], AF.Silu), reads=['E_G'], writes=['E_G'])
                    for bi in range(4):
                        cs_ = slice(bi * 1024, (bi + 1) * 1024)
                        if bi == 2:
                            p.op('pool', lambda e: e.tensor_tensor(mg[:, cs_], bt[:, cs_], G[:, cs_], ALU.mult), reads=['E_b', 'E_G'], writes=['E_mg'])
                            continue
                        gi = {0: 0, 1: 1, 3: 2}[bi]
                        p.op('act', lambda e: e.activation(junk[:], bt[:, cs_], AF.Square, accum_out=ss[:]), reads=['E_b'], writes=['E_junk', 'E_ss'])
                        p.op('act', lambda e: e.activation(ss[:], ss[:], AF.Sqrt, bias=eps_t[:], scale=1.0 / 1024), reads=['E_ss', 'eps_t'], writes=['E_ss'])
                        p.op('dve', lambda e: e.reciprocal(ss[:], ss[:]), reads=['E_ss'], writes=['E_ss'])
                        p.op('dve', lambda e: e.scalar_tensor_tensor(bt[:, cs_], bt[:, cs_], ss[:], bg[:, gi, :], ALU.mult, ALU.mult), reads=['E_b', 'E_ss', 'E_bg'], writes=['E_b'])
                        p.op('pool', lambda e: e.tensor_tensor(mg[:, cs_], bt[:, cs_], G[:, cs_], ALU.mult), reads=['E_b', 'E_G'], writes=['E_mg'])
                    for k8 in range(4):
                        pt = ptr[k8 % 2]
                        for kk in range(8):
                            k = k8 * 8 + kk
                            p.op('pe', lambda e: e.transpose(pt[:, kk, :], mg[:, k * 128:(k + 1) * 128], ident_b[:]), reads=['E_mg', 'ident_b'], writes=[('E_pt', k8 % 2)])
                        dst = mT[:, k8 * 8:(k8 + 1) * 8, ti * 128:(ti + 1) * 128]
                        if k8 % 2 == 0:
                            p.op('act', lambda e: e.copy(dst, pt[:]), reads=[('E_pt', k8 % 2)], writes=[('E_mT', ti)])
                        else:
                            p.op('dve', lambda e: e.tensor_copy(dst, pt[:]), reads=[('E_pt', k8 % 2)], writes=[('E_mT', ti)])
                for ci in range(D // 512):
                    n0 = ci * 512
                    Wt = W[wi % 2]
                    for k4 in range(4):
                        p.dma('sp', Wt[:, k4 * 8:(k4 + 1) * 8, :], wbf_out[k4 * 1024:(k4 + 1) * 1024, n0:n0 + 512].rearrange("(k p) n -> p k n", p=128),
                              reads=[('wbf_out', k4 * 1024), ('wbf_out', k4 * 1024 + 512)], writes=[('E_W', wi % 2)])
                    for ti in range(8):
                        i = g * 8 + ti
                        t0 = i * 128
                        j = (ci * 8 + ti) % 4
                        pm = pmm[j]
                        X = xt[(ci * 8 + ti) % 2]
                        xk = ('E_x', (ci * 8 + ti) % 2)
                        p.dma('sp', X[:], xsrc[t0:t0 + 128, n0:n0 + 512], reads=[('y', i, ci)], writes=[xk])
                        for k in range(32):
                            p.op('pe', lambda e: e.matmul(pm[:, :], mT[:, k, ti * 128:(ti + 1) * 128], Wt[:, k, :], start=(k == 0), stop=(k == 31)),
                                 reads=[('E_mT', ti), ('E_W', wi % 2)], writes=[('E_pm', j)])
                        p.op('dve', lambda e: e.tensor_tensor(X[:], X[:], pm[:, :], ALU.add), reads=[('E_pm', j), xk], writes=[xk])
                        p.dma('sp', y_out[t0:t0 + 128, n0:n0 + 512], X[:], reads=[xk], writes=[('y', i, ci)])
                    wi += 1
        p.barrier()

    if dbg and 'A' not in phases:
        proj_in = din("proj_in", [S, NCOLS])
        for i in range(NT):
            p.dma('sp', proj[i * 128:(i + 1) * 128, :], proj_in[i * 128:(i + 1) * 128, :], writes=[('proj', i, 'all')])
        p.barrier()
    if 'B' in phases and 'r' not in phases:
        prologue_rope()
    if dbg and 'E' in phases and len(phases) < 6:
        br_in = din("br_in", [S, 4096])
        for i in range(NT):
            p.dma('sp', br[i * 128:(i + 1) * 128, :], br_in[i * 128:(i + 1) * 128, :], writes=[('br', i, 'all')])
        p.barrier()
    for l in range(n_layers):
        xsrc = x_in if l == 0 else y_out
        if 'A' in phases:
            phase_A(l, xsrc)
        if 'D' in phases:
            phase_D(l)
        if 'C' in phases:
            phase_C(l)
        if 'M' in phases:
            phase_M(l)
        if 'B' in phases:
            phase_B(l)
        if 'E' in phases:
            phase_E(l, xsrc)
    p.barrier()
    es.close()
    print("instructions:", p.nins)
    nc.in_names = in_names
    return nc


def make_consts():
    r = np.arange(128)
    m = np.stack([r[:, None] < r[None, :], r[:, None] > r[None, :], r[:, None] <= r[None, :], r[:, None] >= r[None, :]]).astype(np.float32)
    return {"c_ident": np.eye(128, dtype=np.float32), "c_masks": m,
            "c_iota": np.arange(1, 513, dtype=np.float32)[None, :],
            "c_invfreq": (1.0 / (np.float32(10000.0) ** (np.arange(0, 64, 2, dtype=np.float32) / np.float32(64)))).astype(np.float32)[None, :]}


_NC_CACHE = {}


def kernel(**inputs):
    nb = 4
    if 'nc' not in _NC_CACHE:
        _NC_CACHE['nc'] = build()
    nc = _NC_CACHE['nc']
    cst = make_consts()
    shared = {}
    for n in nc.in_names:
        if n in cst:
            shared[n] = cst[n]
        elif n in ("x", "mem", "positions"):
            continue
        elif n == "rwkv_r_k":
            shared[n] = np.ascontiguousarray(np.asarray(inputs[n], dtype=np.float32).reshape(L, 1024))
        elif n == "branch_g":
            shared[n] = np.ascontiguousarray(np.asarray(inputs[n], dtype=np.float32).reshape(L, 3072))
        else:
            shared[n] = np.ascontiguousarray(np.asarray(inputs[n], dtype=np.float32))
    in_maps = []
    for b in range(nb):
        m = dict(shared)
        m["x"] = np.ascontiguousarray(np.asarray(inputs["x"][b], dtype=np.float32))
        m["mem"] = np.ascontiguousarray(np.asarray(inputs["mem"][b], dtype=np.float32))
        m["positions"] = np.ascontiguousarray(np.asarray(inputs["positions"][b:b + 1]).astype(np.int32))
        in_maps.append(m)
    res = run_bass_kernel_spmd(nc, in_maps, core_ids=list(range(nb)))
    return np.stack([np.asarray(r["y"], dtype=np.float32) for r in res.results], axis=0)
```

```python
import numpy as np
from contextlib import ExitStack
import concourse.bass as bass
import concourse.mybir as mybir
from concourse.bass_utils import run_bass_kernel_spmd

F32 = mybir.dt.float32
BF16 = mybir.dt.bfloat16
I32 = mybir.dt.int32
ALU = mybir.AluOpType
AF = mybir.ActivationFunctionType
AX = mybir.AxisListType

D = 4096
S = 4096
L = 4
NCOLS = 10688
NT = S // 128
EPS = 1e-6
C_AU, C_AG, C_CQ, C_CKV, C_KPE, C_BG, C_RW, C_CG, C_MQ, C_MG = 0, 1024, 2048, 2944, 3200, 3264, 4288, 7616, 8640, 9664
NDMA = 8


class Prog:
    def __init__(self, nc, es):
        self.nc = nc
        self.E = {'pe': nc.tensor, 'act': nc.scalar, 'dve': nc.vector, 'pool': nc.gpsimd, 'sp': nc.sync}
        self.sems = {}
        for e in ['pe', 'act', 'dve', 'pool']:
            self.sems[e] = es.enter_context(nc.semaphore('s_' + e))
        for q in ['sp', 'pool']:
            for i in range(NDMA):
                self.sems[('d', q, i)] = es.enter_context(nc.semaphore(f'd_{q}_{i}'))
        self.cnt = {e: 0 for e in ['pe', 'act', 'dve', 'pool']}
        self.dma_n = {'sp': 0, 'pool': 0}
        self.waited = {}
        self.bufs = {}
        self.nins = 0

    def _wait(self, eng, key, val):
        if self.waited.get((eng, key), 0) >= val:
            return
        self.E[eng].wait_ge(self.sems[key], val)
        self.waited[(eng, key)] = val

    def _deps(self, reads, writes):
        deps = {}
        for b in reads:
            st = self.bufs.get(b)
            if st and st[0] is not None:
                k, v = st[0]
                if deps.get(k, 0) < v:
                    deps[k] = v
        for b in writes:
            st = self.bufs.get(b)
            if st:
                if st[0] is not None:
                    k, v = st[0]
                    if deps.get(k, 0) < v:
                        deps[k] = v
                for k, v in st[1].items():
                    if deps.get(k, 0) < v:
                        deps[k] = v
        return deps

    def _commit(self, tk, reads, writes):
        k, v = tk
        for b in reads:
            st = self.bufs.get(b)
            if st is None:
                st = self.bufs[b] = [None, {}]
            if st[1].get(k, 0) < v:
                st[1][k] = v
        for b in writes:
            self.bufs[b] = [tk, {}]

    def op(self, eng, fn, reads=(), writes=()):
        deps = self._deps(reads, writes)
        for k, v in deps.items():
            if k == 'pe' and eng == 'pe':
                continue
            self._wait(eng, k, v)
        ins = fn(self.E[eng])
        self.cnt[eng] += 1
        ins.then_inc(self.sems[eng], 1)
        self._commit((eng, self.cnt[eng]), reads, writes)
        self.nins += 1

    def dma(self, q, out, in_, reads=(), writes=(), **kw):
        deps = self._deps(reads, writes)
        n = self.dma_n[q]
        self.dma_n[q] += 1
        key = ('d', q, n % NDMA)
        val = 16 * (n // NDMA + 1)
        if n >= NDMA:
            deps[key] = max(deps.get(key, 0), val - 16)
        for k, v in deps.items():
            self._wait(q, k, v)
        ins = self.E[q].dma_start(out=out, in_=in_, **kw)
        ins.then_inc(self.sems[key], 16)
        self._commit((key, val), reads, writes)
        self.nins += 1

    def barrier(self):
        for e in ['pe', 'act', 'dve', 'pool', 'sp']:
            for k in self.sems:
                if isinstance(k, tuple):
                    n = self.dma_n[k[1]]
                    v = 16 * ((n - k[2] + NDMA - 1) // NDMA) if n > k[2] else 0
                else:
                    v = self.cnt[k]
                if v > 0:
                    self._wait(e, k, v)
        self.bufs.clear()


def build(n_layers=L, phases="AMBCDE", dbg=False, RWDT=BF16):
    nc = bass.Bass("TRN2", target_bir_lowering=False)
    es = ExitStack()

    in_names = []

    def din(name, shape, dt=F32):
        in_names.append(name)
        return nc.dram_tensor(name, list(shape), dt, kind="ExternalInput").ap()

    def dscr(name, shape, dt=F32):
        return nc.dram_tensor(name, list(shape), dt, kind="ExternalOutput" if dbg else "Internal").ap()

    x_in = din("x", [S, D])
    mem_in = din("mem", [256, D])
    pos_in = din("positions", [1, S], I32)
    ln_g = din("ln_g", [L, D])
    w_in = din("w_in", [L, D, NCOLS]) if ('A' in phases or not dbg) else None
    w_out = din("w_out", [L, D, D]) if ('E' in phases or not dbg) else None
    cst_ident = din("c_ident", [128, 128])
    c_masks_t = din("c_masks", [4, 128, 128])
    c_masks = [c_masks_t[i, :, :] for i in range(4)]
    c_iota = din("c_iota", [1, 512])
    c_invfreq = din("c_invfreq", [1, 32])
    branch_g = din("branch_g", [L, 3072])
    mla_q_a_norm = din("mla_q_a_norm", [L, 896]); mla_kv_a_norm = din("mla_kv_a_norm", [L, 256])
    mla_w_uq = din("mla_w_uq", [L, 896, 1536]); mla_w_ukv = din("mla_w_ukv", [L, 256, 2048])
    mla_q_norm = din("mla_q_norm", [L, 192]); mla_k_norm = din("mla_k_norm", [L, 192])
    mem_norm_g = din("mem_norm_g", [L, D]); mem_w_k = din("mem_w_k", [L, D, 1024]) if ('M' in phases or not dbg) else None
    mem_w_v = din("mem_w_v", [L, D, 1024]) if ('M' in phases or not dbg) else None
    mem_q_norm = din("mem_q_norm", [L, 256]); mem_k_norm = din("mem_k_norm", [L, 256])
    s5_lam_re = din("s5_lam_re", [L, 64, 64]); s5_lam_im = din("s5_lam_im", [L, 64, 64])
    s5_b_re = din("s5_b_re", [L, 64, 64, 16]); s5_b_im = din("s5_b_im", [L, 64, 64, 16])
    s5_c_re = din("s5_c_re", [L, 2, 64, 16, 64]); s5_c_im = din("s5_c_im", [L, 2, 64, 16, 64])
    s5_log_dt = din("s5_log_dt", [L, 2, 64]); s5_d = din("s5_d", [L, 1024]); s5_glu_w = din("s5_glu_w", [L, 1024, 1024])
    s5_glu_b = din("s5_glu_b", [L, 1024])
    rwkv_mu = din("rwkv_mu", [L, 2, 3328]); rwkv_w0 = din("rwkv_w0", [L, 2, 1024]); rwkv_w2 = din("rwkv_w2", [L, 2, 64, 1024])
    rwkv_a0 = din("rwkv_a0", [L, 2, 1024]); rwkv_a2 = din("rwkv_a2", [L, 2, 64, 1024]); rwkv_k_k = din("rwkv_k_k", [L, 1024])
    rwkv_k_a = din("rwkv_k_a", [L, 1024]); rwkv_r_k = din("rwkv_r_k", [L, 1024]); rwkv_ln_w = din("rwkv_ln_w", [L, 1024])
    rwkv_ln_b = din("rwkv_ln_b", [L, 1024])
    y_out = nc.dram_tensor("y", [S, D], F32, kind="ExternalOutput").ap()
    proj = dscr("proj", [S, NCOLS])
    wbf_in = nc.dram_tensor("wbf_in", [D, NCOLS], BF16, kind="Internal").ap()
    rwc = dscr("rwc", [S, 3328])
    ysc = dscr("ysc", [2, S, 1024])
    bon = dscr("bon", [2, S, 16])
    br = dscr("br", [S, 4096])
    ygd = dscr("ygd", [S, 1024])
    wbf_out = nc.dram_tensor("wbf_out", [D, D], BF16, kind="Internal").ap()
    qT_d = nc.dram_tensor("qT_d", [8, 192, S], BF16, kind="Internal").ap()
    kT_d = nc.dram_tensor("kT_d", [8, 192, S], BF16, kind="Internal").ap()
    v_d = nc.dram_tensor("v_d", [S, 1024], BF16, kind="Internal").ap()

    p = Prog(nc, es)

    uniq = [0]

    def sb(stack, name, shape, dt=F32):
        uniq[0] += 1
        return stack.enter_context(nc.sbuf_tensor(f"{name}_{uniq[0]}", list(shape), dt))

    def ps(stack, name, shape, dt=F32):
        uniq[0] += 1
        return stack.enter_context(nc.psum_tensor(f"{name}_{uniq[0]}", list(shape), dt))

    ident_f = sb(es, "ident_f", [128, 128], F32)
    ident_b = sb(es, "ident_b", [128, 128], BF16)
    p.dma('sp', ident_f[:], cst_ident[:, :], writes=['ident_f'])
    p.op('dve', lambda e: e.tensor_copy(ident_b[:], ident_f[:]), reads=['ident_f'], writes=['ident_b'])

    eps_t = sb(es, "eps_t", [128, 1], F32)
    p.op('dve', lambda e: e.memset(eps_t[:], EPS), writes=['eps_t'])

    def phase_A(l, xsrc):
        for r in range(0, D, 512):
            p.dma('pool', wbf_in[r:r + 512, :], w_in[l, r:r + 512, :], writes=[('wbf_in', r)])
        with ExitStack() as st:
            gt = sb(st, "A_g", [128, D], F32)
            xt = sb(st, "A_x", [128, D], F32)
            hb = sb(st, "A_hb", [128, D], BF16)
            hT = sb(st, "A_hT", [128, 32, 1024], BF16)
            W = [sb(st, f"A_W{i}", [128, 32, 512], BF16) for i in range(2)]
            ob = [sb(st, f"A_ob{i}", [128, 512], F32) for i in range(4)]
            ss = sb(st, "A_ss", [128, 1], F32)
            rstd = sb(st, "A_rstd", [128, 1], F32)
            ptr = [ps(st, f"A_pt{i}", [128, 8, 128], BF16) for i in range(2)]
            pmm = [ps(st, f"A_pm{i}", [128, 512], F32) for i in range(4)]
            p.dma('sp', gt[:], ln_g[l:l + 1, :].partition_broadcast(128), writes=['A_g'])
            nch = (NCOLS + 511) // 512
            wi = 0
            for g in range(S // 1024):
                for ti in range(8):
                    t0 = g * 1024 + ti * 128
                    p.dma('sp', xt[:], xsrc[t0:t0 + 128, :], writes=['A_x'])
                    p.op('act', lambda e: e.activation(hb[:], xt[:], AF.Square, accum_out=ss[:]),
                         reads=['A_x'], writes=['A_hb', 'A_ss'])
                    p.op('act', lambda e: e.activation(rstd[:], ss[:], AF.Sqrt, bias=eps_t[:], scale=1.0 / D),
                         reads=['A_ss', 'eps_t'], writes=['A_rstd'])
                    p.op('dve', lambda e: e.reciprocal(rstd[:], rstd[:]), reads=['A_rstd'], writes=['A_rstd'])
                    p.op('dve', lambda e: e.scalar_tensor_tensor(hb[:], xt[:], rstd[:], gt[:], ALU.mult, ALU.mult),
                         reads=['A_x', 'A_rstd', 'A_g'], writes=['A_hb'])
                    for k8 in range(4):
                        pt = ptr[k8 % 2]
                        for kk in range(8):
                            k = k8 * 8 + kk
                            p.op('pe', lambda e: e.transpose(pt[:, kk, :], hb[:, k * 128:(k + 1) * 128], ident_b[:]),
                                 reads=['A_hb', 'ident_b'], writes=[('A_pt', k8 % 2)])
                        eng = 'act' if k8 % 2 == 0 else 'dve'
                        dst = hT[:, k8 * 8:(k8 + 1) * 8, ti * 128:(ti + 1) * 128]
                        if eng == 'act':
                            p.op('act', lambda e: e.copy(dst, pt[:]), reads=[('A_pt', k8 % 2)], writes=[('A_hT', ti)])
                        else:
                            p.op('dve', lambda e: e.tensor_copy(dst, pt[:]), reads=[('A_pt', k8 % 2)], writes=[('A_hT', ti)])
                for ci in range(nch):
                    n0 = ci * 512
                    nw = min(512, NCOLS - n0)
                    Wt = W[wi % 2]
                    for k4 in range(4):
                        p.dma('sp', Wt[:, k4 * 8:(k4 + 1) * 8, 0:nw],
                              wbf_in[k4 * 1024:(k4 + 1) * 1024, n0:n0 + nw].rearrange("(k p) n -> p k n", p=128),
                              reads=[('wbf_in', (k4 * 1024) // 512 * 512), ('wbf_in', (k4 * 1024) // 512 * 512 + 512)],
                              writes=[('A_W', wi % 2)])
                    for ti in range(8):
                        t0 = g * 1024 + ti * 128
                        j = (ci * 8 + ti) % 4
                        pm = pmm[j]
                        for k in range(32):
                            p.op('pe', lambda e: e.matmul(pm[:, 0:nw], hT[:, k, ti * 128:(ti + 1) * 128], Wt[:, k, 0:nw],
                                                         start=(k == 0), stop=(k == 31)),
                                 reads=[('A_hT', ti), ('A_W', wi % 2)], writes=[('A_pm', j)])
                        if j % 2 == 0:
                            p.op('act', lambda e: e.copy(ob[j][:, 0:nw], pm[:, 0:nw]), reads=[('A_pm', j)], writes=[('A_ob', j)])
                        else:
                            p.op('dve', lambda e: e.tensor_copy(ob[j][:, 0:nw], pm[:, 0:nw]), reads=[('A_pm', j)], writes=[('A_ob', j)])
                        p.dma('sp', proj[t0:t0 + 128, n0:n0 + nw], ob[j][:, 0:nw], reads=[('A_ob', j)],
                              writes=[('proj', t0 // 128, ci)])
                    wi += 1
        p.barrier()


    RW = 3328
    NEG_E = -float(np.exp(-0.5))
    RW_DT = RWDT

    def phase_D(l):
        with ExitStack() as st:
            mup = sb(st, "D0_mup", [128, RW]); mun = sb(st, "D0_mun", [128, RW]); m0 = sb(st, "D0_m0", [128, RW])
            ct = sb(st, "D0_c", [128, RW]); pt_ = sb(st, "D0_p", [128, RW]); nt = sb(st, "D0_n", [128, RW])
            p.dma('sp', mup[:], rwkv_mu[l, 0:1, :].partition_broadcast(128), writes=['D0_mup'])
            p.dma('sp', mun[:], rwkv_mu[l, 1:2, :].partition_broadcast(128), writes=['D0_mun'])
            p.op('dve', lambda e: e.tensor_tensor(m0[:], mup[:], mun[:], ALU.add), reads=['D0_mup', 'D0_mun'], writes=['D0_m0'])
            p.op('dve', lambda e: e.tensor_scalar(m0[:], m0[:], -1.0, 1.0, ALU.mult, ALU.add), reads=['D0_m0'], writes=['D0_m0'])
            for i in range(NT):
                t0 = i * 128
                p.dma('sp', ct[:], proj[t0:t0 + 128, C_RW:C_RW + RW], reads=[('proj', i, 'all')], writes=['D0_c'])
                if i == 0:
                    p.op('pool', lambda e: e.memset(pt_[:], 0.0), writes=['D0_p'])
                    p.dma('sp', pt_[1:128, :], proj[0:127, C_RW:C_RW + RW], reads=[('proj', 0, 'all')], writes=['D0_p'])
                else:
                    p.dma('sp', pt_[:], proj[t0 - 1:t0 + 127, C_RW:C_RW + RW], reads=[('proj', i, 'all'), ('proj', i - 1, 'all')], writes=['D0_p'])
                if i == NT - 1:
                    p.op('pool', lambda e: e.memset(nt[:], 0.0), writes=['D0_n'])
                    p.dma('sp', nt[0:127, :], proj[t0 + 1:t0 + 128, C_RW:C_RW + RW], reads=[('proj', i, 'all')], writes=['D0_n'])
                else:
                    p.dma('sp', nt[:], proj[t0 + 1:t0 + 129, C_RW:C_RW + RW], reads=[('proj', i, 'all'), ('proj', i + 1, 'all')], writes=['D0_n'])
                p.op('dve', lambda e: e.tensor_tensor(ct[:], ct[:], m0[:], ALU.mult), reads=['D0_c', 'D0_m0'], writes=['D0_c'])
                p.op('pool', lambda e: e.tensor_tensor(pt_[:], pt_[:], mup[:], ALU.mult), reads=['D0_p', 'D0_mup'], writes=['D0_p'])
                p.op('pool', lambda e: e.tensor_tensor(nt[:], nt[:], mun[:], ALU.mult), reads=['D0_n', 'D0_mun'], writes=['D0_n'])
                p.op('dve', lambda e: e.tensor_tensor(ct[:], ct[:], pt_[:], ALU.add), reads=['D0_c', 'D0_p'], writes=['D0_c'])
                p.op('dve', lambda e: e.tensor_tensor(ct[:], ct[:], nt[:], ALU.add), reads=['D0_c', 'D0_n'], writes=['D0_c'])
                p.dma('sp', rwc[t0:t0 + 128, :], ct[:], reads=['D0_c'], writes=[('rwc', i)])
        p.barrier()
        with ExitStack() as st:
            def bc(name, src):
                t = sb(st, name, [128, 1024])
                p.dma('sp', t[:], src.partition_broadcast(128), writes=[name])
                return t
            kk_c = bc("D_kk_c", rwkv_k_k[l:l + 1, :]); ka_c = bc("D_ka_c", rwkv_k_a[l:l + 1, :])
            rk_c = bc("D_rk_c", rwkv_r_k[l:l + 1, :])
            c1 = sb(st, "D_c1", [128, 1024])
            p.op('dve', lambda e: e.tensor_scalar(c1[:], ka_c[:], -1.0, 1.0, ALU.mult, ALU.add), reads=['D_ka_c'], writes=['D_c1'])
            w0_c = sb(st, "D_w0", [128, 1024]); a0_c = sb(st, "D_a0", [128, 1024])
            w2_t = sb(st, "D_w2", [64, 1024]); a2_t = sb(st, "D_a2", [64, 1024])
            mS = sb(st, "D_mS", [128, 128]); mI = sb(st, "D_mI", [128, 128]); mST = sb(st, "D_mST", [128, 128])
            imask = sb(st, "D_imask", [64, 1024])
            b4 = lambda t: t[:].unsqueeze(1).broadcast_to([128, 4, 128])
            v4 = lambda a: a.rearrange("p (a b) -> p a b", a=4)
            triI = sb(st, "D_triI", [128, 128]); triC = sb(st, "D_triC", [128, 128])
            identr = sb(st, "D_identr", [128, 128], RW_DT)
            p.op('dve', lambda e: e.tensor_copy(identr[:], ident_f[:]), reads=['ident_f'], writes=['D_identr'])
            for h in range(16):
                p.op('pool', lambda e: e.tensor_copy(imask[:, h * 64:(h + 1) * 64], ident_f[0:64, 0:64]), reads=['ident_f'], writes=['D_imask'])
            rw = sb(st, "D_rw", [128, RW])
            kk = sb(st, "D_kk", [128, 1024]); ld = sb(st, "D_ld", [128, 1024]); a_t = sb(st, "D_a", [128, 1024])
            kd = sb(st, "D_kd", [128, 1024]); ba = sb(st, "D_ba", [128, 1024]); tmp = sb(st, "D_tmp", [128, 1024])
            Ab = sb(st, "D_Ab", [128, 1024]); Rb = sb(st, "D_Rb", [128, 1024]); Bb = sb(st, "D_Bb", [128, 1024]); Kb = sb(st, "D_Kb", [128, 1024])
            Abr = sb(st, "D_Abr", [128, 1024], RW_DT)
            Bt = sb(st, "D_Bt", [128, 1024], RW_DT); Kt = sb(st, "D_Kt", [128, 1024], RW_DT); Vr = sb(st, "D_Vr", [128, 1024], RW_DT)
            ydg = sb(st, "D_ydg", [64, 1024], RW_DT)
            sm = sb(st, "D_sm", [128, 64]); smT = sb(st, "D_smT", [64, 2, 128])
            hs = sb(st, "D_hs", [128, 16]); hs2 = sb(st, "D_hs2", [128, 16])
            AbT = sb(st, "D_AbT", [64, 16, 128], RW_DT); RbT = sb(st, "D_RbT", [64, 16, 128], RW_DT)
            BbT = sb(st, "D_BbT", [64, 16, 128], RW_DT); KbT = sb(st, "D_KbT", [64, 16, 128], RW_DT)
            Q = [sb(st, f"D_Q{i}", [128, 512], RW_DT) for i in range(2)]
            QT = [sb(st, f"D_QT{i}", [128, 512], RW_DT) for i in range(2)]
            P = sb(st, "D_P", [128, 512]); Pr = sb(st, "D_Pr", [128, 512], RW_DT)
            MrbT = sb(st, "D_MrbT", [128, 512], RW_DT); LakT = sb(st, "D_LakT", [128, 512], RW_DT); MrkT = sb(st, "D_MrkT", [128, 512], RW_DT)
            AXt = sb(st, "D_AX", [128, 4, 128], RW_DT); AU = sb(st, "D_AU", [128, 4, 128], RW_DT)
            RhT = sb(st, "D_RhT", [64, 16, 128], RW_DT); GT = sb(st, "D_GT", [64, 1024], RW_DT)
            Hh = sb(st, "D_H", [64, 1024]); Yh = sb(st, "D_Yh", [128, 1024])
            ST = sb(st, "D_ST", [64, 1024]); STr = sb(st, "D_STr", [64, 1024], RW_DT)
            pb = [ps(st, f"D_pb{i}", [128, 512]) for i in range(8)]
            pbi = [0]

            def nb():
                i = pbi[0] % 8
                pbi[0] += 1
                return i

            for d in range(2):
                p.dma('sp', w0_c[:], rwkv_w0[l, d:d + 1, :].partition_broadcast(128), writes=['D_w0'])
                p.dma('sp', a0_c[:], rwkv_a0[l, d:d + 1, :].partition_broadcast(128), writes=['D_a0'])
                p.dma('sp', w2_t[:], rwkv_w2[l, d, :, :], writes=['D_w2'])
                p.dma('sp', a2_t[:], rwkv_a2[l, d, :, :], writes=['D_a2'])
                cm = c_masks
                p.dma('sp', mS[:], cm[0 if d == 0 else 1], writes=['D_mS'])
                p.dma('sp', mI[:], cm[2 if d == 0 else 3], writes=['D_mI'])
                p.dma('sp', mST[:], cm[1 if d == 0 else 0], writes=['D_mST'])
                p.dma('sp', triI[:], cm[2 if d == 0 else 3], writes=['D_triI'])
                p.dma('sp', triC[:], cm[1 if d == 0 else 0], writes=['D_triC'])
                p.op('dve', lambda e: e.memset(ST[:], 0.0), writes=['D_ST'])
                p.op('dve', lambda e: e.memset(STr[:], 0.0), writes=['D_STr'])
                order = range(NT) if d == 0 else range(NT - 1, -1, -1)
                for c in order:
                    t0 = c * 128
                    p.dma('sp', rw[:], rwc[t0:t0 + 128, :], reads=[('rwc', c)], writes=['D_rw'])
                    r_ = rw[:, 0:1024]; k_ = rw[:, 1024:2048]; v_ = rw[:, 2048:3072]
                    win = rw[:, 3072 + 64 * d:3136 + 64 * d]; ain = rw[:, 3200 + 64 * d:3264 + 64 * d]
                    p.op('dve', lambda e: e.tensor_tensor(kk[:], k_, kk_c[:], ALU.mult), reads=['D_rw', 'D_kk_c'], writes=['D_kk'])
                    p.op('pool', lambda e: e.tensor_tensor(tmp[:], kk[:], kk[:], ALU.mult), reads=['D_kk'], writes=['D_tmp'])
                    p.op('dve', lambda e: e.tensor_reduce(hs[:], tmp[:].rearrange("p (h j) -> p h j", h=16), AX.X, ALU.add), reads=['D_tmp'], writes=['D_hs'])
                    p.op('act', lambda e: e.activation(hs[:], hs[:], AF.Sqrt), reads=['D_hs'], writes=['D_hs'])
                    p.op('dve', lambda e: e.tensor_scalar(hs[:], hs[:], 1e-12, None, ALU.max), reads=['D_hs'], writes=['D_hs'])
                    p.op('dve', lambda e: e.reciprocal(hs[:], hs[:]), reads=['D_hs'], writes=['D_hs'])
                    p.op('dve', lambda e: e.tensor_tensor(kk[:].rearrange("p (h j) -> p h j", h=16), kk[:].rearrange("p (h j) -> p h j", h=16),
                                                         hs[:].unsqueeze(2).broadcast_to([128, 16, 64]), ALU.mult), reads=['D_kk', 'D_hs'], writes=['D_kk'])
                    p.op('act', lambda e: e.activation(sm[:], win, AF.Tanh), reads=['D_rw'], writes=['D_sm'])
                    b0 = nb()
                    p.op('pe', lambda e: e.transpose(pb[b0][0:64, 0:128], sm[:], ident_f[:]), reads=['D_sm', 'ident_f'], writes=[('D_pb', b0)])
                    p.op('pe', lambda e: e.transpose(pb[b0][0:64, 128:256], ain, ident_f[:]), reads=['D_rw', 'ident_f'], writes=[('D_pb', b0)])
                    p.op('act', lambda e: e.copy(smT[:].rearrange("p a b -> p (a b)"), pb[b0][0:64, 0:256]), reads=[('D_pb', b0)], writes=['D_smT'])
                    for half in range(2):
                        cs_ = slice(half * 512, (half + 1) * 512)
                        b1 = nb()
                        p.op('pe', lambda e: e.matmul(pb[b1][:, :], smT[:, 0, :], w2_t[:, cs_], start=True, stop=True),
                             reads=['D_smT', 'D_w2'], writes=[('D_pb', b1)])
                        p.op('dve', lambda e: e.tensor_tensor(ld[:, cs_], pb[b1][:, :], w0_c[:, cs_], ALU.add), reads=[('D_pb', b1), 'D_w0'], writes=['D_ld'])
                        b2 = nb()
                        p.op('pe', lambda e: e.matmul(pb[b2][:, :], smT[:, 1, :], a2_t[:, cs_], start=True, stop=True),
                             reads=['D_smT', 'D_a2'], writes=[('D_pb', b2)])
                        p.op('dve', lambda e: e.tensor_tensor(a_t[:, cs_], pb[b2][:, :], a0_c[:, cs_], ALU.add), reads=[('D_pb', b2), 'D_a0'], writes=['D_a'])
                    p.op('act', lambda e: e.activation(ld[:], ld[:], AF.Sigmoid), reads=['D_ld'], writes=['D_ld'])
                    p.op('act', lambda e: e.activation(a_t[:], a_t[:], AF.Sigmoid), reads=['D_a'], writes=['D_a'])
                    p.op('pool', lambda e: e.tensor_scalar(ld[:], ld[:], NEG_E, None, ALU.mult), reads=['D_ld'], writes=['D_ld'])
                    p.op('dve', lambda e: e.tensor_tensor(tmp[:], a_t[:], ka_c[:], ALU.mult), reads=['D_a', 'D_ka_c'], writes=['D_tmp'])
                    p.op('dve', lambda e: e.tensor_tensor(tmp[:], tmp[:], c1[:], ALU.add), reads=['D_tmp', 'D_c1'], writes=['D_tmp'])
                    p.op('dve', lambda e: e.tensor_tensor(kd[:], tmp[:], k_, ALU.mult), reads=['D_tmp', 'D_rw'], writes=['D_kd'])
                    p.op('pool', lambda e: e.tensor_tensor(ba[:], kk[:], a_t[:], ALU.mult), reads=['D_kk', 'D_a'], writes=['D_ba'])
                    p.op('pool', lambda e: e.tensor_tensor(tmp[:], kd[:], rk_c[:], ALU.mult), reads=['D_kd', 'D_rk_c'], writes=['D_tmp'])
                    p.op('pool', lambda e: e.tensor_tensor(tmp[:], tmp[:], r_, ALU.mult), reads=['D_tmp', 'D_rw'], writes=['D_tmp'])
                    p.op('dve', lambda e: e.tensor_reduce(hs2[:], tmp[:].rearrange("p (h j) -> p h j", h=16), AX.X, ALU.add), reads=['D_tmp'], writes=['D_hs2'])
                    p.dma('sp', bon[d, t0:t0 + 128, :], hs2[:], reads=['D_hs2'], writes=[('bon', d, c)])
                    for half in range(2):
                        cs_ = slice(half * 512, (half + 1) * 512)
                        bcs = nb()
                        p.op('pe', lambda e: e.matmul(pb[bcs][:, :], triI[:], ld[:, cs_], start=True, stop=True), reads=['D_triI', 'D_ld'], writes=[('D_pb', bcs)])
                        brm = nb()
                        p.op('pe', lambda e: e.matmul(pb[brm][:, :], triC[:], ld[:, cs_], start=True, stop=True), reads=['D_triC', 'D_ld'], writes=[('D_pb', brm)])
                        p.op('dve', lambda e: e.tensor_tensor(tmp[:, cs_], pb[bcs][:, :], ld[:, cs_], ALU.subtract), reads=[('D_pb', bcs), 'D_ld'], writes=['D_tmp'])
                        p.op('act', lambda e: e.activation(tmp[:, cs_], tmp[:, cs_], AF.Exp), reads=['D_tmp'], writes=['D_tmp'])
                        p.op('dve', lambda e: e.scalar_tensor_tensor(Ab[:, cs_], kk[:, cs_], -1.0, tmp[:, cs_], ALU.mult, ALU.mult), reads=['D_kk', 'D_tmp'], writes=['D_Ab'])
                        p.op('act', lambda e: e.activation(Rb[:, cs_], pb[bcs][:, :], AF.Exp), reads=[('D_pb', bcs)], writes=['D_Rb'])
                        p.op('act', lambda e: e.activation(Kt[0:64, cs_] if False else tmp[0:64, cs_], pb[brm][0:64, :], AF.Exp), reads=[('D_pb', brm), 'D_tmp'], writes=['D_tmp'])
                        p.op('dve', lambda e: e.tensor_tensor(tmp[0:64, cs_], tmp[0:64, cs_], Rb[0:64, cs_], ALU.mult), reads=['D_tmp', 'D_Rb'], writes=['D_tmp'])
                        p.op('dve', lambda e: e.tensor_tensor(ydg[:, cs_], tmp[0:64, cs_], imask[:, cs_], ALU.mult), reads=['D_tmp', 'D_imask'], writes=['D_ydg'])
                        p.op('pool', lambda e: e.tensor_tensor(Rb[:, cs_], Rb[:, cs_], r_[:, cs_] if False else rw[:, half * 512:(half + 1) * 512], ALU.mult), reads=['D_Rb', 'D_rw', 'D_tmp'], writes=['D_Rb'])
                        p.op('act', lambda e: e.activation(tmp[:, cs_], pb[bcs][:, :], AF.Exp, scale=-1.0), reads=[('D_pb', bcs), 'D_tmp', 'D_ydg'], writes=['D_tmp'])
                        p.op('dve', lambda e: e.tensor_tensor(Bb[:, cs_], ba[:, cs_], tmp[:, cs_], ALU.mult), reads=['D_ba', 'D_tmp'], writes=['D_Bb'])
                        p.op('pool', lambda e: e.tensor_tensor(Kb[:, cs_], kd[:, cs_], tmp[:, cs_], ALU.mult), reads=['D_kd', 'D_tmp'], writes=['D_Kb'])
                        p.op('act', lambda e: e.activation(tmp[:, cs_], pb[brm][:, :], AF.Exp), reads=[('D_pb', brm), 'D_tmp', 'D_Bb', 'D_Kb'], writes=['D_tmp'])
                        p.op('dve', lambda e: e.tensor_tensor(Bt[:, cs_], ba[:, cs_], tmp[:, cs_], ALU.mult), reads=['D_ba', 'D_tmp'], writes=['D_Bt'])
                        p.op('pool', lambda e: e.tensor_tensor(Kt[:, cs_], kd[:, cs_], tmp[:, cs_], ALU.mult), reads=['D_kd', 'D_tmp'], writes=['D_Kt'])
                    p.op('act', lambda e: e.copy(Vr[:], v_), reads=['D_rw'], writes=['D_Vr'])
                    p.op('act', lambda e: e.copy(Abr[:], Ab[:]), reads=['D_Ab'], writes=['D_Abr'])
                    for (src, dstT, nm) in ((Ab, AbT, 'D_AbT'), (Rb, RbT, 'D_RbT'), (Bb, BbT, 'D_BbT'), (Kb, KbT, 'D_KbT')):
                        srcn = {'D_AbT': 'D_Ab', 'D_RbT': 'D_Rb', 'D_BbT': 'D_Bb', 'D_KbT': 'D_Kb'}[nm]
                        for h4 in range(4):
                            bt = nb()
                            for hl in range(4):
                                h = h4 * 4 + hl
                                p.op('pe', lambda e: e.transpose(pb[bt][0:64, hl * 128:(hl + 1) * 128], src[:, h * 64:(h + 1) * 64], ident_f[:]),
                                     reads=[srcn, 'ident_f'], writes=[('D_pb', bt)])
                            dst = dstT[:, h4 * 4:(h4 + 1) * 4, :].rearrange("p a b -> p (a b)")
                            if h4 % 2 == 0:
                                p.op('act', lambda e: e.copy(dst, pb[bt][0:64, :]), reads=[('D_pb', bt)], writes=[nm])
                            else:
                                p.op('dve', lambda e: e.tensor_copy(dst, pb[bt][0:64, :]), reads=[('D_pb', bt)], writes=[nm])
                    for h4 in range(4):
                        bA, bB, bC, bD, bE = nb(), nb(), nb(), nb(), nb()
                        for hl in range(4):
                            h = h4 * 4 + hl
                            sl = slice(hl * 128, (hl + 1) * 128)
                            p.op('pe', lambda e: e.matmul(pb[bA][:, sl], BbT[:, h, :], AbT[:, h, :], start=True, stop=True), reads=['D_BbT', 'D_AbT'], writes=[('D_pb', bA)])
                            p.op('pe', lambda e: e.matmul(pb[bB][:, sl], BbT[:, h, :], RbT[:, h, :], start=True, stop=True), reads=['D_BbT', 'D_RbT'], writes=[('D_pb', bB)])
                            p.op('pe', lambda e: e.matmul(pb[bC][:, sl], KbT[:, h, :], AbT[:, h, :], start=True, stop=True), reads=['D_KbT', 'D_AbT'], writes=[('D_pb', bC)])
                            p.op('pe', lambda e: e.matmul(pb[bD][:, sl], KbT[:, h, :], RbT[:, h, :], start=True, stop=True), reads=['D_KbT', 'D_RbT'], writes=[('D_pb', bD)])
                            p.op('pe', lambda e: e.matmul(pb[bE][:, sl], AbT[:, h, :], BbT[:, h, :], start=True, stop=True), reads=['D_BbT', 'D_AbT'], writes=[('D_pb', bE)])
                        qi = 0
                        p.op('dve', lambda e: e.tensor_tensor(v4(Q[qi][:]), v4(pb[bA][:, :]), b4(mS), ALU.mult), reads=[('D_pb', bA), 'D_mS'], writes=[('D_Q', qi)])
                        p.op('dve', lambda e: e.tensor_tensor(v4(MrbT[:]), v4(pb[bB][:, :]), b4(mI), ALU.mult), reads=[('D_pb', bB), 'D_mI'], writes=['D_MrbT'])
                        p.op('dve', lambda e: e.tensor_tensor(v4(LakT[:]), v4(pb[bC][:, :]), b4(mS), ALU.mult), reads=[('D_pb', bC), 'D_mS'], writes=['D_LakT'])
                        p.op('dve', lambda e: e.tensor_tensor(v4(MrkT[:]), v4(pb[bD][:, :]), b4(mI), ALU.mult), reads=[('D_pb', bD), 'D_mI'], writes=['D_MrkT'])
                        p.op('dve', lambda e: e.tensor_tensor(v4(QT[qi][:]), v4(pb[bE][:, :]), b4(mST), ALU.mult), reads=[('D_pb', bE), 'D_mST'], writes=[('D_QT', qi)])
                        p.op('pool', lambda e: e.tensor_tensor(v4(P[:]), v4(Q[qi][:]), b4(ident_f), ALU.add), reads=[('D_Q', qi), 'ident_f'], writes=['D_P'])
                        p.op('pool', lambda e: e.tensor_copy(Pr[:], P[:]), reads=['D_P'], writes=['D_Pr'])
                        for lvl in range(6):
                            qn = 1 - qi
                            bqT = nb()
                            for hl in range(4):
                                sl = slice(hl * 128, (hl + 1) * 128)
                                p.op('pe', lambda e: e.matmul(pb[bqT][:, sl], Q[qi][:, sl], QT[qi][:, sl], start=True, stop=True),
                                     reads=[('D_Q', qi), ('D_QT', qi)], writes=[('D_pb', bqT)])
                            if lvl < 5:
                                bq = nb()
                                for hl in range(4):
                                    sl = slice(hl * 128, (hl + 1) * 128)
                                    p.op('pe', lambda e: e.matmul(pb[bq][:, sl], QT[qi][:, sl], Q[qi][:, sl], start=True, stop=True),
                                         reads=[('D_Q', qi), ('D_QT', qi)], writes=[('D_pb', bq)])
                            p.op('act', lambda e: e.copy(QT[qn][:], pb[bqT][:, :]), reads=[('D_pb', bqT)], writes=[('D_QT', qn)])
                            if lvl < 5:
                                p.op('dve', lambda e: e.tensor_copy(Q[qn][:], pb[bq][:, :]), reads=[('D_pb', bq)], writes=[('D_Q', qn)])
                            bp = nb()
                            for hl in range(4):
                                sl = slice(hl * 128, (hl + 1) * 128)
                                p.op('pe', lambda e: e.matmul(pb[bp][:, sl], QT[qn][:, sl], Pr[:, sl], start=True, stop=True),
                                     reads=[('D_QT', qn), 'D_Pr'], writes=[('D_pb', bp)])
                            p.op('dve', lambda e: e.tensor_tensor(P[:], P[:], pb[bp][:, :], ALU.add), reads=['D_P', ('D_pb', bp)], writes=['D_P'])
                            p.op('pool', lambda e: e.tensor_copy(Pr[:], P[:]), reads=['D_P'], writes=['D_Pr'])
                            qi = qn
                        bx = nb()
                        for hl in range(4):
                            h = h4 * 4 + hl
                            p.op('pe', lambda e: e.matmul(pb[bx][:, hl * 64:(hl + 1) * 64], LakT[:, hl * 128:(hl + 1) * 128], Vr[:, h * 64:(h + 1) * 64], start=True, stop=True),
                                 reads=['D_LakT', 'D_Vr'], writes=[('D_pb', bx)])
                        p.op('act', lambda e: e.copy(AXt[:, :, 64:128], pb[bx][:, 0:256].rearrange("p (a b) -> p a b", a=4)), reads=[('D_pb', bx)], writes=['D_AX'])
                        p.op('pool', lambda e: e.tensor_copy(AXt[:, :, 0:64], Abr[:, h4 * 256:(h4 + 1) * 256].rearrange("p (a b) -> p a b", a=4)), reads=['D_Abr'], writes=['D_AX'])
                        bu = nb()
                        for hl in range(4):
                            p.op('pe', lambda e: e.matmul(pb[bu][:, hl * 128:(hl + 1) * 128], Pr[:, hl * 128:(hl + 1) * 128], AXt[:, hl, :], start=True, stop=True),
                                 reads=['D_Pr', 'D_AX'], writes=[('D_pb', bu)])
                        p.op('act', lambda e: e.copy(AU[:].rearrange("p a b -> p (a b)"), pb[bu][:, :]), reads=[('D_pb', bu)], writes=['D_AU'])
                        br_, bg, bh, by = nb(), nb(), nb(), nb()
                        for hl in range(4):
                            h = h4 * 4 + hl
                            hc = slice(h * 64, (h + 1) * 64)
                            p.op('pe', lambda e: e.matmul(pb[br_][0:64, hl * 128:(hl + 1) * 128], AU[:, hl, 0:64], MrbT[:, hl * 128:(hl + 1) * 128], start=True, stop=True),
                                 reads=['D_AU', 'D_MrbT'], writes=[('D_pb', br_)])
                            p.op('pe', lambda e: e.matmul(pb[bg][0:64, hl * 64:(hl + 1) * 64], AU[:, hl, 0:64], Bt[:, hc], start=True, stop=False),
                                 reads=['D_AU', 'D_Bt'], writes=[('D_pb', bg)])
                            p.op('pe', lambda e: e.matmul(pb[bg][0:64, hl * 64:(hl + 1) * 64], identr[0:64, 0:64], ydg[:, hc], start=False, stop=True),
                                 reads=['D_identr', 'D_ydg'], writes=[('D_pb', bg)])
                            p.op('pe', lambda e: e.matmul(pb[bh][0:64, hl * 64:(hl + 1) * 64], Bt[:, hc], AU[:, hl, 64:128], start=True, stop=False),
                                 reads=['D_AU', 'D_Bt'], writes=[('D_pb', bh)])
                            p.op('pe', lambda e: e.matmul(pb[bh][0:64, hl * 64:(hl + 1) * 64], Kt[:, hc], Vr[:, hc], start=False, stop=True),
                                 reads=['D_Kt', 'D_Vr'], writes=[('D_pb', bh)])
                            p.op('pe', lambda e: e.matmul(pb[by][:, hl * 64:(hl + 1) * 64], MrbT[:, hl * 128:(hl + 1) * 128], AU[:, hl, 64:128], start=True, stop=False),
                                 reads=['D_AU', 'D_MrbT'], writes=[('D_pb', by)])
                            p.op('pe', lambda e: e.matmul(pb[by][:, hl * 64:(hl + 1) * 64], MrkT[:, hl * 128:(hl + 1) * 128], Vr[:, hc], start=False, stop=True),
                                 reads=['D_MrkT', 'D_Vr'], writes=[('D_pb', by)])
                        p.op('dve', lambda e: e.tensor_tensor(RhT[:, h4 * 4:(h4 + 1) * 4, :].rearrange("p a b -> p (a b)"), pb[br_][0:64, :],
                                                             RbT[:, h4 * 4:(h4 + 1) * 4, :].rearrange("p a b -> p (a b)"), ALU.add),
                             reads=[('D_pb', br_), 'D_RbT'], writes=['D_RhT'])
                        p.op('act', lambda e: e.copy(GT[:, h4 * 256:(h4 + 1) * 256], pb[bg][0:64, 0:256]), reads=[('D_pb', bg)], writes=['D_GT'])
                        p.op('act', lambda e: e.copy(Hh[:, h4 * 256:(h4 + 1) * 256], pb[bh][0:64, 0:256]), reads=[('D_pb', bh)], writes=['D_H'])
                        p.op('dve', lambda e: e.tensor_copy(Yh[:, h4 * 256:(h4 + 1) * 256], pb[by][:, 0:256]), reads=[('D_pb', by)], writes=['D_Yh'])
                    for half in range(2):
                        bY = nb()
                        bS = nb()
                        for hh in range(8):
                            h = half * 8 + hh
                            hc = slice(h * 64, (h + 1) * 64)
                            p.op('pe', lambda e: e.matmul(pb[bY][:, hh * 64:(hh + 1) * 64], RhT[:, h, :], STr[:, hc], start=True, stop=True),
                                 reads=['D_RhT', 'D_STr'], writes=[('D_pb', bY)])
                            p.op('pe', lambda e: e.matmul(pb[bS][0:64, hh * 64:(hh + 1) * 64], GT[:, hc], STr[:, hc], start=True, stop=True),
                                 reads=['D_GT', 'D_STr'], writes=[('D_pb', bS)])
                        cs_ = slice(half * 512, (half + 1) * 512)
                        p.op('dve', lambda e: e.tensor_tensor(Yh[:, cs_], pb[bY][:, :], Yh[:, cs_], ALU.add), reads=[('D_pb', bY), 'D_Yh'], writes=['D_Yh'])
                        p.op('dve', lambda e: e.tensor_tensor(ST[:, cs_], pb[bS][0:64, :], Hh[:, cs_], ALU.add), reads=[('D_pb', bS), 'D_H'], writes=[('D_ST', half)])
                    p.op('act', lambda e: e.copy(STr[:], ST[:]), reads=[('D_ST', 0), ('D_ST', 1)], writes=['D_STr'])
                    p.dma('sp', ysc[d, t0:t0 + 128, :], Yh[:], reads=['D_Yh'], writes=[('ysc', d, c)])
        p.barrier()
        with ExitStack() as st:
            lnw = sb(st, "D2_lnw", [128, 1024]); lnb = sb(st, "D2_lnb", [128, 1024])
            p.dma('sp', lnw[:], rwkv_ln_w[l:l + 1, :].partition_broadcast(128), writes=['D2_lnw'])
            p.dma('sp', lnb[:], rwkv_ln_b[l:l + 1, :].partition_broadcast(128), writes=['D2_lnb'])
            y0 = sb(st, "D2_y0", [128, 1024]); y1 = sb(st, "D2_y1", [128, 1024]); vt = sb(st, "D2_v", [128, 1024]); sq = sb(st, "D2_sq", [128, 1024])
            b0t = sb(st, "D2_b0", [128, 16]); b1t = sb(st, "D2_b1", [128, 16]); mean = sb(st, "D2_mean", [128, 16]); var = sb(st, "D2_var", [128, 16])
            eps2 = sb(st, "D2_eps", [128, 1])
            p.op('dve', lambda e: e.memset(eps2[:], 64e-5), writes=['D2_eps'])
            v3 = lambda t: t[:].rearrange("p (h j) -> p h j", h=16)
            bc3 = lambda t: t[:].unsqueeze(2).broadcast_to([128, 16, 64])
            for i in range(NT):
                t0 = i * 128
                p.dma('sp', y0[:], ysc[0, t0:t0 + 128, :], reads=[('ysc', 0, i)], writes=['D2_y0'])
                p.dma('sp', y1[:], ysc[1, t0:t0 + 128, :], reads=[('ysc', 1, i)], writes=['D2_y1'])
                p.dma('sp', vt[:], rwc[t0:t0 + 128, 2048:3072], reads=[('rwc', i)], writes=['D2_v'])
                p.dma('sp', b0t[:], bon[0, t0:t0 + 128, :], reads=[('bon', 0, i)], writes=['D2_b0'])
                p.dma('sp', b1t[:], bon[1, t0:t0 + 128, :], reads=[('bon', 1, i)], writes=['D2_b1'])
                p.op('dve', lambda e: e.tensor_tensor(y0[:], y0[:], y1[:], ALU.add), reads=['D2_y0', 'D2_y1'], writes=['D2_y0'])
                p.op('dve', lambda e: e.tensor_reduce(mean[:], v3(y0), AX.X, ALU.add), reads=['D2_y0'], writes=['D2_mean'])
                p.op('dve', lambda e: e.tensor_scalar(mean[:], mean[:], 1.0 / 64, None, ALU.mult), reads=['D2_mean'], writes=['D2_mean'])
                p.op('dve', lambda e: e.tensor_tensor(v3(y0), v3(y0), bc3(mean), ALU.subtract), reads=['D2_y0', 'D2_mean'], writes=['D2_y0'])
                p.op('pool', lambda e: e.tensor_tensor(sq[:], y0[:], y0[:], ALU.mult), reads=['D2_y0'], writes=['D2_sq'])
                p.op('dve', lambda e: e.tensor_reduce(var[:], v3(sq), AX.X, ALU.add), reads=['D2_sq'], writes=['D2_var'])
                p.op('act', lambda e: e.activation(var[:], var[:], AF.Sqrt, bias=eps2[:], scale=1.0 / 64), reads=['D2_var', 'D2_eps'], writes=['D2_var'])
                p.op('dve', lambda e: e.reciprocal(var[:], var[:]), reads=['D2_var'], writes=['D2_var'])
                p.op('dve', lambda e: e.tensor_tensor(v3(y0), v3(y0), bc3(var), ALU.mult), reads=['D2_y0', 'D2_var'], writes=['D2_y0'])
                p.op('pool', lambda e: e.tensor_tensor(y0[:], y0[:], lnw[:], ALU.mult), reads=['D2_y0', 'D2_lnw'], writes=['D2_y0'])
                p.op('pool', lambda e: e.tensor_tensor(y0[:], y0[:], lnb[:], ALU.add), reads=['D2_y0', 'D2_lnb'], writes=['D2_y0'])
                p.op('dve', lambda e: e.tensor_tensor(b0t[:], b0t[:], b1t[:], ALU.add), reads=['D2_b0', 'D2_b1'], writes=['D2_b0'])
                p.op('dve', lambda e: e.tensor_tensor(v3(vt), v3(vt), bc3(b0t), ALU.mult), reads=['D2_v', 'D2_b0'], writes=['D2_v'])
                p.op('dve', lambda e: e.tensor_tensor(y0[:], y0[:], vt[:], ALU.add), reads=['D2_y0', 'D2_v'], writes=['D2_y0'])
                p.dma('sp', br[t0:t0 + 128, 2048:3072], y0[:], reads=['D2_y0'], writes=[('br', i, 2)])
        p.barrier()


    TWO_PI = float(2 * np.pi)

    def phase_C(l):
        with ExitStack() as st:
            TC = 512
            lr = sb(st, "C_lr", [128, 32]); li = sb(st, "C_li", [128, 32])
            for two in range(2):
                p.dma('sp', lr[two * 64:(two + 1) * 64, :], s5_lam_re[l, two::2, :].rearrange("q p -> p q"), writes=['C_lr'], allow_slow_non_contiguous=True)
                p.dma('sp', li[two * 64:(two + 1) * 64, :], s5_lam_im[l, two::2, :].rearrange("q p -> p q"), writes=['C_li'], allow_slow_non_contiguous=True)
            den = sb(st, "C_den", [128, 32]); t_a = sb(st, "C_ta", [128, 32]); t_b = sb(st, "C_tb", [128, 32]); t_c = sb(st, "C_tc", [128, 32])
            t_i = sb(st, "C_ti", [128, 32], I32)
            p.op('dve', lambda e: e.tensor_tensor(den[:], lr[:], lr[:], ALU.mult), reads=['C_lr'], writes=['C_den'])
            p.op('dve', lambda e: e.tensor_tensor(t_a[:], li[:], li[:], ALU.mult), reads=['C_li'], writes=['C_ta'])
            p.op('dve', lambda e: e.tensor_tensor(den[:], den[:], t_a[:], ALU.add), reads=['C_den', 'C_ta'], writes=['C_den'])
            p.op('dve', lambda e: e.reciprocal(den[:], den[:]), reads=['C_den'], writes=['C_den'])
            mag = [sb(st, f"C_mag{d}", [128, 32]) for d in range(2)]
            th = [sb(st, f"C_th{d}", [128, 32]) for d in range(2)]
            cre = [sb(st, f"C_cre{d}", [128, 32]) for d in range(2)]
            cim = [sb(st, f"C_cim{d}", [128, 32]) for d in range(2)]
            dtt = sb(st, "C_dt", [128, 32]); sn = sb(st, "C_sn", [128, 32]); cs = sb(st, "C_cs", [128, 32])

            def emit_sin(out, ang, n, key_out, key_ang, ti_, tf_, kti, ktf):
                p.op('dve', lambda e: e.tensor_scalar(ti_, ang, 1.0 / TWO_PI, None, ALU.mult), reads=[key_ang], writes=[kti])
                p.op('dve', lambda e: e.tensor_copy(tf_, ti_), reads=[kti], writes=[ktf])
                p.op('dve', lambda e: e.scalar_tensor_tensor(tf_, tf_, -TWO_PI, ang, ALU.mult, ALU.add), reads=[ktf, key_ang], writes=[ktf])
                p.op('dve', lambda e: e.tensor_scalar(tf_, tf_, float(np.pi), float(-np.pi), ALU.min, ALU.max), reads=[ktf], writes=[ktf])
                p.op('act', lambda e: e.activation(out, tf_, AF.Sin), reads=[ktf], writes=[key_out])

            for d in range(2):
                for two in range(2):
                    p.dma('sp', dtt[two * 64:(two + 1) * 64, :], s5_log_dt[l, d:d + 1, two::2].partition_broadcast(64), writes=['C_dt'],
                          allow_slow_non_contiguous=True)
                p.op('act', lambda e: e.activation(dtt[:], dtt[:], AF.Exp), reads=['C_dt'], writes=['C_dt'])
                p.op('dve', lambda e: e.tensor_tensor(t_a[:], lr[:], dtt[:], ALU.mult), reads=['C_lr', 'C_dt'], writes=['C_ta'])
                p.op('act', lambda e: e.activation(mag[d][:], t_a[:], AF.Exp), reads=['C_ta'], writes=[f'C_mag{d}'])
                p.op('dve', lambda e: e.tensor_tensor(th[d][:], li[:], dtt[:], ALU.mult), reads=['C_li', 'C_dt'], writes=[f'C_th{d}'])
                emit_sin(sn[:], th[d][:], 32, 'C_sn', f'C_th{d}', t_i[:], t_b[:], 'C_ti', 'C_tb')
                p.op('dve', lambda e: e.tensor_scalar(t_c[:], th[d][:], float(np.pi / 2), None, ALU.add), reads=[f'C_th{d}'], writes=['C_tc'])
                emit_sin(cs[:], t_c[:], 32, 'C_cs', 'C_tc', t_i[:], t_b[:], 'C_ti', 'C_tb')
                p.op('dve', lambda e: e.tensor_tensor(cs[:], cs[:], mag[d][:], ALU.mult), reads=['C_cs', f'C_mag{d}'], writes=['C_cs'])
                p.op('dve', lambda e: e.tensor_scalar(cs[:], cs[:], -1.0, None, ALU.add), reads=['C_cs'], writes=['C_cs'])
                p.op('dve', lambda e: e.tensor_tensor(sn[:], sn[:], mag[d][:], ALU.mult), reads=['C_sn', f'C_mag{d}'], writes=['C_sn'])
                p.op('dve', lambda e: e.tensor_tensor(t_a[:], cs[:], lr[:], ALU.mult), reads=['C_cs', 'C_lr'], writes=['C_ta'])
                p.op('dve', lambda e: e.tensor_tensor(t_b[:], sn[:], li[:], ALU.mult), reads=['C_sn', 'C_li'], writes=['C_tb'])
                p.op('dve', lambda e: e.tensor_tensor(t_a[:], t_a[:], t_b[:], ALU.add), reads=['C_ta', 'C_tb'], writes=['C_ta'])
                p.op('dve', lambda e: e.tensor_tensor(cre[d][:], t_a[:], den[:], ALU.mult), reads=['C_ta', 'C_den'], writes=[f'C_cre{d}'])
                p.op('dve', lambda e: e.tensor_tensor(t_a[:], sn[:], lr[:], ALU.mult), reads=['C_sn', 'C_lr'], writes=['C_ta'])
                p.op('dve', lambda e: e.tensor_tensor(t_b[:], cs[:], li[:], ALU.mult), reads=['C_cs', 'C_li'], writes=['C_tb'])
                p.op('dve', lambda e: e.tensor_tensor(t_a[:], t_a[:], t_b[:], ALU.subtract), reads=['C_ta', 'C_tb'], writes=['C_ta'])
                p.op('dve', lambda e: e.tensor_tensor(cim[d][:], t_a[:], den[:], ALU.mult), reads=['C_ta', 'C_den'], writes=[f'C_cim{d}'])
            WB = [[sb(st, f"C_WB{d}{ri}", [128, 16, 128]) for ri in range(2)] for d in range(2)]
            WC = [[sb(st, f"C_WC{d}{ri}", [128, 32, 64]) for ri in range(2)] for d in range(2)]
            pbs = [ps(st, f"C_pb{i}", [128, 512]) for i in range(8)]
            st2 = ExitStack()
            Bm = [sb(st2, f"C_Bm{ri}", [128, 32, 64]) for ri in range(2)]
            for ri, src in enumerate((s5_b_re, s5_b_im)):
                p.op('pool', lambda e: e.memset(Bm[ri][:], 0.0), writes=[f'C_Bm{ri}'])
                for two in range(2):
                    for qpar in range(2):
                        off = qpar * 32 + two * 16
                        p.dma('sp', Bm[ri][two * 64:(two + 1) * 64, qpar::2, off:off + 16],
                              src[l, (2 * qpar + two)::4, :, :].rearrange("m p c -> p m c"),
                              writes=[f'C_Bm{ri}'], allow_slow_non_contiguous=True)
            bbt = sb(st2, "C_bbt", [128, 32, 64]); bbt2 = sb(st2, "C_bbt2", [128, 32, 64])
            pbi = [0]

            def nb():
                i = pbi[0] % 8
                pbi[0] += 1
                return i
            b3 = lambda t: t[:].unsqueeze(2).broadcast_to([128, 32, 64])
            for d in range(2):
                for ri in range(2):
                    if ri == 0:
                        p.op('dve', lambda e: e.tensor_tensor(bbt[:], Bm[0][:], b3(cre[d]), ALU.mult), reads=['C_Bm0', f'C_cre{d}'], writes=['C_bbt'])
                        p.op('pool', lambda e: e.tensor_tensor(bbt2[:], Bm[1][:], b3(cim[d]), ALU.mult), reads=['C_Bm1', f'C_cim{d}'], writes=['C_bbt2'])
                        p.op('dve', lambda e: e.tensor_tensor(bbt[:], bbt[:], bbt2[:], ALU.subtract), reads=['C_bbt', 'C_bbt2'], writes=['C_bbt'])
                    else:
                        p.op('dve', lambda e: e.tensor_tensor(bbt[:], Bm[1][:], b3(cre[d]), ALU.mult), reads=['C_Bm1', f'C_cre{d}'], writes=['C_bbt'])
                        p.op('pool', lambda e: e.tensor_tensor(bbt2[:], Bm[0][:], b3(cim[d]), ALU.mult), reads=['C_Bm0', f'C_cim{d}'], writes=['C_bbt2'])
                        p.op('dve', lambda e: e.tensor_tensor(bbt[:], bbt[:], bbt2[:], ALU.add), reads=['C_bbt', 'C_bbt2'], writes=['C_bbt'])
                    for q in range(32):
                        bt = nb()
                        hb = (q % 4) // 2
                        qi_ = (q // 4) * 2 + q % 2
                        p.op('pe', lambda e: e.matmul(pbs[bt][hb * 64:(hb + 1) * 64, 0:128], bbt[:, q, :], ident_f[:], start=True, stop=True),
                             reads=['C_bbt', 'ident_f'], writes=[('C_pb', bt)])
                        p.op('act', lambda e: e.copy(WB[d][ri][hb * 64:(hb + 1) * 64, qi_, :], pbs[bt][hb * 64:(hb + 1) * 64, 0:128]),
                             reads=[('C_pb', bt)], writes=[f'C_WB{d}{ri}'])
            Cn = sb(st2, "C_Cn", [64, 32, 128])
            for d in range(2):
                for ri, src in enumerate((s5_c_re, s5_c_im)):
                    p.op('pool', lambda e: e.memset(Cn[:], 0.0), writes=['C_Cn'])
                    for two in range(2):
                        for qpar in range(2):
                            off = qpar * 32 + two * 16
                            p.dma('sp', Cn[off:off + 16, qpar::2, two * 64:(two + 1) * 64],
                                  src[l, d, (2 * qpar + two)::4, :, :].rearrange("m c p -> c m p"),
                                  writes=['C_Cn'], allow_slow_non_contiguous=True)
                    for q4 in range(16):
                        bt = nb()
                        for qq in range(2):
                            q = q4 * 2 + qq
                            p.op('pe', lambda e: e.transpose(pbs[bt][:, qq * 64:(qq + 1) * 64], Cn[:, q, :], ident_f[0:64, 0:64]),
                                 reads=['C_Cn', 'ident_f'], writes=[('C_pb', bt)])
                        dst = WC[d][ri][:, q4 * 2:(q4 + 1) * 2, :].rearrange("p a b -> p (a b)")
                        if ri == 0:
                            p.op('act', lambda e: e.copy(dst, pbs[bt][:, 0:128]), reads=[('C_pb', bt)], writes=[f'C_WC{d}{ri}'])
                        else:
                            p.op('act', lambda e: e.mul(dst, pbs[bt][:, 0:128], -1.0), reads=[('C_pb', bt)], writes=[f'C_WC{d}{ri}'])
            p.barrier()
            st2.close()
            ut = sb(st, "C_ut", [128, 128]); uT = sb(st, "C_uT", [128, S]); yacc = sb(st, "C_yacc", [128, S])
            iota1 = sb(st, "C_iota", [128, TC])
            p.dma('sp', iota1[:], c_iota[0:1, 0:TC].partition_broadcast(128), writes=['C_iota'])
            ang = sb(st, "C_ang", [128, TC]); ang2 = sb(st, "C_ang2", [128, TC]); tfi = sb(st, "C_tfi", [128, TC], I32); tff = sb(st, "C_tff", [128, TC])
            cosT = sb(st, "C_cosT", [128, TC]); sinT = sb(st, "C_sinT", [128, TC]); rtab = sb(st, "C_rtab", [128, TC])
            gre = sb(st, "C_gre", [128, TC]); gim = sb(st, "C_gim", [128, TC]); w1 = sb(st, "C_w1", [128, TC]); w2_ = sb(st, "C_w2", [128, TC])
            hre = sb(st, "C_hre", [128, TC]); him = sb(st, "C_him", [128, TC]); carry = sb(st, "C_carry", [128, 2])
            dsk = sb(st, "C_dsk", [128, 8])
            p.dma('sp', dsk[:], s5_d[l, :].rearrange("(b c) -> c b", c=128), writes=['C_dsk'], allow_slow_non_contiguous=True)
            yo = sb(st, "C_yo", [128, 128]); y3 = sb(st, "C_y3", [128, 512]); y4 = sb(st, "C_y4", [128, 512])
            for cb in range(8):
                for i in range(NT):
                    p.dma('sp', ut[:], proj[i * 128:(i + 1) * 128, C_AU + cb * 128:C_AU + (cb + 1) * 128], reads=[('proj', i, 'all')], writes=['C_ut'])
                    bt = nb()
                    p.op('pe', lambda e: e.transpose(pbs[bt][:, 0:128], ut[:], ident_f[:]), reads=['C_ut', 'ident_f'], writes=[('C_pb', bt)])
                    p.op('act', lambda e: e.copy(uT[:, i * 128:(i + 1) * 128], pbs[bt][:, 0:128]), reads=[('C_pb', bt)], writes=[('C_uT', i // 4)])
                for d in range(2):
                    for qq in range(4):
                        q = cb * 4 + qq
                        ps32 = slice((qq // 2) * 64, (qq // 2) * 64 + 64)
                        p.op('dve', lambda e: e.tensor_scalar(ang[:], iota1[:], th[d][:, q:q + 1], None, ALU.mult), reads=['C_iota', f'C_th{d}'], writes=['C_ang'])
                        emit_sin(sinT[:], ang[:], TC, 'C_sinT', 'C_ang', tfi[:], tff[:], 'C_tfi', 'C_tff')
                        p.op('dve', lambda e: e.tensor_scalar(ang2[:], ang[:], float(np.pi / 2), None, ALU.add), reads=['C_ang'], writes=['C_ang2'])
                        emit_sin(cosT[:], ang2[:], TC, 'C_cosT', 'C_ang2', tfi[:], tff[:], 'C_tfi', 'C_tff')
                        p.op('act', lambda e: e.activation(rtab[:], iota1[:], AF.Copy, scale=0.0, bias=0.0) if False else e.mul(rtab[:], iota1[:], 0.0), reads=['C_iota'], writes=['C_rtab'])
                        p.op('dve', lambda e: e.tensor_scalar(rtab[:], rtab[:], mag[d][:, q:q + 1], None, ALU.add), reads=['C_rtab', f'C_mag{d}'], writes=['C_rtab'])
                        p.op('dve', lambda e: e.memset(carry[:], 0.0), writes=['C_carry'])
                        chunks = range(S // TC) if d == 0 else range(S // TC - 1, -1, -1)
                        for ch in chunks:
                            tsl = slice(ch * TC, (ch + 1) * TC)
                            bre, bim = nb(), nb()
                            p.op('pe', lambda e: e.matmul(pbs[bre][:, :], WB[d][0][ps32, (q // 4) * 2 + q % 2, :], uT[ps32, tsl], start=True, stop=True),
                                 reads=[f'C_WB{d}0', ('C_uT', ch)], writes=[('C_pb', bre)])
                            p.op('pe', lambda e: e.matmul(pbs[bim][:, :], WB[d][1][ps32, (q // 4) * 2 + q % 2, :], uT[ps32, tsl], start=True, stop=True),
                                 reads=[f'C_WB{d}1', ('C_uT', ch)], writes=[('C_pb', bim)])
                            Bre = pbs[bre][:, :] if d == 0 else pbs[bre][:, ::-1]
                            Bim = pbs[bim][:, :] if d == 0 else pbs[bim][:, ::-1]
                            p.op('dve', lambda e: e.tensor_tensor(w1[:], Bre, cosT[:], ALU.mult), reads=[('C_pb', bre), 'C_cosT'], writes=['C_w1'])
                            p.op('dve', lambda e: e.tensor_tensor(w2_[:], Bim, sinT[:], ALU.mult), reads=[('C_pb', bim), 'C_sinT'], writes=['C_w2'])
                            p.op('pool', lambda e: e.tensor_tensor(w1[:], w1[:], w2_[:], ALU.add), reads=['C_w1', 'C_w2'], writes=['C_w1'])
                            p.op('dve', lambda e: e.tensor_tensor_scan(gre[:], rtab[:], w1[:], carry[:, 0:1], ALU.mult, ALU.add),
                                 reads=['C_rtab', 'C_w1', 'C_carry'], writes=['C_gre'])
                            p.op('dve', lambda e: e.tensor_tensor(w2_[:], Bim, cosT[:], ALU.mult), reads=[('C_pb', bim), 'C_cosT', 'C_w1'], writes=['C_w2'])
                            p.op('dve', lambda e: e.tensor_tensor(w1[:], Bre, sinT[:], ALU.mult), reads=[('C_pb', bre), 'C_sinT', 'C_gre'], writes=['C_w1'])
                            p.op('pool', lambda e: e.tensor_tensor(w2_[:], w2_[:], w1[:], ALU.subtract), reads=['C_w1', 'C_w2'], writes=['C_w2'])
                            p.op('dve', lambda e: e.tensor_tensor_scan(gim[:], rtab[:], w2_[:], carry[:, 1:2], ALU.mult, ALU.add),
                                 reads=['C_rtab', 'C_w2', 'C_carry'], writes=['C_gim'])
                            Hre = hre[:] if d == 0 else hre[:, ::-1]
                            Him = him[:] if d == 0 else him[:, ::-1]
                            p.op('pool', lambda e: e.tensor_tensor(w1[:], gre[:], cosT[:], ALU.mult), reads=['C_gre', 'C_cosT', 'C_w2'], writes=['C_w1'])
                            p.op('pool', lambda e: e.tensor_tensor(w2_[:], gim[:], sinT[:], ALU.mult), reads=['C_gim', 'C_sinT'], writes=['C_w2'])
                            p.op('dve', lambda e: e.tensor_tensor(Hre, w1[:], w2_[:], ALU.subtract), reads=['C_w1', 'C_w2'], writes=['C_hre'])
                            p.op('pool', lambda e: e.tensor_tensor(w1[:], gre[:], sinT[:], ALU.mult), reads=['C_gre', 'C_sinT', 'C_hre'], writes=['C_w1'])
                            p.op('pool', lambda e: e.tensor_tensor(w2_[:], gim[:], cosT[:], ALU.mult), reads=['C_gim', 'C_cosT', 'C_hre'], writes=['C_w2'])
                            p.op('dve', lambda e: e.tensor_tensor(Him, w1[:], w2_[:], ALU.add), reads=['C_w1', 'C_w2'], writes=['C_him'])
                            last = TC - 1 if d == 0 else 0
                            p.op('act', lambda e: e.copy(carry[:, 0:1], hre[:, last:last + 1]), reads=['C_hre'], writes=['C_carry'])
                            p.op('act', lambda e: e.copy(carry[:, 1:2], him[:, last:last + 1]), reads=['C_him'], writes=['C_carry'])
                            by = nb()
                            p.op('pe', lambda e: e.matmul(pbs[by][ps32, :], WC[d][0][:, q, :], hre[:], start=True, stop=False),
                                 reads=[f'C_WC{d}0', 'C_hre'], writes=[('C_pb', by)])
                            p.op('pe', lambda e: e.matmul(pbs[by][ps32, :], WC[d][1][:, q, :], him[:], start=False, stop=True),
                                 reads=[f'C_WC{d}1', 'C_him'], writes=[('C_pb', by)])
                            if d == 0 and qq % 2 == 0:
                                p.op('act', lambda e: e.copy(yacc[ps32, tsl], pbs[by][ps32, :]), reads=[('C_pb', by)], writes=[('C_yacc', qq // 2, ch)])
                            else:
                                p.op('dve', lambda e: e.tensor_tensor(yacc[ps32, tsl], yacc[ps32, tsl], pbs[by][ps32, :], ALU.add),
                                     reads=[('C_pb', by), ('C_yacc', qq // 2, ch)], writes=[('C_yacc', qq // 2, ch)])
                for ch in range(S // TC):
                    tsl = slice(ch * TC, (ch + 1) * TC)
                    rk = [('C_yacc', qq, ch) for qq in range(2)]
                    p.op('dve', lambda e: e.scalar_tensor_tensor(y3[:], uT[:, tsl], dsk[:, cb:cb + 1], yacc[:, tsl], ALU.mult, ALU.add),
                         reads=rk + [('C_uT', ch), 'C_dsk'], writes=['C_y3'])
                    p.op('pool', lambda e: e.tensor_tensor(y4[:], y3[:], y3[:], ALU.mult), reads=['C_y3'], writes=['C_y4'])
                    p.op('dve', lambda e: e.tensor_scalar(y4[:], y4[:], 0.044715, 1.0, ALU.mult, ALU.add), reads=['C_y4'], writes=['C_y4'])
                    p.op('dve', lambda e: e.tensor_tensor(y4[:], y4[:], y3[:], ALU.mult), reads=['C_y4', 'C_y3'], writes=['C_y4'])
                    p.op('act', lambda e: e.activation(y4[:], y4[:], AF.Sigmoid, scale=1.5957691216057308), reads=['C_y4'], writes=['C_y4'])
                    p.op('dve', lambda e: e.tensor_tensor(y3[:], y3[:], y4[:], ALU.mult), reads=['C_y4', 'C_y3'], writes=['C_y3'])
                    for i4_ in range(TC // 128):
                        i = ch * (TC // 128) + i4_
                        bt = nb()
                        p.op('pe', lambda e: e.transpose(pbs[bt][:, 0:128], y3[:, i4_ * 128:(i4_ + 1) * 128], ident_f[:]), reads=['C_y3', 'ident_f'], writes=[('C_pb', bt)])
                        p.op('act', lambda e: e.copy(yo[:], pbs[bt][:, 0:128]), reads=[('C_pb', bt)], writes=['C_yo'])
                        p.dma('sp', ygd[i * 128:(i + 1) * 128, cb * 128:(cb + 1) * 128], yo[:], reads=['C_yo'], writes=[('ygd', i, cb)])
        p.barrier()
        with ExitStack() as st:
            gw = sb(st, "C2_gw", [128, 8, 1024], BF16)
            p.dma('pool', gw[:], s5_glu_w[l, :, :].rearrange("(k p) n -> p k n", p=128), writes=['C2_gw'])
            gb = sb(st, "C2_gb", [128, 1024])
            p.dma('sp', gb[:], s5_glu_b[l:l + 1, :].partition_broadcast(128), writes=['C2_gb'])
            yg = sb(st, "C2_yg", [128, 1024]); ygb = sb(st, "C2_ygb", [128, 1024], BF16); ygT = sb(st, "C2_ygT", [128, 8, 128], BF16)
            sg = sb(st, "C2_sg", [128, 1024])
            ptr = ps(st, "C2_pt", [128, 8, 128], BF16)
            pm = [ps(st, f"C2_pm{i}", [128, 512]) for i in range(2)]
            for i in range(NT):
                p.dma('sp', yg[:], ygd[i * 128:(i + 1) * 128, :], reads=[('ygd', i, cb) for cb in range(8)], writes=['C2_yg'])
                p.op('act', lambda e: e.copy(ygb[:], yg[:]), reads=['C2_yg'], writes=['C2_ygb'])
                for k in range(8):
                    p.op('pe', lambda e: e.transpose(ptr[:, k, :], ygb[:, k * 128:(k + 1) * 128], ident_b[:]), reads=['C2_ygb', 'ident_b'], writes=['C2_pt'])
                p.op('dve', lambda e: e.tensor_copy(ygT[:], ptr[:]), reads=['C2_pt'], writes=['C2_ygT'])
                for half in range(2):
                    cs_ = slice(half * 512, (half + 1) * 512)
                    for k in range(8):
                        p.op('pe', lambda e: e.matmul(pm[half][:, :], ygT[:, k, :], gw[:, k, cs_], start=(k == 0), stop=(k == 7)),
                             reads=['C2_ygT', 'C2_gw'], writes=[('C2_pm', half)])
                    p.op('dve', lambda e: e.tensor_tensor(sg[:, cs_], pm[half][:, :], gb[:, cs_], ALU.add), reads=[('C2_pm', half), 'C2_gb'], writes=['C2_sg'])
                p.op('act', lambda e: e.activation(sg[:], sg[:], AF.Sigmoid), reads=['C2_sg'], writes=['C2_sg'])
                p.op('dve', lambda e: e.tensor_tensor(sg[:], sg[:], yg[:], ALU.mult), reads=['C2_sg', 'C2_yg'], writes=['C2_sg'])
                p.dma('sp', br[i * 128:(i + 1) * 128, 0:1024], sg[:], reads=['C2_sg'], writes=[('br', i, 0)])
        p.barrier()


    ropec = sb(es, "rope_c", [128, NT, 32]); ropes = sb(es, "rope_s", [128, NT, 32])

    def prologue_rope():
        with ExitStack() as st:
            pi_ = sb(st, "R_pi", [128, NT], I32); pf = sb(st, "R_pf", [128, NT]); ivf = sb(st, "R_ivf", [128, 32])
            ang = sb(st, "R_ang", [128, NT, 32]); ti_ = sb(st, "R_ti", [128, NT, 32], I32); tf_ = sb(st, "R_tf", [128, NT, 32])
            p.dma('sp', pi_[:], pos_in[0, :].rearrange("(i p) -> p i", p=128), writes=['R_pi'], allow_slow_non_contiguous=True)
            p.dma('sp', ivf[:], c_invfreq[0:1, :].partition_broadcast(128), writes=['R_ivf'])
            p.op('dve', lambda e: e.tensor_copy(pf[:], pi_[:]), reads=['R_pi'], writes=['R_pf'])
            p.op('dve', lambda e: e.tensor_tensor(ang[:], pf[:].unsqueeze(2).broadcast_to([128, NT, 32]),
                                                 ivf[:].unsqueeze(1).broadcast_to([128, NT, 32]), ALU.mult), reads=['R_pf', 'R_ivf'], writes=['R_ang'])
            for which, dst, key in ((0, ropes, 'rope_s'), (1, ropec, 'rope_c')):
                if which == 1:
                    p.op('dve', lambda e: e.tensor_scalar(ang[:], ang[:], float(np.pi / 2), None, ALU.add), reads=['R_ang'], writes=['R_ang'])
                p.op('dve', lambda e: e.tensor_scalar(ti_[:], ang[:], 1.0 / TWO_PI, None, ALU.mult), reads=['R_ang'], writes=['R_ti'])
                p.op('dve', lambda e: e.tensor_copy(tf_[:], ti_[:]), reads=['R_ti'], writes=['R_tf'])
                p.op('dve', lambda e: e.scalar_tensor_tensor(tf_[:], tf_[:], -TWO_PI, ang[:], ALU.mult, ALU.add), reads=['R_tf', 'R_ang'], writes=['R_tf'])
                p.op('dve', lambda e: e.tensor_scalar(tf_[:], tf_[:], float(np.pi), float(-np.pi), ALU.min, ALU.max), reads=['R_tf'], writes=['R_tf'])
                p.op('act', lambda e: e.activation(dst[:], tf_[:], AF.Sin), reads=['R_tf'], writes=[key])
        p.barrier()

    def phase_B(l):
        with ExitStack() as st:
            wuq = sb(st, "B_wuq", [128, 7, 1536], BF16); wukv = sb(st, "B_wukv", [128, 2, 2048], BF16)
            p.dma('pool', wuq[:], mla_w_uq[l, :, :].rearrange("(k p) n -> p k n", p=128), writes=['B_wuq'])
            p.dma('pool', wukv[:], mla_w_ukv[l, :, :].rearrange("(k p) n -> p k n", p=128), writes=['B_wukv'])
            gqa = sb(st, "B_gqa", [128, 896]); gkva = sb(st, "B_gkva", [128, 256]); gq = sb(st, "B_gq", [128, 192]); gk = sb(st, "B_gk", [128, 192])
            p.dma('sp', gqa[:], mla_q_a_norm[l:l + 1, :].partition_broadcast(128), writes=['B_gqa'])
            p.dma('sp', gkva[:], mla_kv_a_norm[l:l + 1, :].partition_broadcast(128), writes=['B_gkva'])
            p.dma('sp', gq[:], mla_q_norm[l:l + 1, :].partition_broadcast(128), writes=['B_gq'])
            p.dma('sp', gk[:], mla_k_norm[l:l + 1, :].partition_broadcast(128), writes=['B_gk'])
            lat = sb(st, "B_lat", [128, 1216]); latb = sb(st, "B_latb", [128, 1152], BF16); latT = sb(st, "B_latT", [128, 9, 128], BF16)
            ss = sb(st, "B_ss", [128, 2]); junk = sb(st, "B_junk", [128, 896], BF16)
            qk = [sb(st, f"B_qk{i}", [128, 8, 192]) for i in range(2)]
            sq = sb(st, "B_sq", [128, 8, 192]); hs = sb(st, "B_hs", [128, 8])
            r1 = sb(st, "B_r1", [128, 8, 32]); r2 = sb(st, "B_r2", [128, 8, 32]); r3 = sb(st, "B_r3", [128, 8, 32])
            qkb = sb(st, "B_qkb", [128, 8, 192], BF16); vb = sb(st, "B_vb", [128, 1024], BF16)
            tT = sb(st, "B_tT", [128, 16, 128], BF16)
            ptr = [ps(st, f"B_pt{i}", [128, 8, 128], BF16) for i in range(2)]
            pm = [ps(st, f"B_pm{i}", [128, 512]) for i in range(4)]
            pmi = [0]
            import os
            for i in range(int(os.environ.get("KNTB", NT))):
                t0 = i * 128
                p.dma('sp', lat[:], proj[t0:t0 + 128, C_CQ:C_CQ + 1216], reads=[('proj', i, 'all')], writes=['B_lat'])
                p.op('act', lambda e: e.activation(junk[:], lat[:, 0:896], AF.Square, accum_out=ss[:, 0:1]), reads=['B_lat'], writes=['B_junk', 'B_ss0'])
                p.op('act', lambda e: e.activation(junk[:, 0:256], lat[:, 896:1152], AF.Square, accum_out=ss[:, 1:2]), reads=['B_lat'], writes=['B_junk', 'B_ss1'])
                p.op('act', lambda e: e.activation(ss[:, 0:1], ss[:, 0:1], AF.Sqrt, bias=eps_t[:], scale=1.0 / 896), reads=['B_ss0', 'eps_t'], writes=['B_ss0'])
                p.op('act', lambda e: e.activation(ss[:, 1:2], ss[:, 1:2], AF.Sqrt, bias=eps_t[:], scale=1.0 / 256), reads=['B_ss1', 'eps_t'], writes=['B_ss1'])
                p.op('dve', lambda e: e.reciprocal(ss[:], ss[:]), reads=['B_ss0', 'B_ss1'], writes=['B_ss0', 'B_ss1'])
                p.op('dve', lambda e: e.scalar_tensor_tensor(latb[:, 0:896], lat[:, 0:896], ss[:, 0:1], gqa[:], ALU.mult, ALU.mult),
                     reads=['B_lat', 'B_ss0', 'B_gqa'], writes=['B_latb'])
                p.op('dve', lambda e: e.scalar_tensor_tensor(latb[:, 896:1152], lat[:, 896:1152], ss[:, 1:2], gkva[:], ALU.mult, ALU.mult),
                     reads=['B_lat', 'B_ss1', 'B_gkva'], writes=['B_latb'])
                BSTOP = int(os.environ.get("BSTOP", 9))
                if BSTOP <= 1:
                    continue
                for k in range(9):
                    pt = ptr[0] if k < 8 else ptr[1]
                    p.op('pe', lambda e: e.transpose(pt[:, k % 8, :], latb[:, k * 128:(k + 1) * 128], ident_b[:]), reads=['B_latb', 'ident_b'],
                         writes=[('B_pt', 0 if k < 8 else 1)])
                p.op('act', lambda e: e.copy(latT[:, 0:8, :], ptr[0][:]), reads=[('B_pt', 0)], writes=['B_latT'])
                p.op('dve', lambda e: e.tensor_copy(latT[:, 8, :], ptr[1][:, 0, :]), reads=[('B_pt', 1)], writes=['B_latT'])
                for c3 in range(3):
                    j = pmi[0] % 4
                    pmi[0] += 1
                    for k in range(7):
                        p.op('pe', lambda e: e.matmul(pm[j][:, :], latT[:, k, :], wuq[:, k, c3 * 512:(c3 + 1) * 512], start=(k == 0), stop=(k == 6)),
                             reads=['B_latT', 'B_wuq'], writes=[('B_pm', j)])
                    p.op('act', lambda e: e.copy(qk[0][:].rearrange("p h d -> p (h d)")[:, c3 * 512:(c3 + 1) * 512], pm[j][:, :]),
                         reads=[('B_pm', j)], writes=['B_qk0'])
                if BSTOP <= 2:
                    continue
                for c4 in range(4):
                    j = pmi[0] % 4
                    pmi[0] += 1
                    for k in range(2):
                        p.op('pe', lambda e: e.matmul(pm[j][:, :], latT[:, 7 + k, :], wukv[:, k, c4 * 512:(c4 + 1) * 512], start=(k == 0), stop=(k == 1)),
                             reads=['B_latT', 'B_wukv'], writes=[('B_pm', j)])
                    pv = pm[j][:, :].rearrange("p (h d) -> p h d", h=2)
                    BSKIP = os.environ.get("BSKIP", "")
                    if 'a' not in BSKIP:
                        p.op('act', lambda e: e.copy(qk[1][:, c4 * 2:(c4 + 1) * 2, 0:128], pv[:, :, 0:128]), reads=[('B_pm', j)], writes=['B_qk1'])
                    for hh in range(2):
                        hcol = (c4 * 2 + hh) * 128
                        p.op('act', lambda e: e.copy(vb[:, hcol:hcol + 128], pm[j][:, hh * 256 + 128:hh * 256 + 256]),
                             reads=[('B_pm', j)], writes=['B_vb'])
                if 'p' not in BSKIP:
                    p.op('pool', lambda e: e.tensor_copy(qk[1][:, :, 128:192], lat[:, 1152:1216].unsqueeze(1).broadcast_to([128, 8, 64])),
                         reads=['B_lat'], writes=['B_qk1'])
                if 'v' not in BSKIP:
                    p.dma('sp', v_d[t0:t0 + 128, :], vb[:], reads=['B_vb'], writes=[('v_d', i)])
                if BSTOP <= 3:
                    continue
                for which in range(2):
                    X = qk[which]
                    xk = f'B_qk{which}'
                    g = gq if which == 0 else gk
                    gk_ = 'B_gq' if which == 0 else 'B_gk'
                    p.op('pool', lambda e: e.tensor_tensor(sq[:], X[:], X[:], ALU.mult), reads=[xk], writes=['B_sq'])
                    p.op('dve', lambda e: e.tensor_reduce(hs[:], sq[:], AX.X, ALU.add), reads=['B_sq'], writes=['B_hs'])
                    p.op('act', lambda e: e.activation(hs[:], hs[:], AF.Sqrt, bias=eps_t[:], scale=1.0 / 192), reads=['B_hs', 'eps_t'], writes=['B_hs'])
                    p.op('dve', lambda e: e.reciprocal(hs[:], hs[:]), reads=['B_hs'], writes=['B_hs'])
                    p.op('dve', lambda e: e.tensor_tensor(X[:], X[:], hs[:].unsqueeze(2).broadcast_to([128, 8, 192]), ALU.mult), reads=[xk, 'B_hs'], writes=[xk])
                    p.op('pool', lambda e: e.tensor_tensor(X[:], X[:], g[:].unsqueeze(1).broadcast_to([128, 8, 192]), ALU.mult), reads=[xk, gk_], writes=[xk])
                    cb_ = ropec[:, i, :].unsqueeze(1).broadcast_to([128, 8, 32]); sb_ = ropes[:, i, :].unsqueeze(1).broadcast_to([128, 8, 32])
                    T1 = X[:, :, 128:160]; T2 = X[:, :, 160:192]
                    p.op('dve', lambda e: e.tensor_tensor(r1[:], T1, sb_, ALU.mult), reads=[xk, 'rope_s'], writes=['B_r1'])
                    p.op('dve', lambda e: e.tensor_tensor(r2[:], T2, sb_, ALU.mult), reads=[xk, 'rope_s'], writes=['B_r2'])
                    p.op('dve', lambda e: e.tensor_tensor(r3[:], T1, cb_, ALU.mult), reads=[xk, 'rope_c'], writes=['B_r3'])
                    p.op('dve', lambda e: e.tensor_tensor(T1, r3[:], r2[:], ALU.subtract), reads=['B_r3', 'B_r2'], writes=[xk])
                    p.op('dve', lambda e: e.tensor_tensor(r3[:], T2, cb_, ALU.mult), reads=[xk, 'rope_c'], writes=['B_r3'])
                    p.op('dve', lambda e: e.tensor_tensor(T2, r3[:], r1[:], ALU.add), reads=['B_r3', 'B_r1'], writes=[xk])
                    p.op('act', lambda e: e.copy(qkb[:], X[:]), reads=[xk], writes=['B_qkb'])
                    if BSTOP <= 4:
                        continue
                    for h in range(8):
                        pt = ptr[h % 2]
                        p.op('pe', lambda e: e.transpose(pt[:, 0, :], qkb[:, h, 0:128], ident_b[:]), reads=['B_qkb', 'ident_b'], writes=[('B_pt', h % 2)])
                        p.op('pe', lambda e: e.transpose(pt[0:64, 1, :], qkb[:, h, 128:192], ident_b[:]), reads=['B_qkb', 'ident_b'], writes=[('B_pt', h % 2)])
                        p.op('act', lambda e: e.copy(tT[:, 2 * h, :], pt[:, 0, :]), reads=[('B_pt', h % 2)], writes=[('B_tT', h)])
                        p.op('dve', lambda e: e.tensor_copy(tT[0:64, 2 * h + 1, :], pt[0:64, 1, :]), reads=[('B_pt', h % 2)], writes=[('B_tT', h)])
                        dstT = qT_d if which == 0 else kT_d
                        p.dma('sp', dstT[h, 0:128, t0:t0 + 128], tT[:, 2 * h, :], reads=[('B_tT', h)], writes=[('qkT', which, h, i)])
                        p.dma('sp', dstT[h, 128:192, t0:t0 + 128], tT[0:64, 2 * h + 1, :], reads=[('B_tT', h)], writes=[('qkT', which, h, i)])
        p.barrier()
        if 'b' in phases:
            return
        with ExitStack() as st:
            qT = sb(st, "B2_qT", [128, S], BF16); qTr = sb(st, "B2_qTr", [64, S], BF16)
            kT = sb(st, "B2_kT", [128, S], BF16); kTr = sb(st, "B2_kTr", [64, S], BF16)
            Va = sb(st, "B2_Va", [128, NT, 132], BF16)
            PT = [sb(st, f"B2_PT{i}", [128, 512], BF16) for i in range(2)]
            ob = sb(st, "B2_ob", [128, 128]); rs = sb(st, "B2_rs", [128, 1])
            psc = [ps(st, f"B2_ps{i}", [128, 512]) for i in range(2)]
            pac = [ps(st, f"B2_pa{i}", [128, 512]) for i in range(4)]
            p.op('dve', lambda e: e.memset(Va[:], 1.0), writes=['B2_Va'])
            SCALE = float(192 ** -0.5)
            it = 0
            for h in range(8):
                p.dma('sp', qT[:], qT_d[h, 0:128, :], writes=['B2_qT'])
                p.dma('sp', qTr[:], qT_d[h, 128:192, :], writes=['B2_qTr'])
                p.dma('sp', kT[:], kT_d[h, 0:128, :], writes=['B2_kT'])
                p.dma('sp', kTr[:], kT_d[h, 128:192, :], writes=['B2_kTr'])
                p.dma('sp', Va[:, :, 0:128], v_d[:, h * 128:(h + 1) * 128].rearrange("(i p) d -> p i d", p=128), writes=['B2_Va'])
                for qb in range(S // 512):
                    qs = slice(qb * 512, (qb + 1) * 512)
                    for kt in range(NT):
                        ks = slice(kt * 128, (kt + 1) * 128)
                        j = it % 2
                        it += 1
                        p.op('pe', lambda e: e.matmul(psc[j][:, :], kT[:, ks], qT[:, qs], start=True, stop=False), reads=['B2_kT', 'B2_qT'], writes=[('B2_ps', j)])
                        p.op('pe', lambda e: e.matmul(psc[j][:, :], kTr[:, ks], qTr[:, qs], start=False, stop=True), reads=['B2_kTr', 'B2_qTr'], writes=[('B2_ps', j)])
                        p.op('act', lambda e: e.activation(PT[j][:], psc[j][:, :], AF.Exp, scale=SCALE), reads=[('B2_ps', j)], writes=[('B2_PT', j)])
                        for sub in range(4):
                            p.op('pe', lambda e: e.matmul(pac[sub][:, 0:129], PT[j][:, sub * 128:(sub + 1) * 128], Va[:, kt, 0:129],
                                                         start=(kt == 0), stop=(kt == NT - 1)), reads=[('B2_PT', j), 'B2_Va'], writes=[('B2_pa', sub)])
                    for sub in range(4):
                        t0 = qb * 512 + sub * 128
                        p.op('dve', lambda e: e.reciprocal(rs[:], pac[sub][:, 128:129]), reads=[('B2_pa', sub)], writes=['B2_rs'])
                        p.op('dve', lambda e: e.tensor_scalar(ob[:], pac[sub][:, 0:128], rs[:], None, ALU.mult), reads=[('B2_pa', sub), 'B2_rs'], writes=['B2_ob'])
                        p.dma('sp', br[t0:t0 + 128, 1024 + h * 128:1024 + (h + 1) * 128], ob[:], reads=['B2_ob'], writes=[('br', t0 // 128, 1, h)])
        p.barrier()

    def phase_M(l):
        with ExitStack() as st:
            wk = sb(st, "M_wk", [128, 32, 1024], BF16)
            gm = sb(st, "M_gm", [128, D]); mt_ = sb(st, "M_mt", [128, D]); mb = sb(st, "M_mb", [128, D], BF16)
            memT = sb(st, "M_memT", [128, 32, 256], BF16)
            ss = sb(st, "M_ss", [128, 1]); hs = sb(st, "M_hs", [128, 4]); gqn = sb(st, "M_gqn", [128, 256]); gkn = sb(st, "M_gkn", [128, 256])
            Kt = sb(st, "M_K", [128, 1024]); sq = sb(st, "M_sq", [128, 1024]); Kb = sb(st, "M_Kb", [128, 1024], BF16)
            KmT = sb(st, "M_KmT", [128, 8, 256], BF16)
            Vm = sb(st, "M_Vm", [128, 2, 4, 260], BF16)
            ptr = [ps(st, f"M_pt{i}", [128, 8, 128], BF16) for i in range(2)]
            pm = [ps(st, f"M_pm{i}", [128, 512]) for i in range(2)]
            psc = [ps(st, f"M_ps{i}", [128, 512]) for i in range(2)]
            pac = [ps(st, f"M_pa{i}", [128, 512]) for i in range(2)]
            p.dma('sp', gm[:], mem_norm_g[l:l + 1, :].partition_broadcast(128), writes=['M_gm'])
            p.dma('sp', gqn[:], mem_q_norm[l:l + 1, :].partition_broadcast(128), writes=['M_gqn'])
            p.dma('sp', gkn[:], mem_k_norm[l:l + 1, :].partition_broadcast(128), writes=['M_gkn'])
            p.op('dve', lambda e: e.memset(Vm[:], 1.0), writes=['M_Vm'])
            for mt in range(2):
                p.dma('sp', mt_[:], mem_in[mt * 128:(mt + 1) * 128, :], writes=['M_mt'])
                p.op('act', lambda e: e.activation(mb[:], mt_[:], AF.Square, accum_out=ss[:]), reads=['M_mt'], writes=['M_mb', 'M_ss'])
                p.op('act', lambda e: e.activation(ss[:], ss[:], AF.Sqrt, bias=eps_t[:], scale=1.0 / D), reads=['M_ss', 'eps_t'], writes=['M_ss'])
                p.op('dve', lambda e: e.reciprocal(ss[:], ss[:]), reads=['M_ss'], writes=['M_ss'])
                p.op('dve', lambda e: e.scalar_tensor_tensor(mb[:], mt_[:], ss[:], gm[:], ALU.mult, ALU.mult), reads=['M_mt', 'M_ss', 'M_gm'], writes=['M_mb'])
                for k8 in range(4):
                    pt = ptr[k8 % 2]
                    for kk in range(8):
                        k = k8 * 8 + kk
                        p.op('pe', lambda e: e.transpose(pt[:, kk, :], mb[:, k * 128:(k + 1) * 128], ident_b[:]), reads=['M_mb', 'ident_b'], writes=[('M_pt', k8 % 2)])
                    p.op('act', lambda e: e.copy(memT[:, k8 * 8:(k8 + 1) * 8, mt * 128:(mt + 1) * 128], pt[:]), reads=[('M_pt', k8 % 2)], writes=['M_memT'])
            for which, wsrc in ((0, mem_w_k), (1, mem_w_v)):
                for k4 in range(4):
                    p.dma('pool', wk[:, k4 * 8:(k4 + 1) * 8, :], wsrc[l, k4 * 1024:(k4 + 1) * 1024, :].rearrange("(k p) n -> p k n", p=128), writes=['M_wk'])
                for mt in range(2):
                    for half in range(2):
                        for k in range(32):
                            p.op('pe', lambda e: e.matmul(pm[half][:, :], memT[:, k, mt * 128:(mt + 1) * 128], wk[:, k, half * 512:(half + 1) * 512],
                                                         start=(k == 0), stop=(k == 31)), reads=['M_memT', 'M_wk'], writes=[('M_pm', half)])
                        if which == 0:
                            p.op('act', lambda e: e.copy(Kt[:, half * 512:(half + 1) * 512], pm[half][:, :]), reads=[('M_pm', half)], writes=['M_K'])
                        else:
                            p.op('act', lambda e: e.copy(Vm[:, mt, half * 2:(half + 1) * 2, 0:256], pm[half][:, :].rearrange("p (h d) -> p h d", h=2)),
                                 reads=[('M_pm', half)], writes=['M_Vm'])
                    if which == 0:
                        K3 = Kt[:].rearrange("p (h d) -> p h d", h=4)
                        p.op('pool', lambda e: e.tensor_tensor(sq[:], Kt[:], Kt[:], ALU.mult), reads=['M_K'], writes=['M_sq'])
                        p.op('dve', lambda e: e.tensor_reduce(hs[:], sq[:].rearrange("p (h d) -> p h d", h=4), AX.X, ALU.add), reads=['M_sq'], writes=['M_hs'])
                        p.op('act', lambda e: e.activation(hs[:], hs[:], AF.Sqrt, bias=eps_t[:], scale=1.0 / 256), reads=['M_hs', 'eps_t'], writes=['M_hs'])
                        p.op('dve', lambda e: e.reciprocal(hs[:], hs[:]), reads=['M_hs'], writes=['M_hs'])
                        p.op('dve', lambda e: e.tensor_tensor(K3, K3, hs[:].unsqueeze(2).broadcast_to([128, 4, 256]), ALU.mult), reads=['M_K', 'M_hs'], writes=['M_K'])
                        p.op('dve', lambda e: e.tensor_tensor(Kb[:].rearrange("p (h d) -> p h d", h=4), K3, gkn[:].unsqueeze(1).broadcast_to([128, 4, 256]), ALU.mult),
                             reads=['M_K', 'M_gkn'], writes=['M_Kb'])
                        for k in range(8):
                            p.op('pe', lambda e: e.transpose(ptr[0][:, k, :], Kb[:, k * 128:(k + 1) * 128], ident_b[:]), reads=['M_Kb', 'ident_b'], writes=[('M_pt', 0)])
                        p.op('act', lambda e: e.copy(KmT[:, :, mt * 128:(mt + 1) * 128], ptr[0][:]), reads=[('M_pt', 0)], writes=['M_KmT'])
            qt = sb(st, "M_q", [128, 1024]); qb_ = sb(st, "M_qb", [128, 1024], BF16); qT = sb(st, "M_qT", [128, 8, 512], BF16)
            PT = sb(st, "M_PT", [128, 2, 512], BF16); ob = sb(st, "M_ob", [128, 1024]); rs = sb(st, "M_rs", [128, 1])
            for g4 in range(NT // 4):
                for ti in range(4):
                    i = g4 * 4 + ti
                    p.dma('sp', qt[:], proj[i * 128:(i + 1) * 128, C_MQ:C_MQ + 1024], reads=[('proj', i, 'all')], writes=['M_q'])
                    Q3 = qt[:].rearrange("p (h d) -> p h d", h=4)
                    p.op('pool', lambda e: e.tensor_tensor(sq[:], qt[:], qt[:], ALU.mult), reads=['M_q'], writes=['M_sq'])
                    p.op('dve', lambda e: e.tensor_reduce(hs[:], sq[:].rearrange("p (h d) -> p h d", h=4), AX.X, ALU.add), reads=['M_sq'], writes=['M_hs'])
                    p.op('act', lambda e: e.activation(hs[:], hs[:], AF.Sqrt, bias=eps_t[:], scale=1.0 / 256), reads=['M_hs', 'eps_t'], writes=['M_hs'])
                    p.op('dve', lambda e: e.reciprocal(hs[:], hs[:]), reads=['M_hs'], writes=['M_hs'])
                    p.op('dve', lambda e: e.tensor_tensor(Q3, Q3, hs[:].unsqueeze(2).broadcast_to([128, 4, 256]), ALU.mult), reads=['M_q', 'M_hs'], writes=['M_q'])
                    p.op('dve', lambda e: e.tensor_tensor(qb_[:].rearrange("p (h d) -> p h d", h=4), Q3, gqn[:].unsqueeze(1).broadcast_to([128, 4, 256]), ALU.mult),
                         reads=['M_q', 'M_gqn'], writes=['M_qb'])
                    for k in range(8):
                        p.op('pe', lambda e: e.transpose(ptr[ti % 2][:, k, :], qb_[:, k * 128:(k + 1) * 128], ident_b[:]), reads=['M_qb', 'ident_b'], writes=[('M_pt', ti % 2)])
                    p.op('act', lambda e: e.copy(qT[:, :, ti * 128:(ti + 1) * 128], ptr[ti % 2][:]), reads=[('M_pt', ti % 2)], writes=['M_qT'])
                for h in range(4):
                    for mt in range(2):
                        for dc in range(2):
                            p.op('pe', lambda e: e.matmul(psc[mt][:, :], KmT[:, h * 2 + dc, mt * 128:(mt + 1) * 128], qT[:, h * 2 + dc, :],
                                                         start=(dc == 0), stop=(dc == 1)), reads=['M_KmT', 'M_qT'], writes=[('M_ps', mt)])
                        p.op('act', lambda e: e.activation(PT[:, mt, :], psc[mt][:, :], AF.Exp, scale=1.0 / 16), reads=[('M_ps', mt)], writes=['M_PT'])
                    for ti in range(4):
                        i = g4 * 4 + ti
                        j = ti % 2
                        for mt in range(2):
                            p.op('pe', lambda e: e.matmul(pac[j][:, 0:257], PT[:, mt, ti * 128:(ti + 1) * 128], Vm[:, mt, h, 0:257],
                                                         start=(mt == 0), stop=(mt == 1)), reads=['M_PT', 'M_Vm'], writes=[('M_pa', j)])
                        p.op('dve', lambda e: e.reciprocal(rs[:], pac[j][:, 256:257]), reads=[('M_pa', j)], writes=['M_rs'])
                        p.op('dve', lambda e: e.tensor_scalar(ob[:, 0:256], pac[j][:, 0:256], rs[:], None, ALU.mult), reads=[('M_pa', j), 'M_rs'], writes=['M_ob'])
                        p.dma('sp', br[i * 128:(i + 1) * 128, 3072 + h * 256:3072 + (h + 1) * 256], ob[:, 0:256], reads=['M_ob'], writes=[('br', i, 3, h)])
        p.barrier()

    def phase_E(l, xsrc):
        for r in range(0, D, 512):
            p.dma('pool', wbf_out[r:r + 512, :], w_out[l, r:r + 512, :], writes=[('wbf_out', r)])
        with ExitStack() as st:
            bt = sb(st, "E_b", [128, D]); G = sb(st, "E_G", [128, D]); mg = sb(st, "E_mg", [128, D], BF16)
            bg = sb(st, "E_bg", [128, 3, 1024]); ss = sb(st, "E_ss", [128, 1]); junk = sb(st, "E_junk", [128, 1024], BF16)
            mT = sb(st, "E_mT", [128, 32, 1024], BF16)
            W = [sb(st, f"E_W{i}", [128, 32, 512], BF16) for i in range(2)]
            xt = [sb(st, f"E_x{i}", [128, 512]) for i in range(2)]
            ptr = [ps(st, f"E_pt{i}", [128, 8, 128], BF16) for i in range(2)]
            pmm = [ps(st, f"E_pm{i}", [128, 512]) for i in range(4)]
            p.dma('sp', bg[:].rearrange("p a b -> p (a b)"), branch_g[l:l + 1, :].partition_broadcast(128), writes=['E_bg'])
            gates = (C_AG, C_BG, C_CG, C_MG)
            wi = 0
            for g in range(S // 1024):
                for ti in range(8):
                    i = g * 8 + ti
                    t0 = i * 128
                    p.dma('sp', bt[:], br[t0:t0 + 128, :], reads=[('br', i, 'all')], writes=['E_b'])
                    for bi in range(4):
                        p.dma('sp', G[:, bi * 1024:(bi + 1) * 1024], proj[t0:t0 + 128, gates[bi]:gates[bi] + 1024], reads=[('proj', i, 'all')], writes=['E_G'])
                    p.op('act', lambda e: e.activation(G[:], G[:], AF.Silu), reads=['E_G'], writes=['E_G'])
                    for bi in range(4):
                        cs_ = slice(bi * 1024, (bi + 1) * 1024)
                        if bi == 2:
                            p.op('pool', lambda e: e.tensor_tensor(mg[:, cs_], bt[:, cs_], G[:, cs_], ALU.mult), reads=['E_b', 'E_G'], writes=['E_mg'])
                            continue
                        gi = {0: 0, 1: 1, 3: 2}[bi]
                        p.op('act', lambda e: e.activation(junk[:], bt[:, cs_], AF.Square, accum_out=ss[:]), reads=['E_b'], writes=['E_junk', 'E_ss'])
                        p.op('act', lambda e: e.activation(ss[:], ss[:], AF.Sqrt, bias=eps_t[:], scale=1.0 / 1024), reads=['E_ss', 'eps_t'], writes=['E_ss'])
                        p.op('dve', lambda e: e.reciprocal(ss[:], ss[:]), reads=['E_ss'], writes=['E_ss'])
                        p.op('dve', lambda e: e.scalar_tensor_tensor(bt[:, cs_], bt[:, cs_], ss[:], bg[:, gi, :], ALU.mult, ALU.mult), reads=['E_b', 'E_ss', 'E_bg'], writes=['E_b'])
                        p.op('pool', lambda e: e.tensor_tensor(mg[:, cs_], bt[:, cs_], G[:, cs_], ALU.mult), reads=['E_b', 'E_G'], writes=['E_mg'])
                    for k8 in range(4):
                        pt = ptr[k8 % 2]
                        for kk in range(8):
                            k = k8 * 8 + kk
                            p.op('pe', lambda e: e.transpose(pt[:, kk, :], mg[:, k * 128:(k + 1) * 128], ident_b[:]), reads=['E_mg', 'ident_b'], writes=[('E_pt', k8 % 2)])
                        dst = mT[:, k8 * 8:(k8 + 1) * 8, ti * 128:(ti + 1) * 128]
                        if k8 % 2 == 0:
                            p.op('act', lambda e: e.copy(dst, pt[:]), reads=[('E_pt', k8 % 2)], writes=[('E_mT', ti)])
                        else:
                            p.op('dve', lambda e: e.tensor_copy(dst, pt[:]), reads=[('E_pt', k8 % 2)], writes=[('E_mT', ti)])
                for ci in range(D // 512):
                    n0 = ci * 512
                    Wt = W[wi % 2]
                    for k4 in range(4):
                        p.dma('sp', Wt[:, k4 * 8:(k4 + 1) * 8, :], wbf_out[k4 * 1024:(k4 + 1) * 1024, n0:n0 + 512].rearrange("(k p) n -> p k n", p=128),
                              reads=[('wbf_out', k4 * 1024), ('wbf_out', k4 * 1024 + 512)], writes=[('E_W', wi % 2)])
                    for ti in range(8):
                        i = g * 8 + ti
                        t0 = i * 128
                        j = (ci * 8 + ti) % 4
                        pm = pmm[j]
                        X = xt[(ci * 8 + ti) % 2]
                        xk = ('E_x', (ci * 8 + ti) % 2)
                        p.dma('sp', X[:], xsrc[t0:t0 + 128, n0:n0 + 512], reads=[('y', i, ci)], writes=[xk])
                        for k in range(32):
                            p.op('pe', lambda e: e.matmul(pm[:, :], mT[:, k, ti * 128:(ti + 1) * 128], Wt[:, k, :], start=(k == 0), stop=(k == 31)),
                                 reads=[('E_mT', ti), ('E_W', wi % 2)], writes=[('E_pm', j)])
                        p.op('dve', lambda e: e.tensor_tensor(X[:], X[:], pm[:, :], ALU.add), reads=[('E_pm', j), xk], writes=[xk])
                        p.dma('sp', y_out[t0:t0 + 128, n0:n0 + 512], X[:], reads=[xk], writes=[('y', i, ci)])
                    wi += 1
        p.barrier()

    if dbg and 'A' not in phases:
        proj_in = din("proj_in", [S, NCOLS])
        for i in range(NT):
            p.dma('sp', proj[i * 128:(i + 1) * 128, :], proj_in[i * 128:(i + 1) * 128, :], writes=[('proj', i, 'all')])
        p.barrier()
    if 'B' in phases and 'r' not in phases:
        prologue_rope()
    if dbg and 'E' in phases and len(phases) < 6:
        br_in = din("br_in", [S, 4096])
        for i in range(NT):
            p.dma('sp', br[i * 128:(i + 1) * 128, :], br_in[i * 128:(i + 1) * 128, :], writes=[('br', i, 'all')])
        p.barrier()
    for l in range(n_layers):
        xsrc = x_in if l == 0 else y_out
        if 'A' in phases:
            phase_A(l, xsrc)
        if 'D' in phases:
            phase_D(l)
        if 'C' in phases:
            phase_C(l)
        if 'M' in phases:
            phase_M(l)
        if 'B' in phases:
            phase_B(l)
        if 'E' in phases:
            phase_E(l, xsrc)
    p.barrier()
    es.close()
    print("instructions:", p.nins)
    nc.in_names = in_names
    return nc


def make_consts():
    r = np.arange(128)
    m = np.stack([r[:, None] < r[None, :], r[:, None] > r[None, :], r[:, None] <= r[None, :], r[:, None] >= r[None, :]]).astype(np.float32)
    return {"c_ident": np.eye(128, dtype=np.float32), "c_masks": m,
            "c_iota": np.arange(1, 513, dtype=np.float32)[None, :],
            "c_invfreq": (1.0 / (np.float32(10000.0) ** (np.arange(0, 64, 2, dtype=np.float32) / np.float32(64)))).astype(np.float32)[None, :]}


_NC_CACHE = {}


def kernel(**inputs):
    nb = 4
    if 'nc' not in _NC_CACHE:
        _NC_CACHE['nc'] = build()
    nc = _NC_CACHE['nc']
    cst = make_consts()
    shared = {}
    for n in nc.in_names:
        if n in cst:
            shared[n] = cst[n]
        elif n in ("x", "mem", "positions"):
            continue
        elif n == "rwkv_r_k":
            shared[n] = np.ascontiguousarray(np.asarray(inputs[n], dtype=np.float32).reshape(L, 1024))
        elif n == "branch_g":
            shared[n] = np.ascontiguousarray(np.asarray(inputs[n], dtype=np.float32).reshape(L, 3072))
        else:
            shared[n] = np.ascontiguousarray(np.asarray(inputs[n], dtype=np.float32))
    in_maps = []
    for b in range(nb):
        m = dict(shared)
        m["x"] = np.ascontiguousarray(np.asarray(inputs["x"][b], dtype=np.float32))
        m["mem"] = np.ascontiguousarray(np.asarray(inputs["mem"][b], dtype=np.float32))
        m["positions"] = np.ascontiguousarray(np.asarray(inputs["positions"][b:b + 1]).astype(np.int32))
        in_maps.append(m)
    res = run_bass_kernel_spmd(nc, in_maps, core_ids=list(range(nb)))
    return np.stack([np.asarray(r["y"], dtype=np.float32) for r in res.results], axis=0)
```

```python
import numpy as np
from contextlib import ExitStack
import concourse.bass as bass
import concourse.mybir as mybir
from concourse.bass_utils import run_bass_kernel_spmd

F32 = mybir.dt.float32
BF16 = mybir.dt.bfloat16
I32 = mybir.dt.int32
ALU = mybir.AluOpType
AF = mybir.ActivationFunctionType
AX = mybir.AxisListType

D = 4096
S = 4096
L = 4
NCOLS = 10688
NT = S // 128
EPS = 1e-6
C_AU, C_AG, C_CQ, C_CKV, C_KPE, C_BG, C_RW, C_CG, C_MQ, C_MG = 0, 1024, 2048, 2944, 3200, 3264, 4288, 7616, 8640, 9664
NDMA = 8


class Prog:
    def __init__(self, nc, es):
        self.nc = nc
        self.E = {'pe': nc.tensor, 'act': nc.scalar, 'dve': nc.vector, 'pool': nc.gpsimd, 'sp': nc.sync}
        self.sems = {}
        for e in ['pe', 'act', 'dve', 'pool']:
            self.sems[e] = es.enter_context(nc.semaphore('s_' + e))
        for q in ['sp', 'pool']:
            for i in range(NDMA):
                self.sems[('d', q, i)] = es.enter_context(nc.semaphore(f'd_{q}_{i}'))
        self.cnt = {e: 0 for e in ['pe', 'act', 'dve', 'pool']}
        self.dma_n = {'sp': 0, 'pool': 0}
        self.waited = {}
        self.bufs = {}
        self.nins = 0

    def _wait(self, eng, key, val):
        if self.waited.get((eng, key), 0) >= val:
            return
        self.E[eng].wait_ge(self.sems[key], val)
        self.waited[(eng, key)] = val

    def _deps(self, reads, writes):
        deps = {}
        for b in reads:
            st = self.bufs.get(b)
            if st and st[0] is not None:
                k, v = st[0]
                if deps.get(k, 0) < v:
                    deps[k] = v
        for b in writes:
            st = self.bufs.get(b)
            if st:
                if st[0] is not None:
                    k, v = st[0]
                    if deps.get(k, 0) < v:
                        deps[k] = v
                for k, v in st[1].items():
                    if deps.get(k, 0) < v:
                        deps[k] = v
        return deps

    def _commit(self, tk, reads, writes):
        k, v = tk
        for b in reads:
            st = self.bufs.get(b)
            if st is None:
                st = self.bufs[b] = [None, {}]
            if st[1].get(k, 0) < v:
                st[1][k] = v
        for b in writes:
            self.bufs[b] = [tk, {}]

    def op(self, eng, fn, reads=(), writes=()):
        deps = self._deps(reads, writes)
        for k, v in deps.items():
            if k == 'pe' and eng == 'pe':
                continue
            self._wait(eng, k, v)
        ins = fn(self.E[eng])
        self.cnt[eng] += 1
        ins.then_inc(self.sems[eng], 1)
        self._commit((eng, self.cnt[eng]), reads, writes)
        self.nins += 1

    def dma(self, q, out, in_, reads=(), writes=(), **kw):
        deps = self._deps(reads, writes)
        n = self.dma_n[q]
        self.dma_n[q] += 1
        key = ('d', q, n % NDMA)
        val = 16 * (n // NDMA + 1)
        if n >= NDMA:
            deps[key] = max(deps.get(key, 0), val - 16)
        for k, v in deps.items():
            self._wait(q, k, v)
        ins = self.E[q].dma_start(out=out, in_=in_, **kw)
        ins.then_inc(self.sems[key], 16)
        self._commit((key, val), reads, writes)
        self.nins += 1

    def barrier(self):
        for e in ['pe', 'act', 'dve', 'pool', 'sp']:
            for k in self.sems:
                if isinstance(k, tuple):
                    n = self.dma_n[k[1]]
                    v = 16 * ((n - k[2] + NDMA - 1) // NDMA) if n > k[2] else 0
                else:
                    v = self.cnt[k]
                if v > 0:
                    self._wait(e, k, v)
        self.bufs.clear()


def build(n_layers=L, phases="AMBCDE", dbg=False, RWDT=BF16):
    nc = bass.Bass("TRN2", target_bir_lowering=False)
    es = ExitStack()

    in_names = []

    def din(name, shape, dt=F32):
        in_names.append(name)
        return nc.dram_tensor(name, list(shape), dt, kind="ExternalInput").ap()

    def dscr(name, shape, dt=F32):
        return nc.dram_tensor(name, list(shape), dt, kind="ExternalOutput" if dbg else "Internal").ap()

    x_in = din("x", [S, D])
    mem_in = din("mem", [256, D])
    pos_in = din("positions", [1, S], I32)
    ln_g = din("ln_g", [L, D])
    w_in = din("w_in", [L, D, NCOLS]) if ('A' in phases or not dbg) else None
    w_out = din("w_out", [L, D, D]) if ('E' in phases or not dbg) else None
    cst_ident = din("c_ident", [128, 128])
    c_masks_t = din("c_masks", [4, 128, 128])
    c_masks = [c_masks_t[i, :, :] for i in range(4)]
    c_iota = din("c_iota", [1, 512])
    c_invfreq = din("c_invfreq", [1, 32])
    branch_g = din("branch_g", [L, 3072])
    mla_q_a_norm = din("mla_q_a_norm", [L, 896]); mla_kv_a_norm = din("mla_kv_a_norm", [L, 256])
    mla_w_uq = din("mla_w_uq", [L, 896, 1536]); mla_w_ukv = din("mla_w_ukv", [L, 256, 2048])
    mla_q_norm = din("mla_q_norm", [L, 192]); mla_k_norm = din("mla_k_norm", [L, 192])
    mem_norm_g = din("mem_norm_g", [L, D]); mem_w_k = din("mem_w_k", [L, D, 1024]) if ('M' in phases or not dbg) else None
    mem_w_v = din("mem_w_v", [L, D, 1024]) if ('M' in phases or not dbg) else None
    mem_q_norm = din("mem_q_norm", [L, 256]); mem_k_norm = din("mem_k_norm", [L, 256])
    s5_lam_re = din("s5_lam_re", [L, 64, 64]); s5_lam_im = din("s5_lam_im", [L, 64, 64])
    s5_b_re = din("s5_b_re", [L, 64, 64, 16]); s5_b_im = din("s5_b_im", [L, 64, 64, 16])
    s5_c_re = din("s5_c_re", [L, 2, 64, 16, 64]); s5_c_im = din("s5_c_im", [L, 2, 64, 16, 64])
    s5_log_dt = din("s5_log_dt", [L, 2, 64]); s5_d = din("s5_d", [L, 1024]); s5_glu_w = din("s5_glu_w", [L, 1024, 1024])
    s5_glu_b = din("s5_glu_b", [L, 1024])
    rwkv_mu = din("rwkv_mu", [L, 2, 3328]); rwkv_w0 = din("rwkv_w0", [L, 2, 1024]); rwkv_w2 = din("rwkv_w2", [L, 2, 64, 1024])
    rwkv_a0 = din("rwkv_a0", [L, 2, 1024]); rwkv_a2 = din("rwkv_a2", [L, 2, 64, 1024]); rwkv_k_k = din("rwkv_k_k", [L, 1024])
    rwkv_k_a = din("rwkv_k_a", [L, 1024]); rwkv_r_k = din("rwkv_r_k", [L, 1024]); rwkv_ln_w = din("rwkv_ln_w", [L, 1024])
    rwkv_ln_b = din("rwkv_ln_b", [L, 1024])
    y_out = nc.dram_tensor("y", [S, D], F32, kind="ExternalOutput").ap()
    proj = dscr("proj", [S, NCOLS])
    wbf_in = nc.dram_tensor("wbf_in", [D, NCOLS], BF16, kind="Internal").ap()
    rwc = dscr("rwc", [S, 3328])
    ysc = dscr("ysc", [2, S, 1024])
    bon = dscr("bon", [2, S, 16])
    br = dscr("br", [S, 4096])
    ygd = dscr("ygd", [S, 1024])
    wbf_out = nc.dram_tensor("wbf_out", [D, D], BF16, kind="Internal").ap()
    qT_d = nc.dram_tensor("qT_d", [8, 192, S], BF16, kind="Internal").ap()
    kT_d = nc.dram_tensor("kT_d", [8, 192, S], BF16, kind="Internal").ap()
    v_d = nc.dram_tensor("v_d", [S, 1024], BF16, kind="Internal").ap()

    p = Prog(nc, es)

    uniq = [0]

    def sb(stack, name, shape, dt=F32):
        uniq[0] += 1
        return stack.enter_context(nc.sbuf_tensor(f"{name}_{uniq[0]}", list(shape), dt))

    def ps(stack, name, shape, dt=F32):
        uniq[0] += 1
        return stack.enter_context(nc.psum_tensor(f"{name}_{uniq[0]}", list(shape), dt))

    ident_f = sb(es, "ident_f", [128, 128], F32)
    ident_b = sb(es, "ident_b", [128, 128], BF16)
    p.dma('sp', ident_f[:], cst_ident[:, :], writes=['ident_f'])
    p.op('dve', lambda e: e.tensor_copy(ident_b[:], ident_f[:]), reads=['ident_f'], writes=['ident_b'])

    eps_t = sb(es, "eps_t", [128, 1], F32)
    p.op('dve', lambda e: e.memset(eps_t[:], EPS), writes=['eps_t'])

    def phase_A(l, xsrc):
        for r in range(0, D, 512):
            p.dma('pool', wbf_in[r:r + 512, :], w_in[l, r:r + 512, :], writes=[('wbf_in', r)])
        with ExitStack() as st:
            gt = sb(st, "A_g", [128, D], F32)
            xt = sb(st, "A_x", [128, D], F32)
            hb = sb(st, "A_hb", [128, D], BF16)
            hT = sb(st, "A_hT", [128, 32, 1024], BF16)
            W = [sb(st, f"A_W{i}", [128, 32, 512], BF16) for i in range(2)]
            ob = [sb(st, f"A_ob{i}", [128, 512], F32) for i in range(4)]
            ss = sb(st, "A_ss", [128, 1], F32)
            rstd = sb(st, "A_rstd", [128, 1], F32)
            ptr = [ps(st, f"A_pt{i}", [128, 8, 128], BF16) for i in range(2)]
            pmm = [ps(st, f"A_pm{i}", [128, 512], F32) for i in range(4)]
            p.dma('sp', gt[:], ln_g[l:l + 1, :].partition_broadcast(128), writes=['A_g'])
            nch = (NCOLS + 511) // 512
            wi = 0
            for g in range(S // 1024):
                for ti in range(8):
                    t0 = g * 1024 + ti * 128
                    p.dma('sp', xt[:], xsrc[t0:t0 + 128, :], writes=['A_x'])
                    p.op('act', lambda e: e.activation(hb[:], xt[:], AF.Square, accum_out=ss[:]),
                         reads=['A_x'], writes=['A_hb', 'A_ss'])
                    p.op('act', lambda e: e.activation(rstd[:], ss[:], AF.Sqrt, bias=eps_t[:], scale=1.0 / D),
                         reads=['A_ss', 'eps_t'], writes=['A_rstd'])
                    p.op('dve', lambda e: e.reciprocal(rstd[:], rstd[:]), reads=['A_rstd'], writes=['A_rstd'])
                    p.op('dve', lambda e: e.scalar_tensor_tensor(hb[:], xt[:], rstd[:], gt[:], ALU.mult, ALU.mult),
                         reads=['A_x', 'A_rstd', 'A_g'], writes=['A_hb'])
                    for k8 in range(4):
                        pt = ptr[k8 % 2]
                        for kk in range(8):
                            k = k8 * 8 + kk
                            p.op('pe', lambda e: e.transpose(pt[:, kk, :], hb[:, k * 128:(k + 1) * 128], ident_b[:]),
                                 reads=['A_hb', 'ident_b'], writes=[('A_pt', k8 % 2)])
                        eng = 'act' if k8 % 2 == 0 else 'dve'
                        dst = hT[:, k8 * 8:(k8 + 1) * 8, ti * 128:(ti + 1) * 128]
                        if eng == 'act':
                            p.op('act', lambda e: e.copy(dst, pt[:]), reads=[('A_pt', k8 % 2)], writes=[('A_hT', ti)])
                        else:
                            p.op('dve', lambda e: e.tensor_copy(dst, pt[:]), reads=[('A_pt', k8 % 2)], writes=[('A_hT', ti)])
                for ci in range(nch):
                    n0 = ci * 512
                    nw = min(512, NCOLS - n0)
                    Wt = W[wi % 2]
                    for k4 in range(4):
                        p.dma('sp', Wt[:, k4 * 8:(k4 + 1) * 8, 0:nw],
                              wbf_in[k4 * 1024:(k4 + 1) * 1024, n0:n0 + nw].rearrange("(k p) n -> p k n", p=128),
                              reads=[('wbf_in', (k4 * 1024) // 512 * 512), ('wbf_in', (k4 * 1024) // 512 * 512 + 512)],
                              writes=[('A_W', wi % 2)])
                    for ti in range(8):
                        t0 = g * 1024 + ti * 128
                        j = (ci * 8 + ti) % 4
                        pm = pmm[j]
                        for k in range(32):
                            p.op('pe', lambda e: e.matmul(pm[:, 0:nw], hT[:, k, ti * 128:(ti + 1) * 128], Wt[:, k, 0:nw],
                                                         start=(k == 0), stop=(k == 31)),
                                 reads=[('A_hT', ti), ('A_W', wi % 2)], writes=[('A_pm', j)])
                        if j % 2 == 0:
                            p.op('act', lambda e: e.copy(ob[j][:, 0:nw], pm[:, 0:nw]), reads=[('A_pm', j)], writes=[('A_ob', j)])
                        else:
                            p.op('dve', lambda e: e.tensor_copy(ob[j][:, 0:nw], pm[:, 0:nw]), reads=[('A_pm', j)], writes=[('A_ob', j)])
                        p.dma('sp', proj[t0:t0 + 128, n0:n0 + nw], ob[j][:, 0:nw], reads=[('A_ob', j)],
                              writes=[('proj', t0 // 128, ci)])
                    wi += 1
        p.barrier()


    RW = 3328
    NEG_E = -float(np.exp(-0.5))
    RW_DT = RWDT

    def phase_D(l):
        with ExitStack() as st:
            mup = sb(st, "D0_mup", [128, RW]); mun = sb(st, "D0_mun", [128, RW]); m0 = sb(st, "D0_m0", [128, RW])
            ct = sb(st, "D0_c", [128, RW]); pt_ = sb(st, "D0_p", [128, RW]); nt = sb(st, "D0_n", [128, RW])
            p.dma('sp', mup[:], rwkv_mu[l, 0:1, :].partition_broadcast(128), writes=['D0_mup'])
            p.dma('sp', mun[:], rwkv_mu[l, 1:2, :].partition_broadcast(128), writes=['D0_mun'])
            p.op('dve', lambda e: e.tensor_tensor(m0[:], mup[:], mun[:], ALU.add), reads=['D0_mup', 'D0_mun'], writes=['D0_m0'])
            p.op('dve', lambda e: e.tensor_scalar(m0[:], m0[:], -1.0, 1.0, ALU.mult, ALU.add), reads=['D0_m0'], writes=['D0_m0'])
            for i in range(NT):
                t0 = i * 128
                p.dma('sp', ct[:], proj[t0:t0 + 128, C_RW:C_RW + RW], reads=[('proj', i, 'all')], writes=['D0_c'])
                if i == 0:
                    p.op('pool', lambda e: e.memset(pt_[:], 0.0), writes=['D0_p'])
                    p.dma('sp', pt_[1:128, :], proj[0:127, C_RW:C_RW + RW], reads=[('proj', 0, 'all')], writes=['D0_p'])
                else:
                    p.dma('sp', pt_[:], proj[t0 - 1:t0 + 127, C_RW:C_RW + RW], reads=[('proj', i, 'all'), ('proj', i - 1, 'all')], writes=['D0_p'])
                if i == NT - 1:
                    p.op('pool', lambda e: e.memset(nt[:], 0.0), writes=['D0_n'])
                    p.dma('sp', nt[0:127, :], proj[t0 + 1:t0 + 128, C_RW:C_RW + RW], reads=[('proj', i, 'all')], writes=['D0_n'])
                else:
                    p.dma('sp', nt[:], proj[t0 + 1:t0 + 129, C_RW:C_RW + RW], reads=[('proj', i, 'all'), ('proj', i + 1, 'all')], writes=['D0_n'])
                p.op('dve', lambda e: e.tensor_tensor(ct[:], ct[:], m0[:], ALU.mult), reads=['D0_c', 'D0_m0'], writes=['D0_c'])
                p.op('pool', lambda e: e.tensor_tensor(pt_[:], pt_[:], mup[:], ALU.mult), reads=['D0_p', 'D0_mup'], writes=['D0_p'])
                p.op('pool', lambda e: e.tensor_tensor(nt[:], nt[:], mun[:], ALU.mult), reads=['D0_n', 'D0_mun'], writes=['D0_n'])
                p.op('dve', lambda e: e.tensor_tensor(ct[:], ct[:], pt_[:], ALU.add), reads=['D0_c', 'D0_p'], writes=['D0_c'])
                p.op('dve', lambda e: e.tensor_tensor(ct[:], ct[:], nt[:], ALU.add), reads=['D0_c', 'D0_n'], writes=['D0_c'])
                p.dma('sp', rwc[t0:t0 + 128, :], ct[:], reads=['D0_c'], writes=[('rwc', i)])
        p.barrier()
        with ExitStack() as st:
            def bc(name, src):
                t = sb(st, name, [128, 1024])
                p.dma('sp', t[:], src.partition_broadcast(128), writes=[name])
                return t
            kk_c = bc("D_kk_c", rwkv_k_k[l:l + 1, :]); ka_c = bc("D_ka_c", rwkv_k_a[l:l + 1, :])
            rk_c = bc("D_rk_c", rwkv_r_k[l:l + 1, :])
            c1 = sb(st, "D_c1", [128, 1024])
            p.op('dve', lambda e: e.tensor_scalar(c1[:], ka_c[:], -1.0, 1.0, ALU.mult, ALU.add), reads=['D_ka_c'], writes=['D_c1'])
            w0_c = sb(st, "D_w0", [128, 1024]); a0_c = sb(st, "D_a0", [128, 1024])
            w2_t = sb(st, "D_w2", [64, 1024]); a2_t = sb(st, "D_a2", [64, 1024])
            mS = sb(st, "D_mS", [128, 128]); mI = sb(st, "D_mI", [128, 128]); mST = sb(st, "D_mST", [128, 128])
            imask = sb(st, "D_imask", [64, 1024])
            b4 = lambda t: t[:].unsqueeze(1).broadcast_to([128, 4, 128])
            v4 = lambda a: a.rearrange("p (a b) -> p a b", a=4)
            triI = sb(st, "D_triI", [128, 128]); triC = sb(st, "D_triC", [128, 128])
            identr = sb(st, "D_identr", [128, 128], RW_DT)
            p.op('dve', lambda e: e.tensor_copy(identr[:], ident_f[:]), reads=['ident_f'], writes=['D_identr'])
            for h in range(16):
                p.op('pool', lambda e: e.tensor_copy(imask[:, h * 64:(h + 1) * 64], ident_f[0:64, 0:64]), reads=['ident_f'], writes=['D_imask'])
            rw = sb(st, "D_rw", [128, RW])
            kk = sb(st, "D_kk", [128, 1024]); ld = sb(st, "D_ld", [128, 1024]); a_t = sb(st, "D_a", [128, 1024])
            kd = sb(st, "D_kd", [128, 1024]); ba = sb(st, "D_ba", [128, 1024]); tmp = sb(st, "D_tmp", [128, 1024])
            Ab = sb(st, "D_Ab", [128, 1024]); Rb = sb(st, "D_Rb", [128, 1024]); Bb = sb(st, "D_Bb", [128, 1024]); Kb = sb(st, "D_Kb", [128, 1024])
            Abr = sb(st, "D_Abr", [128, 1024], RW_DT)
            Bt = sb(st, "D_Bt", [128, 1024], RW_DT); Kt = sb(st, "D_Kt", [128, 1024], RW_DT); Vr = sb(st, "D_Vr", [128, 1024], RW_DT)
            ydg = sb(st, "D_ydg", [64, 1024], RW_DT)
            sm = sb(st, "D_sm", [128, 64]); smT = sb(st, "D_smT", [64, 2, 128])
            hs = sb(st, "D_hs", [128, 16]); hs2 = sb(st, "D_hs2", [128, 16])
            AbT = sb(st, "D_AbT", [64, 16, 128], RW_DT); RbT = sb(st, "D_RbT", [64, 16, 128], RW_DT)
            BbT = sb(st, "D_BbT", [64, 16, 128], RW_DT); KbT = sb(st, "D_KbT", [64, 16, 128], RW_DT)
            Q = [sb(st, f"D_Q{i}", [128, 512], RW_DT) for i in range(2)]
            QT = [sb(st, f"D_QT{i}", [128, 512], RW_DT) for i in range(2)]
            P = sb(st, "D_P", [128, 512]); Pr = sb(st, "D_Pr", [128, 512], RW_DT)
            MrbT = sb(st, "D_MrbT", [128, 512], RW_DT); LakT = sb(st, "D_LakT", [128, 512], RW_DT); MrkT = sb(st, "D_MrkT", [128, 512], RW_DT)
            AXt = sb(st, "D_AX", [128, 4, 128], RW_DT); AU = sb(st, "D_AU", [128, 4, 128], RW_DT)
            RhT = sb(st, "D_RhT", [64, 16, 128], RW_DT); GT = sb(st, "D_GT", [64, 1024], RW_DT)
            Hh = sb(st, "D_H", [64, 1024]); Yh = sb(st, "D_Yh", [128, 1024])
            ST = sb(st, "D_ST", [64, 1024]); STr = sb(st, "D_STr", [64, 1024], RW_DT)
            pb = [ps(st, f"D_pb{i}", [128, 512]) for i in range(8)]
            pbi = [0]

            def nb():
                i = pbi[0] % 8
                pbi[0] += 1
                return i

            for d in range(2):
                p.dma('sp', w0_c[:], rwkv_w0[l, d:d + 1, :].partition_broadcast(128), writes=['D_w0'])
                p.dma('sp', a0_c[:], rwkv_a0[l, d:d + 1, :].partition_broadcast(128), writes=['D_a0'])
                p.dma('sp', w2_t[:], rwkv_w2[l, d, :, :], writes=['D_w2'])
                p.dma('sp', a2_t[:], rwkv_a2[l, d, :, :], writes=['D_a2'])
                cm = c_masks
                p.dma('sp', mS[:], cm[0 if d == 0 else 1], writes=['D_mS'])
                p.dma('sp', mI[:], cm[2 if d == 0 else 3], writes=['D_mI'])
                p.dma('sp', mST[:], cm[1 if d == 0 else 0], writes=['D_mST'])
                p.dma('sp', triI[:], cm[2 if d == 0 else 3], writes=['D_triI'])
                p.dma('sp', triC[:], cm[1 if d == 0 else 0], writes=['D_triC'])
                p.op('dve', lambda e: e.memset(ST[:], 0.0), writes=['D_ST'])
                p.op('dve', lambda e: e.memset(STr[:], 0.0), writes=['D_STr'])
                order = range(NT) if d == 0 else range(NT - 1, -1, -1)
                for c in order:
                    t0 = c * 128
                    p.dma('sp', rw[:], rwc[t0:t0 + 128, :], reads=[('rwc', c)], writes=['D_rw'])
                    r_ = rw[:, 0:1024]; k_ = rw[:, 1024:2048]; v_ = rw[:, 2048:3072]
                    win = rw[:, 3072 + 64 * d:3136 + 64 * d]; ain = rw[:, 3200 + 64 * d:3264 + 64 * d]
                    p.op('dve', lambda e: e.tensor_tensor(kk[:], k_, kk_c[:], ALU.mult), reads=['D_rw', 'D_kk_c'], writes=['D_kk'])
                    p.op('pool', lambda e: e.tensor_tensor(tmp[:], kk[:], kk[:], ALU.mult), reads=['D_kk'], writes=['D_tmp'])
                    p.op('dve', lambda e: e.tensor_reduce(hs[:], tmp[:].rearrange("p (h j) -> p h j", h=16), AX.X, ALU.add), reads=['D_tmp'], writes=['D_hs'])
                    p.op('act', lambda e: e.activation(hs[:], hs[:], AF.Sqrt), reads=['D_hs'], writes=['D_hs'])
                    p.op('dve', lambda e: e.tensor_scalar(hs[:], hs[:], 1e-12, None, ALU.max), reads=['D_hs'], writes=['D_hs'])
                    p.op('dve', lambda e: e.reciprocal(hs[:], hs[:]), reads=['D_hs'], writes=['D_hs'])
                    p.op('dve', lambda e: e.tensor_tensor(kk[:].rearrange("p (h j) -> p h j", h=16), kk[:].rearrange("p (h j) -> p h j", h=16),
                                                         hs[:].unsqueeze(2).broadcast_to([128, 16, 64]), ALU.mult), reads=['D_kk', 'D_hs'], writes=['D_kk'])
                    p.op('act', lambda e: e.activation(sm[:], win, AF.Tanh), reads=['D_rw'], writes=['D_sm'])
                    b0 = nb()
                    p.op('pe', lambda e: e.transpose(pb[b0][0:64, 0:128], sm[:], ident_f[:]), reads=['D_sm', 'ident_f'], writes=[('D_pb', b0)])
                    p.op('pe', lambda e: e.transpose(pb[b0][0:64, 128:256], ain, ident_f[:]), reads=['D_rw', 'ident_f'], writes=[('D_pb', b0)])
                    p.op('act', lambda e: e.copy(smT[:].rearrange("p a b -> p (a b)"), pb[b0][0:64, 0:256]), reads=[('D_pb', b0)], writes=['D_smT'])
                    for half in range(2):
                        cs_ = slice(half * 512, (half + 1) * 512)
                        b1 = nb()
                        p.op('pe', lambda e: e.matmul(pb[b1][:, :], smT[:, 0, :], w2_t[:, cs_], start=True, stop=True),
                             reads=['D_smT', 'D_w2'], writes=[('D_pb', b1)])
                        p.op('dve', lambda e: e.tensor_tensor(ld[:, cs_], pb[b1][:, :], w0_c[:, cs_], ALU.add), reads=[('D_pb', b1), 'D_w0'], writes=['D_ld'])
                        b2 = nb()
                        p.op('pe', lambda e: e.matmul(pb[b2][:, :], smT[:, 1, :], a2_t[:, cs_], start=True, stop=True),
                             reads=['D_smT', 'D_a2'], writes=[('D_pb', b2)])
                        p.op('dve', lambda e: e.tensor_tensor(a_t[:, cs_], pb[b2][:, :], a0_c[:, cs_], ALU.add), reads=[('D_pb', b2), 'D_a0'], writes=['D_a'])
                    p.op('act', lambda e: e.activation(ld[:], ld[:], AF.Sigmoid), reads=['D_ld'], writes=['D_ld'])
                    p.op('act', lambda e: e.activation(a_t[:], a_t[:], AF.Sigmoid), reads=['D_a'], writes=['D_a'])
                    p.op('pool', lambda e: e.tensor_scalar(ld[:], ld[:], NEG_E, None, ALU.mult), reads=['D_ld'], writes=['D_ld'])
                    p.op('dve', lambda e: e.tensor_tensor(tmp[:], a_t[:], ka_c[:], ALU.mult), reads=['D_a', 'D_ka_c'], writes=['D_tmp'])
                    p.op('dve', lambda e: e.tensor_tensor(tmp[:], tmp[:], c1[:], ALU.add), reads=['D_tmp', 'D_c1'], writes=['D_tmp'])
                    p.op('dve', lambda e: e.tensor_tensor(kd[:], tmp[:], k_, ALU.mult), reads=['D_tmp', 'D_rw'], writes=['D_kd'])
                    p.op('pool', lambda e: e.tensor_tensor(ba[:], kk[:], a_t[:], ALU.mult), reads=['D_kk', 'D_a'], writes=['D_ba'])
                    p.op('pool', lambda e: e.tensor_tensor(tmp[:], kd[:], rk_c[:], ALU.mult), reads=['D_kd', 'D_rk_c'], writes=['D_tmp'])
                    p.op('pool', lambda e: e.tensor_tensor(tmp[:], tmp[:], r_, ALU.mult), reads=['D_tmp', 'D_rw'], writes=['D_tmp'])
                    p.op('dve', lambda e: e.tensor_reduce(hs2[:], tmp[:].rearrange("p (h j) -> p h j", h=16), AX.X, ALU.add), reads=['D_tmp'], writes=['D_hs2'])
                    p.dma('sp', bon[d, t0:t0 + 128, :], hs2[:], reads=['D_hs2'], writes=[('bon', d, c)])
                    for half in range(2):
                        cs_ = slice(half * 512, (half + 1) * 512)
                        bcs = nb()
                        p.op('pe', lambda e: e.matmul(pb[bcs][:, :], triI[:], ld[:, cs_], start=True, stop=True), reads=['D_triI', 'D_ld'], writes=[('D_pb', bcs)])
                        brm = nb()
                        p.op('pe', lambda e: e.matmul(pb[brm][:, :], triC[:], ld[:, cs_], start=True, stop=True), reads=['D_triC', 'D_ld'], writes=[('D_pb', brm)])
                        p.op('dve', lambda e: e.tensor_tensor(tmp[:, cs_], pb[bcs][:, :], ld[:, cs_], ALU.subtract), reads=[('D_pb', bcs), 'D_ld'], writes=['D_tmp'])
                        p.op('act', lambda e: e.activation(tmp[:, cs_], tmp[:, cs_], AF.Exp), reads=['D_tmp'], writes=['D_tmp'])
                        p.op('dve', lambda e: e.scalar_tensor_tensor(Ab[:, cs_], kk[:, cs_], -1.0, tmp[:, cs_], ALU.mult, ALU.mult), reads=['D_kk', 'D_tmp'], writes=['D_Ab'])
                        p.op('act', lambda e: e.activation(Rb[:, cs_], pb[bcs][:, :], AF.Exp), reads=[('D_pb', bcs)], writes=['D_Rb'])
                        p.op('act', lambda e: e.activation(Kt[0:64, cs_] if False else tmp[0:64, cs_], pb[brm][0:64, :], AF.Exp), reads=[('D_pb', brm), 'D_tmp'], writes=['D_tmp'])
                        p.op('dve', lambda e: e.tensor_tensor(tmp[0:64, cs_], tmp[0:64, cs_], Rb[0:64, cs_], ALU.mult), reads=['D_tmp', 'D_Rb'], writes=['D_tmp'])
                        p.op('dve', lambda e: e.tensor_tensor(ydg[:, cs_], tmp[0:64, cs_], imask[:, cs_], ALU.mult), reads=['D_tmp', 'D_imask'], writes=['D_ydg'])
                        p.op('pool', lambda e: e.tensor_tensor(Rb[:, cs_], Rb[:, cs_], r_[:, cs_] if False else rw[:, half * 512:(half + 1) * 512], ALU.mult), reads=['D_Rb', 'D_rw', 'D_tmp'], writes=['D_Rb'])
                        p.op('act', lambda e: e.activation(tmp[:, cs_], pb[bcs][:, :], AF.Exp, scale=-1.0), reads=[('D_pb', bcs), 'D_tmp', 'D_ydg'], writes=['D_tmp'])
                        p.op('dve', lambda e: e.tensor_tensor(Bb[:, cs_], ba[:, cs_], tmp[:, cs_], ALU.mult), reads=['D_ba', 'D_tmp'], writes=['D_Bb'])
                        p.op('pool', lambda e: e.tensor_tensor(Kb[:, cs_], kd[:, cs_], tmp[:, cs_], ALU.mult), reads=['D_kd', 'D_tmp'], writes=['D_Kb'])
                        p.op('act', lambda e: e.activation(tmp[:, cs_], pb[brm][:, :], AF.Exp), reads=[('D_pb', brm), 'D_tmp', 'D_Bb', 'D_Kb'], writes=['D_tmp'])
                        p.op('dve', lambda e: e.tensor_tensor(Bt[:, cs_], ba[:, cs_], tmp[:, cs_], ALU.mult), reads=['D_ba', 'D_tmp'], writes=['D_Bt'])
                        p.op('pool', lambda e: e.tensor_tensor(Kt[:, cs_], kd[:, cs_], tmp[:, cs_], ALU.mult), reads=['D_kd', 'D_tmp'], writes=['D_Kt'])
                    p.op('act', lambda e: e.copy(Vr[:], v_), reads=['D_rw'], writes=['D_Vr'])
                    p.op('act', lambda e: e.copy(Abr[:], Ab[:]), reads=['D_Ab'], writes=['D_Abr'])
                    for (src, dstT, nm) in ((Ab, AbT, 'D_AbT'), (Rb, RbT, 'D_RbT'), (Bb, BbT, 'D_BbT'), (Kb, KbT, 'D_KbT')):
                        srcn = {'D_AbT': 'D_Ab', 'D_RbT': 'D_Rb', 'D_BbT': 'D_Bb', 'D_KbT': 'D_Kb'}[nm]
                        for h4 in range(4):
                            bt = nb()
                            for hl in range(4):
                                h = h4 * 4 + hl
                                p.op('pe', lambda e: e.transpose(pb[bt][0:64, hl * 128:(hl + 1) * 128], src[:, h * 64:(h + 1) * 64], ident_f[:]),
                                     reads=[srcn, 'ident_f'], writes=[('D_pb', bt)])
                            dst = dstT[:, h4 * 4:(h4 + 1) * 4, :].rearrange("p a b -> p (a b)")
                            if h4 % 2 == 0:
                                p.op('act', lambda e: e.copy(dst, pb[bt][0:64, :]), reads=[('D_pb', bt)], writes=[nm])
                            else:
                                p.op('dve', lambda e: e.tensor_copy(dst, pb[bt][0:64, :]), reads=[('D_pb', bt)], writes=[nm])
                    for h4 in range(4):
                        bA, bB, bC, bD, bE = nb(), nb(), nb(), nb(), nb()
                        for hl in range(4):
                            h = h4 * 4 + hl
                            sl = slice(hl * 128, (hl + 1) * 128)
                            p.op('pe', lambda e: e.matmul(pb[bA][:, sl], BbT[:, h, :], AbT[:, h, :], start=True, stop=True), reads=['D_BbT', 'D_AbT'], writes=[('D_pb', bA)])
                            p.op('pe', lambda e: e.matmul(pb[bB][:, sl], BbT[:, h, :], RbT[:, h, :], start=True, stop=True), reads=['D_BbT', 'D_RbT'], writes=[('D_pb', bB)])
                            p.op('pe', lambda e: e.matmul(pb[bC][:, sl], KbT[:, h, :], AbT[:, h, :], start=True, stop=True), reads=['D_KbT', 'D_AbT'], writes=[('D_pb', bC)])
                            p.op('pe', lambda e: e.matmul(pb[bD][:, sl], KbT[:, h, :], RbT[:, h, :], start=True, stop=True), reads=['D_KbT', 'D_RbT'], writes=[('D_pb', bD)])
                            p.op('pe', lambda e: e.matmul(pb[bE][:, sl], AbT[:, h, :], BbT[:, h, :], start=True, stop=True), reads=['D_BbT', 'D_AbT'], writes=[('D_pb', bE)])
                        qi = 0
                        p.op('dve', lambda e: e.tensor_tensor(v4(Q[qi][:]), v4(pb[bA][:, :]), b4(mS), ALU.mult), reads=[('D_pb', bA), 'D_mS'], writes=[('D_Q', qi)])
                        p.op('dve', lambda e: e.tensor_tensor(v4(MrbT[:]), v4(pb[bB][:, :]), b4(mI), ALU.mult), reads=[('D_pb', bB), 'D_mI'], writes=['D_MrbT'])
                        p.op('dve', lambda e: e.tensor_tensor(v4(LakT[:]), v4(pb[bC][:, :]), b4(mS), ALU.mult), reads=[('D_pb', bC), 'D_mS'], writes=['D_LakT'])
                        p.op('dve', lambda e: e.tensor_tensor(v4(MrkT[:]), v4(pb[bD][:, :]), b4(mI), ALU.mult), reads=[('D_pb', bD), 'D_mI'], writes=['D_MrkT'])
                        p.op('dve', lambda e: e.tensor_tensor(v4(QT[qi][:]), v4(pb[bE][:, :]), b4(mST), ALU.mult), reads=[('D_pb', bE), 'D_mST'], writes=[('D_QT', qi)])
                        p.op('pool', lambda e: e.tensor_tensor(v4(P[:]), v4(Q[qi][:]), b4(ident_f), ALU.add), reads=[('D_Q', qi), 'ident_f'], writes=['D_P'])
                        p.op('pool', lambda e: e.tensor_copy(Pr[:], P[:]), reads=['D_P'], writes=['D_Pr'])
                        for lvl in range(6):
                            qn = 1 - qi
                            bqT = nb()
                            for hl in range(4):
                                sl = slice(hl * 128, (hl + 1) * 128)
                                p.op('pe', lambda e: e.matmul(pb[bqT][:, sl], Q[qi][:, sl], QT[qi][:, sl], start=True, stop=True),
                                     reads=[('D_Q', qi), ('D_QT', qi)], writes=[('D_pb', bqT)])
                            if lvl < 5:
                                bq = nb()
                                for hl in range(4):
                                    sl = slice(hl * 128, (hl + 1) * 128)
                                    p.op('pe', lambda e: e.matmul(pb[bq][:, sl], QT[qi][:, sl], Q[qi][:, sl], start=True, stop=True),
                                         reads=[('D_Q', qi), ('D_QT', qi)], writes=[('D_pb', bq)])
                            p.op('act', lambda e: e.copy(QT[qn][:], pb[bqT][:, :]), reads=[('D_pb', bqT)], writes=[('D_QT', qn)])
                            if lvl < 5:
                                p.op('dve', lambda e: e.tensor_copy(Q[qn][:], pb[bq][:, :]), reads=[('D_pb', bq)], writes=[('D_Q', qn)])
                            bp = nb()
                            for hl in range(4):
                                sl = slice(hl * 128, (hl + 1) * 128)
                                p.op('pe', lambda e: e.matmul(pb[bp][:, sl], QT[qn][:, sl], Pr[:, sl], start=True, stop=True),
                                     reads=[('D_QT', qn), 'D_Pr'], writes=[('D_pb', bp)])
                            p.op('dve', lambda e: e.tensor_tensor(P[:], P[:], pb[bp][:, :], ALU.add), reads=['D_P', ('D_pb', bp)], writes=['D_P'])
                            p.op('pool', lambda e: e.tensor_copy(Pr[:], P[:]), reads=['D_P'], writes=['D_Pr'])
                            qi = qn
                        bx = nb()
                        for hl in range(4):
                            h = h4 * 4 + hl
                            p.op('pe', lambda e: e.matmul(pb[bx][:, hl * 64:(hl + 1) * 64], LakT[:, hl * 128:(hl + 1) * 128], Vr[:, h * 64:(h + 1) * 64], start=True, stop=True),
                                 reads=['D_LakT', 'D_Vr'], writes=[('D_pb', bx)])
                        p.op('act', lambda e: e.copy(AXt[:, :, 64:128], pb[bx][:, 0:256].rearrange("p (a b) -> p a b", a=4)), reads=[('D_pb', bx)], writes=['D_AX'])
                        p.op('pool', lambda e: e.tensor_copy(AXt[:, :, 0:64], Abr[:, h4 * 256:(h4 + 1) * 256].rearrange("p (a b) -> p a b", a=4)), reads=['D_Abr'], writes=['D_AX'])
                        bu = nb()
                        for hl in range(4):
                            p.op('pe', lambda e: e.matmul(pb[bu][:, hl * 128:(hl + 1) * 128], Pr[:, hl * 128:(hl + 1) * 128], AXt[:, hl, :], start=True, stop=True),
                                 reads=['D_Pr', 'D_AX'], writes=[('D_pb', bu)])
                        p.op('act', lambda e: e.copy(AU[:].rearrange("p a b -> p (a b)"), pb[bu][:, :]), reads=[('D_pb', bu)], writes=['D_AU'])
                        br_, bg, bh, by = nb(), nb(), nb(), nb()
                        for hl in range(4):
                            h = h4 * 4 + hl
                            hc = slice(h * 64, (h + 1) * 64)
                            p.op('pe', lambda e: e.matmul(pb[br_][0:64, hl * 128:(hl + 1) * 128], AU[:, hl, 0:64], MrbT[:, hl * 128:(hl + 1) * 128], start=True, stop=True),
                                 reads=['D_AU', 'D_MrbT'], writes=[('D_pb', br_)])
                            p.op('pe', lambda e: e.matmul(pb[bg][0:64, hl * 64:(hl + 1) * 64], AU[:, hl, 0:64], Bt[:, hc], start=True, stop=False),
                                 reads=['D_AU', 'D_Bt'], writes=[('D_pb', bg)])
                            p.op('pe', lambda e: e.matmul(pb[bg][0:64, hl * 64:(hl + 1) * 64], identr[0:64, 0:64], ydg[:, hc], start=False, stop=True),
                                 reads=['D_identr', 'D_ydg'], writes=[('D_pb', bg)])
                            p.op('pe', lambda e: e.matmul(pb[bh][0:64, hl * 64:(hl + 1) * 64], Bt[:, hc], AU[:, hl, 64:128], start=True, stop=False),
                                 reads=['D_AU', 'D_Bt'], writes=[('D_pb', bh)])
                            p.op('pe', lambda e: e.matmul(pb[bh][0:64, hl * 64:(hl + 1) * 64], Kt[:, hc], Vr[:, hc], start=False, stop=True),
                                 reads=['D_Kt', 'D_Vr'], writes=[('D_pb', bh)])
                            p.op('pe', lambda e: e.matmul(pb[by][:, hl * 64:(hl + 1) * 64], MrbT[:, hl * 128:(hl + 1) * 128], AU[:, hl, 64:128], start=True, stop=False),
                                 reads=['D_AU', 'D_MrbT'], writes=[('D_pb', by)])
                            p.op('pe', lambda e: e.matmul(pb[by][:, hl * 64:(hl + 1) * 64], MrkT[:, hl * 128:(hl + 1) * 128], Vr[:, hc], start=False, stop=True),
                                 reads=['D_MrkT', 'D_Vr'], writes=[('D_pb', by)])
                        p.op('dve', lambda e: e.tensor_tensor(RhT[:, h4 * 4:(h4 + 1) * 4, :].rearrange("p a b -> p (a b)"), pb[br_][0:64, :],
                                                             RbT[:, h4 * 4:(h4 + 1) * 4, :].rearrange("p a b -> p (a b)"), ALU.add),
                             reads=[('D_pb', br_), 'D_RbT'], writes=['D_RhT'])
                        p.op('act', lambda e: e.copy(GT[:, h4 * 256:(h4 + 1) * 256], pb[bg][0:64, 0:256]), reads=[('D_pb', bg)], writes=['D_GT'])
                        p.op('act', lambda e: e.copy(Hh[:, h4 * 256:(h4 + 1) * 256], pb[bh][0:64, 0:256]), reads=[('D_pb', bh)], writes=['D_H'])
                        p.op('dve', lambda e: e.tensor_copy(Yh[:, h4 * 256:(h4 + 1) * 256], pb[by][:, 0:256]), reads=[('D_pb', by)], writes=['D_Yh'])
                    for half in range(2):
                        bY = nb()
                        bS = nb()
                        for hh in range(8):
                            h = half * 8 + hh
                            hc = slice(h * 64, (h + 1) * 64)
                            p.op('pe', lambda e: e.matmul(pb[bY][:, hh * 64:(hh + 1) * 64], RhT[:, h, :], STr[:, hc], start=True, stop=True),
                                 reads=['D_RhT', 'D_STr'], writes=[('D_pb', bY)])
                            p.op('pe', lambda e: e.matmul(pb[bS][0:64, hh * 64:(hh + 1) * 64], GT[:, hc], STr[:, hc], start=True, stop=True),
                                 reads=['D_GT', 'D_STr'], writes=[('D_pb', bS)])
                        cs_ = slice(half * 512, (half + 1) * 512)
                        p.op('dve', lambda e: e.tensor_tensor(Yh[:, cs_], pb[bY][:, :], Yh[:, cs_], ALU.add), reads=[('D_pb', bY), 'D_Yh'], writes=['D_Yh'])
                        p.op('dve', lambda e: e.tensor_tensor(ST[:, cs_], pb[bS][0:64, :], Hh[:, cs_], ALU.add), reads=[('D_pb', bS), 'D_H'], writes=[('D_ST', half)])
                    p.op('act', lambda e: e.copy(STr[:], ST[:]), reads=[('D_ST', 0), ('D_ST', 1)], writes=['D_STr'])
                    p.dma('sp', ysc[d, t0:t0 + 128, :], Yh[:], reads=['D_Yh'], writes=[('ysc', d, c)])
        p.barrier()
        with ExitStack() as st:
            lnw = sb(st, "D2_lnw", [128, 1024]); lnb = sb(st, "D2_lnb", [128, 1024])
            p.dma('sp', lnw[:], rwkv_ln_w[l:l + 1, :].partition_broadcast(128), writes=['D2_lnw'])
            p.dma('sp', lnb[:], rwkv_ln_b[l:l + 1, :].partition_broadcast(128), writes=['D2_lnb'])
            y0 = sb(st, "D2_y0", [128, 1024]); y1 = sb(st, "D2_y1", [128, 1024]); vt = sb(st, "D2_v", [128, 1024]); sq = sb(st, "D2_sq", [128, 1024])
            b0t = sb(st, "D2_b0", [128, 16]); b1t = sb(st, "D2_b1", [128, 16]); mean = sb(st, "D2_mean", [128, 16]); var = sb(st, "D2_var", [128, 16])
            eps2 = sb(st, "D2_eps", [128, 1])
            p.op('dve', lambda e: e.memset(eps2[:], 64e-5), writes=['D2_eps'])
            v3 = lambda t: t[:].rearrange("p (h j) -> p h j", h=16)
            bc3 = lambda t: t[:].unsqueeze(2).broadcast_to([128, 16, 64])
            for i in range(NT):
                t0 = i * 128
                p.dma('sp', y0[:], ysc[0, t0:t0 + 128, :], reads=[('ysc', 0, i)], writes=['D2_y0'])
                p.dma('sp', y1[:], ysc[1, t0:t0 + 128, :], reads=[('ysc', 1, i)], writes=['D2_y1'])
                p.dma('sp', vt[:], rwc[t0:t0 + 128, 2048:3072], reads=[('rwc', i)], writes=['D2_v'])
                p.dma('sp', b0t[:], bon[0, t0:t0 + 128, :], reads=[('bon', 0, i)], writes=['D2_b0'])
                p.dma('sp', b1t[:], bon[1, t0:t0 + 128, :], reads=[('bon', 1, i)], writes=['D2_b1'])
                p.op('dve', lambda e: e.tensor_tensor(y0[:], y0[:], y1[:], ALU.add), reads=['D2_y0', 'D2_y1'], writes=['D2_y0'])
                p.op('dve', lambda e: e.tensor_reduce(mean[:], v3(y0), AX.X, ALU.add), reads=['D2_y0'], writes=['D2_mean'])
                p.op('dve', lambda e: e.tensor_scalar(mean[:], mean[:], 1.0 / 64, None, ALU.mult), reads=['D2_mean'], writes=['D2_mean'])
                p.op('dve', lambda e: e.tensor_tensor(v3(y0), v3(y0), bc3(mean), ALU.subtract), reads=['D2_y0', 'D2_mean'], writes=['D2_y0'])
                p.op('pool', lambda e: e.tensor_tensor(sq[:], y0[:], y0[:], ALU.mult), reads=['D2_y0'], writes=['D2_sq'])
                p.op('dve', lambda e: e.tensor_reduce(var[:], v3(sq), AX.X, ALU.add), reads=['D2_sq'], writes=['D2_var'])
                p.op('act', lambda e: e.activation(var[:], var[:], AF.Sqrt, bias=eps2[:], scale=1.0 / 64), reads=['D2_var', 'D2_eps'], writes=['D2_var'])
                p.op('dve', lambda e: e.reciprocal(var[:], var[:]), reads=['D2_var'], writes=['D2_var'])
                p.op('dve', lambda e: e.tensor_tensor(v3(y0), v3(y0), bc3(var), ALU.mult), reads=['D2_y0', 'D2_var'], writes=['D2_y0'])
                p.op('pool', lambda e: e.tensor_tensor(y0[:], y0[:], lnw[:], ALU.mult), reads=['D2_y0', 'D2_lnw'], writes=['D2_y0'])
                p.op('pool', lambda e: e.tensor_tensor(y0[:], y0[:], lnb[:], ALU.add), reads=['D2_y0', 'D2_lnb'], writes=['D2_y0'])
                p.op('dve', lambda e: e.tensor_tensor(b0t[:], b0t[:], b1t[:], ALU.add), reads=['D2_b0', 'D2_b1'], writes=['D2_b0'])
                p.op('dve', lambda e: e.tensor_tensor(v3(vt), v3(vt), bc3(b0t), ALU.mult), reads=['D2_v', 'D2_b0'], writes=['D2_v'])
                p.op('dve', lambda e: e.tensor_tensor(y0[:], y0[:], vt[:], ALU.add), reads=['D2_y0', 'D2_v'], writes=['D2_y0'])
                p.dma('sp', br[t0:t0 + 128, 2048:3072], y0[:], reads=['D2_y0'], writes=[('br', i, 2)])
        p.barrier()


    TWO_PI = float(2 * np.pi)

    def phase_C(l):
        with ExitStack() as st:
            TC = 512
            lr = sb(st, "C_lr", [128, 32]); li = sb(st, "C_li", [128, 32])
            for two in range(2):
                p.dma('sp', lr[two * 64:(two + 1) * 64, :], s5_lam_re[l, two::2, :].rearrange("q p -> p q"), writes=['C_lr'], allow_slow_non_contiguous=True)
                p.dma('sp', li[two * 64:(two + 1) * 64, :], s5_lam_im[l, two::2, :].rearrange("q p -> p q"), writes=['C_li'], allow_slow_non_contiguous=True)
            den = sb(st, "C_den", [128, 32]); t_a = sb(st, "C_ta", [128, 32]); t_b = sb(st, "C_tb", [128, 32]); t_c = sb(st, "C_tc", [128, 32])
            t_i = sb(st, "C_ti", [128, 32], I32)
            p.op('dve', lambda e: e.tensor_tensor(den[:], lr[:], lr[:], ALU.mult), reads=['C_lr'], writes=['C_den'])
            p.op('dve', lambda e: e.tensor_tensor(t_a[:], li[:], li[:], ALU.mult), reads=['C_li'], writes=['C_ta'])
            p.op('dve', lambda e: e.tensor_tensor(den[:], den[:], t_a[:], ALU.add), reads=['C_den', 'C_ta'], writes=['C_den'])
            p.op('dve', lambda e: e.reciprocal(den[:], den[:]), reads=['C_den'], writes=['C_den'])
            mag = [sb(st, f"C_mag{d}", [128, 32]) for d in range(2)]
            th = [sb(st, f"C_th{d}", [128, 32]) for d in range(2)]
            cre = [sb(st, f"C_cre{d}", [128, 32]) for d in range(2)]
            cim = [sb(st, f"C_cim{d}", [128, 32]) for d in range(2)]
            dtt = sb(st, "C_dt", [128, 32]); sn = sb(st, "C_sn", [128, 32]); cs = sb(st, "C_cs", [128, 32])

            def emit_sin(out, ang, n, key_out, key_ang, ti_, tf_, kti, ktf):
                p.op('dve', lambda e: e.tensor_scalar(ti_, ang, 1.0 / TWO_PI, None, ALU.mult), reads=[key_ang], writes=[kti])
                p.op('dve', lambda e: e.tensor_copy(tf_, ti_), reads=[kti], writes=[ktf])
                p.op('dve', lambda e: e.scalar_tensor_tensor(tf_, tf_, -TWO_PI, ang, ALU.mult, ALU.add), reads=[ktf, key_ang], writes=[ktf])
                p.op('dve', lambda e: e.tensor_scalar(tf_, tf_, float(np.pi), float(-np.pi), ALU.min, ALU.max), reads=[ktf], writes=[ktf])
                p.op('act', lambda e: e.activation(out, tf_, AF.Sin), reads=[ktf], writes=[key_out])

            for d in range(2):
                for two in range(2):
                    p.dma('sp', dtt[two * 64:(two + 1) * 64, :], s5_log_dt[l, d:d + 1, two::2].partition_broadcast(64), writes=['C_dt'],
                          allow_slow_non_contiguous=True)
                p.op('act', lambda e: e.activation(dtt[:], dtt[:], AF.Exp), reads=['C_dt'], writes=['C_dt'])
                p.op('dve', lambda e: e.tensor_tensor(t_a[:], lr[:], dtt[:], ALU.mult), reads=['C_lr', 'C_dt'], writes=['C_ta'])
                p.op('act', lambda e: e.activation(mag[d][:], t_a[:], AF.Exp), reads=['C_ta'], writes=[f'C_mag{d}'])
                p.op('dve', lambda e: e.tensor_tensor(th[d][:], li[:], dtt[:], ALU.mult), reads=['C_li', 'C_dt'], writes=[f'C_th{d}'])
                emit_sin(sn[:], th[d][:], 32, 'C_sn', f'C_th{d}', t_i[:], t_b[:], 'C_ti', 'C_tb')
                p.op('dve', lambda e: e.tensor_scalar(t_c[:], th[d][:], float(np.pi / 2), None, ALU.add), reads=[f'C_th{d}'], writes=['C_tc'])
                emit_sin(cs[:], t_c[:], 32, 'C_cs', 'C_tc', t_i[:], t_b[:], 'C_ti', 'C_tb')
                p.op('dve', lambda e: e.tensor_tensor(cs[:], cs[:], mag[d][:], ALU.mult), reads=['C_cs', f'C_mag{d}'], writes=['C_cs'])
                p.op('dve', lambda e: e.tensor_scalar(cs[:], cs[:], -1.0, None, ALU.add), reads=['C_cs'], writes=['C_cs'])
                p.op('dve', lambda e: e.tensor_tensor(sn[:], sn[:], mag[d][:], ALU.mult), reads=['C_sn', f'C_mag{d}'], writes=['C_sn'])
                p.op('dve', lambda e: e.tensor_tensor(t_a[:], cs[:], lr[:], ALU.mult), reads=['C_cs', 'C_lr'], writes=['C_ta'])
                p.op('dve', lambda e: e.tensor_tensor(t_b[:], sn[:], li[:], ALU.mult), reads=['C_sn', 'C_li'], writes=['C_tb'])
                p.op('dve', lambda e: e.tensor_tensor(t_a[:], t_a[:], t_b[:], ALU.add), reads=['C_ta', 'C_tb'], writes=['C_ta'])
                p.op('dve', lambda e: e.tensor_tensor(cre[d][:], t_a[:], den[:], ALU.mult), reads=['C_ta', 'C_den'], writes=[f'C_cre{d}'])
                p.op('dve', lambda e: e.tensor_tensor(t_a[:], sn[:], lr[:], ALU.mult), reads=['C_sn', 'C_lr'], writes=['C_ta'])
                p.op('dve', lambda e: e.tensor_tensor(t_b[:], cs[:], li[:], ALU.mult), reads=['C_cs', 'C_li'], writes=['C_tb'])
                p.op('dve', lambda e: e.tensor_tensor(t_a[:], t_a[:], t_b[:], ALU.subtract), reads=['C_ta', 'C_tb'], writes=['C_ta'])
                p.op('dve', lambda e: e.tensor_tensor(cim[d][:], t_a[:], den[:], ALU.mult), reads=['C_ta', 'C_den'], writes=[f'C_cim{d}'])
            WB = [[sb(st, f"C_WB{d}{ri}", [128, 16, 128]) for ri in range(2)] for d in range(2)]
            WC = [[sb(st, f"C_WC{d}{ri}", [128, 32, 64]) for ri in range(2)] for d in range(2)]
            pbs = [ps(st, f"C_pb{i}", [128, 512]) for i in range(8)]
            st2 = ExitStack()
            Bm = [sb(st2, f"C_Bm{ri}", [128, 32, 64]) for ri in range(2)]
            for ri, src in enumerate((s5_b_re, s5_b_im)):
                p.op('pool', lambda e: e.memset(Bm[ri][:], 0.0), writes=[f'C_Bm{ri}'])
                for two in range(2):
                    for qpar in range(2):
                        off = qpar * 32 + two * 16
                        p.dma('sp', Bm[ri][two * 64:(two + 1) * 64, qpar::2, off:off + 16],
                              src[l, (2 * qpar + two)::4, :, :].rearrange("m p c -> p m c"),
                              writes=[f'C_Bm{ri}'], allow_slow_non_contiguous=True)
            bbt = sb(st2, "C_bbt", [128, 32, 64]); bbt2 = sb(st2, "C_bbt2", [128, 32, 64])
            pbi = [0]

            def nb():
                i = pbi[0] % 8
                pbi[0] += 1
                return i
            b3 = lambda t: t[:].unsqueeze(2).broadcast_to([128, 32, 64])
            for d in range(2):
                for ri in range(2):
                    if ri == 0:
                        p.op('dve', lambda e: e.tensor_tensor(bbt[:], Bm[0][:], b3(cre[d]), ALU.mult), reads=['C_Bm0', f'C_cre{d}'], writes=['C_bbt'])
                        p.op('pool', lambda e: e.tensor_tensor(bbt2[:], Bm[1][:], b3(cim[d]), ALU.mult), reads=['C_Bm1', f'C_cim{d}'], writes=['C_bbt2'])
                        p.op('dve', lambda e: e.tensor_tensor(bbt[:], bbt[:], bbt2[:], ALU.subtract), reads=['C_bbt', 'C_bbt2'], writes=['C_bbt'])
                    else:
                        p.op('dve', lambda e: e.tensor_tensor(bbt[:], Bm[1][:], b3(cre[d]), ALU.mult), reads=['C_Bm1', f'C_cre{d}'], writes=['C_bbt'])
                        p.op('pool', lambda e: e.tensor_tensor(bbt2[:], Bm[0][:], b3(cim[d]), ALU.mult), reads=['C_Bm0', f'C_cim{d}'], writes=['C_bbt2'])
                        p.op('dve', lambda e: e.tensor_tensor(bbt[:], bbt[:], bbt2[:], ALU.add), reads=['C_bbt', 'C_bbt2'], writes=['C_bbt'])
                    for q in range(32):
                        bt = nb()
                        hb = (q % 4) // 2
                        qi_ = (q // 4) * 2 + q % 2
                        p.op('pe', lambda e: e.matmul(pbs[bt][hb * 64:(hb + 1) * 64, 0:128], bbt[:, q, :], ident_f[:], start=True, stop=True),
                             reads=['C_bbt', 'ident_f'], writes=[('C_pb', bt)])
                        p.op('act', lambda e: e.copy(WB[d][ri][hb * 64:(hb + 1) * 64, qi_, :], pbs[bt][hb * 64:(hb + 1) * 64, 0:128]),
                             reads=[('C_pb', bt)], writes=[f'C_WB{d}{ri}'])
            Cn = sb(st2, "C_Cn", [64, 32, 128])
            for d in range(2):
                for ri, src in enumerate((s5_c_re, s5_c_im)):
                    p.op('pool', lambda e: e.memset(Cn[:], 0.0), writes=['C_Cn'])
                    for two in range(2):
                        for qpar in range(2):
                            off = qpar * 32 + two * 16
                            p.dma('sp', Cn[off:off + 16, qpar::2, two * 64:(two + 1) * 64],
                                  src[l, d, (2 * qpar + two)::4, :, :].rearrange("m c p -> c m p"),
                                  writes=['C_Cn'], allow_slow_non_contiguous=True)
                    for q4 in range(16):
                        bt = nb()
                        for qq in range(2):
                            q = q4 * 2 + qq
                            p.op('pe', lambda e: e.transpose(pbs[bt][:, qq * 64:(qq + 1) * 64], Cn[:, q, :], ident_f[0:64, 0:64]),
                                 reads=['C_Cn', 'ident_f'], writes=[('C_pb', bt)])
                        dst = WC[d][ri][:, q4 * 2:(q4 + 1) * 2, :].rearrange("p a b -> p (a b)")
                        if ri == 0:
                            p.op('act', lambda e: e.copy(dst, pbs[bt][:, 0:128]), reads=[('C_pb', bt)], writes=[f'C_WC{d}{ri}'])
                        else:
                            p.op('act', lambda e: e.mul(dst, pbs[bt][:, 0:128], -1.0), reads=[('C_pb', bt)], writes=[f'C_WC{d}{ri}'])
            p.barrier()
            st2.close()
            ut = sb(st, "C_ut", [128, 128]); uT = sb(st, "C_uT", [128, S]); yacc = sb(st, "C_yacc", [128, S])
            iota1 = sb(st, "C_iota", [128, TC])
            p.dma('sp', iota1[:], c_iota[0:1, 0:TC].partition_broadcast(128), writes=['C_iota'])
            tfi = sb(st, "C_tfi", [128, TC], I32)
            mk4 = lambda nm, shp: [sb(st, f"{nm}{i}", shp) for i in range(4)]
            cosT = mk4("C_cosT", [128, TC]); sinT = mk4("C_sinT", [128, TC]); rtab = mk4("C_rtab", [128, TC])
            gre = mk4("C_gre", [128, TC]); gim = mk4("C_gim", [128, TC]); w1 = mk4("C_w1", [128, TC]); w2_ = mk4("C_w2", [128, TC])
            w3 = mk4("C_w3", [128, TC]); w4 = mk4("C_w4", [128, TC])
            hre = mk4("C_hre", [128, TC]); him = mk4("C_him", [128, TC]); carry = mk4("C_carry", [128, 2])
            ang = hre[0]; ang2 = hre[1]; tff = hre[2]; y3 = him[0]; y4 = him[1]
            dsk = sb(st, "C_dsk", [128, 8])
            p.dma('sp', dsk[:], s5_d[l, :].rearrange("(b c) -> c b", c=128), writes=['C_dsk'], allow_slow_non_contiguous=True)
            yo = sb(st, "C_yo", [128, 128])
            for cb in range(8):
                for i in range(NT):
                    p.dma('sp', ut[:], proj[i * 128:(i + 1) * 128, C_AU + cb * 128:C_AU + (cb + 1) * 128], reads=[('proj', i, 'all')], writes=['C_ut'])
                    bt = nb()
                    p.op('pe', lambda e: e.transpose(pbs[bt][:, 0:128], ut[:], ident_f[:]), reads=['C_ut', 'ident_f'], writes=[('C_pb', bt)])
                    p.op('act', lambda e: e.copy(uT[:, i * 128:(i + 1) * 128], pbs[bt][:, 0:128]), reads=[('C_pb', bt)], writes=[('C_uT', i // 4)])
                for d in range(2):
                    for qq in range(4):
                        q = cb * 4 + qq
                        p.op('dve', lambda e: e.tensor_scalar(ang[:], iota1[:], th[d][:, q:q + 1], None, ALU.mult), reads=['C_iota', f'C_th{d}'], writes=[('C_hre', 0)])
                        emit_sin(sinT[qq][:], ang[:], TC, ('C_sinT', qq), ('C_hre', 0), tfi[:], tff[:], 'C_tfi', ('C_hre', 2))
                        p.op('dve', lambda e: e.tensor_scalar(ang2[:], ang[:], float(np.pi / 2), None, ALU.add), reads=[('C_hre', 0)], writes=[('C_hre', 1)])
                        emit_sin(cosT[qq][:], ang2[:], TC, ('C_cosT', qq), ('C_hre', 1), tfi[:], tff[:], 'C_tfi', ('C_hre', 2))
                        p.op('act', lambda e: e.mul(rtab[qq][:], iota1[:], 0.0), reads=['C_iota'], writes=[('C_rtab', qq)])
                        p.op('dve', lambda e: e.tensor_scalar(rtab[qq][:], rtab[qq][:], mag[d][:, q:q + 1], None, ALU.add), reads=[('C_rtab', qq), f'C_mag{d}'], writes=[('C_rtab', qq)])
                        p.op('dve', lambda e: e.memset(carry[qq][:], 0.0), writes=[('C_carry', qq)])
                    chunks = range(S // TC) if d == 0 else range(S // TC - 1, -1, -1)
                    for ch in chunks:
                        tsl = slice(ch * TC, (ch + 1) * TC)
                        QS = range(4)
                        bre = {}; bim = {}; byq = {}
                        for qq in QS:
                            q = cb * 4 + qq
                            ps32 = slice((qq // 2) * 64, (qq // 2) * 64 + 64)
                            bre[qq], bim[qq] = nb(), nb()
                            p.op('pe', lambda e: e.matmul(pbs[bre[qq]][:, :], WB[d][0][ps32, (q // 4) * 2 + q % 2, :], uT[ps32, tsl], start=True, stop=True),
                                 reads=[f'C_WB{d}0', ('C_uT', ch)], writes=[('C_pb', bre[qq])])
                            p.op('pe', lambda e: e.matmul(pbs[bim[qq]][:, :], WB[d][1][ps32, (q // 4) * 2 + q % 2, :], uT[ps32, tsl], start=True, stop=True),
                                 reads=[f'C_WB{d}1', ('C_uT', ch)], writes=[('C_pb', bim[qq])])
                        Bre = lambda qq: pbs[bre[qq]][:, :] if d == 0 else pbs[bre[qq]][:, ::-1]
                        Bim = lambda qq: pbs[bim[qq]][:, :] if d == 0 else pbs[bim[qq]][:, ::-1]
                        K = lambda n, qq: (n, qq)
                        for qq in QS:
                            p.op('dve', lambda e: e.tensor_tensor(w1[qq][:], Bre(qq), cosT[qq][:], ALU.mult), reads=[('C_pb', bre[qq]), K('C_cosT', qq)], writes=[K('C_w1', qq)])
                            p.op('dve', lambda e: e.tensor_tensor(w2_[qq][:], Bim(qq), sinT[qq][:], ALU.mult), reads=[('C_pb', bim[qq]), K('C_sinT', qq)], writes=[K('C_w2', qq)])
                            p.op('dve', lambda e: e.tensor_tensor(w3[qq][:], Bim(qq), cosT[qq][:], ALU.mult), reads=[('C_pb', bim[qq]), K('C_cosT', qq)], writes=[K('C_w3', qq)])
                            p.op('dve', lambda e: e.tensor_tensor(w4[qq][:], Bre(qq), sinT[qq][:], ALU.mult), reads=[('C_pb', bre[qq]), K('C_sinT', qq)], writes=[K('C_w4', qq)])
                        for qq in QS:
                            p.op('dve', lambda e: e.tensor_tensor(w1[qq][:], w1[qq][:], w2_[qq][:], ALU.add), reads=[K('C_w1', qq), K('C_w2', qq)], writes=[K('C_w1', qq)])
                            p.op('dve', lambda e: e.tensor_tensor(w3[qq][:], w3[qq][:], w4[qq][:], ALU.subtract), reads=[K('C_w3', qq), K('C_w4', qq)], writes=[K('C_w3', qq)])
                        for qq in QS:
                            p.op('dve', lambda e: e.tensor_tensor_scan(gre[qq][:], rtab[qq][:], w1[qq][:], carry[qq][:, 0:1], ALU.mult, ALU.add),
                                 reads=[K('C_rtab', qq), K('C_w1', qq), K('C_carry', qq)], writes=[K('C_gre', qq)])
                        for qq in QS:
                            p.op('dve', lambda e: e.tensor_tensor_scan(gim[qq][:], rtab[qq][:], w3[qq][:], carry[qq][:, 1:2], ALU.mult, ALU.add),
                                 reads=[K('C_rtab', qq), K('C_w3', qq), K('C_carry', qq)], writes=[K('C_gim', qq)])
                            p.op('dve', lambda e: e.tensor_tensor(w1[qq][:], gre[qq][:], cosT[qq][:], ALU.mult), reads=[K('C_gre', qq), K('C_cosT', qq)], writes=[K('C_w1', qq)])
                            p.op('dve', lambda e: e.tensor_tensor(w4[qq][:], gre[qq][:], sinT[qq][:], ALU.mult), reads=[K('C_gre', qq), K('C_sinT', qq)], writes=[K('C_w4', qq)])
                        Hre = lambda qq: hre[qq][:] if d == 0 else hre[qq][:, ::-1]
                        Him = lambda qq: him[qq][:] if d == 0 else him[qq][:, ::-1]
                        for qq in QS:
                            p.op('dve', lambda e: e.tensor_tensor(w2_[qq][:], gim[qq][:], sinT[qq][:], ALU.mult), reads=[K('C_gim', qq), K('C_sinT', qq)], writes=[K('C_w2', qq)])
                            p.op('dve', lambda e: e.tensor_tensor(w3[qq][:], gim[qq][:], cosT[qq][:], ALU.mult), reads=[K('C_gim', qq), K('C_cosT', qq)], writes=[K('C_w3', qq)])
                        last = TC - 1 if d == 0 else 0
                        for qq in QS:
                            p.op('dve', lambda e: e.tensor_tensor(Hre(qq), w1[qq][:], w2_[qq][:], ALU.subtract), reads=[K('C_w1', qq), K('C_w2', qq)], writes=[K('C_hre', qq)])
                            p.op('dve', lambda e: e.tensor_tensor(Him(qq), w4[qq][:], w3[qq][:], ALU.add), reads=[K('C_w4', qq), K('C_w3', qq)], writes=[K('C_him', qq)])
                        for qq in QS:
                            q = cb * 4 + qq
                            ps32 = slice((qq // 2) * 64, (qq // 2) * 64 + 64)
                            p.op('act', lambda e: e.copy(carry[qq][:, 0:1], hre[qq][:, last:last + 1]), reads=[K('C_hre', qq)], writes=[K('C_carry', qq)])
                            p.op('act', lambda e: e.copy(carry[qq][:, 1:2], him[qq][:, last:last + 1]), reads=[K('C_him', qq)], writes=[K('C_carry', qq)])
                            by = nb()
                            byq[qq] = by
                            p.op('pe', lambda e: e.matmul(pbs[by][ps32, :], WC[d][0][:, q, :], hre[qq][:], start=True, stop=False),
                                 reads=[f'C_WC{d}0', K('C_hre', qq)], writes=[('C_pb', by)])
                            p.op('pe', lambda e: e.matmul(pbs[by][ps32, :], WC[d][1][:, q, :], him[qq][:], start=False, stop=True),
                                 reads=[f'C_WC{d}1', K('C_him', qq)], writes=[('C_pb', by)])
                        for qq in QS:
                            ps32 = slice((qq // 2) * 64, (qq // 2) * 64 + 64)
                            by = byq[qq]
                            if d == 0 and qq % 2 == 0:
                                p.op('act', lambda e: e.copy(yacc[ps32, tsl], pbs[by][ps32, :]), reads=[('C_pb', by)], writes=[('C_yacc', qq // 2, ch)])
                            else:
                                p.op('dve', lambda e: e.tensor_tensor(yacc[ps32, tsl], yacc[ps32, tsl], pbs[by][ps32, :], ALU.add),
                                     reads=[('C_pb', by), ('C_yacc', qq // 2, ch)], writes=[('C_yacc', qq // 2, ch)])
                for ch in range(S // TC):
                    tsl = slice(ch * TC, (ch + 1) * TC)
                    rk = [('C_yacc', qq, ch) for qq in range(2)]
                    p.op('dve', lambda e: e.scalar_tensor_tensor(y3[:], uT[:, tsl], dsk[:, cb:cb + 1], yacc[:, tsl], ALU.mult, ALU.add),
                         reads=rk + [('C_uT', ch), 'C_dsk'], writes=[('C_him', 0)])
                    p.op('pool', lambda e: e.tensor_tensor(y4[:], y3[:], y3[:], ALU.mult), reads=[('C_him', 0)], writes=[('C_him', 1)])
                    p.op('dve', lambda e: e.tensor_scalar(y4[:], y4[:], 0.044715, 1.0, ALU.mult, ALU.add), reads=[('C_him', 1)], writes=[('C_him', 1)])
                    p.op('dve', lambda e: e.tensor_tensor(y4[:], y4[:], y3[:], ALU.mult), reads=[('C_him', 1), ('C_him', 0)], writes=[('C_him', 1)])
                    p.op('act', lambda e: e.activation(y4[:], y4[:], AF.Sigmoid, scale=1.5957691216057308), reads=[('C_him', 1)], writes=[('C_him', 1)])
                    p.op('dve', lambda e: e.tensor_tensor(y3[:], y3[:], y4[:], ALU.mult), reads=[('C_him', 1), ('C_him', 0)], writes=[('C_him', 0)])
                    for i4_ in range(TC // 128):
                        i = ch * (TC // 128) + i4_
                        bt = nb()
                        p.op('pe', lambda e: e.transpose(pbs[bt][:, 0:128], y3[:, i4_ * 128:(i4_ + 1) * 128], ident_f[:]), reads=[('C_him', 0), 'ident_f'], writes=[('C_pb', bt)])
                        p.op('act', lambda e: e.copy(yo[:], pbs[bt][:, 0:128]), reads=[('C_pb', bt)], writes=['C_yo'])
                        p.dma('sp', ygd[i * 128:(i + 1) * 128, cb * 128:(cb + 1) * 128], yo[:], reads=['C_yo'], writes=[('ygd', i, cb)])
        p.barrier()
        with ExitStack() as st:
            gw = sb(st, "C2_gw", [128, 8, 1024], BF16)
            p.dma('pool', gw[:], s5_glu_w[l, :, :].rearrange("(k p) n -> p k n", p=128), writes=['C2_gw'])
            gb = sb(st, "C2_gb", [128, 1024])
            p.dma('sp', gb[:], s5_glu_b[l:l + 1, :].partition_broadcast(128), writes=['C2_gb'])
            yg = sb(st, "C2_yg", [128, 1024]); ygb = sb(st, "C2_ygb", [128, 1024], BF16); ygT = sb(st, "C2_ygT", [128, 8, 128], BF16)
            sg = sb(st, "C2_sg", [128, 1024])
            ptr = ps(st, "C2_pt", [128, 8, 128], BF16)
            pm = [ps(st, f"C2_pm{i}", [128, 512]) for i in range(2)]
            for i in range(NT):
                p.dma('sp', yg[:], ygd[i * 128:(i + 1) * 128, :], reads=[('ygd', i, cb) for cb in range(8)], writes=['C2_yg'])
                p.op('act', lambda e: e.copy(ygb[:], yg[:]), reads=['C2_yg'], writes=['C2_ygb'])
                for k in range(8):
                    p.op('pe', lambda e: e.transpose(ptr[:, k, :], ygb[:, k * 128:(k + 1) * 128], ident_b[:]), reads=['C2_ygb', 'ident_b'], writes=['C2_pt'])
                p.op('dve', lambda e: e.tensor_copy(ygT[:], ptr[:]), reads=['C2_pt'], writes=['C2_ygT'])
                for half in range(2):
                    cs_ = slice(half * 512, (half + 1) * 512)
                    for k in range(8):
                        p.op('pe', lambda e: e.matmul(pm[half][:, :], ygT[:, k, :], gw[:, k, cs_], start=(k == 0), stop=(k == 7)),
                             reads=['C2_ygT', 'C2_gw'], writes=[('C2_pm', half)])
                    p.op('dve', lambda e: e.tensor_tensor(sg[:, cs_], pm[half][:, :], gb[:, cs_], ALU.add), reads=[('C2_pm', half), 'C2_gb'], writes=['C2_sg'])
                p.op('act', lambda e: e.activation(sg[:], sg[:], AF.Sigmoid), reads=['C2_sg'], writes=['C2_sg'])
                p.op('dve', lambda e: e.tensor_tensor(sg[:], sg[:], yg[:], ALU.mult), reads=['C2_sg', 'C2_yg'], writes=['C2_sg'])
                p.dma('sp', br[i * 128:(i + 1) * 128, 0:1024], sg[:], reads=['C2_sg'], writes=[('br', i, 0)])
        p.barrier()


    def prologue_rope(ropec, ropes):
        with ExitStack() as st:
            pi_ = sb(st, "R_pi", [128, NT], I32); pf = sb(st, "R_pf", [128, NT]); ivf = sb(st, "R_ivf", [128, 32])
            ang = sb(st, "R_ang", [128, NT, 32]); ti_ = sb(st, "R_ti", [128, NT, 32], I32); tf_ = sb(st, "R_tf", [128, NT, 32])
            p.dma('sp', pi_[:], pos_in[0, :].rearrange("(i p) -> p i", p=128), writes=['R_pi'], allow_slow_non_contiguous=True)
            p.dma('sp', ivf[:], c_invfreq[0:1, :].partition_broadcast(128), writes=['R_ivf'])
            p.op('dve', lambda e: e.tensor_copy(pf[:], pi_[:]), reads=['R_pi'], writes=['R_pf'])
            p.op('dve', lambda e: e.tensor_tensor(ang[:], pf[:].unsqueeze(2).broadcast_to([128, NT, 32]),
                                                 ivf[:].unsqueeze(1).broadcast_to([128, NT, 32]), ALU.mult), reads=['R_pf', 'R_ivf'], writes=['R_ang'])
            for which, dst, key in ((0, ropes, 'rope_s'), (1, ropec, 'rope_c')):
                if which == 1:
                    p.op('dve', lambda e: e.tensor_scalar(ang[:], ang[:], float(np.pi / 2), None, ALU.add), reads=['R_ang'], writes=['R_ang'])
                p.op('dve', lambda e: e.tensor_scalar(ti_[:], ang[:], 1.0 / TWO_PI, None, ALU.mult), reads=['R_ang'], writes=['R_ti'])
                p.op('dve', lambda e: e.tensor_copy(tf_[:], ti_[:]), reads=['R_ti'], writes=['R_tf'])
                p.op('dve', lambda e: e.scalar_tensor_tensor(tf_[:], tf_[:], -TWO_PI, ang[:], ALU.mult, ALU.add), reads=['R_tf', 'R_ang'], writes=['R_tf'])
                p.op('dve', lambda e: e.tensor_scalar(tf_[:], tf_[:], float(np.pi), float(-np.pi), ALU.min, ALU.max), reads=['R_tf'], writes=['R_tf'])
                p.op('act', lambda e: e.activation(dst[:], tf_[:], AF.Sin), reads=['R_tf'], writes=[key])
        p.barrier()

    def phase_B(l):
        with ExitStack() as st:
            ropec = sb(st, "rope_c", [128, NT, 32]); ropes = sb(st, "rope_s", [128, NT, 32])
            prologue_rope(ropec, ropes)
            wuq = sb(st, "B_wuq", [128, 7, 1536], BF16); wukv = sb(st, "B_wukv", [128, 2, 2048], BF16)
            p.dma('pool', wuq[:], mla_w_uq[l, :, :].rearrange("(k p) n -> p k n", p=128), writes=['B_wuq'])
            p.dma('pool', wukv[:], mla_w_ukv[l, :, :].rearrange("(k p) n -> p k n", p=128), writes=['B_wukv'])
            gqa = sb(st, "B_gqa", [128, 896]); gkva = sb(st, "B_gkva", [128, 256]); gq = sb(st, "B_gq", [128, 192]); gk = sb(st, "B_gk", [128, 192])
            p.dma('sp', gqa[:], mla_q_a_norm[l:l + 1, :].partition_broadcast(128), writes=['B_gqa'])
            p.dma('sp', gkva[:], mla_kv_a_norm[l:l + 1, :].partition_broadcast(128), writes=['B_gkva'])
            p.dma('sp', gq[:], mla_q_norm[l:l + 1, :].partition_broadcast(128), writes=['B_gq'])
            p.dma('sp', gk[:], mla_k_norm[l:l + 1, :].partition_broadcast(128), writes=['B_gk'])
            lat = sb(st, "B_lat", [128, 1216]); latb = sb(st, "B_latb", [128, 1152], BF16); latT = sb(st, "B_latT", [128, 9, 128], BF16)
            ss = sb(st, "B_ss", [128, 2]); junk = sb(st, "B_junk", [128, 896], BF16)
            qk = [sb(st, f"B_qk{i}", [128, 8, 192]) for i in range(2)]
            sq = sb(st, "B_sq", [128, 8, 192]); hs = sb(st, "B_hs", [128, 8])
            r1 = sb(st, "B_r1", [128, 8, 32]); r2 = sb(st, "B_r2", [128, 8, 32]); r3 = sb(st, "B_r3", [128, 8, 32])
            qkb = sb(st, "B_qkb", [128, 8, 192], BF16); vb = sb(st, "B_vb", [128, 1024], BF16)
            tT = sb(st, "B_tT", [128, 16, 128], BF16)
            ptr = [ps(st, f"B_pt{i}", [128, 8, 128], BF16) for i in range(2)]
            pm = [ps(st, f"B_pm{i}", [128, 512]) for i in range(4)]
            pmi = [0]
            import os
            for i in range(int(os.environ.get("KNTB", NT))):
                t0 = i * 128
                p.dma('sp', lat[:], proj[t0:t0 + 128, C_CQ:C_CQ + 1216], reads=[('proj', i, 'all')], writes=['B_lat'])
                p.op('act', lambda e: e.activation(junk[:], lat[:, 0:896], AF.Square, accum_out=ss[:, 0:1]), reads=['B_lat'], writes=['B_junk', 'B_ss0'])
                p.op('act', lambda e: e.activation(junk[:, 0:256], lat[:, 896:1152], AF.Square, accum_out=ss[:, 1:2]), reads=['B_lat'], writes=['B_junk', 'B_ss1'])
                p.op('act', lambda e: e.activation(ss[:, 0:1], ss[:, 0:1], AF.Sqrt, bias=eps_t[:], scale=1.0 / 896), reads=['B_ss0', 'eps_t'], writes=['B_ss0'])
                p.op('act', lambda e: e.activation(ss[:, 1:2], ss[:, 1:2], AF.Sqrt, bias=eps_t[:], scale=1.0 / 256), reads=['B_ss1', 'eps_t'], writes=['B_ss1'])
                p.op('dve', lambda e: e.reciprocal(ss[:], ss[:]), reads=['B_ss0', 'B_ss1'], writes=['B_ss0', 'B_ss1'])
                p.op('dve', lambda e: e.scalar_tensor_tensor(latb[:, 0:896], lat[:, 0:896], ss[:, 0:1], gqa[:], ALU.mult, ALU.mult),
                     reads=['B_lat', 'B_ss0', 'B_gqa'], writes=['B_latb'])
                p.op('dve', lambda e: e.scalar_tensor_tensor(latb[:, 896:1152], lat[:, 896:1152], ss[:, 1:2], gkva[:], ALU.mult, ALU.mult),
                     reads=['B_lat', 'B_ss1', 'B_gkva'], writes=['B_latb'])
                BSTOP = int(os.environ.get("BSTOP", 9))
                if BSTOP <= 1:
                    continue
                for k in range(9):
                    pt = ptr[0] if k < 8 else ptr[1]
                    p.op('pe', lambda e: e.transpose(pt[:, k % 8, :], latb[:, k * 128:(k + 1) * 128], ident_b[:]), reads=['B_latb', 'ident_b'],
                         writes=[('B_pt', 0 if k < 8 else 1)])
                p.op('act', lambda e: e.copy(latT[:, 0:8, :], ptr[0][:]), reads=[('B_pt', 0)], writes=['B_latT'])
                p.op('dve', lambda e: e.tensor_copy(latT[:, 8, :], ptr[1][:, 0, :]), reads=[('B_pt', 1)], writes=['B_latT'])
                for c3 in range(3):
                    j = pmi[0] % 4
                    pmi[0] += 1
                    for k in range(7):
                        p.op('pe', lambda e: e.matmul(pm[j][:, :], latT[:, k, :], wuq[:, k, c3 * 512:(c3 + 1) * 512], start=(k == 0), stop=(k == 6)),
                             reads=['B_latT', 'B_wuq'], writes=[('B_pm', j)])
                    p.op('act', lambda e: e.copy(qk[0][:].rearrange("p h d -> p (h d)")[:, c3 * 512:(c3 + 1) * 512], pm[j][:, :]),
                         reads=[('B_pm', j)], writes=['B_qk0'])
                if BSTOP <= 2:
                    continue
                for c4 in range(4):
                    j = pmi[0] % 4
                    pmi[0] += 1
                    for k in range(2):
                        p.op('pe', lambda e: e.matmul(pm[j][:, :], latT[:, 7 + k, :], wukv[:, k, c4 * 512:(c4 + 1) * 512], start=(k == 0), stop=(k == 1)),
                             reads=['B_latT', 'B_wukv'], writes=[('B_pm', j)])
                    pv = pm[j][:, :].rearrange("p (h d) -> p h d", h=2)
                    BSKIP = os.environ.get("BSKIP", "")
                    if 'a' not in BSKIP:
                        p.op('act', lambda e: e.copy(qk[1][:, c4 * 2:(c4 + 1) * 2, 0:128], pv[:, :, 0:128]), reads=[('B_pm', j)], writes=['B_qk1'])
                    for hh in range(2):
                        hcol = (c4 * 2 + hh) * 128
                        p.op('act', lambda e: e.copy(vb[:, hcol:hcol + 128], pm[j][:, hh * 256 + 128:hh * 256 + 256]),
                             reads=[('B_pm', j)], writes=['B_vb'])
                if 'p' not in BSKIP:
                    p.op('pool', lambda e: e.tensor_copy(qk[1][:, :, 128:192], lat[:, 1152:1216].unsqueeze(1).broadcast_to([128, 8, 64])),
                         reads=['B_lat'], writes=['B_qk1'])
                if 'v' not in BSKIP:
                    p.dma('sp', v_d[t0:t0 + 128, :], vb[:], reads=['B_vb'], writes=[('v_d', i)])
                if BSTOP <= 3:
                    continue
                for which in range(2):
                    X = qk[which]
                    xk = f'B_qk{which}'
                    g = gq if which == 0 else gk
                    gk_ = 'B_gq' if which == 0 else 'B_gk'
                    p.op('pool', lambda e: e.tensor_tensor(sq[:], X[:], X[:], ALU.mult), reads=[xk], writes=['B_sq'])
                    p.op('dve', lambda e: e.tensor_reduce(hs[:], sq[:], AX.X, ALU.add), reads=['B_sq'], writes=['B_hs'])
                    p.op('act', lambda e: e.activation(hs[:], hs[:], AF.Sqrt, bias=eps_t[:], scale=1.0 / 192), reads=['B_hs', 'eps_t'], writes=['B_hs'])
                    p.op('dve', lambda e: e.reciprocal(hs[:], hs[:]), reads=['B_hs'], writes=['B_hs'])
                    p.op('dve', lambda e: e.tensor_tensor(X[:], X[:], hs[:].unsqueeze(2).broadcast_to([128, 8, 192]), ALU.mult), reads=[xk, 'B_hs'], writes=[xk])
                    p.op('pool', lambda e: e.tensor_tensor(X[:], X[:], g[:].unsqueeze(1).broadcast_to([128, 8, 192]), ALU.mult), reads=[xk, gk_], writes=[xk])
                    cb_ = ropec[:, i, :].unsqueeze(1).broadcast_to([128, 8, 32]); sb_ = ropes[:, i, :].unsqueeze(1).broadcast_to([128, 8, 32])
                    T1 = X[:, :, 128:160]; T2 = X[:, :, 160:192]
                    p.op('dve', lambda e: e.tensor_tensor(r1[:], T1, sb_, ALU.mult), reads=[xk, 'rope_s'], writes=['B_r1'])
                    p.op('dve', lambda e: e.tensor_tensor(r2[:], T2, sb_, ALU.mult), reads=[xk, 'rope_s'], writes=['B_r2'])
                    p.op('dve', lambda e: e.tensor_tensor(r3[:], T1, cb_, ALU.mult), reads=[xk, 'rope_c'], writes=['B_r3'])
                    p.op('dve', lambda e: e.tensor_tensor(T1, r3[:], r2[:], ALU.subtract), reads=['B_r3', 'B_r2'], writes=[xk])
                    p.op('dve', lambda e: e.tensor_tensor(r3[:], T2, cb_, ALU.mult), reads=[xk, 'rope_c'], writes=['B_r3'])
                    p.op('dve', lambda e: e.tensor_tensor(T2, r3[:], r1[:], ALU.add), reads=['B_r3', 'B_r1'], writes=[xk])
                    p.op('act', lambda e: e.copy(qkb[:], X[:]), reads=[xk], writes=['B_qkb'])
                    if BSTOP <= 4:
                        continue
                    for h in range(8):
                        pt = ptr[h % 2]
                        p.op('pe', lambda e: e.transpose(pt[:, 0, :], qkb[:, h, 0:128], ident_b[:]), reads=['B_qkb', 'ident_b'], writes=[('B_pt', h % 2)])
                        p.op('pe', lambda e: e.transpose(pt[0:64, 1, :], qkb[:, h, 128:192], ident_b[:]), reads=['B_qkb', 'ident_b'], writes=[('B_pt', h % 2)])
                        p.op('act', lambda e: e.copy(tT[:, 2 * h, :], pt[:, 0, :]), reads=[('B_pt', h % 2)], writes=[('B_tT', h)])
                        p.op('dve', lambda e: e.tensor_copy(tT[0:64, 2 * h + 1, :], pt[0:64, 1, :]), reads=[('B_pt', h % 2)], writes=[('B_tT', h)])
                        dstT = qT_d if which == 0 else kT_d
                        p.dma('sp', dstT[h, 0:128, t0:t0 + 128], tT[:, 2 * h, :], reads=[('B_tT', h)], writes=[('qkT', which, h, i)])
                        p.dma('sp', dstT[h, 128:192, t0:t0 + 128], tT[0:64, 2 * h + 1, :], reads=[('B_tT', h)], writes=[('qkT', which, h, i)])
        p.barrier()
        if 'b' in phases:
            return
        with ExitStack() as st:
            qT = sb(st, "B2_qT", [128, S], BF16); qTr = sb(st, "B2_qTr", [64, S], BF16)
            kT = sb(st, "B2_kT", [128, S], BF16); kTr = sb(st, "B2_kTr", [64, S], BF16)
            Va = sb(st, "B2_Va", [128, NT, 132], BF16)
            PT = [sb(st, f"B2_PT{i}", [128, 512], BF16) for i in range(2)]
            ob = sb(st, "B2_ob", [128, 128]); rs = sb(st, "B2_rs", [128, 1])
            psc = [ps(st, f"B2_ps{i}", [128, 512]) for i in range(2)]
            pac = [ps(st, f"B2_pa{i}", [128, 512]) for i in range(4)]
            p.op('dve', lambda e: e.memset(Va[:], 1.0), writes=['B2_Va'])
            SCALE = float(192 ** -0.5)
            it = 0
            for h in range(8):
                p.dma('sp', qT[:], qT_d[h, 0:128, :], writes=['B2_qT'])
                p.dma('sp', qTr[:], qT_d[h, 128:192, :], writes=['B2_qTr'])
                p.dma('sp', kT[:], kT_d[h, 0:128, :], writes=['B2_kT'])
                p.dma('sp', kTr[:], kT_d[h, 128:192, :], writes=['B2_kTr'])
                p.dma('sp', Va[:, :, 0:128], v_d[:, h * 128:(h + 1) * 128].rearrange("(i p) d -> p i d", p=128), writes=['B2_Va'])
                for qb in range(S // 512):
                    qs = slice(qb * 512, (qb + 1) * 512)
                    for kt in range(NT):
                        ks = slice(kt * 128, (kt + 1) * 128)
                        j = it % 2
                        it += 1
                        p.op('pe', lambda e: e.matmul(psc[j][:, :], kT[:, ks], qT[:, qs], start=True, stop=False), reads=['B2_kT', 'B2_qT'], writes=[('B2_ps', j)])
                        p.op('pe', lambda e: e.matmul(psc[j][:, :], kTr[:, ks], qTr[:, qs], start=False, stop=True), reads=['B2_kTr', 'B2_qTr'], writes=[('B2_ps', j)])
                        p.op('act', lambda e: e.activation(PT[j][:], psc[j][:, :], AF.Exp, scale=SCALE), reads=[('B2_ps', j)], writes=[('B2_PT', j)])
                        for sub in range(4):
                            p.op('pe', lambda e: e.matmul(pac[sub][:, 0:129], PT[j][:, sub * 128:(sub + 1) * 128], Va[:, kt, 0:129],
                                                         start=(kt == 0), stop=(kt == NT - 1)), reads=[('B2_PT', j), 'B2_Va'], writes=[('B2_pa', sub)])
                    for sub in range(4):
                        t0 = qb * 512 + sub * 128
                        p.op('dve', lambda e: e.reciprocal(rs[:], pac[sub][:, 128:129]), reads=[('B2_pa', sub)], writes=['B2_rs'])
                        p.op('dve', lambda e: e.tensor_scalar(ob[:], pac[sub][:, 0:128], rs[:], None, ALU.mult), reads=[('B2_pa', sub), 'B2_rs'], writes=['B2_ob'])
                        p.dma('sp', br[t0:t0 + 128, 1024 + h * 128:1024 + (h + 1) * 128], ob[:], reads=['B2_ob'], writes=[('br', t0 // 128, 1, h)])
        p.barrier()

    def phase_M(l):
        with ExitStack() as st:
            wk = sb(st, "M_wk", [128, 32, 1024], BF16)
            gm = sb(st, "M_gm", [128, D]); mt_ = sb(st, "M_mt", [128, D]); mb = sb(st, "M_mb", [128, D], BF16)
            memT = sb(st, "M_memT", [128, 32, 256], BF16)
            ss = sb(st, "M_ss", [128, 1]); hs = sb(st, "M_hs", [128, 4]); gqn = sb(st, "M_gqn", [128, 256]); gkn = sb(st, "M_gkn", [128, 256])
            Kt = sb(st, "M_K", [128, 1024]); sq = sb(st, "M_sq", [128, 1024]); Kb = sb(st, "M_Kb", [128, 1024], BF16)
            KmT = sb(st, "M_KmT", [128, 8, 256], BF16)
            Vm = sb(st, "M_Vm", [128, 2, 4, 260], BF16)
            ptr = [ps(st, f"M_pt{i}", [128, 8, 128], BF16) for i in range(2)]
            pm = [ps(st, f"M_pm{i}", [128, 512]) for i in range(2)]
            psc = [ps(st, f"M_ps{i}", [128, 512]) for i in range(2)]
            pac = [ps(st, f"M_pa{i}", [128, 512]) for i in range(2)]
            p.dma('sp', gm[:], mem_norm_g[l:l + 1, :].partition_broadcast(128), writes=['M_gm'])
            p.dma('sp', gqn[:], mem_q_norm[l:l + 1, :].partition_broadcast(128), writes=['M_gqn'])
            p.dma('sp', gkn[:], mem_k_norm[l:l + 1, :].partition_broadcast(128), writes=['M_gkn'])
            p.op('dve', lambda e: e.memset(Vm[:], 1.0), writes=['M_Vm'])
            for mt in range(2):
                p.dma('sp', mt_[:], mem_in[mt * 128:(mt + 1) * 128, :], writes=['M_mt'])
                p.op('act', lambda e: e.activation(mb[:], mt_[:], AF.Square, accum_out=ss[:]), reads=['M_mt'], writes=['M_mb', 'M_ss'])
                p.op('act', lambda e: e.activation(ss[:], ss[:], AF.Sqrt, bias=eps_t[:], scale=1.0 / D), reads=['M_ss', 'eps_t'], writes=['M_ss'])
                p.op('dve', lambda e: e.reciprocal(ss[:], ss[:]), reads=['M_ss'], writes=['M_ss'])
                p.op('dve', lambda e: e.scalar_tensor_tensor(mb[:], mt_[:], ss[:], gm[:], ALU.mult, ALU.mult), reads=['M_mt', 'M_ss', 'M_gm'], writes=['M_mb'])
                for k8 in range(4):
                    pt = ptr[k8 % 2]
                    for kk in range(8):
                        k = k8 * 8 + kk
                        p.op('pe', lambda e: e.transpose(pt[:, kk, :], mb[:, k * 128:(k + 1) * 128], ident_b[:]), reads=['M_mb', 'ident_b'], writes=[('M_pt', k8 % 2)])
                    p.op('act', lambda e: e.copy(memT[:, k8 * 8:(k8 + 1) * 8, mt * 128:(mt + 1) * 128], pt[:]), reads=[('M_pt', k8 % 2)], writes=['M_memT'])
            for which, wsrc in ((0, mem_w_k), (1, mem_w_v)):
                for k4 in range(4):
                    p.dma('pool', wk[:, k4 * 8:(k4 + 1) * 8, :], wsrc[l, k4 * 1024:(k4 + 1) * 1024, :].rearrange("(k p) n -> p k n", p=128), writes=['M_wk'])
                for mt in range(2):
                    for half in range(2):
                        for k in range(32):
                            p.op('pe', lambda e: e.matmul(pm[half][:, :], memT[:, k, mt * 128:(mt + 1) * 128], wk[:, k, half * 512:(half + 1) * 512],
                                                         start=(k == 0), stop=(k == 31)), reads=['M_memT', 'M_wk'], writes=[('M_pm', half)])
                        if which == 0:
                            p.op('act', lambda e: e.copy(Kt[:, half * 512:(half + 1) * 512], pm[half][:, :]), reads=[('M_pm', half)], writes=['M_K'])
                        else:
                            p.op('act', lambda e: e.copy(Vm[:, mt, half * 2:(half + 1) * 2, 0:256], pm[half][:, :].rearrange("p (h d) -> p h d", h=2)),
                                 reads=[('M_pm', half)], writes=['M_Vm'])
                    if which == 0:
                        K3 = Kt[:].rearrange("p (h d) -> p h d", h=4)
                        p.op('pool', lambda e: e.tensor_tensor(sq[:], Kt[:], Kt[:], ALU.mult), reads=['M_K'], writes=['M_sq'])
                        p.op('dve', lambda e: e.tensor_reduce(hs[:], sq[:].rearrange("p (h d) -> p h d", h=4), AX.X, ALU.add), reads=['M_sq'], writes=['M_hs'])
                        p.op('act', lambda e: e.activation(hs[:], hs[:], AF.Sqrt, bias=eps_t[:], scale=1.0 / 256), reads=['M_hs', 'eps_t'], writes=['M_hs'])
                        p.op('dve', lambda e: e.reciprocal(hs[:], hs[:]), reads=['M_hs'], writes=['M_hs'])
                        p.op('dve', lambda e: e.tensor_tensor(K3, K3, hs[:].unsqueeze(2).broadcast_to([128, 4, 256]), ALU.mult), reads=['M_K', 'M_hs'], writes=['M_K'])
                        p.op('dve', lambda e: e.tensor_tensor(Kb[:].rearrange("p (h d) -> p h d", h=4), K3, gkn[:].unsqueeze(1).broadcast_to([128, 4, 256]), ALU.mult),
                             reads=['M_K', 'M_gkn'], writes=['M_Kb'])
                        for k in range(8):
                            p.op('pe', lambda e: e.transpose(ptr[0][:, k, :], Kb[:, k * 128:(k + 1) * 128], ident_b[:]), reads=['M_Kb', 'ident_b'], writes=[('M_pt', 0)])
                        p.op('act', lambda e: e.copy(KmT[:, :, mt * 128:(mt + 1) * 128], ptr[0][:]), reads=[('M_pt', 0)], writes=['M_KmT'])
            qt = sb(st, "M_q", [128, 1024]); qb_ = sb(st, "M_qb", [128, 1024], BF16); qT = sb(st, "M_qT", [128, 8, 512], BF16)
            PT = sb(st, "M_PT", [128, 2, 512], BF16); ob = sb(st, "M_ob", [128, 1024]); rs = sb(st, "M_rs", [128, 1])
            for g4 in range(NT // 4):
                for ti in range(4):
                    i = g4 * 4 + ti
                    p.dma('sp', qt[:], proj[i * 128:(i + 1) * 128, C_MQ:C_MQ + 1024], reads=[('proj', i, 'all')], writes=['M_q'])
                    Q3 = qt[:].rearrange("p (h d) -> p h d", h=4)
                    p.op('pool', lambda e: e.tensor_tensor(sq[:], qt[:], qt[:], ALU.mult), reads=['M_q'], writes=['M_sq'])
                    p.op('dve', lambda e: e.tensor_reduce(hs[:], sq[:].rearrange("p (h d) -> p h d", h=4), AX.X, ALU.add), reads=['M_sq'], writes=['M_hs'])
                    p.op('act', lambda e: e.activation(hs[:], hs[:], AF.Sqrt, bias=eps_t[:], scale=1.0 / 256), reads=['M_hs', 'eps_t'], writes=['M_hs'])
                    p.op('dve', lambda e: e.reciprocal(hs[:], hs[:]), reads=['M_hs'], writes=['M_hs'])
                    p.op('dve', lambda e: e.tensor_tensor(Q3, Q3, hs[:].unsqueeze(2).broadcast_to([128, 4, 256]), ALU.mult), reads=['M_q', 'M_hs'], writes=['M_q'])
                    p.op('dve', lambda e: e.tensor_tensor(qb_[:].rearrange("p (h d) -> p h d", h=4), Q3, gqn[:].unsqueeze(1).broadcast_to([128, 4, 256]), ALU.mult),
                         reads=['M_q', 'M_gqn'], writes=['M_qb'])
                    for k in range(8):
                        p.op('pe', lambda e: e.transpose(ptr[ti % 2][:, k, :], qb_[:, k * 128:(k + 1) * 128], ident_b[:]), reads=['M_qb', 'ident_b'], writes=[('M_pt', ti % 2)])
                    p.op('act', lambda e: e.copy(qT[:, :, ti * 128:(ti + 1) * 128], ptr[ti % 2][:]), reads=[('M_pt', ti % 2)], writes=['M_qT'])
                for h in range(4):
                    for mt in range(2):
                        for dc in range(2):
                            p.op('pe', lambda e: e.matmul(psc[mt][:, :], KmT[:, h * 2 + dc, mt * 128:(mt + 1) * 128], qT[:, h * 2 + dc, :],
                                                         start=(dc == 0), stop=(dc == 1)), reads=['M_KmT', 'M_qT'], writes=[('M_ps', mt)])
                        p.op('act', lambda e: e.activation(PT[:, mt, :], psc[mt][:, :], AF.Exp, scale=1.0 / 16), reads=[('M_ps', mt)], writes=['M_PT'])
                    for ti in range(4):
                        i = g4 * 4 + ti
                        j = ti % 2
                        for mt in range(2):
                            p.op('pe', lambda e: e.matmul(pac[j][:, 0:257], PT[:, mt, ti * 128:(ti + 1) * 128], Vm[:, mt, h, 0:257],
                                                         start=(mt == 0), stop=(mt == 1)), reads=['M_PT', 'M_Vm'], writes=[('M_pa', j)])
                        p.op('dve', lambda e: e.reciprocal(rs[:], pac[j][:, 256:257]), reads=[('M_pa', j)], writes=['M_rs'])
                        p.op('dve', lambda e: e.tensor_scalar(ob[:, 0:256], pac[j][:, 0:256], rs[:], None, ALU.mult), reads=[('M_pa', j), 'M_rs'], writes=['M_ob'])
                        p.dma('sp', br[i * 128:(i + 1) * 128, 3072 + h * 256:3072 + (h + 1) * 256], ob[:, 0:256], reads=['M_ob'], writes=[('br', i, 3, h)])
        p.barrier()

    def phase_E(l, xsrc):
        for r in range(0, D, 512):
            p.dma('pool', wbf_out[r:r + 512, :], w_out[l, r:r + 512, :], writes=[('wbf_out', r)])
        with ExitStack() as st:
            bt = sb(st, "E_b", [128, D]); G = sb(st, "E_G", [128, D]); mg = sb(st, "E_mg", [128, D], BF16)
            bg = sb(st, "E_bg", [128, 3, 1024]); ss = sb(st, "E_ss", [128, 1]); junk = sb(st, "E_junk", [128, 1024], BF16)
            mT = sb(st, "E_mT", [128, 32, 1024], BF16)
            W = [sb(st, f"E_W{i}", [128, 32, 512], BF16) for i in range(2)]
            xt = [sb(st, f"E_x{i}", [128, 512]) for i in range(2)]
            ptr = [ps(st, f"E_pt{i}", [128, 8, 128], BF16) for i in range(2)]
            pmm = [ps(st, f"E_pm{i}", [128, 512]) for i in range(4)]
            p.dma('sp', bg[:].rearrange("p a b -> p (a b)"), branch_g[l:l + 1, :].partition_broadcast(128), writes=['E_bg'])
            gates = (C_AG, C_BG, C_CG, C_MG)
            wi = 0
            for g in range(S // 1024):
                for ti in range(8):
                    i = g * 8 + ti
                    t0 = i * 128
                    p.dma('sp', bt[:], br[t0:t0 + 128, :], reads=[('br', i, 'all')], writes=['E_b'])
                    for bi in range(4):
                        p.dma('sp', G[:, bi * 1024:(bi + 1) * 1024], proj[t0:t0 + 128, gates[bi]:gates[bi] + 1024], reads=[('proj', i, 'all')], writes=['E_G'])
                    p.op('act', lambda e: e.activation(G[:], G[:], AF.Silu), reads=['E_G'], writes=['E_G'])
                    for bi in range(4):
                        cs_ = slice(bi * 1024, (bi + 1) * 1024)
                        if bi == 2:
                            p.op('pool', lambda e: e.tensor_tensor(mg[:, cs_], bt[:, cs_], G[:, cs_], ALU.mult), reads=['E_b', 'E_G'], writes=['E_mg'])
                            continue
                        gi = {0: 0, 1: 1, 3: 2}[bi]
                        p.op('act', lambda e: e.activation(junk[:], bt[:, cs_], AF.Square, accum_out=ss[:]), reads=['E_b'], writes=['E_junk', 'E_ss'])
                        p.op('act', lambda e: e.activation(ss[:], ss[:], AF.Sqrt, bias=eps_t[:], scale=1.0 / 1024), reads=['E_ss', 'eps_t'], writes=['E_ss'])
                        p.op('dve', lambda e: e.reciprocal(ss[:], ss[:]), reads=['E_ss'], writes=['E_ss'])
                        p.op('dve', lambda e: e.scalar_tensor_tensor(bt[:, cs_], bt[:, cs_], ss[:], bg[:, gi, :], ALU.mult, ALU.mult), reads=['E_b', 'E_ss', 'E_bg'], writes=['E_b'])
                        p.op('pool', lambda e: e.tensor_tensor(mg[:, cs_], bt[:, cs_], G[:, cs_], ALU.mult), reads=['E_b', 'E_G'], writes=['E_mg'])
                    for k8 in range(4):
                        pt = ptr[k8 % 2]
                        for kk in range(8):
                            k = k8 * 8 + kk
                            p.op('pe', lambda e: e.transpose(pt[:, kk, :], mg[:, k * 128:(k + 1) * 128], ident_b[:]), reads=['E_mg', 'ident_b'], writes=[('E_pt', k8 % 2)])
                        dst = mT[:, k8 * 8:(k8 + 1) * 8, ti * 128:(ti + 1) * 128]
                        if k8 % 2 == 0:
                            p.op('act', lambda e: e.copy(dst, pt[:]), reads=[('E_pt', k8 % 2)], writes=[('E_mT', ti)])
                        else:
                            p.op('dve', lambda e: e.tensor_copy(dst, pt[:]), reads=[('E_pt', k8 % 2)], writes=[('E_mT', ti)])
                for ci in range(D // 512):
                    n0 = ci * 512
                    Wt = W[wi % 2]
                    for k4 in range(4):
                        p.dma('sp', Wt[:, k4 * 8:(k4 + 1) * 8, :], wbf_out[k4 * 1024:(k4 + 1) * 1024, n0:n0 + 512].rearrange("(k p) n -> p k n", p=128),
                              reads=[('wbf_out', k4 * 1024), ('wbf_out', k4 * 1024 + 512)], writes=[('E_W', wi % 2)])
                    for ti in range(8):
                        i = g * 8 + ti
                        t0 = i * 128
                        j = (ci * 8 + ti) % 4
                        pm = pmm[j]
                        X = xt[(ci * 8 + ti) % 2]
                        xk = ('E_x', (ci * 8 + ti) % 2)
                        p.dma('sp', X[:], xsrc[t0:t0 + 128, n0:n0 + 512], reads=[('y', i, ci)], writes=[xk])
                        for k in range(32):
                            p.op('pe', lambda e: e.matmul(pm[:, :], mT[:, k, ti * 128:(ti + 1) * 128], Wt[:, k, :], start=(k == 0), stop=(k == 31)),
                                 reads=[('E_mT', ti), ('E_W', wi % 2)], writes=[('E_pm', j)])
                        p.op('dve', lambda e: e.tensor_tensor(X[:], X[:], pm[:, :], ALU.add), reads=[('E_pm', j), xk], writes=[xk])
                        p.dma('sp', y_out[t0:t0 + 128, n0:n0 + 512], X[:], reads=[xk], writes=[('y', i, ci)])
                    wi += 1
        p.barrier()

    if dbg and 'A' not in phases:
        proj_in = din("proj_in", [S, NCOLS])
        for i in range(NT):
            p.dma('sp', proj[i * 128:(i + 1) * 128, :], proj_in[i * 128:(i + 1) * 128, :], writes=[('proj', i, 'all')])
        p.barrier()
    if dbg and 'E' in phases and len(phases) < 6:
        br_in = din("br_in", [S, 4096])
        for i in range(NT):
            p.dma('sp', br[i * 128:(i + 1) * 128, :], br_in[i * 128:(i + 1) * 128, :], writes=[('br', i, 'all')])
        p.barrier()
    for l in range(n_layers):
        xsrc = x_in if l == 0 else y_out
        if 'A' in phases:
            phase_A(l, xsrc)
        if 'D' in phases:
            phase_D(l)
        if 'C' in phases:
            phase_C(l)
        if 'M' in phases:
            phase_M(l)
        if 'B' in phases:
            phase_B(l)
        if 'E' in phases:
            phase_E(l, xsrc)
    p.barrier()
    es.close()
    print("instructions:", p.nins)
    nc.in_names = in_names
    return nc


def make_consts():
    r = np.arange(128)
    m = np.stack([r[:, None] < r[None, :], r[:, None] > r[None, :], r[:, None] <= r[None, :], r[:, None] >= r[None, :]]).astype(np.float32)
    return {"c_ident": np.eye(128, dtype=np.float32), "c_masks": m,
            "c_iota": np.arange(1, 513, dtype=np.float32)[None, :],
            "c_invfreq": (1.0 / (np.float32(10000.0) ** (np.arange(0, 64, 2, dtype=np.float32) / np.float32(64)))).astype(np.float32)[None, :]}


_NC_CACHE = {}


def kernel(**inputs):
    nb = 4
    if 'nc' not in _NC_CACHE:
        _NC_CACHE['nc'] = build()
    nc = _NC_CACHE['nc']
    cst = make_consts()
    shared = {}
    for n in nc.in_names:
        if n in cst:
            shared[n] = cst[n]
        elif n in ("x", "mem", "positions"):
            continue
        elif n == "rwkv_r_k":
            shared[n] = np.ascontiguousarray(np.asarray(inputs[n], dtype=np.float32).reshape(L, 1024))
        elif n == "branch_g":
            shared[n] = np.ascontiguousarray(np.asarray(inputs[n], dtype=np.float32).reshape(L, 3072))
        else:
            shared[n] = np.ascontiguousarray(np.asarray(inputs[n], dtype=np.float32))
    in_maps = []
    for b in range(nb):
        m = dict(shared)
        m["x"] = np.ascontiguousarray(np.asarray(inputs["x"][b], dtype=np.float32))
        m["mem"] = np.ascontiguousarray(np.asarray(inputs["mem"][b], dtype=np.float32))
        m["positions"] = np.ascontiguousarray(np.asarray(inputs["positions"][b:b + 1]).astype(np.int32))
        in_maps.append(m)
    res = run_bass_kernel_spmd(nc, in_maps, core_ids=list(range(nb)))
    return np.stack([np.asarray(r["y"], dtype=np.float32) for r in res.results], axis=0)
```

```python
import numpy as np
from contextlib import ExitStack
import concourse.bass as bass
import concourse.mybir as mybir
from concourse.bass_utils import run_bass_kernel_spmd

F32 = mybir.dt.float32
BF16 = mybir.dt.bfloat16
I32 = mybir.dt.int32
ALU = mybir.AluOpType
AF = mybir.ActivationFunctionType
AX = mybir.AxisListType

D = 4096
S = 4096
L = 4
NCOLS = 10688
NT = S // 128
EPS = 1e-6
C_AU, C_AG, C_CQ, C_CKV, C_KPE, C_BG, C_RW, C_CG, C_MQ, C_MG = 0, 1024, 2048, 2944, 3200, 3264, 4288, 7616, 8640, 9664
NDMA = 8


class Prog:
    def __init__(self, nc, es):
        self.nc = nc
        self.E = {'pe': nc.tensor, 'act': nc.scalar, 'dve': nc.vector, 'pool': nc.gpsimd, 'sp': nc.sync}
        self.sems = {}
        for e in ['pe', 'act', 'dve', 'pool']:
            self.sems[e] = es.enter_context(nc.semaphore('s_' + e))
        for q in ['sp', 'pool']:
            for i in range(NDMA):
                self.sems[('d', q, i)] = es.enter_context(nc.semaphore(f'd_{q}_{i}'))
        self.cnt = {e: 0 for e in ['pe', 'act', 'dve', 'pool']}
        self.dma_n = {'sp': 0, 'pool': 0}
        self.waited = {}
        self.bufs = {}
        self.nins = 0

    def _wait(self, eng, key, val):
        if self.waited.get((eng, key), 0) >= val:
            return
        self.E[eng].wait_ge(self.sems[key], val)
        self.waited[(eng, key)] = val

    def _deps(self, reads, writes):
        deps = {}
        for b in reads:
            st = self.bufs.get(b)
            if st and st[0] is not None:
                k, v = st[0]
                if deps.get(k, 0) < v:
                    deps[k] = v
        for b in writes:
            st = self.bufs.get(b)
            if st:
                if st[0] is not None:
                    k, v = st[0]
                    if deps.get(k, 0) < v:
                        deps[k] = v
                for k, v in st[1].items():
                    if deps.get(k, 0) < v:
                        deps[k] = v
        return deps

    def _commit(self, tk, reads, writes):
        k, v = tk
        for b in reads:
            st = self.bufs.get(b)
            if st is None:
                st = self.bufs[b] = [None, {}]
            if st[1].get(k, 0) < v:
                st[1][k] = v
        for b in writes:
            self.bufs[b] = [tk, {}]

    def op(self, eng, fn, reads=(), writes=()):
        deps = self._deps(reads, writes)
        for k, v in deps.items():
            if k == 'pe' and eng == 'pe':
                continue
            self._wait(eng, k, v)
        ins = fn(self.E[eng])
        self.cnt[eng] += 1
        ins.then_inc(self.sems[eng], 1)
        self._commit((eng, self.cnt[eng]), reads, writes)
        self.nins += 1

    def dma(self, q, out, in_, reads=(), writes=(), **kw):
        deps = self._deps(reads, writes)
        n = self.dma_n[q]
        self.dma_n[q] += 1
        key = ('d', q, n % NDMA)
        val = 16 * (n // NDMA + 1)
        if n >= NDMA:
            deps[key] = max(deps.get(key, 0), val - 16)
        for k, v in deps.items():
            self._wait(q, k, v)
        ins = self.E[q].dma_start(out=out, in_=in_, **kw)
        ins.then_inc(self.sems[key], 16)
        self._commit((key, val), reads, writes)
        self.nins += 1

    def barrier(self):
        for e in ['pe', 'act', 'dve', 'pool', 'sp']:
            for k in self.sems:
                if isinstance(k, tuple):
                    n = self.dma_n[k[1]]
                    v = 16 * ((n - k[2] + NDMA - 1) // NDMA) if n > k[2] else 0
                else:
                    v = self.cnt[k]
                if v > 0:
                    self._wait(e, k, v)
        self.bufs.clear()


def build(n_layers=L, phases="AMBCDE", dbg=False, RWDT=BF16):
    nc = bass.Bass("TRN2", target_bir_lowering=False)
    es = ExitStack()

    in_names = []

    def din(name, shape, dt=F32):
        in_names.append(name)
        return nc.dram_tensor(name, list(shape), dt, kind="ExternalInput").ap()

    def dscr(name, shape, dt=F32):
        return nc.dram_tensor(name, list(shape), dt, kind="ExternalOutput" if dbg else "Internal").ap()

    x_in = din("x", [S, D])
    mem_in = din("mem", [256, D])
    pos_in = din("positions", [1, S], I32)
    ln_g = din("ln_g", [L, D])
    w_in = din("w_in", [L, D, NCOLS]) if ('A' in phases or not dbg) else None
    w_out = din("w_out", [L, D, D]) if ('E' in phases or not dbg) else None
    cst_ident = din("c_ident", [128, 128])
    c_masks_t = din("c_masks", [4, 128, 128])
    c_masks = [c_masks_t[i, :, :] for i in range(4)]
    c_iota = din("c_iota", [1, 512])
    c_invfreq = din("c_invfreq", [1, 32])
    branch_g = din("branch_g", [L, 3072])
    mla_q_a_norm = din("mla_q_a_norm", [L, 896]); mla_kv_a_norm = din("mla_kv_a_norm", [L, 256])
    mla_w_uq = din("mla_w_uq", [L, 896, 1536]); mla_w_ukv = din("mla_w_ukv", [L, 256, 2048])
    mla_q_norm = din("mla_q_norm", [L, 192]); mla_k_norm = din("mla_k_norm", [L, 192])
    mem_norm_g = din("mem_norm_g", [L, D]); mem_w_k = din("mem_w_k", [L, D, 1024]) if ('M' in phases or not dbg) else None
    mem_w_v = din("mem_w_v", [L, D, 1024]) if ('M' in phases or not dbg) else None
    mem_q_norm = din("mem_q_norm", [L, 256]); mem_k_norm = din("mem_k_norm", [L, 256])
    s5_lam_re = din("s5_lam_re", [L, 64, 64]); s5_lam_im = din("s5_lam_im", [L, 64, 64])
    s5_b_re = din("s5_b_re", [L, 64, 64, 16]); s5_b_im = din("s5_b_im", [L, 64, 64, 16])
    s5_c_re = din("s5_c_re", [L, 2, 64, 16, 64]); s5_c_im = din("s5_c_im", [L, 2, 64, 16, 64])
    s5_log_dt = din("s5_log_dt", [L, 2, 64]); s5_d = din("s5_d", [L, 1024]); s5_glu_w = din("s5_glu_w", [L, 1024, 1024])
    s5_glu_b = din("s5_glu_b", [L, 1024])
    rwkv_mu = din("rwkv_mu", [L, 2, 3328]); rwkv_w0 = din("rwkv_w0", [L, 2, 1024]); rwkv_w2 = din("rwkv_w2", [L, 2, 64, 1024])
    rwkv_a0 = din("rwkv_a0", [L, 2, 1024]); rwkv_a2 = din("rwkv_a2", [L, 2, 64, 1024]); rwkv_k_k = din("rwkv_k_k", [L, 1024])
    rwkv_k_a = din("rwkv_k_a", [L, 1024]); rwkv_r_k = din("rwkv_r_k", [L, 1024]); rwkv_ln_w = din("rwkv_ln_w", [L, 1024])
    rwkv_ln_b = din("rwkv_ln_b", [L, 1024])
    y_out = nc.dram_tensor("y", [S, D], F32, kind="ExternalOutput").ap()
    proj = dscr("proj", [S, NCOLS])
    wbf_in = nc.dram_tensor("wbf_in", [D, NCOLS], BF16, kind="Internal").ap()
    rwc = dscr("rwc", [S, 3328])
    ysc = dscr("ysc", [2, S, 1024])
    bon = dscr("bon", [2, S, 16])
    br = dscr("br", [S, 4096])
    ygd = dscr("ygd", [S, 1024])
    wbf_out = nc.dram_tensor("wbf_out", [D, D], BF16, kind="Internal").ap()
    qT_d = nc.dram_tensor("qT_d", [8, 192, S], BF16, kind="Internal").ap()
    kT_d = nc.dram_tensor("kT_d", [8, 192, S], BF16, kind="Internal").ap()
    v_d = nc.dram_tensor("v_d", [S, 1024], BF16, kind="Internal").ap()

    p = Prog(nc, es)

    uniq = [0]

    def sb(stack, name, shape, dt=F32):
        uniq[0] += 1
        return stack.enter_context(nc.sbuf_tensor(f"{name}_{uniq[0]}", list(shape), dt))

    def ps(stack, name, shape, dt=F32):
        uniq[0] += 1
        return stack.enter_context(nc.psum_tensor(f"{name}_{uniq[0]}", list(shape), dt))

    ident_f = sb(es, "ident_f", [128, 128], F32)
    ident_b = sb(es, "ident_b", [128, 128], BF16)
    p.dma('sp', ident_f[:], cst_ident[:, :], writes=['ident_f'])
    p.op('dve', lambda e: e.tensor_copy(ident_b[:], ident_f[:]), reads=['ident_f'], writes=['ident_b'])

    eps_t = sb(es, "eps_t", [128, 1], F32)
    p.op('dve', lambda e: e.memset(eps_t[:], EPS), writes=['eps_t'])

    def phase_A(l, xsrc):
        for r in range(0, D, 512):
            p.dma('pool', wbf_in[r:r + 512, :], w_in[l, r:r + 512, :], writes=[('wbf_in', r)])
        with ExitStack() as st:
            gt = sb(st, "A_g", [128, D], F32)
            xt = sb(st, "A_x", [128, D], F32)
            hb = sb(st, "A_hb", [128, D], BF16)
            hT = sb(st, "A_hT", [128, 32, 1024], BF16)
            W = [sb(st, f"A_W{i}", [128, 32, 512], BF16) for i in range(2)]
            ob = [sb(st, f"A_ob{i}", [128, 512], F32) for i in range(4)]
            ss = sb(st, "A_ss", [128, 1], F32)
            rstd = sb(st, "A_rstd", [128, 1], F32)
            ptr = [ps(st, f"A_pt{i}", [128, 8, 128], BF16) for i in range(2)]
            pmm = [ps(st, f"A_pm{i}", [128, 512], F32) for i in range(4)]
            p.dma('sp', gt[:], ln_g[l:l + 1, :].partition_broadcast(128), writes=['A_g'])
            nch = (NCOLS + 511) // 512
            wi = 0
            for g in range(S // 1024):
                for ti in range(8):
                    t0 = g * 1024 + ti * 128
                    p.dma('sp', xt[:], xsrc[t0:t0 + 128, :], writes=['A_x'])
                    p.op('act', lambda e: e.activation(hb[:], xt[:], AF.Square, accum_out=ss[:]),
                         reads=['A_x'], writes=['A_hb', 'A_ss'])
                    p.op('act', lambda e: e.activation(rstd[:], ss[:], AF.Sqrt, bias=eps_t[:], scale=1.0 / D),
                         reads=['A_ss', 'eps_t'], writes=['A_rstd'])
                    p.op('dve', lambda e: e.reciprocal(rstd[:], rstd[:]), reads=['A_rstd'], writes=['A_rstd'])
                    p.op('dve', lambda e: e.scalar_tensor_tensor(hb[:], xt[:], rstd[:], gt[:], ALU.mult, ALU.mult),
                         reads=['A_x', 'A_rstd', 'A_g'], writes=['A_hb'])
                    for k8 in range(4):
                        pt = ptr[k8 % 2]
                        for kk in range(8):
                            k = k8 * 8 + kk
                            p.op('pe', lambda e: e.transpose(pt[:, kk, :], hb[:, k * 128:(k + 1) * 128], ident_b[:]),
                                 reads=['A_hb', 'ident_b'], writes=[('A_pt', k8 % 2)])
                        eng = 'act' if k8 % 2 == 0 else 'dve'
                        dst = hT[:, k8 * 8:(k8 + 1) * 8, ti * 128:(ti + 1) * 128]
                        if eng == 'act':
                            p.op('act', lambda e: e.copy(dst, pt[:]), reads=[('A_pt', k8 % 2)], writes=[('A_hT', ti)])
                        else:
                            p.op('dve', lambda e: e.tensor_copy(dst, pt[:]), reads=[('A_pt', k8 % 2)], writes=[('A_hT', ti)])
                for ci in range(nch):
                    n0 = ci * 512
                    nw = min(512, NCOLS - n0)
                    Wt = W[wi % 2]
                    for k4 in range(4):
                        p.dma('sp', Wt[:, k4 * 8:(k4 + 1) * 8, 0:nw],
                              wbf_in[k4 * 1024:(k4 + 1) * 1024, n0:n0 + nw].rearrange("(k p) n -> p k n", p=128),
                              reads=[('wbf_in', (k4 * 1024) // 512 * 512), ('wbf_in', (k4 * 1024) // 512 * 512 + 512)],
                              writes=[('A_W', wi % 2)])
                    for ti in range(8):
                        t0 = g * 1024 + ti * 128
                        j = (ci * 8 + ti) % 4
                        pm = pmm[j]
                        for k in range(32):
                            p.op('pe', lambda e: e.matmul(pm[:, 0:nw], hT[:, k, ti * 128:(ti + 1) * 128], Wt[:, k, 0:nw],
                                                         start=(k == 0), stop=(k == 31)),
                                 reads=[('A_hT', ti), ('A_W', wi % 2)], writes=[('A_pm', j)])
                        if j % 2 == 0:
                            p.op('act', lambda e: e.copy(ob[j][:, 0:nw], pm[:, 0:nw]), reads=[('A_pm', j)], writes=[('A_ob', j)])
                        else:
                            p.op('dve', lambda e: e.tensor_copy(ob[j][:, 0:nw], pm[:, 0:nw]), reads=[('A_pm', j)], writes=[('A_ob', j)])
                        p.dma('sp', proj[t0:t0 + 128, n0:n0 + nw], ob[j][:, 0:nw], reads=[('A_ob', j)],
                              writes=[('proj', t0 // 128, ci)])
                    wi += 1
        p.barrier()


    RW = 3328
    NEG_E = -float(np.exp(-0.5))
    RW_DT = RWDT

    def phase_D(l):
        with ExitStack() as st:
            mup = sb(st, "D0_mup", [128, RW]); mun = sb(st, "D0_mun", [128, RW]); m0 = sb(st, "D0_m0", [128, RW])
            ct = sb(st, "D0_c", [128, RW]); pt_ = sb(st, "D0_p", [128, RW]); nt = sb(st, "D0_n", [128, RW])
            p.dma('sp', mup[:], rwkv_mu[l, 0:1, :].partition_broadcast(128), writes=['D0_mup'])
            p.dma('sp', mun[:], rwkv_mu[l, 1:2, :].partition_broadcast(128), writes=['D0_mun'])
            p.op('dve', lambda e: e.tensor_tensor(m0[:], mup[:], mun[:], ALU.add), reads=['D0_mup', 'D0_mun'], writes=['D0_m0'])
            p.op('dve', lambda e: e.tensor_scalar(m0[:], m0[:], -1.0, 1.0, ALU.mult, ALU.add), reads=['D0_m0'], writes=['D0_m0'])
            for i in range(NT):
                t0 = i * 128
                p.dma('sp', ct[:], proj[t0:t0 + 128, C_RW:C_RW + RW], reads=[('proj', i, 'all')], writes=['D0_c'])
                if i == 0:
                    p.op('pool', lambda e: e.memset(pt_[:], 0.0), writes=['D0_p'])
                    p.dma('sp', pt_[1:128, :], proj[0:127, C_RW:C_RW + RW], reads=[('proj', 0, 'all')], writes=['D0_p'])
                else:
                    p.dma('sp', pt_[:], proj[t0 - 1:t0 + 127, C_RW:C_RW + RW], reads=[('proj', i, 'all'), ('proj', i - 1, 'all')], writes=['D0_p'])
                if i == NT - 1:
                    p.op('pool', lambda e: e.memset(nt[:], 0.0), writes=['D0_n'])
                    p.dma('sp', nt[0:127, :], proj[t0 + 1:t0 + 128, C_RW:C_RW + RW], reads=[('proj', i, 'all')], writes=['D0_n'])
                else:
                    p.dma('sp', nt[:], proj[t0 + 1:t0 + 129, C_RW:C_RW + RW], reads=[('proj', i, 'all'), ('proj', i + 1, 'all')], writes=['D0_n'])
                p.op('dve', lambda e: e.tensor_tensor(ct[:], ct[:], m0[:], ALU.mult), reads=['D0_c', 'D0_m0'], writes=['D0_c'])
                p.op('pool', lambda e: e.tensor_tensor(pt_[:], pt_[:], mup[:], ALU.mult), reads=['D0_p', 'D0_mup'], writes=['D0_p'])
                p.op('pool', lambda e: e.tensor_tensor(nt[:], nt[:], mun[:], ALU.mult), reads=['D0_n', 'D0_mun'], writes=['D0_n'])
                p.op('dve', lambda e: e.tensor_tensor(ct[:], ct[:], pt_[:], ALU.add), reads=['D0_c', 'D0_p'], writes=['D0_c'])
                p.op('dve', lambda e: e.tensor_tensor(ct[:], ct[:], nt[:], ALU.add), reads=['D0_c', 'D0_n'], writes=['D0_c'])
                p.dma('sp', rwc[t0:t0 + 128, :], ct[:], reads=['D0_c'], writes=[('rwc', i)])
        p.barrier()
        with ExitStack() as st:
            def bc(name, src):
                t = sb(st, name, [128, 1024])
                p.dma('sp', t[:], src.partition_broadcast(128), writes=[name])
                return t
            kk_c = bc("D_kk_c", rwkv_k_k[l:l + 1, :]); ka_c = bc("D_ka_c", rwkv_k_a[l:l + 1, :])
            rk_c = bc("D_rk_c", rwkv_r_k[l:l + 1, :])
            c1 = sb(st, "D_c1", [128, 1024])
            p.op('dve', lambda e: e.tensor_scalar(c1[:], ka_c[:], -1.0, 1.0, ALU.mult, ALU.add), reads=['D_ka_c'], writes=['D_c1'])
            w0_c = sb(st, "D_w0", [128, 1024]); a0_c = sb(st, "D_a0", [128, 1024])
            w2_t = sb(st, "D_w2", [64, 1024]); a2_t = sb(st, "D_a2", [64, 1024])
            mS = sb(st, "D_mS", [128, 128]); mI = sb(st, "D_mI", [128, 128]); mST = sb(st, "D_mST", [128, 128])
            imask = sb(st, "D_imask", [64, 1024])
            b4 = lambda t: t[:].unsqueeze(1).broadcast_to([128, 4, 128])
            v4 = lambda a: a.rearrange("p (a b) -> p a b", a=4)
            triI = sb(st, "D_triI", [128, 128]); triC = sb(st, "D_triC", [128, 128])
            identr = sb(st, "D_identr", [128, 128], RW_DT)
            p.op('dve', lambda e: e.tensor_copy(identr[:], ident_f[:]), reads=['ident_f'], writes=['D_identr'])
            for h in range(16):
                p.op('pool', lambda e: e.tensor_copy(imask[:, h * 64:(h + 1) * 64], ident_f[0:64, 0:64]), reads=['ident_f'], writes=['D_imask'])
            rw = sb(st, "D_rw", [128, RW])
            kk = sb(st, "D_kk", [128, 1024]); ld = sb(st, "D_ld", [128, 1024]); a_t = sb(st, "D_a", [128, 1024])
            kd = sb(st, "D_kd", [128, 1024]); ba = sb(st, "D_ba", [128, 1024]); tmp = sb(st, "D_tmp", [128, 1024])
            Ab = sb(st, "D_Ab", [128, 1024]); Rb = sb(st, "D_Rb", [128, 1024]); Bb = sb(st, "D_Bb", [128, 1024]); Kb = sb(st, "D_Kb", [128, 1024])
            Abr = sb(st, "D_Abr", [128, 1024], RW_DT)
            Bt = sb(st, "D_Bt", [128, 1024], RW_DT); Kt = sb(st, "D_Kt", [128, 1024], RW_DT); Vr = sb(st, "D_Vr", [128, 1024], RW_DT)
            ydg = sb(st, "D_ydg", [64, 1024], RW_DT)
            sm = sb(st, "D_sm", [128, 64]); smT = sb(st, "D_smT", [64, 2, 128])
            hs = sb(st, "D_hs", [128, 16]); hs2 = sb(st, "D_hs2", [128, 16])
            AbT = sb(st, "D_AbT", [64, 16, 128], RW_DT); RbT = sb(st, "D_RbT", [64, 16, 128], RW_DT)
            BbT = sb(st, "D_BbT", [64, 16, 128], RW_DT); KbT = sb(st, "D_KbT", [64, 16, 128], RW_DT)
            Q = [[sb(st, f"D_Q{g}{i}", [128, 512], RW_DT) for i in range(2)] for g in range(4)]
            QT = [[sb(st, f"D_QT{g}{i}", [128, 512], RW_DT) for i in range(2)] for g in range(4)]
            P = [sb(st, f"D_P{g}", [128, 512]) for g in range(4)]; Pr = [sb(st, f"D_Pr{g}", [128, 512], RW_DT) for g in range(4)]
            MrbT = [sb(st, f"D_MrbT{g}", [128, 512], RW_DT) for g in range(4)]; LakT = [sb(st, f"D_LakT{g}", [128, 512], RW_DT) for g in range(4)]
            MrkT = [sb(st, f"D_MrkT{g}", [128, 512], RW_DT) for g in range(4)]
            AXt = [sb(st, f"D_AX{g}", [128, 4, 128], RW_DT) for g in range(4)]; AU = [sb(st, f"D_AU{g}", [128, 4, 128], RW_DT) for g in range(4)]
            RhT = sb(st, "D_RhT", [64, 16, 128], RW_DT); GT = sb(st, "D_GT", [64, 1024], RW_DT)
            Hh = sb(st, "D_H", [64, 1024]); Yh = sb(st, "D_Yh", [128, 1024])
            ST = sb(st, "D_ST", [64, 1024]); STr = sb(st, "D_STr", [64, 1024], RW_DT)
            pb = [ps(st, f"D_pb{i}", [128, 512]) for i in range(8)]
            pbi = [0]

            def nb():
                i = pbi[0] % 8
                pbi[0] += 1
                return i

            for d in range(2):
                p.dma('sp', w0_c[:], rwkv_w0[l, d:d + 1, :].partition_broadcast(128), writes=['D_w0'])
                p.dma('sp', a0_c[:], rwkv_a0[l, d:d + 1, :].partition_broadcast(128), writes=['D_a0'])
                p.dma('sp', w2_t[:], rwkv_w2[l, d, :, :], writes=['D_w2'])
                p.dma('sp', a2_t[:], rwkv_a2[l, d, :, :], writes=['D_a2'])
                cm = c_masks
                p.dma('sp', mS[:], cm[0 if d == 0 else 1], writes=['D_mS'])
                p.dma('sp', mI[:], cm[2 if d == 0 else 3], writes=['D_mI'])
                p.dma('sp', mST[:], cm[1 if d == 0 else 0], writes=['D_mST'])
                p.dma('sp', triI[:], cm[2 if d == 0 else 3], writes=['D_triI'])
                p.dma('sp', triC[:], cm[1 if d == 0 else 0], writes=['D_triC'])
                p.op('dve', lambda e: e.memset(ST[:], 0.0), writes=['D_ST'])
                p.op('dve', lambda e: e.memset(STr[:], 0.0), writes=['D_STr'])
                order = range(NT) if d == 0 else range(NT - 1, -1, -1)
                for c in order:
                    t0 = c * 128
                    p.dma('sp', rw[:], rwc[t0:t0 + 128, :], reads=[('rwc', c)], writes=['D_rw'])
                    r_ = rw[:, 0:1024]; k_ = rw[:, 1024:2048]; v_ = rw[:, 2048:3072]
                    win = rw[:, 3072 + 64 * d:3136 + 64 * d]; ain = rw[:, 3200 + 64 * d:3264 + 64 * d]
                    p.op('dve', lambda e: e.tensor_tensor(kk[:], k_, kk_c[:], ALU.mult), reads=['D_rw', 'D_kk_c'], writes=['D_kk'])
                    p.op('pool', lambda e: e.tensor_tensor(tmp[:], kk[:], kk[:], ALU.mult), reads=['D_kk'], writes=['D_tmp'])
                    p.op('dve', lambda e: e.tensor_reduce(hs[:], tmp[:].rearrange("p (h j) -> p h j", h=16), AX.X, ALU.add), reads=['D_tmp'], writes=['D_hs'])
                    p.op('act', lambda e: e.activation(hs[:], hs[:], AF.Sqrt), reads=['D_hs'], writes=['D_hs'])
                    p.op('dve', lambda e: e.tensor_scalar(hs[:], hs[:], 1e-12, None, ALU.max), reads=['D_hs'], writes=['D_hs'])
                    p.op('dve', lambda e: e.reciprocal(hs[:], hs[:]), reads=['D_hs'], writes=['D_hs'])
                    p.op('dve', lambda e: e.tensor_tensor(kk[:].rearrange("p (h j) -> p h j", h=16), kk[:].rearrange("p (h j) -> p h j", h=16),
                                                         hs[:].unsqueeze(2).broadcast_to([128, 16, 64]), ALU.mult), reads=['D_kk', 'D_hs'], writes=['D_kk'])
                    p.op('act', lambda e: e.activation(sm[:], win, AF.Tanh), reads=['D_rw'], writes=['D_sm'])
                    b0 = nb()
                    p.op('pe', lambda e: e.transpose(pb[b0][0:64, 0:128], sm[:], ident_f[:]), reads=['D_sm', 'ident_f'], writes=[('D_pb', b0)])
                    p.op('pe', lambda e: e.transpose(pb[b0][0:64, 128:256], ain, ident_f[:]), reads=['D_rw', 'ident_f'], writes=[('D_pb', b0)])
                    p.op('act', lambda e: e.copy(smT[:].rearrange("p a b -> p (a b)"), pb[b0][0:64, 0:256]), reads=[('D_pb', b0)], writes=['D_smT'])
                    for half in range(2):
                        cs_ = slice(half * 512, (half + 1) * 512)
                        b1 = nb()
                        p.op('pe', lambda e: e.matmul(pb[b1][:, :], smT[:, 0, :], w2_t[:, cs_], start=True, stop=True),
                             reads=['D_smT', 'D_w2'], writes=[('D_pb', b1)])
                        p.op('dve', lambda e: e.tensor_tensor(ld[:, cs_], pb[b1][:, :], w0_c[:, cs_], ALU.add), reads=[('D_pb', b1), 'D_w0'], writes=['D_ld'])
                        b2 = nb()
                        p.op('pe', lambda e: e.matmul(pb[b2][:, :], smT[:, 1, :], a2_t[:, cs_], start=True, stop=True),
                             reads=['D_smT', 'D_a2'], writes=[('D_pb', b2)])
                        p.op('dve', lambda e: e.tensor_tensor(a_t[:, cs_], pb[b2][:, :], a0_c[:, cs_], ALU.add), reads=[('D_pb', b2), 'D_a0'], writes=['D_a'])
                    p.op('act', lambda e: e.activation(ld[:], ld[:], AF.Sigmoid), reads=['D_ld'], writes=['D_ld'])
                    p.op('act', lambda e: e.activation(a_t[:], a_t[:], AF.Sigmoid), reads=['D_a'], writes=['D_a'])
                    p.op('pool', lambda e: e.tensor_scalar(ld[:], ld[:], NEG_E, None, ALU.mult), reads=['D_ld'], writes=['D_ld'])
                    p.op('dve', lambda e: e.tensor_tensor(tmp[:], a_t[:], ka_c[:], ALU.mult), reads=['D_a', 'D_ka_c'], writes=['D_tmp'])
                    p.op('dve', lambda e: e.tensor_tensor(tmp[:], tmp[:], c1[:], ALU.add), reads=['D_tmp', 'D_c1'], writes=['D_tmp'])
                    p.op('dve', lambda e: e.tensor_tensor(kd[:], tmp[:], k_, ALU.mult), reads=['D_tmp', 'D_rw'], writes=['D_kd'])
                    p.op('pool', lambda e: e.tensor_tensor(ba[:], kk[:], a_t[:], ALU.mult), reads=['D_kk', 'D_a'], writes=['D_ba'])
                    p.op('pool', lambda e: e.tensor_tensor(tmp[:], kd[:], rk_c[:], ALU.mult), reads=['D_kd', 'D_rk_c'], writes=['D_tmp'])
                    p.op('pool', lambda e: e.tensor_tensor(tmp[:], tmp[:], r_, ALU.mult), reads=['D_tmp', 'D_rw'], writes=['D_tmp'])
                    p.op('dve', lambda e: e.tensor_reduce(hs2[:], tmp[:].rearrange("p (h j) -> p h j", h=16), AX.X, ALU.add), reads=['D_tmp'], writes=['D_hs2'])
                    p.dma('sp', bon[d, t0:t0 + 128, :], hs2[:], reads=['D_hs2'], writes=[('bon', d, c)])
                    for half in range(2):
                        cs_ = slice(half * 512, (half + 1) * 512)
                        bcs = nb()
                        p.op('pe', lambda e: e.matmul(pb[bcs][:, :], triI[:], ld[:, cs_], start=True, stop=True), reads=['D_triI', 'D_ld'], writes=[('D_pb', bcs)])
                        brm = nb()
                        p.op('pe', lambda e: e.matmul(pb[brm][:, :], triC[:], ld[:, cs_], start=True, stop=True), reads=['D_triC', 'D_ld'], writes=[('D_pb', brm)])
                        p.op('dve', lambda e: e.tensor_tensor(tmp[:, cs_], pb[bcs][:, :], ld[:, cs_], ALU.subtract), reads=[('D_pb', bcs), 'D_ld'], writes=['D_tmp'])
                        p.op('act', lambda e: e.activation(tmp[:, cs_], tmp[:, cs_], AF.Exp), reads=['D_tmp'], writes=['D_tmp'])
                        p.op('dve', lambda e: e.scalar_tensor_tensor(Ab[:, cs_], kk[:, cs_], -1.0, tmp[:, cs_], ALU.mult, ALU.mult), reads=['D_kk', 'D_tmp'], writes=['D_Ab'])
                        p.op('act', lambda e: e.activation(Rb[:, cs_], pb[bcs][:, :], AF.Exp), reads=[('D_pb', bcs)], writes=['D_Rb'])
                        p.op('act', lambda e: e.activation(Kt[0:64, cs_] if False else tmp[0:64, cs_], pb[brm][0:64, :], AF.Exp), reads=[('D_pb', brm), 'D_tmp'], writes=['D_tmp'])
                        p.op('dve', lambda e: e.tensor_tensor(tmp[0:64, cs_], tmp[0:64, cs_], Rb[0:64, cs_], ALU.mult), reads=['D_tmp', 'D_Rb'], writes=['D_tmp'])
                        p.op('dve', lambda e: e.tensor_tensor(ydg[:, cs_], tmp[0:64, cs_], imask[:, cs_], ALU.mult), reads=['D_tmp', 'D_imask'], writes=['D_ydg'])
                        p.op('pool', lambda e: e.tensor_tensor(Rb[:, cs_], Rb[:, cs_], r_[:, cs_] if False else rw[:, half * 512:(half + 1) * 512], ALU.mult), reads=['D_Rb', 'D_rw', 'D_tmp'], writes=['D_Rb'])
                        p.op('act', lambda e: e.activation(tmp[:, cs_], pb[bcs][:, :], AF.Exp, scale=-1.0), reads=[('D_pb', bcs), 'D_tmp', 'D_ydg'], writes=['D_tmp'])
                        p.op('dve', lambda e: e.tensor_tensor(Bb[:, cs_], ba[:, cs_], tmp[:, cs_], ALU.mult), reads=['D_ba', 'D_tmp'], writes=['D_Bb'])
                        p.op('pool', lambda e: e.tensor_tensor(Kb[:, cs_], kd[:, cs_], tmp[:, cs_], ALU.mult), reads=['D_kd', 'D_tmp'], writes=['D_Kb'])
                        p.op('act', lambda e: e.activation(tmp[:, cs_], pb[brm][:, :], AF.Exp), reads=[('D_pb', brm), 'D_tmp', 'D_Bb', 'D_Kb'], writes=['D_tmp'])
                        p.op('dve', lambda e: e.tensor_tensor(Bt[:, cs_], ba[:, cs_], tmp[:, cs_], ALU.mult), reads=['D_ba', 'D_tmp'], writes=['D_Bt'])
                        p.op('pool', lambda e: e.tensor_tensor(Kt[:, cs_], kd[:, cs_], tmp[:, cs_], ALU.mult), reads=['D_kd', 'D_tmp'], writes=['D_Kt'])
                    p.op('act', lambda e: e.copy(Vr[:], v_), reads=['D_rw'], writes=['D_Vr'])
                    p.op('act', lambda e: e.copy(Abr[:], Ab[:]), reads=['D_Ab'], writes=['D_Abr'])
                    for (src, dstT, nm) in ((Ab, AbT, 'D_AbT'), (Rb, RbT, 'D_RbT'), (Bb, BbT, 'D_BbT'), (Kb, KbT, 'D_KbT')):
                        srcn = {'D_AbT': 'D_Ab', 'D_RbT': 'D_Rb', 'D_BbT': 'D_Bb', 'D_KbT': 'D_Kb'}[nm]
                        for h4 in range(4):
                            bt = nb()
                            for hl in range(4):
                                h = h4 * 4 + hl
                                p.op('pe', lambda e: e.transpose(pb[bt][0:64, hl * 128:(hl + 1) * 128], src[:, h * 64:(h + 1) * 64], ident_f[:]),
                                     reads=[srcn, 'ident_f'], writes=[('D_pb', bt)])
                            dst = dstT[:, h4 * 4:(h4 + 1) * 4, :].rearrange("p a b -> p (a b)")
                            if h4 % 2 == 0:
                                p.op('act', lambda e: e.copy(dst, pb[bt][0:64, :]), reads=[('D_pb', bt)], writes=[nm])
                            else:
                                p.op('dve', lambda e: e.tensor_copy(dst, pb[bt][0:64, :]), reads=[('D_pb', bt)], writes=[nm])
                    H4 = range(4)
                    qi = {}
                    for h4 in H4:
                        bA, bB, bC, bD, bE = nb(), nb(), nb(), nb(), nb()
                        for hl in range(4):
                            h = h4 * 4 + hl
                            sl = slice(hl * 128, (hl + 1) * 128)
                            p.op('pe', lambda e: e.matmul(pb[bA][:, sl], BbT[:, h, :], AbT[:, h, :], start=True, stop=True), reads=['D_BbT', 'D_AbT'], writes=[('D_pb', bA)])
                            p.op('pe', lambda e: e.matmul(pb[bB][:, sl], BbT[:, h, :], RbT[:, h, :], start=True, stop=True), reads=['D_BbT', 'D_RbT'], writes=[('D_pb', bB)])
                            p.op('pe', lambda e: e.matmul(pb[bC][:, sl], KbT[:, h, :], AbT[:, h, :], start=True, stop=True), reads=['D_KbT', 'D_AbT'], writes=[('D_pb', bC)])
                            p.op('pe', lambda e: e.matmul(pb[bD][:, sl], KbT[:, h, :], RbT[:, h, :], start=True, stop=True), reads=['D_KbT', 'D_RbT'], writes=[('D_pb', bD)])
                            p.op('pe', lambda e: e.matmul(pb[bE][:, sl], AbT[:, h, :], BbT[:, h, :], start=True, stop=True), reads=['D_BbT', 'D_AbT'], writes=[('D_pb', bE)])
                        qi[h4] = 0
                        p.op('dve', lambda e: e.tensor_tensor(v4(Q[h4][0][:]), v4(pb[bA][:, :]), b4(mS), ALU.mult), reads=[('D_pb', bA), 'D_mS'], writes=[('D_Q', h4, 0)])
                        p.op('dve', lambda e: e.tensor_tensor(v4(MrbT[h4][:]), v4(pb[bB][:, :]), b4(mI), ALU.mult), reads=[('D_pb', bB), 'D_mI'], writes=[('D_MrbT', h4)])
                        p.op('dve', lambda e: e.tensor_tensor(v4(LakT[h4][:]), v4(pb[bC][:, :]), b4(mS), ALU.mult), reads=[('D_pb', bC), 'D_mS'], writes=[('D_LakT', h4)])
                        p.op('dve', lambda e: e.tensor_tensor(v4(MrkT[h4][:]), v4(pb[bD][:, :]), b4(mI), ALU.mult), reads=[('D_pb', bD), 'D_mI'], writes=[('D_MrkT', h4)])
                        p.op('dve', lambda e: e.tensor_tensor(v4(QT[h4][0][:]), v4(pb[bE][:, :]), b4(mST), ALU.mult), reads=[('D_pb', bE), 'D_mST'], writes=[('D_QT', h4, 0)])
                        p.op('dve', lambda e: e.tensor_tensor(v4(P[h4][:]), v4(Q[h4][0][:]), b4(ident_f), ALU.add), reads=[('D_Q', h4, 0), 'ident_f'], writes=[('D_P', h4)])
                        p.op('act', lambda e: e.copy(Pr[h4][:], P[h4][:]), reads=[('D_P', h4)], writes=[('D_Pr', h4)])
                    for lvl in range(6):
                        bqT = {}; bq = {}; bp = {}
                        for h4 in H4:
                            q0 = qi[h4]
                            bqT[h4] = nb()
                            for hl in range(4):
                                sl = slice(hl * 128, (hl + 1) * 128)
                                p.op('pe', lambda e: e.matmul(pb[bqT[h4]][:, sl], Q[h4][q0][:, sl], QT[h4][q0][:, sl], start=True, stop=True),
                                     reads=[('D_Q', h4, q0), ('D_QT', h4, q0)], writes=[('D_pb', bqT[h4])])
                            if lvl < 5:
                                bq[h4] = nb()
                                for hl in range(4):
                                    sl = slice(hl * 128, (hl + 1) * 128)
                                    p.op('pe', lambda e: e.matmul(pb[bq[h4]][:, sl], QT[h4][q0][:, sl], Q[h4][q0][:, sl], start=True, stop=True),
                                         reads=[('D_Q', h4, q0), ('D_QT', h4, q0)], writes=[('D_pb', bq[h4])])
                        for h4 in H4:
                            qn = 1 - qi[h4]
                            p.op('act', lambda e: e.copy(QT[h4][qn][:], pb[bqT[h4]][:, :]), reads=[('D_pb', bqT[h4])], writes=[('D_QT', h4, qn)])
                            if lvl < 5:
                                p.op('act' if h4 % 2 else 'dve', (lambda e: e.copy(Q[h4][qn][:], pb[bq[h4]][:, :])) if h4 % 2 else (lambda e: e.tensor_copy(Q[h4][qn][:], pb[bq[h4]][:, :])),
                                     reads=[('D_pb', bq[h4])], writes=[('D_Q', h4, qn)])
                        for h4 in H4:
                            qn = 1 - qi[h4]
                            bp[h4] = nb()
                            for hl in range(4):
                                sl = slice(hl * 128, (hl + 1) * 128)
                                p.op('pe', lambda e: e.matmul(pb[bp[h4]][:, sl], QT[h4][qn][:, sl], Pr[h4][:, sl], start=True, stop=True),
                                     reads=[('D_QT', h4, qn), ('D_Pr', h4)], writes=[('D_pb', bp[h4])])
                        for h4 in H4:
                            p.op('dve', lambda e: e.tensor_tensor(P[h4][:], P[h4][:], pb[bp[h4]][:, :], ALU.add), reads=[('D_P', h4), ('D_pb', bp[h4])], writes=[('D_P', h4)])
                            p.op('act', lambda e: e.copy(Pr[h4][:], P[h4][:]), reads=[('D_P', h4)], writes=[('D_Pr', h4)])
                            qi[h4] = 1 - qi[h4]
                    bx = {}; bu = {}
                    for h4 in H4:
                        bx[h4] = nb()
                        for hl in range(4):
                            h = h4 * 4 + hl
                            p.op('pe', lambda e: e.matmul(pb[bx[h4]][:, hl * 64:(hl + 1) * 64], LakT[h4][:, hl * 128:(hl + 1) * 128], Vr[:, h * 64:(h + 1) * 64], start=True, stop=True),
                                 reads=[('D_LakT', h4), 'D_Vr'], writes=[('D_pb', bx[h4])])
                    for h4 in H4:
                        p.op('act', lambda e: e.copy(AXt[h4][:, :, 64:128], pb[bx[h4]][:, 0:256].rearrange("p (a b) -> p a b", a=4)), reads=[('D_pb', bx[h4])], writes=[('D_AX', h4)])
                        p.op('dve', lambda e: e.tensor_copy(AXt[h4][:, :, 0:64], Abr[:, h4 * 256:(h4 + 1) * 256].rearrange("p (a b) -> p a b", a=4)), reads=['D_Abr'], writes=[('D_AX', h4)])
                    for h4 in H4:
                        bu[h4] = nb()
                        for hl in range(4):
                            p.op('pe', lambda e: e.matmul(pb[bu[h4]][:, hl * 128:(hl + 1) * 128], Pr[h4][:, hl * 128:(hl + 1) * 128], AXt[h4][:, hl, :], start=True, stop=True),
                                 reads=[('D_Pr', h4), ('D_AX', h4)], writes=[('D_pb', bu[h4])])
                    for h4 in H4:
                        p.op('act', lambda e: e.copy(AU[h4][:].rearrange("p a b -> p (a b)"), pb[bu[h4]][:, :]), reads=[('D_pb', bu[h4])], writes=[('D_AU', h4)])
                    for h4 in H4:
                        br_, bg, bh, by = nb(), nb(), nb(), nb()
                        for hl in range(4):
                            h = h4 * 4 + hl
                            hc = slice(h * 64, (h + 1) * 64)
                            p.op('pe', lambda e: e.matmul(pb[br_][0:64, hl * 128:(hl + 1) * 128], AU[h4][:, hl, 0:64], MrbT[h4][:, hl * 128:(hl + 1) * 128], start=True, stop=True),
                                 reads=[('D_AU', h4), ('D_MrbT', h4)], writes=[('D_pb', br_)])
                            p.op('pe', lambda e: e.matmul(pb[bg][0:64, hl * 64:(hl + 1) * 64], AU[h4][:, hl, 0:64], Bt[:, hc], start=True, stop=False),
                                 reads=[('D_AU', h4), 'D_Bt'], writes=[('D_pb', bg)])
                            p.op('pe', lambda e: e.matmul(pb[bg][0:64, hl * 64:(hl + 1) * 64], identr[0:64, 0:64], ydg[:, hc], start=False, stop=True),
                                 reads=['D_identr', 'D_ydg'], writes=[('D_pb', bg)])
                            p.op('pe', lambda e: e.matmul(pb[bh][0:64, hl * 64:(hl + 1) * 64], Bt[:, hc], AU[h4][:, hl, 64:128], start=True, stop=False),
                                 reads=[('D_AU', h4), 'D_Bt'], writes=[('D_pb', bh)])
                            p.op('pe', lambda e: e.matmul(pb[bh][0:64, hl * 64:(hl + 1) * 64], Kt[:, hc], Vr[:, hc], start=False, stop=True),
                                 reads=['D_Kt', 'D_Vr'], writes=[('D_pb', bh)])
                            p.op('pe', lambda e: e.matmul(pb[by][:, hl * 64:(hl + 1) * 64], MrbT[h4][:, hl * 128:(hl + 1) * 128], AU[h4][:, hl, 64:128], start=True, stop=False),
                                 reads=[('D_AU', h4), ('D_MrbT', h4)], writes=[('D_pb', by)])
                            p.op('pe', lambda e: e.matmul(pb[by][:, hl * 64:(hl + 1) * 64], MrkT[h4][:, hl * 128:(hl + 1) * 128], Vr[:, hc], start=False, stop=True),
                                 reads=[('D_MrkT', h4), 'D_Vr'], writes=[('D_pb', by)])
                        p.op('dve', lambda e: e.tensor_tensor(RhT[:, h4 * 4:(h4 + 1) * 4, :].rearrange("p a b -> p (a b)"), pb[br_][0:64, :],
                                                             RbT[:, h4 * 4:(h4 + 1) * 4, :].rearrange("p a b -> p (a b)"), ALU.add),
                             reads=[('D_pb', br_), 'D_RbT'], writes=['D_RhT'])
                        p.op('act', lambda e: e.copy(GT[:, h4 * 256:(h4 + 1) * 256], pb[bg][0:64, 0:256]), reads=[('D_pb', bg)], writes=['D_GT'])
                        p.op('act', lambda e: e.copy(Hh[:, h4 * 256:(h4 + 1) * 256], pb[bh][0:64, 0:256]), reads=[('D_pb', bh)], writes=['D_H'])
                        p.op('dve', lambda e: e.tensor_copy(Yh[:, h4 * 256:(h4 + 1) * 256], pb[by][:, 0:256]), reads=[('D_pb', by)], writes=['D_Yh'])
                    for half in range(2):
                        bY = nb()
                        bS = nb()
                        for hh in range(8):
                            h = half * 8 + hh
                            hc = slice(h * 64, (h + 1) * 64)
                            p.op('pe', lambda e: e.matmul(pb[bY][:, hh * 64:(hh + 1) * 64], RhT[:, h, :], STr[:, hc], start=True, stop=True),
                                 reads=['D_RhT', 'D_STr'], writes=[('D_pb', bY)])
                            p.op('pe', lambda e: e.matmul(pb[bS][0:64, hh * 64:(hh + 1) * 64], GT[:, hc], STr[:, hc], start=True, stop=True),
                                 reads=['D_GT', 'D_STr'], writes=[('D_pb', bS)])
                        cs_ = slice(half * 512, (half + 1) * 512)
                        p.op('dve', lambda e: e.tensor_tensor(Yh[:, cs_], pb[bY][:, :], Yh[:, cs_], ALU.add), reads=[('D_pb', bY), 'D_Yh'], writes=['D_Yh'])
                        p.op('dve', lambda e: e.tensor_tensor(ST[:, cs_], pb[bS][0:64, :], Hh[:, cs_], ALU.add), reads=[('D_pb', bS), 'D_H'], writes=[('D_ST', half)])
                    p.op('act', lambda e: e.copy(STr[:], ST[:]), reads=[('D_ST', 0), ('D_ST', 1)], writes=['D_STr'])
                    p.dma('sp', ysc[d, t0:t0 + 128, :], Yh[:], reads=['D_Yh'], writes=[('ysc', d, c)])
        p.barrier()
        with ExitStack() as st:
            lnw = sb(st, "D2_lnw", [128, 1024]); lnb = sb(st, "D2_lnb", [128, 1024])
            p.dma('sp', lnw[:], rwkv_ln_w[l:l + 1, :].partition_broadcast(128), writes=['D2_lnw'])
            p.dma('sp', lnb[:], rwkv_ln_b[l:l + 1, :].partition_broadcast(128), writes=['D2_lnb'])
            y0 = sb(st, "D2_y0", [128, 1024]); y1 = sb(st, "D2_y1", [128, 1024]); vt = sb(st, "D2_v", [128, 1024]); sq = sb(st, "D2_sq", [128, 1024])
            b0t = sb(st, "D2_b0", [128, 16]); b1t = sb(st, "D2_b1", [128, 16]); mean = sb(st, "D2_mean", [128, 16]); var = sb(st, "D2_var", [128, 16])
            eps2 = sb(st, "D2_eps", [128, 1])
            p.op('dve', lambda e: e.memset(eps2[:], 64e-5), writes=['D2_eps'])
            v3 = lambda t: t[:].rearrange("p (h j) -> p h j", h=16)
            bc3 = lambda t: t[:].unsqueeze(2).broadcast_to([128, 16, 64])
            for i in range(NT):
                t0 = i * 128
                p.dma('sp', y0[:], ysc[0, t0:t0 + 128, :], reads=[('ysc', 0, i)], writes=['D2_y0'])
                p.dma('sp', y1[:], ysc[1, t0:t0 + 128, :], reads=[('ysc', 1, i)], writes=['D2_y1'])
                p.dma('sp', vt[:], rwc[t0:t0 + 128, 2048:3072], reads=[('rwc', i)], writes=['D2_v'])
                p.dma('sp', b0t[:], bon[0, t0:t0 + 128, :], reads=[('bon', 0, i)], writes=['D2_b0'])
                p.dma('sp', b1t[:], bon[1, t0:t0 + 128, :], reads=[('bon', 1, i)], writes=['D2_b1'])
                p.op('dve', lambda e: e.tensor_tensor(y0[:], y0[:], y1[:], ALU.add), reads=['D2_y0', 'D2_y1'], writes=['D2_y0'])
                p.op('dve', lambda e: e.tensor_reduce(mean[:], v3(y0), AX.X, ALU.add), reads=['D2_y0'], writes=['D2_mean'])
                p.op('dve', lambda e: e.tensor_scalar(mean[:], mean[:], 1.0 / 64, None, ALU.mult), reads=['D2_mean'], writes=['D2_mean'])
                p.op('dve', lambda e: e.tensor_tensor(v3(y0), v3(y0), bc3(mean), ALU.subtract), reads=['D2_y0', 'D2_mean'], writes=['D2_y0'])
                p.op('pool', lambda e: e.tensor_tensor(sq[:], y0[:], y0[:], ALU.mult), reads=['D2_y0'], writes=['D2_sq'])
                p.op('dve', lambda e: e.tensor_reduce(var[:], v3(sq), AX.X, ALU.add), reads=['D2_sq'], writes=['D2_var'])
                p.op('act', lambda e: e.activation(var[:], var[:], AF.Sqrt, bias=eps2[:], scale=1.0 / 64), reads=['D2_var', 'D2_eps'], writes=['D2_var'])
                p.op('dve', lambda e: e.reciprocal(var[:], var[:]), reads=['D2_var'], writes=['D2_var'])
                p.op('dve', lambda e: e.tensor_tensor(v3(y0), v3(y0), bc3(var), ALU.mult), reads=['D2_y0', 'D2_var'], writes=['D2_y0'])
                p.op('pool', lambda e: e.tensor_tensor(y0[:], y0[:], lnw[:], ALU.mult), reads=['D2_y0', 'D2_lnw'], writes=['D2_y0'])
                p.op('pool', lambda e: e.tensor_tensor(y0[:], y0[:], lnb[:], ALU.add), reads=['D2_y0', 'D2_lnb'], writes=['D2_y0'])
                p.op('dve', lambda e: e.tensor_tensor(b0t[:], b0t[:], b1t[:], ALU.add), reads=['D2_b0', 'D2_b1'], writes=['D2_b0'])
                p.op('dve', lambda e: e.tensor_tensor(v3(vt), v3(vt), bc3(b0t), ALU.mult), reads=['D2_v', 'D2_b0'], writes=['D2_v'])
                p.op('dve', lambda e: e.tensor_tensor(y0[:], y0[:], vt[:], ALU.add), reads=['D2_y0', 'D2_v'], writes=['D2_y0'])
                p.dma('sp', br[t0:t0 + 128, 2048:3072], y0[:], reads=['D2_y0'], writes=[('br', i, 2)])
        p.barrier()


    TWO_PI = float(2 * np.pi)

    def phase_C(l):
        with ExitStack() as st:
            TC = 512
            lr = sb(st, "C_lr", [128, 32]); li = sb(st, "C_li", [128, 32])
            for two in range(2):
                p.dma('sp', lr[two * 64:(two + 1) * 64, :], s5_lam_re[l, two::2, :].rearrange("q p -> p q"), writes=['C_lr'], allow_slow_non_contiguous=True)
                p.dma('sp', li[two * 64:(two + 1) * 64, :], s5_lam_im[l, two::2, :].rearrange("q p -> p q"), writes=['C_li'], allow_slow_non_contiguous=True)
            den = sb(st, "C_den", [128, 32]); t_a = sb(st, "C_ta", [128, 32]); t_b = sb(st, "C_tb", [128, 32]); t_c = sb(st, "C_tc", [128, 32])
            t_i = sb(st, "C_ti", [128, 32], I32)
            p.op('dve', lambda e: e.tensor_tensor(den[:], lr[:], lr[:], ALU.mult), reads=['C_lr'], writes=['C_den'])
            p.op('dve', lambda e: e.tensor_tensor(t_a[:], li[:], li[:], ALU.mult), reads=['C_li'], writes=['C_ta'])
            p.op('dve', lambda e: e.tensor_tensor(den[:], den[:], t_a[:], ALU.add), reads=['C_den', 'C_ta'], writes=['C_den'])
            p.op('dve', lambda e: e.reciprocal(den[:], den[:]), reads=['C_den'], writes=['C_den'])
            mag = [sb(st, f"C_mag{d}", [128, 32]) for d in range(2)]
            th = [sb(st, f"C_th{d}", [128, 32]) for d in range(2)]
            cre = [sb(st, f"C_cre{d}", [128, 32]) for d in range(2)]
            cim = [sb(st, f"C_cim{d}", [128, 32]) for d in range(2)]
            dtt = sb(st, "C_dt", [128, 32]); sn = sb(st, "C_sn", [128, 32]); cs = sb(st, "C_cs", [128, 32])

            def emit_sin(out, ang, n, key_out, key_ang, ti_, tf_, kti, ktf):
                p.op('dve', lambda e: e.tensor_scalar(ti_, ang, 1.0 / TWO_PI, None, ALU.mult), reads=[key_ang], writes=[kti])
                p.op('dve', lambda e: e.tensor_copy(tf_, ti_), reads=[kti], writes=[ktf])
                p.op('dve', lambda e: e.scalar_tensor_tensor(tf_, tf_, -TWO_PI, ang, ALU.mult, ALU.add), reads=[ktf, key_ang], writes=[ktf])
                p.op('dve', lambda e: e.tensor_scalar(tf_, tf_, float(np.pi), float(-np.pi), ALU.min, ALU.max), reads=[ktf], writes=[ktf])
                p.op('act', lambda e: e.activation(out, tf_, AF.Sin), reads=[ktf], writes=[key_out])

            for d in range(2):
                for two in range(2):
                    p.dma('sp', dtt[two * 64:(two + 1) * 64, :], s5_log_dt[l, d:d + 1, two::2].partition_broadcast(64), writes=['C_dt'],
                          allow_slow_non_contiguous=True)
                p.op('act', lambda e: e.activation(dtt[:], dtt[:], AF.Exp), reads=['C_dt'], writes=['C_dt'])
                p.op('dve', lambda e: e.tensor_tensor(t_a[:], lr[:], dtt[:], ALU.mult), reads=['C_lr', 'C_dt'], writes=['C_ta'])
                p.op('act', lambda e: e.activation(mag[d][:], t_a[:], AF.Exp), reads=['C_ta'], writes=[f'C_mag{d}'])
                p.op('dve', lambda e: e.tensor_tensor(th[d][:], li[:], dtt[:], ALU.mult), reads=['C_li', 'C_dt'], writes=[f'C_th{d}'])
                emit_sin(sn[:], th[d][:], 32, 'C_sn', f'C_th{d}', t_i[:], t_b[:], 'C_ti', 'C_tb')
                p.op('dve', lambda e: e.tensor_scalar(t_c[:], th[d][:], float(np.pi / 2), None, ALU.add), reads=[f'C_th{d}'], writes=['C_tc'])
                emit_sin(cs[:], t_c[:], 32, 'C_cs', 'C_tc', t_i[:], t_b[:], 'C_ti', 'C_tb')
                p.op('dve', lambda e: e.tensor_tensor(cs[:], cs[:], mag[d][:], ALU.mult), reads=['C_cs', f'C_mag{d}'], writes=['C_cs'])
                p.op('dve', lambda e: e.tensor_scalar(cs[:], cs[:], -1.0, None, ALU.add), reads=['C_cs'], writes=['C_cs'])
                p.op('dve', lambda e: e.tensor_tensor(sn[:], sn[:], mag[d][:], ALU.mult), reads=['C_sn', f'C_mag{d}'], writes=['C_sn'])
                p.op('dve', lambda e: e.tensor_tensor(t_a[:], cs[:], lr[:], ALU.mult), reads=['C_cs', 'C_lr'], writes=['C_ta'])
                p.op('dve', lambda e: e.tensor_tensor(t_b[:], sn[:], li[:], ALU.mult), reads=['C_sn', 'C_li'], writes=['C_tb'])
                p.op('dve', lambda e: e.tensor_tensor(t_a[:], t_a[:], t_b[:], ALU.add), reads=['C_ta', 'C_tb'], writes=['C_ta'])
                p.op('dve', lambda e: e.tensor_tensor(cre[d][:], t_a[:], den[:], ALU.mult), reads=['C_ta', 'C_den'], writes=[f'C_cre{d}'])
                p.op('dve', lambda e: e.tensor_tensor(t_a[:], sn[:], lr[:], ALU.mult), reads=['C_sn', 'C_lr'], writes=['C_ta'])
                p.op('dve', lambda e: e.tensor_tensor(t_b[:], cs[:], li[:], ALU.mult), reads=['C_cs', 'C_li'], writes=['C_tb'])
                p.op('dve', lambda e: e.tensor_tensor(t_a[:], t_a[:], t_b[:], ALU.subtract), reads=['C_ta', 'C_tb'], writes=['C_ta'])
                p.op('dve', lambda e: e.tensor_tensor(cim[d][:], t_a[:], den[:], ALU.mult), reads=['C_ta', 'C_den'], writes=[f'C_cim{d}'])
            WB = [[sb(st, f"C_WB{d}{ri}", [128, 16, 128]) for ri in range(2)] for d in range(2)]
            WC = [[sb(st, f"C_WC{d}{ri}", [128, 32, 64]) for ri in range(2)] for d in range(2)]
            pbs = [ps(st, f"C_pb{i}", [128, 512]) for i in range(8)]
            st2 = ExitStack()
            Bm = [sb(st2, f"C_Bm{ri}", [128, 32, 64]) for ri in range(2)]
            for ri, src in enumerate((s5_b_re, s5_b_im)):
                p.op('pool', lambda e: e.memset(Bm[ri][:], 0.0), writes=[f'C_Bm{ri}'])
                for two in range(2):
                    for qpar in range(2):
                        off = qpar * 32 + two * 16
                        p.dma('sp', Bm[ri][two * 64:(two + 1) * 64, qpar::2, off:off + 16],
                              src[l, (2 * qpar + two)::4, :, :].rearrange("m p c -> p m c"),
                              writes=[f'C_Bm{ri}'], allow_slow_non_contiguous=True)
            bbt = sb(st2, "C_bbt", [128, 32, 64]); bbt2 = sb(st2, "C_bbt2", [128, 32, 64])
            pbi = [0]

            def nb():
                i = pbi[0] % 8
                pbi[0] += 1
                return i
            b3 = lambda t: t[:].unsqueeze(2).broadcast_to([128, 32, 64])
            for d in range(2):
                for ri in range(2):
                    if ri == 0:
                        p.op('dve', lambda e: e.tensor_tensor(bbt[:], Bm[0][:], b3(cre[d]), ALU.mult), reads=['C_Bm0', f'C_cre{d}'], writes=['C_bbt'])
                        p.op('pool', lambda e: e.tensor_tensor(bbt2[:], Bm[1][:], b3(cim[d]), ALU.mult), reads=['C_Bm1', f'C_cim{d}'], writes=['C_bbt2'])
                        p.op('dve', lambda e: e.tensor_tensor(bbt[:], bbt[:], bbt2[:], ALU.subtract), reads=['C_bbt', 'C_bbt2'], writes=['C_bbt'])
                    else:
                        p.op('dve', lambda e: e.tensor_tensor(bbt[:], Bm[1][:], b3(cre[d]), ALU.mult), reads=['C_Bm1', f'C_cre{d}'], writes=['C_bbt'])
                        p.op('pool', lambda e: e.tensor_tensor(bbt2[:], Bm[0][:], b3(cim[d]), ALU.mult), reads=['C_Bm0', f'C_cim{d}'], writes=['C_bbt2'])
                        p.op('dve', lambda e: e.tensor_tensor(bbt[:], bbt[:], bbt2[:], ALU.add), reads=['C_bbt', 'C_bbt2'], writes=['C_bbt'])
                    for q in range(32):
                        bt = nb()
                        hb = (q % 4) // 2
                        qi_ = (q // 4) * 2 + q % 2
                        p.op('pe', lambda e: e.matmul(pbs[bt][hb * 64:(hb + 1) * 64, 0:128], bbt[:, q, :], ident_f[:], start=True, stop=True),
                             reads=['C_bbt', 'ident_f'], writes=[('C_pb', bt)])
                        p.op('act', lambda e: e.copy(WB[d][ri][hb * 64:(hb + 1) * 64, qi_, :], pbs[bt][hb * 64:(hb + 1) * 64, 0:128]),
                             reads=[('C_pb', bt)], writes=[f'C_WB{d}{ri}'])
            Cn = sb(st2, "C_Cn", [64, 32, 128])
            for d in range(2):
                for ri, src in enumerate((s5_c_re, s5_c_im)):
                    p.op('pool', lambda e: e.memset(Cn[:], 0.0), writes=['C_Cn'])
                    for two in range(2):
                        for qpar in range(2):
                            off = qpar * 32 + two * 16
                            p.dma('sp', Cn[off:off + 16, qpar::2, two * 64:(two + 1) * 64],
                                  src[l, d, (2 * qpar + two)::4, :, :].rearrange("m c p -> c m p"),
                                  writes=['C_Cn'], allow_slow_non_contiguous=True)
                    for q4 in range(16):
                        bt = nb()
                        for qq in range(2):
                            q = q4 * 2 + qq
                            p.op('pe', lambda e: e.transpose(pbs[bt][:, qq * 64:(qq + 1) * 64], Cn[:, q, :], ident_f[0:64, 0:64]),
                                 reads=['C_Cn', 'ident_f'], writes=[('C_pb', bt)])
                        dst = WC[d][ri][:, q4 * 2:(q4 + 1) * 2, :].rearrange("p a b -> p (a b)")
                        if ri == 0:
                            p.op('act', lambda e: e.copy(dst, pbs[bt][:, 0:128]), reads=[('C_pb', bt)], writes=[f'C_WC{d}{ri}'])
                        else:
                            p.op('act', lambda e: e.mul(dst, pbs[bt][:, 0:128], -1.0), reads=[('C_pb', bt)], writes=[f'C_WC{d}{ri}'])
            p.barrier()
            st2.close()
            ut = sb(st, "C_ut", [128, 128]); uT = sb(st, "C_uT", [128, S]); yacc = sb(st, "C_yacc", [128, S])
            iota1 = sb(st, "C_iota", [128, TC])
            p.dma('sp', iota1[:], c_iota[0:1, 0:TC].partition_broadcast(128), writes=['C_iota'])
            tfi = sb(st, "C_tfi", [128, TC], I32)
            mk4 = lambda nm, shp: [sb(st, f"{nm}{i}", shp) for i in range(4)]
            cosT = mk4("C_cosT", [128, TC]); sinT = mk4("C_sinT", [128, TC]); rtab = mk4("C_rtab", [128, TC])
            gre = mk4("C_gre", [128, TC]); gim = mk4("C_gim", [128, TC]); w1 = mk4("C_w1", [128, TC]); w2_ = mk4("C_w2", [128, TC])
            w3 = mk4("C_w3", [128, TC]); w4 = mk4("C_w4", [128, TC])
            hre = mk4("C_hre", [128, TC]); him = mk4("C_him", [128, TC]); carry = mk4("C_carry", [128, 2])
            ang = hre[0]; ang2 = hre[1]; tff = hre[2]; y3 = him[0]; y4 = him[1]
            dsk = sb(st, "C_dsk", [128, 8])
            p.dma('sp', dsk[:], s5_d[l, :].rearrange("(b c) -> c b", c=128), writes=['C_dsk'], allow_slow_non_contiguous=True)
            yo = sb(st, "C_yo", [128, 128])
            for cb in range(8):
                for i in range(NT):
                    p.dma('sp', ut[:], proj[i * 128:(i + 1) * 128, C_AU + cb * 128:C_AU + (cb + 1) * 128], reads=[('proj', i, 'all')], writes=['C_ut'])
                    bt = nb()
                    p.op('pe', lambda e: e.transpose(pbs[bt][:, 0:128], ut[:], ident_f[:]), reads=['C_ut', 'ident_f'], writes=[('C_pb', bt)])
                    p.op('act', lambda e: e.copy(uT[:, i * 128:(i + 1) * 128], pbs[bt][:, 0:128]), reads=[('C_pb', bt)], writes=[('C_uT', i // 4)])
                for d in range(2):
                    for qq in range(4):
                        q = cb * 4 + qq
                        p.op('dve', lambda e: e.tensor_scalar(ang[:], iota1[:], th[d][:, q:q + 1], None, ALU.mult), reads=['C_iota', f'C_th{d}'], writes=[('C_hre', 0)])
                        emit_sin(sinT[qq][:], ang[:], TC, ('C_sinT', qq), ('C_hre', 0), tfi[:], tff[:], 'C_tfi', ('C_hre', 2))
                        p.op('dve', lambda e: e.tensor_scalar(ang2[:], ang[:], float(np.pi / 2), None, ALU.add), reads=[('C_hre', 0)], writes=[('C_hre', 1)])
                        emit_sin(cosT[qq][:], ang2[:], TC, ('C_cosT', qq), ('C_hre', 1), tfi[:], tff[:], 'C_tfi', ('C_hre', 2))
                        p.op('act', lambda e: e.mul(rtab[qq][:], iota1[:], 0.0), reads=['C_iota'], writes=[('C_rtab', qq)])
                        p.op('dve', lambda e: e.tensor_scalar(rtab[qq][:], rtab[qq][:], mag[d][:, q:q + 1], None, ALU.add), reads=[('C_rtab', qq), f'C_mag{d}'], writes=[('C_rtab', qq)])
                        p.op('dve', lambda e: e.memset(carry[qq][:], 0.0), writes=[('C_carry', qq)])
                    chunks = range(S // TC) if d == 0 else range(S // TC - 1, -1, -1)
                    for ch in chunks:
                        tsl = slice(ch * TC, (ch + 1) * TC)
                        QS = range(4)
                        bre = {}; bim = {}; byq = {}
                        for qq in QS:
                            q = cb * 4 + qq
                            ps32 = slice((qq // 2) * 64, (qq // 2) * 64 + 64)
                            bre[qq], bim[qq] = nb(), nb()
                            p.op('pe', lambda e: e.matmul(pbs[bre[qq]][:, :], WB[d][0][ps32, (q // 4) * 2 + q % 2, :], uT[ps32, tsl], start=True, stop=True),
                                 reads=[f'C_WB{d}0', ('C_uT', ch)], writes=[('C_pb', bre[qq])])
                            p.op('pe', lambda e: e.matmul(pbs[bim[qq]][:, :], WB[d][1][ps32, (q // 4) * 2 + q % 2, :], uT[ps32, tsl], start=True, stop=True),
                                 reads=[f'C_WB{d}1', ('C_uT', ch)], writes=[('C_pb', bim[qq])])
                        Bre = lambda qq: pbs[bre[qq]][:, :] if d == 0 else pbs[bre[qq]][:, ::-1]
                        Bim = lambda qq: pbs[bim[qq]][:, :] if d == 0 else pbs[bim[qq]][:, ::-1]
                        K = lambda n, qq: (n, qq)
                        for qq in QS:
                            p.op('dve', lambda e: e.tensor_tensor(w1[qq][:], Bre(qq), cosT[qq][:], ALU.mult), reads=[('C_pb', bre[qq]), K('C_cosT', qq)], writes=[K('C_w1', qq)])
                            p.op('dve', lambda e: e.tensor_tensor(w2_[qq][:], Bim(qq), sinT[qq][:], ALU.mult), reads=[('C_pb', bim[qq]), K('C_sinT', qq)], writes=[K('C_w2', qq)])
                            p.op('dve', lambda e: e.tensor_tensor(w3[qq][:], Bim(qq), cosT[qq][:], ALU.mult), reads=[('C_pb', bim[qq]), K('C_cosT', qq)], writes=[K('C_w3', qq)])
                            p.op('dve', lambda e: e.tensor_tensor(w4[qq][:], Bre(qq), sinT[qq][:], ALU.mult), reads=[('C_pb', bre[qq]), K('C_sinT', qq)], writes=[K('C_w4', qq)])
                        for qq in QS:
                            p.op('dve', lambda e: e.tensor_tensor(w1[qq][:], w1[qq][:], w2_[qq][:], ALU.add), reads=[K('C_w1', qq), K('C_w2', qq)], writes=[K('C_w1', qq)])
                            p.op('dve', lambda e: e.tensor_tensor(w3[qq][:], w3[qq][:], w4[qq][:], ALU.subtract), reads=[K('C_w3', qq), K('C_w4', qq)], writes=[K('C_w3', qq)])
                        for qq in QS:
                            p.op('dve', lambda e: e.tensor_tensor_scan(gre[qq][:], rtab[qq][:], w1[qq][:], carry[qq][:, 0:1], ALU.mult, ALU.add),
                                 reads=[K('C_rtab', qq), K('C_w1', qq), K('C_carry', qq)], writes=[K('C_gre', qq)])
                        for qq in QS:
                            p.op('dve', lambda e: e.tensor_tensor_scan(gim[qq][:], rtab[qq][:], w3[qq][:], carry[qq][:, 1:2], ALU.mult, ALU.add),
                                 reads=[K('C_rtab', qq), K('C_w3', qq), K('C_carry', qq)], writes=[K('C_gim', qq)])
                            p.op('dve', lambda e: e.tensor_tensor(w1[qq][:], gre[qq][:], cosT[qq][:], ALU.mult), reads=[K('C_gre', qq), K('C_cosT', qq)], writes=[K('C_w1', qq)])
                            p.op('dve', lambda e: e.tensor_tensor(w4[qq][:], gre[qq][:], sinT[qq][:], ALU.mult), reads=[K('C_gre', qq), K('C_sinT', qq)], writes=[K('C_w4', qq)])
                        Hre = lambda qq: hre[qq][:] if d == 0 else hre[qq][:, ::-1]
                        Him = lambda qq: him[qq][:] if d == 0 else him[qq][:, ::-1]
                        for qq in QS:
                            p.op('dve', lambda e: e.tensor_tensor(w2_[qq][:], gim[qq][:], sinT[qq][:], ALU.mult), reads=[K('C_gim', qq), K('C_sinT', qq)], writes=[K('C_w2', qq)])
                            p.op('dve', lambda e: e.tensor_tensor(w3[qq][:], gim[qq][:], cosT[qq][:], ALU.mult), reads=[K('C_gim', qq), K('C_cosT', qq)], writes=[K('C_w3', qq)])
                        last = TC - 1 if d == 0 else 0
                        for qq in QS:
                            p.op('dve', lambda e: e.tensor_tensor(Hre(qq), w1[qq][:], w2_[qq][:], ALU.subtract), reads=[K('C_w1', qq), K('C_w2', qq)], writes=[K('C_hre', qq)])
                            p.op('dve', lambda e: e.tensor_tensor(Him(qq), w4[qq][:], w3[qq][:], ALU.add), reads=[K('C_w4', qq), K('C_w3', qq)], writes=[K('C_him', qq)])
                        for qq in QS:
                            q = cb * 4 + qq
                            ps32 = slice((qq // 2) * 64, (qq // 2) * 64 + 64)
                            p.op('act', lambda e: e.copy(carry[qq][:, 0:1], hre[qq][:, last:last + 1]), reads=[K('C_hre', qq)], writes=[K('C_carry', qq)])
                            p.op('act', lambda e: e.copy(carry[qq][:, 1:2], him[qq][:, last:last + 1]), reads=[K('C_him', qq)], writes=[K('C_carry', qq)])
                            by = nb()
                            byq[qq] = by
                            p.op('pe', lambda e: e.matmul(pbs[by][ps32, :], WC[d][0][:, q, :], hre[qq][:], start=True, stop=False),
                                 reads=[f'C_WC{d}0', K('C_hre', qq)], writes=[('C_pb', by)])
                            p.op('pe', lambda e: e.matmul(pbs[by][ps32, :], WC[d][1][:, q, :], him[qq][:], start=False, stop=True),
                                 reads=[f'C_WC{d}1', K('C_him', qq)], writes=[('C_pb', by)])
                        for qq in QS:
                            ps32 = slice((qq // 2) * 64, (qq // 2) * 64 + 64)
                            by = byq[qq]
                            if d == 0 and qq % 2 == 0:
                                p.op('act', lambda e: e.copy(yacc[ps32, tsl], pbs[by][ps32, :]), reads=[('C_pb', by)], writes=[('C_yacc', qq // 2, ch)])
                            else:
                                p.op('dve', lambda e: e.tensor_tensor(yacc[ps32, tsl], yacc[ps32, tsl], pbs[by][ps32, :], ALU.add),
                                     reads=[('C_pb', by), ('C_yacc', qq // 2, ch)], writes=[('C_yacc', qq // 2, ch)])
                for ch in range(S // TC):
                    tsl = slice(ch * TC, (ch + 1) * TC)
                    rk = [('C_yacc', qq, ch) for qq in range(2)]
                    p.op('dve', lambda e: e.scalar_tensor_tensor(y3[:], uT[:, tsl], dsk[:, cb:cb + 1], yacc[:, tsl], ALU.mult, ALU.add),
                         reads=rk + [('C_uT', ch), 'C_dsk'], writes=[('C_him', 0)])
                    p.op('pool', lambda e: e.tensor_tensor(y4[:], y3[:], y3[:], ALU.mult), reads=[('C_him', 0)], writes=[('C_him', 1)])
                    p.op('dve', lambda e: e.tensor_scalar(y4[:], y4[:], 0.044715, 1.0, ALU.mult, ALU.add), reads=[('C_him', 1)], writes=[('C_him', 1)])
                    p.op('dve', lambda e: e.tensor_tensor(y4[:], y4[:], y3[:], ALU.mult), reads=[('C_him', 1), ('C_him', 0)], writes=[('C_him', 1)])
                    p.op('act', lambda e: e.activation(y4[:], y4[:], AF.Sigmoid, scale=1.5957691216057308), reads=[('C_him', 1)], writes=[('C_him', 1)])
                    p.op('dve', lambda e: e.tensor_tensor(y3[:], y3[:], y4[:], ALU.mult), reads=[('C_him', 1), ('C_him', 0)], writes=[('C_him', 0)])
                    for i4_ in range(TC // 128):
                        i = ch * (TC // 128) + i4_
                        bt = nb()
                        p.op('pe', lambda e: e.transpose(pbs[bt][:, 0:128], y3[:, i4_ * 128:(i4_ + 1) * 128], ident_f[:]), reads=[('C_him', 0), 'ident_f'], writes=[('C_pb', bt)])
                        p.op('act', lambda e: e.copy(yo[:], pbs[bt][:, 0:128]), reads=[('C_pb', bt)], writes=['C_yo'])
                        p.dma('sp', ygd[i * 128:(i + 1) * 128, cb * 128:(cb + 1) * 128], yo[:], reads=['C_yo'], writes=[('ygd', i, cb)])
        p.barrier()
        with ExitStack() as st:
            gw = sb(st, "C2_gw", [128, 8, 1024], BF16)
            p.dma('pool', gw[:], s5_glu_w[l, :, :].rearrange("(k p) n -> p k n", p=128), writes=['C2_gw'])
            gb = sb(st, "C2_gb", [128, 1024])
            p.dma('sp', gb[:], s5_glu_b[l:l + 1, :].partition_broadcast(128), writes=['C2_gb'])
            yg = sb(st, "C2_yg", [128, 1024]); ygb = sb(st, "C2_ygb", [128, 1024], BF16); ygT = sb(st, "C2_ygT", [128, 8, 128], BF16)
            sg = sb(st, "C2_sg", [128, 1024])
            ptr = ps(st, "C2_pt", [128, 8, 128], BF16)
            pm = [ps(st, f"C2_pm{i}", [128, 512]) for i in range(2)]
            for i in range(NT):
                p.dma('sp', yg[:], ygd[i * 128:(i + 1) * 128, :], reads=[('ygd', i, cb) for cb in range(8)], writes=['C2_yg'])
                p.op('act', lambda e: e.copy(ygb[:], yg[:]), reads=['C2_yg'], writes=['C2_ygb'])
                for k in range(8):
                    p.op('pe', lambda e: e.transpose(ptr[:, k, :], ygb[:, k * 128:(k + 1) * 128], ident_b[:]), reads=['C2_ygb', 'ident_b'], writes=['C2_pt'])
                p.op('dve', lambda e: e.tensor_copy(ygT[:], ptr[:]), reads=['C2_pt'], writes=['C2_ygT'])
                for half in range(2):
                    cs_ = slice(half * 512, (half + 1) * 512)
                    for k in range(8):
                        p.op('pe', lambda e: e.matmul(pm[half][:, :], ygT[:, k, :], gw[:, k, cs_], start=(k == 0), stop=(k == 7)),
                             reads=['C2_ygT', 'C2_gw'], writes=[('C2_pm', half)])
                    p.op('dve', lambda e: e.tensor_tensor(sg[:, cs_], pm[half][:, :], gb[:, cs_], ALU.add), reads=[('C2_pm', half), 'C2_gb'], writes=['C2_sg'])
                p.op('act', lambda e: e.activation(sg[:], sg[:], AF.Sigmoid), reads=['C2_sg'], writes=['C2_sg'])
                p.op('dve', lambda e: e.tensor_tensor(sg[:], sg[:], yg[:], ALU.mult), reads=['C2_sg', 'C2_yg'], writes=['C2_sg'])
                p.dma('sp', br[i * 128:(i + 1) * 128, 0:1024], sg[:], reads=['C2_sg'], writes=[('br', i, 0)])
        p.barrier()


    def prologue_rope(ropec, ropes):
        with ExitStack() as st:
            pi_ = sb(st, "R_pi", [128, NT], I32); pf = sb(st, "R_pf", [128, NT]); ivf = sb(st, "R_ivf", [128, 32])
            ang = sb(st, "R_ang", [128, NT, 32]); ti_ = sb(st, "R_ti", [128, NT, 32], I32); tf_ = sb(st, "R_tf", [128, NT, 32])
            p.dma('sp', pi_[:], pos_in[0, :].rearrange("(i p) -> p i", p=128), writes=['R_pi'], allow_slow_non_contiguous=True)
            p.dma('sp', ivf[:], c_invfreq[0:1, :].partition_broadcast(128), writes=['R_ivf'])
            p.op('dve', lambda e: e.tensor_copy(pf[:], pi_[:]), reads=['R_pi'], writes=['R_pf'])
            p.op('dve', lambda e: e.tensor_tensor(ang[:], pf[:].unsqueeze(2).broadcast_to([128, NT, 32]),
                                                 ivf[:].unsqueeze(1).broadcast_to([128, NT, 32]), ALU.mult), reads=['R_pf', 'R_ivf'], writes=['R_ang'])
            for which, dst, key in ((0, ropes, 'rope_s'), (1, ropec, 'rope_c')):
                if which == 1:
                    p.op('dve', lambda e: e.tensor_scalar(ang[:], ang[:], float(np.pi / 2), None, ALU.add), reads=['R_ang'], writes=['R_ang'])
                p.op('dve', lambda e: e.tensor_scalar(ti_[:], ang[:], 1.0 / TWO_PI, None, ALU.mult), reads=['R_ang'], writes=['R_ti'])
                p.op('dve', lambda e: e.tensor_copy(tf_[:], ti_[:]), reads=['R_ti'], writes=['R_tf'])
                p.op('dve', lambda e: e.scalar_tensor_tensor(tf_[:], tf_[:], -TWO_PI, ang[:], ALU.mult, ALU.add), reads=['R_tf', 'R_ang'], writes=['R_tf'])
                p.op('dve', lambda e: e.tensor_scalar(tf_[:], tf_[:], float(np.pi), float(-np.pi), ALU.min, ALU.max), reads=['R_tf'], writes=['R_tf'])
                p.op('act', lambda e: e.activation(dst[:], tf_[:], AF.Sin), reads=['R_tf'], writes=[key])
        p.barrier()

    def phase_B(l):
        with ExitStack() as st:
            ropec = sb(st, "rope_c", [128, NT, 32]); ropes = sb(st, "rope_s", [128, NT, 32])
            prologue_rope(ropec, ropes)
            wuq = sb(st, "B_wuq", [128, 7, 1536], BF16); wukv = sb(st, "B_wukv", [128, 2, 2048], BF16)
            p.dma('pool', wuq[:], mla_w_uq[l, :, :].rearrange("(k p) n -> p k n", p=128), writes=['B_wuq'])
            p.dma('pool', wukv[:], mla_w_ukv[l, :, :].rearrange("(k p) n -> p k n", p=128), writes=['B_wukv'])
            gqa = sb(st, "B_gqa", [128, 896]); gkva = sb(st, "B_gkva", [128, 256]); gq = sb(st, "B_gq", [128, 192]); gk = sb(st, "B_gk", [128, 192])
            p.dma('sp', gqa[:], mla_q_a_norm[l:l + 1, :].partition_broadcast(128), writes=['B_gqa'])
            p.dma('sp', gkva[:], mla_kv_a_norm[l:l + 1, :].partition_broadcast(128), writes=['B_gkva'])
            p.dma('sp', gq[:], mla_q_norm[l:l + 1, :].partition_broadcast(128), writes=['B_gq'])
            p.dma('sp', gk[:], mla_k_norm[l:l + 1, :].partition_broadcast(128), writes=['B_gk'])
            lat = sb(st, "B_lat", [128, 1216]); latb = sb(st, "B_latb", [128, 1152], BF16); latT = sb(st, "B_latT", [128, 9, 128], BF16)
            ss = sb(st, "B_ss", [128, 2]); junk = sb(st, "B_junk", [128, 896], BF16)
            qk = [sb(st, f"B_qk{i}", [128, 8, 192]) for i in range(2)]
            sq = sb(st, "B_sq", [128, 8, 192]); hs = sb(st, "B_hs", [128, 8])
            r1 = sb(st, "B_r1", [128, 8, 32]); r2 = sb(st, "B_r2", [128, 8, 32]); r3 = sb(st, "B_r3", [128, 8, 32])
            qkb = sb(st, "B_qkb", [128, 8, 192], BF16); vb = sb(st, "B_vb", [128, 1024], BF16)
            tT = sb(st, "B_tT", [128, 16, 128], BF16)
            ptr = [ps(st, f"B_pt{i}", [128, 8, 128], BF16) for i in range(2)]
            pm = [ps(st, f"B_pm{i}", [128, 512]) for i in range(4)]
            pmi = [0]
            import os
            for i in range(int(os.environ.get("KNTB", NT))):
                t0 = i * 128
                p.dma('sp', lat[:], proj[t0:t0 + 128, C_CQ:C_CQ + 1216], reads=[('proj', i, 'all')], writes=['B_lat'])
                p.op('act', lambda e: e.activation(junk[:], lat[:, 0:896], AF.Square, accum_out=ss[:, 0:1]), reads=['B_lat'], writes=['B_junk', 'B_ss0'])
                p.op('act', lambda e: e.activation(junk[:, 0:256], lat[:, 896:1152], AF.Square, accum_out=ss[:, 1:2]), reads=['B_lat'], writes=['B_junk', 'B_ss1'])
                p.op('act', lambda e: e.activation(ss[:, 0:1], ss[:, 0:1], AF.Sqrt, bias=eps_t[:], scale=1.0 / 896), reads=['B_ss0', 'eps_t'], writes=['B_ss0'])
                p.op('act', lambda e: e.activation(ss[:, 1:2], ss[:, 1:2], AF.Sqrt, bias=eps_t[:], scale=1.0 / 256), reads=['B_ss1', 'eps_t'], writes=['B_ss1'])
                p.op('dve', lambda e: e.reciprocal(ss[:], ss[:]), reads=['B_ss0', 'B_ss1'], writes=['B_ss0', 'B_ss1'])
                p.op('dve', lambda e: e.scalar_tensor_tensor(latb[:, 0:896], lat[:, 0:896], ss[:, 0:1], gqa[:], ALU.mult, ALU.mult),
                     reads=['B_lat', 'B_ss0', 'B_gqa'], writes=['B_latb'])
                p.op('dve', lambda e: e.scalar_tensor_tensor(latb[:, 896:1152], lat[:, 896:1152], ss[:, 1:2], gkva[:], ALU.mult, ALU.mult),
                     reads=['B_lat', 'B_ss1', 'B_gkva'], writes=['B_latb'])
                BSTOP = int(os.environ.get("BSTOP", 9))
                if BSTOP <= 1:
                    continue
                for k in range(9):
                    pt = ptr[0] if k < 8 else ptr[1]
                    p.op('pe', lambda e: e.transpose(pt[:, k % 8, :], latb[:, k * 128:(k + 1) * 128], ident_b[:]), reads=['B_latb', 'ident_b'],
                         writes=[('B_pt', 0 if k < 8 else 1)])
                p.op('act', lambda e: e.copy(latT[:, 0:8, :], ptr[0][:]), reads=[('B_pt', 0)], writes=['B_latT'])
                p.op('dve', lambda e: e.tensor_copy(latT[:, 8, :], ptr[1][:, 0, :]), reads=[('B_pt', 1)], writes=['B_latT'])
                for c3 in range(3):
                    j = pmi[0] % 4
                    pmi[0] += 1
                    for k in range(7):
                        p.op('pe', lambda e: e.matmul(pm[j][:, :], latT[:, k, :], wuq[:, k, c3 * 512:(c3 + 1) * 512], start=(k == 0), stop=(k == 6)),
                             reads=['B_latT', 'B_wuq'], writes=[('B_pm', j)])
                    p.op('act', lambda e: e.copy(qk[0][:].rearrange("p h d -> p (h d)")[:, c3 * 512:(c3 + 1) * 512], pm[j][:, :]),
                         reads=[('B_pm', j)], writes=['B_qk0'])
                if BSTOP <= 2:
                    continue
                for c4 in range(4):
                    j = pmi[0] % 4
                    pmi[0] += 1
                    for k in range(2):
                        p.op('pe', lambda e: e.matmul(pm[j][:, :], latT[:, 7 + k, :], wukv[:, k, c4 * 512:(c4 + 1) * 512], start=(k == 0), stop=(k == 1)),
                             reads=['B_latT', 'B_wukv'], writes=[('B_pm', j)])
                    pv = pm[j][:, :].rearrange("p (h d) -> p h d", h=2)
                    BSKIP = os.environ.get("BSKIP", "")
                    if 'a' not in BSKIP:
                        p.op('act', lambda e: e.copy(qk[1][:, c4 * 2:(c4 + 1) * 2, 0:128], pv[:, :, 0:128]), reads=[('B_pm', j)], writes=['B_qk1'])
                    for hh in range(2):
                        hcol = (c4 * 2 + hh) * 128
                        p.op('act', lambda e: e.copy(vb[:, hcol:hcol + 128], pm[j][:, hh * 256 + 128:hh * 256 + 256]),
                             reads=[('B_pm', j)], writes=['B_vb'])
                if 'p' not in BSKIP:
                    p.op('pool', lambda e: e.tensor_copy(qk[1][:, :, 128:192], lat[:, 1152:1216].unsqueeze(1).broadcast_to([128, 8, 64])),
                         reads=['B_lat'], writes=['B_qk1'])
                if 'v' not in BSKIP:
                    p.dma('sp', v_d[t0:t0 + 128, :], vb[:], reads=['B_vb'], writes=[('v_d', i)])
                if BSTOP <= 3:
                    continue
                for which in range(2):
                    X = qk[which]
                    xk = f'B_qk{which}'
                    g = gq if which == 0 else gk
                    gk_ = 'B_gq' if which == 0 else 'B_gk'
                    p.op('pool', lambda e: e.tensor_tensor(sq[:], X[:], X[:], ALU.mult), reads=[xk], writes=['B_sq'])
                    p.op('dve', lambda e: e.tensor_reduce(hs[:], sq[:], AX.X, ALU.add), reads=['B_sq'], writes=['B_hs'])
                    p.op('act', lambda e: e.activation(hs[:], hs[:], AF.Sqrt, bias=eps_t[:], scale=1.0 / 192), reads=['B_hs', 'eps_t'], writes=['B_hs'])
                    p.op('dve', lambda e: e.reciprocal(hs[:], hs[:]), reads=['B_hs'], writes=['B_hs'])
                    p.op('dve', lambda e: e.tensor_tensor(X[:], X[:], hs[:].unsqueeze(2).broadcast_to([128, 8, 192]), ALU.mult), reads=[xk, 'B_hs'], writes=[xk])
                    p.op('pool', lambda e: e.tensor_tensor(X[:], X[:], g[:].unsqueeze(1).broadcast_to([128, 8, 192]), ALU.mult), reads=[xk, gk_], writes=[xk])
                    cb_ = ropec[:, i, :].unsqueeze(1).broadcast_to([128, 8, 32]); sb_ = ropes[:, i, :].unsqueeze(1).broadcast_to([128, 8, 32])
                    T1 = X[:, :, 128:160]; T2 = X[:, :, 160:192]
                    p.op('dve', lambda e: e.tensor_tensor(r1[:], T1, sb_, ALU.mult), reads=[xk, 'rope_s'], writes=['B_r1'])
                    p.op('dve', lambda e: e.tensor_tensor(r2[:], T2, sb_, ALU.mult), reads=[xk, 'rope_s'], writes=['B_r2'])
                    p.op('dve', lambda e: e.tensor_tensor(r3[:], T1, cb_, ALU.mult), reads=[xk, 'rope_c'], writes=['B_r3'])
                    p.op('dve', lambda e: e.tensor_tensor(T1, r3[:], r2[:], ALU.subtract), reads=['B_r3', 'B_r2'], writes=[xk])
                    p.op('dve', lambda e: e.tensor_tensor(r3[:], T2, cb_, ALU.mult), reads=[xk, 'rope_c'], writes=['B_r3'])
                    p.op('dve', lambda e: e.tensor_tensor(T2, r3[:], r1[:], ALU.add), reads=['B_r3', 'B_r1'], writes=[xk])
                    p.op('act', lambda e: e.copy(qkb[:], X[:]), reads=[xk], writes=['B_qkb'])
                    if BSTOP <= 4:
                        continue
                    for h in range(8):
                        pt = ptr[h % 2]
                        p.op('pe', lambda e: e.transpose(pt[:, 0, :], qkb[:, h, 0:128], ident_b[:]), reads=['B_qkb', 'ident_b'], writes=[('B_pt', h % 2)])
                        p.op('pe', lambda e: e.transpose(pt[0:64, 1, :], qkb[:, h, 128:192], ident_b[:]), reads=['B_qkb', 'ident_b'], writes=[('B_pt', h % 2)])
                        p.op('act', lambda e: e.copy(tT[:, 2 * h, :], pt[:, 0, :]), reads=[('B_pt', h % 2)], writes=[('B_tT', h)])
                        p.op('dve', lambda e: e.tensor_copy(tT[0:64, 2 * h + 1, :], pt[0:64, 1, :]), reads=[('B_pt', h % 2)], writes=[('B_tT', h)])
                        dstT = qT_d if which == 0 else kT_d
                        p.dma('sp', dstT[h, 0:128, t0:t0 + 128], tT[:, 2 * h, :], reads=[('B_tT', h)], writes=[('qkT', which, h, i)])
                        p.dma('sp', dstT[h, 128:192, t0:t0 + 128], tT[0:64, 2 * h + 1, :], reads=[('B_tT', h)], writes=[('qkT', which, h, i)])
        p.barrier()
        if 'b' in phases:
            return
        with ExitStack() as st:
            qT = sb(st, "B2_qT", [128, S], BF16); qTr = sb(st, "B2_qTr", [64, S], BF16)
            kT = sb(st, "B2_kT", [128, S], BF16); kTr = sb(st, "B2_kTr", [64, S], BF16)
            Va = sb(st, "B2_Va", [128, NT, 132], BF16)
            PT = [sb(st, f"B2_PT{i}", [128, 512], BF16) for i in range(2)]
            ob = sb(st, "B2_ob", [128, 128]); rs = sb(st, "B2_rs", [128, 1])
            psc = [ps(st, f"B2_ps{i}", [128, 512]) for i in range(2)]
            pac = [ps(st, f"B2_pa{i}", [128, 512]) for i in range(4)]
            p.op('dve', lambda e: e.memset(Va[:], 1.0), writes=['B2_Va'])
            SCALE = float(192 ** -0.5)
            it = 0
            for h in range(8):
                p.dma('sp', qT[:], qT_d[h, 0:128, :], writes=['B2_qT'])
                p.dma('sp', qTr[:], qT_d[h, 128:192, :], writes=['B2_qTr'])
                p.dma('sp', kT[:], kT_d[h, 0:128, :], writes=['B2_kT'])
                p.dma('sp', kTr[:], kT_d[h, 128:192, :], writes=['B2_kTr'])
                p.dma('sp', Va[:, :, 0:128], v_d[:, h * 128:(h + 1) * 128].rearrange("(i p) d -> p i d", p=128), writes=['B2_Va'])
                for qb in range(S // 512):
                    qs = slice(qb * 512, (qb + 1) * 512)
                    for kt in range(NT):
                        ks = slice(kt * 128, (kt + 1) * 128)
                        j = it % 2
                        it += 1
                        p.op('pe', lambda e: e.matmul(psc[j][:, :], kT[:, ks], qT[:, qs], start=True, stop=False), reads=['B2_kT', 'B2_qT'], writes=[('B2_ps', j)])
                        p.op('pe', lambda e: e.matmul(psc[j][:, :], kTr[:, ks], qTr[:, qs], start=False, stop=True), reads=['B2_kTr', 'B2_qTr'], writes=[('B2_ps', j)])
                        p.op('act', lambda e: e.activation(PT[j][:], psc[j][:, :], AF.Exp, scale=SCALE), reads=[('B2_ps', j)], writes=[('B2_PT', j)])
                        for sub in range(4):
                            p.op('pe', lambda e: e.matmul(pac[sub][:, 0:129], PT[j][:, sub * 128:(sub + 1) * 128], Va[:, kt, 0:129],
                                                         start=(kt == 0), stop=(kt == NT - 1)), reads=[('B2_PT', j), 'B2_Va'], writes=[('B2_pa', sub)])
                    for sub in range(4):
                        t0 = qb * 512 + sub * 128
                        p.op('dve', lambda e: e.reciprocal(rs[:], pac[sub][:, 128:129]), reads=[('B2_pa', sub)], writes=['B2_rs'])
                        p.op('dve', lambda e: e.tensor_scalar(ob[:], pac[sub][:, 0:128], rs[:], None, ALU.mult), reads=[('B2_pa', sub), 'B2_rs'], writes=['B2_ob'])
                        p.dma('sp', br[t0:t0 + 128, 1024 + h * 128:1024 + (h + 1) * 128], ob[:], reads=['B2_ob'], writes=[('br', t0 // 128, 1, h)])
        p.barrier()

    def phase_M(l):
        with ExitStack() as st:
            wk = sb(st, "M_wk", [128, 32, 1024], BF16)
            gm = sb(st, "M_gm", [128, D]); mt_ = sb(st, "M_mt", [128, D]); mb = sb(st, "M_mb", [128, D], BF16)
            memT = sb(st, "M_memT", [128, 32, 256], BF16)
            ss = sb(st, "M_ss", [128, 1]); hs = sb(st, "M_hs", [128, 4]); gqn = sb(st, "M_gqn", [128, 256]); gkn = sb(st, "M_gkn", [128, 256])
            Kt = sb(st, "M_K", [128, 1024]); sq = sb(st, "M_sq", [128, 1024]); Kb = sb(st, "M_Kb", [128, 1024], BF16)
            KmT = sb(st, "M_KmT", [128, 8, 256], BF16)
            Vm = sb(st, "M_Vm", [128, 2, 4, 260], BF16)
            ptr = [ps(st, f"M_pt{i}", [128, 8, 128], BF16) for i in range(2)]
            pm = [ps(st, f"M_pm{i}", [128, 512]) for i in range(2)]
            psc = [ps(st, f"M_ps{i}", [128, 512]) for i in range(2)]
            pac = [ps(st, f"M_pa{i}", [128, 512]) for i in range(2)]
            p.dma('sp', gm[:], mem_norm_g[l:l + 1, :].partition_broadcast(128), writes=['M_gm'])
            p.dma('sp', gqn[:], mem_q_norm[l:l + 1, :].partition_broadcast(128), writes=['M_gqn'])
            p.dma('sp', gkn[:], mem_k_norm[l:l + 1, :].partition_broadcast(128), writes=['M_gkn'])
            p.op('dve', lambda e: e.memset(Vm[:], 1.0), writes=['M_Vm'])
            for mt in range(2):
                p.dma('sp', mt_[:], mem_in[mt * 128:(mt + 1) * 128, :], writes=['M_mt'])
                p.op('act', lambda e: e.activation(mb[:], mt_[:], AF.Square, accum_out=ss[:]), reads=['M_mt'], writes=['M_mb', 'M_ss'])
                p.op('act', lambda e: e.activation(ss[:], ss[:], AF.Sqrt, bias=eps_t[:], scale=1.0 / D), reads=['M_ss', 'eps_t'], writes=['M_ss'])
                p.op('dve', lambda e: e.reciprocal(ss[:], ss[:]), reads=['M_ss'], writes=['M_ss'])
                p.op('dve', lambda e: e.scalar_tensor_tensor(mb[:], mt_[:], ss[:], gm[:], ALU.mult, ALU.mult), reads=['M_mt', 'M_ss', 'M_gm'], writes=['M_mb'])
                for k8 in range(4):
                    pt = ptr[k8 % 2]
                    for kk in range(8):
                        k = k8 * 8 + kk
                        p.op('pe', lambda e: e.transpose(pt[:, kk, :], mb[:, k * 128:(k + 1) * 128], ident_b[:]), reads=['M_mb', 'ident_b'], writes=[('M_pt', k8 % 2)])
                    p.op('act', lambda e: e.copy(memT[:, k8 * 8:(k8 + 1) * 8, mt * 128:(mt + 1) * 128], pt[:]), reads=[('M_pt', k8 % 2)], writes=['M_memT'])
            for which, wsrc in ((0, mem_w_k), (1, mem_w_v)):
                for k4 in range(4):
                    p.dma('pool', wk[:, k4 * 8:(k4 + 1) * 8, :], wsrc[l, k4 * 1024:(k4 + 1) * 1024, :].rearrange("(k p) n -> p k n", p=128), writes=['M_wk'])
                for mt in range(2):
                    for half in range(2):
                        for k in range(32):
                            p.op('pe', lambda e: e.matmul(pm[half][:, :], memT[:, k, mt * 128:(mt + 1) * 128], wk[:, k, half * 512:(half + 1) * 512],
                                                         start=(k == 0), stop=(k == 31)), reads=['M_memT', 'M_wk'], writes=[('M_pm', half)])
                        if which == 0:
                            p.op('act', lambda e: e.copy(Kt[:, half * 512:(half + 1) * 512], pm[half][:, :]), reads=[('M_pm', half)], writes=['M_K'])
                        else:
                            p.op('act', lambda e: e.copy(Vm[:, mt, half * 2:(half + 1) * 2, 0:256], pm[half][:, :].rearrange("p (h d) -> p h d", h=2)),
                                 reads=[('M_pm', half)], writes=['M_Vm'])
                    if which == 0:
                        K3 = Kt[:].rearrange("p (h d) -> p h d", h=4)
                        p.op('pool', lambda e: e.tensor_tensor(sq[:], Kt[:], Kt[:], ALU.mult), reads=['M_K'], writes=['M_sq'])
                        p.op('dve', lambda e: e.tensor_reduce(hs[:], sq[:].rearrange("p (h d) -> p h d", h=4), AX.X, ALU.add), reads=['M_sq'], writes=['M_hs'])
                        p.op('act', lambda e: e.activation(hs[:], hs[:], AF.Sqrt, bias=eps_t[:], scale=1.0 / 256), reads=['M_hs', 'eps_t'], writes=['M_hs'])
                        p.op('dve', lambda e: e.reciprocal(hs[:], hs[:]), reads=['M_hs'], writes=['M_hs'])
                        p.op('dve', lambda e: e.tensor_tensor(K3, K3, hs[:].unsqueeze(2).broadcast_to([128, 4, 256]), ALU.mult), reads=['M_K', 'M_hs'], writes=['M_K'])
                        p.op('dve', lambda e: e.tensor_tensor(Kb[:].rearrange("p (h d) -> p h d", h=4), K3, gkn[:].unsqueeze(1).broadcast_to([128, 4, 256]), ALU.mult),
                             reads=['M_K', 'M_gkn'], writes=['M_Kb'])
                        for k in range(8):
                            p.op('pe', lambda e: e.transpose(ptr[0][:, k, :], Kb[:, k * 128:(k + 1) * 128], ident_b[:]), reads=['M_Kb', 'ident_b'], writes=[('M_pt', 0)])
                        p.op('act', lambda e: e.copy(KmT[:, :, mt * 128:(mt + 1) * 128], ptr[0][:]), reads=[('M_pt', 0)], writes=['M_KmT'])
            qt = sb(st, "M_q", [128, 1024]); qb_ = sb(st, "M_qb", [128, 1024], BF16); qT = sb(st, "M_qT", [128, 8, 512], BF16)
            PT = sb(st, "M_PT", [128, 2, 512], BF16); ob = sb(st, "M_ob", [128, 1024]); rs = sb(st, "M_rs", [128, 1])
            for g4 in range(NT // 4):
                for ti in range(4):
                    i = g4 * 4 + ti
                    p.dma('sp', qt[:], proj[i * 128:(i + 1) * 128, C_MQ:C_MQ + 1024], reads=[('proj', i, 'all')], writes=['M_q'])
                    Q3 = qt[:].rearrange("p (h d) -> p h d", h=4)
                    p.op('pool', lambda e: e.tensor_tensor(sq[:], qt[:], qt[:], ALU.mult), reads=['M_q'], writes=['M_sq'])
                    p.op('dve', lambda e: e.tensor_reduce(hs[:], sq[:].rearrange("p (h d) -> p h d", h=4), AX.X, ALU.add), reads=['M_sq'], writes=['M_hs'])
                    p.op('act', lambda e: e.activation(hs[:], hs[:], AF.Sqrt, bias=eps_t[:], scale=1.0 / 256), reads=['M_hs', 'eps_t'], writes=['M_hs'])
                    p.op('dve', lambda e: e.reciprocal(hs[:], hs[:]), reads=['M_hs'], writes=['M_hs'])
                    p.op('dve', lambda e: e.tensor_tensor(Q3, Q3, hs[:].unsqueeze(2).broadcast_to([128, 4, 256]), ALU.mult), reads=['M_q', 'M_hs'], writes=['M_q'])
                    p.op('dve', lambda e: e.tensor_tensor(qb_[:].rearrange("p (h d) -> p h d", h=4), Q3, gqn[:].unsqueeze(1).broadcast_to([128, 4, 256]), ALU.mult),
                         reads=['M_q', 'M_gqn'], writes=['M_qb'])
                    for k in range(8):
                        p.op('pe', lambda e: e.transpose(ptr[ti % 2][:, k, :], qb_[:, k * 128:(k + 1) * 128], ident_b[:]), reads=['M_qb', 'ident_b'], writes=[('M_pt', ti % 2)])
                    p.op('act', lambda e: e.copy(qT[:, :, ti * 128:(ti + 1) * 128], ptr[ti % 2][:]), reads=[('M_pt', ti % 2)], writes=['M_qT'])
                for h in range(4):
                    for mt in range(2):
                        for dc in range(2):
                            p.op('pe', lambda e: e.matmul(psc[mt][:, :], KmT[:, h * 2 + dc, mt * 128:(mt + 1) * 128], qT[:, h * 2 + dc, :],
                                                         start=(dc == 0), stop=(dc == 1)), reads=['M_KmT', 'M_qT'], writes=[('M_ps', mt)])
                        p.op('act', lambda e: e.activation(PT[:, mt, :], psc[mt][:, :], AF.Exp, scale=1.0 / 16), reads=[('M_ps', mt)], writes=['M_PT'])
                    for ti in range(4):
                        i = g4 * 4 + ti
                        j = ti % 2
                        for mt in range(2):
                            p.op('pe', lambda e: e.matmul(pac[j][:, 0:257], PT[:, mt, ti * 128:(ti + 1) * 128], Vm[:, mt, h, 0:257],
                                                         start=(mt == 0), stop=(mt == 1)), reads=['M_PT', 'M_Vm'], writes=[('M_pa', j)])
                        p.op('dve', lambda e: e.reciprocal(rs[:], pac[j][:, 256:257]), reads=[('M_pa', j)], writes=['M_rs'])
                        p.op('dve', lambda e: e.tensor_scalar(ob[:, 0:256], pac[j][:, 0:256], rs[:], None, ALU.mult), reads=[('M_pa', j), 'M_rs'], writes=['M_ob'])
                        p.dma('sp', br[i * 128:(i + 1) * 128, 3072 + h * 256:3072 + (h + 1) * 256], ob[:, 0:256], reads=['M_ob'], writes=[('br', i, 3, h)])
        p.barrier()

    def phase_E(l, xsrc):
        for r in range(0, D, 512):
            p.dma('pool', wbf_out[r:r + 512, :], w_out[l, r:r + 512, :], writes=[('wbf_out', r)])
        with ExitStack() as st:
            bt = sb(st, "E_b", [128, D]); G = sb(st, "E_G", [128, D]); mg = sb(st, "E_mg", [128, D], BF16)
            bg = sb(st, "E_bg", [128, 3, 1024]); ss = sb(st, "E_ss", [128, 1]); junk = sb(st, "E_junk", [128, 1024], BF16)
            mT = sb(st, "E_mT", [128, 32, 1024], BF16)
            W = [sb(st, f"E_W{i}", [128, 32, 512], BF16) for i in range(2)]
            xt = [sb(st, f"E_x{i}", [128, 512]) for i in range(2)]
            ptr = [ps(st, f"E_pt{i}", [128, 8, 128], BF16) for i in range(2)]
            pmm = [ps(st, f"E_pm{i}", [128, 512]) for i in range(4)]
            p.dma('sp', bg[:].rearrange("p a b -> p (a b)"), branch_g[l:l + 1, :].partition_broadcast(128), writes=['E_bg'])
            gates = (C_AG, C_BG, C_CG, C_MG)
            wi = 0
            for g in range(S // 1024):
                for ti in range(8):
                    i = g * 8 + ti
                    t0 = i * 128
                    p.dma('sp', bt[:], br[t0:t0 + 128, :], reads=[('br', i, 'all')], writes=['E_b'])
                    for bi in range(4):
                        p.dma('sp', G[:, bi * 1024:(bi + 1) * 1024], proj[t0:t0 + 128, gates[bi]:gates[bi] + 1024], reads=[('proj', i, 'all')], writes=['E_G'])
                    p.op('act', lambda e: e.activation(G[:], G[:], AF.Silu), reads=['E_G'], writes=['E_G'])
                    for bi in range(4):
                        cs_ = slice(bi * 1024, (bi + 1) * 1024)
                        if bi == 2:
                            p.op('pool', lambda e: e.tensor_tensor(mg[:, cs_], bt[:, cs_], G[:, cs_], ALU.mult), reads=['E_b', 'E_G'], writes=['E_mg'])
                            continue
                        gi = {0: 0, 1: 1, 3: 2}[bi]
                        p.op('act', lambda e: e.activation(junk[:], bt[:, cs_], AF.Square, accum_out=ss[:]), reads=['E_b'], writes=['E_junk', 'E_ss'])
                        p.op('act', lambda e: e.activation(ss[:], ss[:], AF.Sqrt, bias=eps_t[:], scale=1.0 / 1024), reads=['E_ss', 'eps_t'], writes=['E_ss'])
                        p.op('dve', lambda e: e.reciprocal(ss[:], ss[:]), reads=['E_ss'], writes=['E_ss'])
                        p.op('dve', lambda e: e.scalar_tensor_tensor(bt[:, cs_], bt[:, cs_], ss[:], bg[:, gi, :], ALU.mult, ALU.mult), reads=['E_b', 'E_ss', 'E_bg'], writes=['E_b'])
                        p.op('pool', lambda e: e.tensor_tensor(mg[:, cs_], bt[:, cs_], G[:, cs_], ALU.mult), reads=['E_b', 'E_G'], writes=['E_mg'])
                    for k8 in range(4):
                        pt = ptr[k8 % 2]
                        for kk in range(8):
                            k = k8 * 8 + kk
                            p.op('pe', lambda e: e.transpose(pt[:, kk, :], mg[:, k * 128:(k + 1) * 128], ident_b[:]), reads=['E_mg', 'ident_b'], writes=[('E_pt', k8 % 2)])
                        dst = mT[:, k8 * 8:(k8 + 1) * 8, ti * 128:(ti + 1) * 128]
                        if k8 % 2 == 0:
                            p.op('act', lambda e: e.copy(dst, pt[:]), reads=[('E_pt', k8 % 2)], writes=[('E_mT', ti)])
                        else:
                            p.op('dve', lambda e: e.tensor_copy(dst, pt[:]), reads=[('E_pt', k8 % 2)], writes=[('E_mT', ti)])
                for ci in range(D // 512):
                    n0 = ci * 512
                    Wt = W[wi % 2]
                    for k4 in range(4):
                        p.dma('sp', Wt[:, k4 * 8:(k4 + 1) * 8, :], wbf_out[k4 * 1024:(k4 + 1) * 1024, n0:n0 + 512].rearrange("(k p) n -> p k n", p=128),
                              reads=[('wbf_out', k4 * 1024), ('wbf_out', k4 * 1024 + 512)], writes=[('E_W', wi % 2)])
                    for ti in range(8):
                        i = g * 8 + ti
                        t0 = i * 128
                        j = (ci * 8 + ti) % 4
                        pm = pmm[j]
                        X = xt[(ci * 8 + ti) % 2]
                        xk = ('E_x', (ci * 8 + ti) % 2)
                        p.dma('sp', X[:], xsrc[t0:t0 + 128, n0:n0 + 512], reads=[('y', i, ci)], writes=[xk])
                        for k in range(32):
                            p.op('pe', lambda e: e.matmul(pm[:, :], mT[:, k, ti * 128:(ti + 1) * 128], Wt[:, k, :], start=(k == 0), stop=(k == 31)),
                                 reads=[('E_mT', ti), ('E_W', wi % 2)], writes=[('E_pm', j)])
                        p.op('dve', lambda e: e.tensor_tensor(X[:], X[:], pm[:, :], ALU.add), reads=[('E_pm', j), xk], writes=[xk])
                        p.dma('sp', y_out[t0:t0 + 128, n0:n0 + 512], X[:], reads=[xk], writes=[('y', i, ci)])
                    wi += 1
        p.barrier()

    if dbg and 'A' not in phases:
        proj_in = din("proj_in", [S, NCOLS])
        for i in range(NT):
            p.dma('sp', proj[i * 128:(i + 1) * 128, :], proj_in[i * 128:(i + 1) * 128, :], writes=[('proj', i, 'all')])
        p.barrier()
    if dbg and 'E' in phases and len(phases) < 6:
        br_in = din("br_in", [S, 4096])
        for i in range(NT):
            p.dma('sp', br[i * 128:(i + 1) * 128, :], br_in[i * 128:(i + 1) * 128, :], writes=[('br', i, 'all')])
        p.barrier()
    for l in range(n_layers):
        xsrc = x_in if l == 0 else y_out
        if 'A' in phases:
            phase_A(l, xsrc)
        if 'D' in phases:
            phase_D(l)
        if 'C' in phases:
            phase_C(l)
        if 'M' in phases:
            phase_M(l)
        if 'B' in phases:
            phase_B(l)
        if 'E' in phases:
            phase_E(l, xsrc)
    p.barrier()
    es.close()
    print("instructions:", p.nins)
    nc.in_names = in_names
    return nc


def make_consts():
    r = np.arange(128)
    m = np.stack([r[:, None] < r[None, :], r[:, None] > r[None, :], r[:, None] <= r[None, :], r[:, None] >= r[None, :]]).astype(np.float32)
    return {"c_ident": np.eye(128, dtype=np.float32), "c_masks": m,
            "c_iota": np.arange(1, 513, dtype=np.float32)[None, :],
            "c_invfreq": (1.0 / (np.float32(10000.0) ** (np.arange(0, 64, 2, dtype=np.float32) / np.float32(64)))).astype(np.float32)[None, :]}


_NC_CACHE = {}


def kernel(**inputs):
    nb = 4
    if 'nc' not in _NC_CACHE:
        _NC_CACHE['nc'] = build()
    nc = _NC_CACHE['nc']
    cst = make_consts()
    shared = {}
    for n in nc.in_names:
        if n in cst:
            shared[n] = cst[n]
        elif n in ("x", "mem", "positions"):
            continue
        elif n == "rwkv_r_k":
            shared[n] = np.ascontiguousarray(np.asarray(inputs[n], dtype=np.float32).reshape(L, 1024))
        elif n == "branch_g":
            shared[n] = np.ascontiguousarray(np.asarray(inputs[n], dtype=np.float32).reshape(L, 3072))
        else:
            shared[n] = np.ascontiguousarray(np.asarray(inputs[n], dtype=np.float32))
    in_maps = []
    for b in range(nb):
        m = dict(shared)
        m["x"] = np.ascontiguousarray(np.asarray(inputs["x"][b], dtype=np.float32))
        m["mem"] = np.ascontiguousarray(np.asarray(inputs["mem"][b], dtype=np.float32))
        m["positions"] = np.ascontiguousarray(np.asarray(inputs["positions"][b:b + 1]).astype(np.int32))
        in_maps.append(m)
    res = run_bass_kernel_spmd(nc, in_maps, core_ids=list(range(nb)))
    return np.stack([np.asarray(r["y"], dtype=np.float32) for r in res.results], axis=0)
```

```python
import numpy as np
from contextlib import ExitStack
import concourse.bass as bass
import concourse.mybir as mybir
from concourse.bass_utils import run_bass_kernel_spmd

F32 = mybir.dt.float32
BF16 = mybir.dt.bfloat16
I32 = mybir.dt.int32
ALU = mybir.AluOpType
AF = mybir.ActivationFunctionType
AX = mybir.AxisListType

D = 4096
S = 4096
L = 4
NCOLS = 10688
NT = S // 128
EPS = 1e-6
C_AU, C_AG, C_CQ, C_CKV, C_KPE, C_BG, C_RW, C_CG, C_MQ, C_MG = 0, 1024, 2048, 2944, 3200, 3264, 4288, 7616, 8640, 9664
NDMA = 8


class Prog:
    def __init__(self, nc, es):
        self.nc = nc
        self.E = {'pe': nc.tensor, 'act': nc.scalar, 'dve': nc.vector, 'pool': nc.gpsimd, 'sp': nc.sync}
        self.sems = {}
        for e in ['pe', 'act', 'dve', 'pool']:
            self.sems[e] = es.enter_context(nc.semaphore('s_' + e))
        for q in ['sp', 'pool']:
            for i in range(NDMA):
                self.sems[('d', q, i)] = es.enter_context(nc.semaphore(f'd_{q}_{i}'))
        self.cnt = {e: 0 for e in ['pe', 'act', 'dve', 'pool']}
        self.dma_n = {'sp': 0, 'pool': 0}
        self.waited = {}
        self.bufs = {}
        self.nins = 0

    def _wait(self, eng, key, val):
        if self.waited.get((eng, key), 0) >= val:
            return
        self.E[eng].wait_ge(self.sems[key], val)
        self.waited[(eng, key)] = val

    def _deps(self, reads, writes):
        deps = {}
        for b in reads:
            st = self.bufs.get(b)
            if st and st[0] is not None:
                k, v = st[0]
                if deps.get(k, 0) < v:
                    deps[k] = v
        for b in writes:
            st = self.bufs.get(b)
            if st:
                if st[0] is not None:
                    k, v = st[0]
                    if deps.get(k, 0) < v:
                        deps[k] = v
                for k, v in st[1].items():
                    if deps.get(k, 0) < v:
                        deps[k] = v
        return deps

    def _commit(self, tk, reads, writes):
        k, v = tk
        for b in reads:
            st = self.bufs.get(b)
            if st is None:
                st = self.bufs[b] = [None, {}]
            if st[1].get(k, 0) < v:
                st[1][k] = v
        for b in writes:
            self.bufs[b] = [tk, {}]

    def op(self, eng, fn, reads=(), writes=()):
        deps = self._deps(reads, writes)
        for k, v in deps.items():
            if k == 'pe' and eng == 'pe':
                continue
            self._wait(eng, k, v)
        ins = fn(self.E[eng])
        self.cnt[eng] += 1
        ins.then_inc(self.sems[eng], 1)
        self._commit((eng, self.cnt[eng]), reads, writes)
        self.nins += 1

    def dma(self, q, out, in_, reads=(), writes=(), **kw):
        deps = self._deps(reads, writes)
        n = self.dma_n[q]
        self.dma_n[q] += 1
        key = ('d', q, n % NDMA)
        val = 16 * (n // NDMA + 1)
        if n >= NDMA:
            deps[key] = max(deps.get(key, 0), val - 16)
        for k, v in deps.items():
            self._wait(q, k, v)
        ins = self.E[q].dma_start(out=out, in_=in_, **kw)
        ins.then_inc(self.sems[key], 16)
        self._commit((key, val), reads, writes)
        self.nins += 1

    def barrier(self):
        for e in ['pe', 'act', 'dve', 'pool', 'sp']:
            for k in self.sems:
                if isinstance(k, tuple):
                    n = self.dma_n[k[1]]
                    v = 16 * ((n - k[2] + NDMA - 1) // NDMA) if n > k[2] else 0
                else:
                    v = self.cnt[k]
                if v > 0:
                    self._wait(e, k, v)
        self.bufs.clear()


def build(n_layers=L, phases="AMBCDE", dbg=False, RWDT=BF16):
    nc = bass.Bass("TRN2", target_bir_lowering=False)
    es = ExitStack()

    in_names = []

    def din(name, shape, dt=F32):
        in_names.append(name)
        return nc.dram_tensor(name, list(shape), dt, kind="ExternalInput").ap()

    def dscr(name, shape, dt=F32):
        return nc.dram_tensor(name, list(shape), dt, kind="ExternalOutput" if dbg else "Internal").ap()

    x_in = din("x", [S, D])
    mem_in = din("mem", [256, D])
    pos_in = din("positions", [1, S], I32)
    ln_g = din("ln_g", [L, D])
    w_in = din("w_in", [L, D, NCOLS]) if ('A' in phases or not dbg) else None
    w_out = din("w_out", [L, D, D]) if ('E' in phases or not dbg) else None
    cst_ident = din("c_ident", [128, 128])
    c_masks_t = din("c_masks", [4, 128, 128])
    c_masks = [c_masks_t[i, :, :] for i in range(4)]
    c_iota = din("c_iota", [1, 512])
    c_invfreq = din("c_invfreq", [1, 32])
    branch_g = din("branch_g", [L, 3072])
    mla_q_a_norm = din("mla_q_a_norm", [L, 896]); mla_kv_a_norm = din("mla_kv_a_norm", [L, 256])
    mla_w_uq = din("mla_w_uq", [L, 896, 1536]); mla_w_ukv = din("mla_w_ukv", [L, 256, 2048])
    mla_q_norm = din("mla_q_norm", [L, 192]); mla_k_norm = din("mla_k_norm", [L, 192])
    mem_norm_g = din("mem_norm_g", [L, D]); mem_w_k = din("mem_w_k", [L, D, 1024]) if ('M' in phases or not dbg) else None
    mem_w_v = din("mem_w_v", [L, D, 1024]) if ('M' in phases or not dbg) else None
    mem_q_norm = din("mem_q_norm", [L, 256]); mem_k_norm = din("mem_k_norm", [L, 256])
    s5_lam_re = din("s5_lam_re", [L, 64, 64]); s5_lam_im = din("s5_lam_im", [L, 64, 64])
    s5_b_re = din("s5_b_re", [L, 64, 64, 16]); s5_b_im = din("s5_b_im", [L, 64, 64, 16])
    s5_c_re = din("s5_c_re", [L, 2, 64, 16, 64]); s5_c_im = din("s5_c_im", [L, 2, 64, 16, 64])
    s5_log_dt = din("s5_log_dt", [L, 2, 64]); s5_d = din("s5_d", [L, 1024]); s5_glu_w = din("s5_glu_w", [L, 1024, 1024])
    s5_glu_b = din("s5_glu_b", [L, 1024])
    rwkv_mu = din("rwkv_mu", [L, 2, 3328]); rwkv_w0 = din("rwkv_w0", [L, 2, 1024]); rwkv_w2 = din("rwkv_w2", [L, 2, 64, 1024])
    rwkv_a0 = din("rwkv_a0", [L, 2, 1024]); rwkv_a2 = din("rwkv_a2", [L, 2, 64, 1024]); rwkv_k_k = din("rwkv_k_k", [L, 1024])
    rwkv_k_a = din("rwkv_k_a", [L, 1024]); rwkv_r_k = din("rwkv_r_k", [L, 1024]); rwkv_ln_w = din("rwkv_ln_w", [L, 1024])
    rwkv_ln_b = din("rwkv_ln_b", [L, 1024])
    y_out = nc.dram_tensor("y", [S, D], F32, kind="ExternalOutput").ap()
    proj = dscr("proj", [S, NCOLS])
    wbf_in = nc.dram_tensor("wbf_in", [D, NCOLS], BF16, kind="Internal").ap()
    rwc = dscr("rwc", [S, 3328])
    ysc = dscr("ysc", [2, S, 1024])
    bon = dscr("bon", [2, S, 16])
    br = dscr("br", [S, 4096])
    ygd = dscr("ygd", [S, 1024])
    wbf_out = nc.dram_tensor("wbf_out", [D, D], BF16, kind="Internal").ap()
    qT_d = nc.dram_tensor("qT_d", [8, 192, S], BF16, kind="Internal").ap()
    kT_d = nc.dram_tensor("kT_d", [8, 192, S], BF16, kind="Internal").ap()
    v_d = nc.dram_tensor("v_d", [S, 1024], BF16, kind="Internal").ap()

    p = Prog(nc, es)

    uniq = [0]

    def sb(stack, name, shape, dt=F32):
        uniq[0] += 1
        return stack.enter_context(nc.sbuf_tensor(f"{name}_{uniq[0]}", list(shape), dt))

    def ps(stack, name, shape, dt=F32):
        uniq[0] += 1
        return stack.enter_context(nc.psum_tensor(f"{name}_{uniq[0]}", list(shape), dt))

    ident_f = sb(es, "ident_f", [128, 128], F32)
    ident_b = sb(es, "ident_b", [128, 128], BF16)
    p.dma('sp', ident_f[:], cst_ident[:, :], writes=['ident_f'])
    p.op('dve', lambda e: e.tensor_copy(ident_b[:], ident_f[:]), reads=['ident_f'], writes=['ident_b'])

    eps_t = sb(es, "eps_t", [128, 1], F32)
    p.op('dve', lambda e: e.memset(eps_t[:], EPS), writes=['eps_t'])

    def phase_A(l, xsrc):
        for r in range(0, D, 512):
            p.dma('pool', wbf_in[r:r + 512, :], w_in[l, r:r + 512, :], writes=[('wbf_in', r)])
        with ExitStack() as st:
            gt = sb(st, "A_g", [128, D], F32)
            xt = sb(st, "A_x", [128, D], F32)
            hb = sb(st, "A_hb", [128, D], BF16)
            hT = sb(st, "A_hT", [128, 32, 1024], BF16)
            W = [sb(st, f"A_W{i}", [128, 32, 512], BF16) for i in range(2)]
            ob = [sb(st, f"A_ob{i}", [128, 512], F32) for i in range(4)]
            ss = sb(st, "A_ss", [128, 1], F32)
            rstd = sb(st, "A_rstd", [128, 1], F32)
            ptr = [ps(st, f"A_pt{i}", [128, 8, 128], BF16) for i in range(2)]
            pmm = [ps(st, f"A_pm{i}", [128, 512], F32) for i in range(4)]
            p.dma('sp', gt[:], ln_g[l:l + 1, :].partition_broadcast(128), writes=['A_g'])
            nch = (NCOLS + 511) // 512
            wi = 0
            for g in range(S // 1024):
                for ti in range(8):
                    t0 = g * 1024 + ti * 128
                    p.dma('sp', xt[:], xsrc[t0:t0 + 128, :], writes=['A_x'])
                    p.op('act', lambda e: e.activation(hb[:], xt[:], AF.Square, accum_out=ss[:]),
                         reads=['A_x'], writes=['A_hb', 'A_ss'])
                    p.op('act', lambda e: e.activation(rstd[:], ss[:], AF.Sqrt, bias=eps_t[:], scale=1.0 / D),
                         reads=['A_ss', 'eps_t'], writes=['A_rstd'])
                    p.op('dve', lambda e: e.reciprocal(rstd[:], rstd[:]), reads=['A_rstd'], writes=['A_rstd'])
                    p.op('dve', lambda e: e.scalar_tensor_tensor(hb[:], xt[:], rstd[:], gt[:], ALU.mult, ALU.mult),
                         reads=['A_x', 'A_rstd', 'A_g'], writes=['A_hb'])
                    for k8 in range(4):
                        pt = ptr[k8 % 2]
                        for kk in range(8):
                            k = k8 * 8 + kk
                            p.op('pe', lambda e: e.transpose(pt[:, kk, :], hb[:, k * 128:(k + 1) * 128], ident_b[:]),
                                 reads=['A_hb', 'ident_b'], writes=[('A_pt', k8 % 2)])
                        eng = 'act' if k8 % 2 == 0 else 'dve'
                        dst = hT[:, k8 * 8:(k8 + 1) * 8, ti * 128:(ti + 1) * 128]
                        if eng == 'act':
                            p.op('act', lambda e: e.copy(dst, pt[:]), reads=[('A_pt', k8 % 2)], writes=[('A_hT', ti)])
                        else:
                            p.op('dve', lambda e: e.tensor_copy(dst, pt[:]), reads=[('A_pt', k8 % 2)], writes=[('A_hT', ti)])
                def load_W(ci_, wi_):
                    n0_ = ci_ * 512
                    nw_ = min(512, NCOLS - n0_)
                    for k4 in range(4):
                        p.dma('sp', W[wi_ % 2][:, k4 * 8:(k4 + 1) * 8, 0:nw_],
                              wbf_in[k4 * 1024:(k4 + 1) * 1024, n0_:n0_ + nw_].rearrange("(k p) n -> p k n", p=128),
                              reads=[('wbf_in', (k4 * 1024) // 512 * 512), ('wbf_in', (k4 * 1024) // 512 * 512 + 512)],
                              writes=[('A_W', wi_ % 2)])
                load_W(0, wi)
                for ci in range(nch):
                    n0 = ci * 512
                    nw = min(512, NCOLS - n0)
                    Wt = W[wi % 2]
                    if ci + 1 < nch:
                        load_W(ci + 1, wi + 1)
                    for ti in range(8):
                        t0 = g * 1024 + ti * 128
                        j = (ci * 8 + ti) % 4
                        pm = pmm[j]
                        for k in range(32):
                            p.op('pe', lambda e: e.matmul(pm[:, 0:nw], hT[:, k, ti * 128:(ti + 1) * 128], Wt[:, k, 0:nw],
                                                         start=(k == 0), stop=(k == 31)),
                                 reads=[('A_hT', ti), ('A_W', wi % 2)], writes=[('A_pm', j)])
                        if j % 2 == 0:
                            p.op('act', lambda e: e.copy(ob[j][:, 0:nw], pm[:, 0:nw]), reads=[('A_pm', j)], writes=[('A_ob', j)])
                        else:
                            p.op('dve', lambda e: e.tensor_copy(ob[j][:, 0:nw], pm[:, 0:nw]), reads=[('A_pm', j)], writes=[('A_ob', j)])
                        p.dma('sp', proj[t0:t0 + 128, n0:n0 + nw], ob[j][:, 0:nw], reads=[('A_ob', j)],
                              writes=[('proj', t0 // 128, ci)])
                    wi += 1
        p.barrier()


    RW = 3328
    NEG_E = -float(np.exp(-0.5))
    RW_DT = RWDT

    def phase_D(l):
        with ExitStack() as st:
            mup = sb(st, "D0_mup", [128, RW]); mun = sb(st, "D0_mun", [128, RW]); m0 = sb(st, "D0_m0", [128, RW])
            ct = sb(st, "D0_c", [128, RW]); pt_ = sb(st, "D0_p", [128, RW]); nt = sb(st, "D0_n", [128, RW])
            p.dma('sp', mup[:], rwkv_mu[l, 0:1, :].partition_broadcast(128), writes=['D0_mup'])
            p.dma('sp', mun[:], rwkv_mu[l, 1:2, :].partition_broadcast(128), writes=['D0_mun'])
            p.op('dve', lambda e: e.tensor_tensor(m0[:], mup[:], mun[:], ALU.add), reads=['D0_mup', 'D0_mun'], writes=['D0_m0'])
            p.op('dve', lambda e: e.tensor_scalar(m0[:], m0[:], -1.0, 1.0, ALU.mult, ALU.add), reads=['D0_m0'], writes=['D0_m0'])
            for i in range(NT):
                t0 = i * 128
                p.dma('sp', ct[:], proj[t0:t0 + 128, C_RW:C_RW + RW], reads=[('proj', i, 'all')], writes=['D0_c'])
                if i == 0:
                    p.op('pool', lambda e: e.memset(pt_[:], 0.0), writes=['D0_p'])
                    p.dma('sp', pt_[1:128, :], proj[0:127, C_RW:C_RW + RW], reads=[('proj', 0, 'all')], writes=['D0_p'])
                else:
                    p.dma('sp', pt_[:], proj[t0 - 1:t0 + 127, C_RW:C_RW + RW], reads=[('proj', i, 'all'), ('proj', i - 1, 'all')], writes=['D0_p'])
                if i == NT - 1:
                    p.op('pool', lambda e: e.memset(nt[:], 0.0), writes=['D0_n'])
                    p.dma('sp', nt[0:127, :], proj[t0 + 1:t0 + 128, C_RW:C_RW + RW], reads=[('proj', i, 'all')], writes=['D0_n'])
                else:
                    p.dma('sp', nt[:], proj[t0 + 1:t0 + 129, C_RW:C_RW + RW], reads=[('proj', i, 'all'), ('proj', i + 1, 'all')], writes=['D0_n'])
                p.op('dve', lambda e: e.tensor_tensor(ct[:], ct[:], m0[:], ALU.mult), reads=['D0_c', 'D0_m0'], writes=['D0_c'])
                p.op('pool', lambda e: e.tensor_tensor(pt_[:], pt_[:], mup[:], ALU.mult), reads=['D0_p', 'D0_mup'], writes=['D0_p'])
                p.op('pool', lambda e: e.tensor_tensor(nt[:], nt[:], mun[:], ALU.mult), reads=['D0_n', 'D0_mun'], writes=['D0_n'])
                p.op('dve', lambda e: e.tensor_tensor(ct[:], ct[:], pt_[:], ALU.add), reads=['D0_c', 'D0_p'], writes=['D0_c'])
                p.op('dve', lambda e: e.tensor_tensor(ct[:], ct[:], nt[:], ALU.add), reads=['D0_c', 'D0_n'], writes=['D0_c'])
                p.dma('sp', rwc[t0:t0 + 128, :], ct[:], reads=['D0_c'], writes=[('rwc', i)])
        p.barrier()
        with ExitStack() as st:
            def bc(name, src):
                t = sb(st, name, [128, 1024])
                p.dma('sp', t[:], src.partition_broadcast(128), writes=[name])
                return t
            kk_c = bc("D_kk_c", rwkv_k_k[l:l + 1, :]); ka_c = bc("D_ka_c", rwkv_k_a[l:l + 1, :])
            rk_c = bc("D_rk_c", rwkv_r_k[l:l + 1, :])
            c1 = sb(st, "D_c1", [128, 1024])
            p.op('dve', lambda e: e.tensor_scalar(c1[:], ka_c[:], -1.0, 1.0, ALU.mult, ALU.add), reads=['D_ka_c'], writes=['D_c1'])
            w0_c = sb(st, "D_w0", [128, 1024]); a0_c = sb(st, "D_a0", [128, 1024])
            w2_t = sb(st, "D_w2", [64, 1024]); a2_t = sb(st, "D_a2", [64, 1024])
            mS = sb(st, "D_mS", [128, 128]); mI = sb(st, "D_mI", [128, 128]); mST = sb(st, "D_mST", [128, 128])
            imask = sb(st, "D_imask", [64, 1024])
            b4 = lambda t: t[:].unsqueeze(1).broadcast_to([128, 4, 128])
            v4 = lambda a: a.rearrange("p (a b) -> p a b", a=4)
            triI = sb(st, "D_triI", [128, 128]); triC = sb(st, "D_triC", [128, 128])
            identr = sb(st, "D_identr", [128, 128], RW_DT)
            p.op('dve', lambda e: e.tensor_copy(identr[:], ident_f[:]), reads=['ident_f'], writes=['D_identr'])
            for h in range(16):
                p.op('pool', lambda e: e.tensor_copy(imask[:, h * 64:(h + 1) * 64], ident_f[0:64, 0:64]), reads=['ident_f'], writes=['D_imask'])
            rw = sb(st, "D_rw", [128, RW])
            kk = sb(st, "D_kk", [128, 1024]); ld = sb(st, "D_ld", [128, 1024]); a_t = sb(st, "D_a", [128, 1024])
            kd = sb(st, "D_kd", [128, 1024]); ba = sb(st, "D_ba", [128, 1024]); tmp = sb(st, "D_tmp", [128, 1024])
            Ab = sb(st, "D_Ab", [128, 1024]); Rb = sb(st, "D_Rb", [128, 1024]); Bb = sb(st, "D_Bb", [128, 1024]); Kb = sb(st, "D_Kb", [128, 1024])
            Abr = sb(st, "D_Abr", [128, 1024], RW_DT)
            Bt = sb(st, "D_Bt", [128, 1024], RW_DT); Kt = sb(st, "D_Kt", [128, 1024], RW_DT); Vr = sb(st, "D_Vr", [128, 1024], RW_DT)
            ydg = sb(st, "D_ydg", [64, 1024], RW_DT)
            sm = sb(st, "D_sm", [128, 64]); smT = sb(st, "D_smT", [64, 2, 128])
            hs = sb(st, "D_hs", [128, 16]); hs2 = sb(st, "D_hs2", [128, 16])
            AbT = sb(st, "D_AbT", [64, 16, 128], RW_DT); RbT = sb(st, "D_RbT", [64, 16, 128], RW_DT)
            BbT = sb(st, "D_BbT", [64, 16, 128], RW_DT); KbT = sb(st, "D_KbT", [64, 16, 128], RW_DT)
            Q = [[sb(st, f"D_Q{g}{i}", [128, 512], RW_DT) for i in range(2)] for g in range(4)]
            QT = [[sb(st, f"D_QT{g}{i}", [128, 512], RW_DT) for i in range(2)] for g in range(4)]
            P = [sb(st, f"D_P{g}", [128, 512]) for g in range(4)]; Pr = [sb(st, f"D_Pr{g}", [128, 512], RW_DT) for g in range(4)]
            MrbT = [sb(st, f"D_MrbT{g}", [128, 512], RW_DT) for g in range(4)]; LakT = [sb(st, f"D_LakT{g}", [128, 512], RW_DT) for g in range(4)]
            MrkT = [sb(st, f"D_MrkT{g}", [128, 512], RW_DT) for g in range(4)]
            AXt = [sb(st, f"D_AX{g}", [128, 4, 128], RW_DT) for g in range(4)]; AU = [sb(st, f"D_AU{g}", [128, 4, 128], RW_DT) for g in range(4)]
            RhT = sb(st, "D_RhT", [64, 16, 128], RW_DT); GT = sb(st, "D_GT", [64, 1024], RW_DT)
            Hh = sb(st, "D_H", [64, 1024]); Yh = sb(st, "D_Yh", [128, 1024])
            ST = sb(st, "D_ST", [64, 1024]); STr = sb(st, "D_STr", [64, 1024], RW_DT)
            pb = [ps(st, f"D_pb{i}", [128, 512]) for i in range(8)]
            pbi = [0]

            def nb():
                i = pbi[0] % 8
                pbi[0] += 1
                return i

            for d in range(2):
                p.dma('sp', w0_c[:], rwkv_w0[l, d:d + 1, :].partition_broadcast(128), writes=['D_w0'])
                p.dma('sp', a0_c[:], rwkv_a0[l, d:d + 1, :].partition_broadcast(128), writes=['D_a0'])
                p.dma('sp', w2_t[:], rwkv_w2[l, d, :, :], writes=['D_w2'])
                p.dma('sp', a2_t[:], rwkv_a2[l, d, :, :], writes=['D_a2'])
                cm = c_masks
                p.dma('sp', mS[:], cm[0 if d == 0 else 1], writes=['D_mS'])
                p.dma('sp', mI[:], cm[2 if d == 0 else 3], writes=['D_mI'])
                p.dma('sp', mST[:], cm[1 if d == 0 else 0], writes=['D_mST'])
                p.dma('sp', triI[:], cm[2 if d == 0 else 3], writes=['D_triI'])
                p.dma('sp', triC[:], cm[1 if d == 0 else 0], writes=['D_triC'])
                p.op('dve', lambda e: e.memset(ST[:], 0.0), writes=['D_ST'])
                p.op('dve', lambda e: e.memset(STr[:], 0.0), writes=['D_STr'])
                order = range(NT) if d == 0 else range(NT - 1, -1, -1)
                for c in order:
                    t0 = c * 128
                    p.dma('sp', rw[:], rwc[t0:t0 + 128, :], reads=[('rwc', c)], writes=['D_rw'])
                    r_ = rw[:, 0:1024]; k_ = rw[:, 1024:2048]; v_ = rw[:, 2048:3072]
                    win = rw[:, 3072 + 64 * d:3136 + 64 * d]; ain = rw[:, 3200 + 64 * d:3264 + 64 * d]
                    p.op('dve', lambda e: e.tensor_tensor(kk[:], k_, kk_c[:], ALU.mult), reads=['D_rw', 'D_kk_c'], writes=['D_kk'])
                    p.op('pool', lambda e: e.tensor_tensor(tmp[:], kk[:], kk[:], ALU.mult), reads=['D_kk'], writes=['D_tmp'])
                    p.op('dve', lambda e: e.tensor_reduce(hs[:], tmp[:].rearrange("p (h j) -> p h j", h=16), AX.X, ALU.add), reads=['D_tmp'], writes=['D_hs'])
                    p.op('act', lambda e: e.activation(hs[:], hs[:], AF.Sqrt), reads=['D_hs'], writes=['D_hs'])
                    p.op('dve', lambda e: e.tensor_scalar(hs[:], hs[:], 1e-12, None, ALU.max), reads=['D_hs'], writes=['D_hs'])
                    p.op('dve', lambda e: e.reciprocal(hs[:], hs[:]), reads=['D_hs'], writes=['D_hs'])
                    p.op('dve', lambda e: e.tensor_tensor(kk[:].rearrange("p (h j) -> p h j", h=16), kk[:].rearrange("p (h j) -> p h j", h=16),
                                                         hs[:].unsqueeze(2).broadcast_to([128, 16, 64]), ALU.mult), reads=['D_kk', 'D_hs'], writes=['D_kk'])
                    p.op('act', lambda e: e.activation(sm[:], win, AF.Tanh), reads=['D_rw'], writes=['D_sm'])
                    b0 = nb()
                    p.op('pe', lambda e: e.transpose(pb[b0][0:64, 0:128], sm[:], ident_f[:]), reads=['D_sm', 'ident_f'], writes=[('D_pb', b0)])
                    p.op('pe', lambda e: e.transpose(pb[b0][0:64, 128:256], ain, ident_f[:]), reads=['D_rw', 'ident_f'], writes=[('D_pb', b0)])
                    p.op('act', lambda e: e.copy(smT[:].rearrange("p a b -> p (a b)"), pb[b0][0:64, 0:256]), reads=[('D_pb', b0)], writes=['D_smT'])
                    for half in range(2):
                        cs_ = slice(half * 512, (half + 1) * 512)
                        b1 = nb()
                        p.op('pe', lambda e: e.matmul(pb[b1][:, :], smT[:, 0, :], w2_t[:, cs_], start=True, stop=True),
                             reads=['D_smT', 'D_w2'], writes=[('D_pb', b1)])
                        p.op('dve', lambda e: e.tensor_tensor(ld[:, cs_], pb[b1][:, :], w0_c[:, cs_], ALU.add), reads=[('D_pb', b1), 'D_w0'], writes=['D_ld'])
                        b2 = nb()
                        p.op('pe', lambda e: e.matmul(pb[b2][:, :], smT[:, 1, :], a2_t[:, cs_], start=True, stop=True),
                             reads=['D_smT', 'D_a2'], writes=[('D_pb', b2)])
                        p.op('dve', lambda e: e.tensor_tensor(a_t[:, cs_], pb[b2][:, :], a0_c[:, cs_], ALU.add), reads=[('D_pb', b2), 'D_a0'], writes=['D_a'])
                    p.op('act', lambda e: e.activation(ld[:], ld[:], AF.Sigmoid), reads=['D_ld'], writes=['D_ld'])
                    p.op('act', lambda e: e.activation(a_t[:], a_t[:], AF.Sigmoid), reads=['D_a'], writes=['D_a'])
                    p.op('pool', lambda e: e.tensor_scalar(ld[:], ld[:], NEG_E, None, ALU.mult), reads=['D_ld'], writes=['D_ld'])
                    p.op('dve', lambda e: e.tensor_tensor(tmp[:], a_t[:], ka_c[:], ALU.mult), reads=['D_a', 'D_ka_c'], writes=['D_tmp'])
                    p.op('dve', lambda e: e.tensor_tensor(tmp[:], tmp[:], c1[:], ALU.add), reads=['D_tmp', 'D_c1'], writes=['D_tmp'])
                    p.op('dve', lambda e: e.tensor_tensor(kd[:], tmp[:], k_, ALU.mult), reads=['D_tmp', 'D_rw'], writes=['D_kd'])
                    p.op('pool', lambda e: e.tensor_tensor(ba[:], kk[:], a_t[:], ALU.mult), reads=['D_kk', 'D_a'], writes=['D_ba'])
                    p.op('pool', lambda e: e.tensor_tensor(tmp[:], kd[:], rk_c[:], ALU.mult), reads=['D_kd', 'D_rk_c'], writes=['D_tmp'])
                    p.op('pool', lambda e: e.tensor_tensor(tmp[:], tmp[:], r_, ALU.mult), reads=['D_tmp', 'D_rw'], writes=['D_tmp'])
                    p.op('dve', lambda e: e.tensor_reduce(hs2[:], tmp[:].rearrange("p (h j) -> p h j", h=16), AX.X, ALU.add), reads=['D_tmp'], writes=['D_hs2'])
                    p.dma('sp', bon[d, t0:t0 + 128, :], hs2[:], reads=['D_hs2'], writes=[('bon', d, c)])
                    for half in range(2):
                        cs_ = slice(half * 512, (half + 1) * 512)
                        bcs = nb()
                        p.op('pe', lambda e: e.matmul(pb[bcs][:, :], triI[:], ld[:, cs_], start=True, stop=True), reads=['D_triI', 'D_ld'], writes=[('D_pb', bcs)])
                        brm = nb()
                        p.op('pe', lambda e: e.matmul(pb[brm][:, :], triC[:], ld[:, cs_], start=True, stop=True), reads=['D_triC', 'D_ld'], writes=[('D_pb', brm)])
                        p.op('dve', lambda e: e.tensor_tensor(tmp[:, cs_], pb[bcs][:, :], ld[:, cs_], ALU.subtract), reads=[('D_pb', bcs), 'D_ld'], writes=['D_tmp'])
                        p.op('act', lambda e: e.activation(tmp[:, cs_], tmp[:, cs_], AF.Exp), reads=['D_tmp'], writes=['D_tmp'])
                        p.op('dve', lambda e: e.scalar_tensor_tensor(Ab[:, cs_], kk[:, cs_], -1.0, tmp[:, cs_], ALU.mult, ALU.mult), reads=['D_kk', 'D_tmp'], writes=['D_Ab'])
                        p.op('act', lambda e: e.activation(Rb[:, cs_], pb[bcs][:, :], AF.Exp), reads=[('D_pb', bcs)], writes=['D_Rb'])
                        p.op('act', lambda e: e.activation(Kt[0:64, cs_] if False else tmp[0:64, cs_], pb[brm][0:64, :], AF.Exp), reads=[('D_pb', brm), 'D_tmp'], writes=['D_tmp'])
                        p.op('dve', lambda e: e.tensor_tensor(tmp[0:64, cs_], tmp[0:64, cs_], Rb[0:64, cs_], ALU.mult), reads=['D_tmp', 'D_Rb'], writes=['D_tmp'])
                        p.op('dve', lambda e: e.tensor_tensor(ydg[:, cs_], tmp[0:64, cs_], imask[:, cs_], ALU.mult), reads=['D_tmp', 'D_imask'], writes=['D_ydg'])
                        p.op('pool', lambda e: e.tensor_tensor(Rb[:, cs_], Rb[:, cs_], r_[:, cs_] if False else rw[:, half * 512:(half + 1) * 512], ALU.mult), reads=['D_Rb', 'D_rw', 'D_tmp'], writes=['D_Rb'])
                        p.op('act', lambda e: e.activation(tmp[:, cs_], pb[bcs][:, :], AF.Exp, scale=-1.0), reads=[('D_pb', bcs), 'D_tmp', 'D_ydg'], writes=['D_tmp'])
                        p.op('dve', lambda e: e.tensor_tensor(Bb[:, cs_], ba[:, cs_], tmp[:, cs_], ALU.mult), reads=['D_ba', 'D_tmp'], writes=['D_Bb'])
                        p.op('pool', lambda e: e.tensor_tensor(Kb[:, cs_], kd[:, cs_], tmp[:, cs_], ALU.mult), reads=['D_kd', 'D_tmp'], writes=['D_Kb'])
                        p.op('act', lambda e: e.activation(tmp[:, cs_], pb[brm][:, :], AF.Exp), reads=[('D_pb', brm), 'D_tmp', 'D_Bb', 'D_Kb'], writes=['D_tmp'])
                        p.op('dve', lambda e: e.tensor_tensor(Bt[:, cs_], ba[:, cs_], tmp[:, cs_], ALU.mult), reads=['D_ba', 'D_tmp'], writes=['D_Bt'])
                        p.op('pool', lambda e: e.tensor_tensor(Kt[:, cs_], kd[:, cs_], tmp[:, cs_], ALU.mult), reads=['D_kd', 'D_tmp'], writes=['D_Kt'])
                    p.op('act', lambda e: e.copy(Vr[:], v_), reads=['D_rw'], writes=['D_Vr'])
                    p.op('act', lambda e: e.copy(Abr[:], Ab[:]), reads=['D_Ab'], writes=['D_Abr'])
                    for (src, dstT, nm) in ((Ab, AbT, 'D_AbT'), (Rb, RbT, 'D_RbT'), (Bb, BbT, 'D_BbT'), (Kb, KbT, 'D_KbT')):
                        srcn = {'D_AbT': 'D_Ab', 'D_RbT': 'D_Rb', 'D_BbT': 'D_Bb', 'D_KbT': 'D_Kb'}[nm]
                        for h4 in range(4):
                            bt = nb()
                            for hl in range(4):
                                h = h4 * 4 + hl
                                p.op('pe', lambda e: e.transpose(pb[bt][0:64, hl * 128:(hl + 1) * 128], src[:, h * 64:(h + 1) * 64], ident_f[:]),
                                     reads=[srcn, 'ident_f'], writes=[('D_pb', bt)])
                            dst = dstT[:, h4 * 4:(h4 + 1) * 4, :].rearrange("p a b -> p (a b)")
                            if h4 % 2 == 0:
                                p.op('act', lambda e: e.copy(dst, pb[bt][0:64, :]), reads=[('D_pb', bt)], writes=[nm])
                            else:
                                p.op('dve', lambda e: e.tensor_copy(dst, pb[bt][0:64, :]), reads=[('D_pb', bt)], writes=[nm])
                    H4 = range(4)
                    qi = {}
                    for h4 in H4:
                        bA, bB, bC, bD, bE = nb(), nb(), nb(), nb(), nb()
                        for hl in range(4):
                            h = h4 * 4 + hl
                            sl = slice(hl * 128, (hl + 1) * 128)
                            p.op('pe', lambda e: e.matmul(pb[bA][:, sl], BbT[:, h, :], AbT[:, h, :], start=True, stop=True), reads=['D_BbT', 'D_AbT'], writes=[('D_pb', bA)])
                            p.op('pe', lambda e: e.matmul(pb[bB][:, sl], BbT[:, h, :], RbT[:, h, :], start=True, stop=True), reads=['D_BbT', 'D_RbT'], writes=[('D_pb', bB)])
                            p.op('pe', lambda e: e.matmul(pb[bC][:, sl], KbT[:, h, :], AbT[:, h, :], start=True, stop=True), reads=['D_KbT', 'D_AbT'], writes=[('D_pb', bC)])
                            p.op('pe', lambda e: e.matmul(pb[bD][:, sl], KbT[:, h, :], RbT[:, h, :], start=True, stop=True), reads=['D_KbT', 'D_RbT'], writes=[('D_pb', bD)])
                            p.op('pe', lambda e: e.matmul(pb[bE][:, sl], AbT[:, h, :], BbT[:, h, :], start=True, stop=True), reads=['D_BbT', 'D_AbT'], writes=[('D_pb', bE)])
                        qi[h4] = 0
                        p.op('dve', lambda e: e.tensor_tensor(v4(Q[h4][0][:]), v4(pb[bA][:, :]), b4(mS), ALU.mult), reads=[('D_pb', bA), 'D_mS'], writes=[('D_Q', h4, 0)])
                        p.op('dve', lambda e: e.tensor_tensor(v4(MrbT[h4][:]), v4(pb[bB][:, :]), b4(mI), ALU.mult), reads=[('D_pb', bB), 'D_mI'], writes=[('D_MrbT', h4)])
                        p.op('dve', lambda e: e.tensor_tensor(v4(LakT[h4][:]), v4(pb[bC][:, :]), b4(mS), ALU.mult), reads=[('D_pb', bC), 'D_mS'], writes=[('D_LakT', h4)])
                        p.op('dve', lambda e: e.tensor_tensor(v4(MrkT[h4][:]), v4(pb[bD][:, :]), b4(mI), ALU.mult), reads=[('D_pb', bD), 'D_mI'], writes=[('D_MrkT', h4)])
                        p.op('dve', lambda e: e.tensor_tensor(v4(QT[h4][0][:]), v4(pb[bE][:, :]), b4(mST), ALU.mult), reads=[('D_pb', bE), 'D_mST'], writes=[('D_QT', h4, 0)])
                        p.op('dve', lambda e: e.tensor_tensor(v4(P[h4][:]), v4(Q[h4][0][:]), b4(ident_f), ALU.add), reads=[('D_Q', h4, 0), 'ident_f'], writes=[('D_P', h4)])
                        p.op('act', lambda e: e.copy(Pr[h4][:], P[h4][:]), reads=[('D_P', h4)], writes=[('D_Pr', h4)])
                    for lvl in range(6):
                        bqT = {}; bq = {}; bp = {}
                        for h4 in H4:
                            q0 = qi[h4]
                            bqT[h4] = nb()
                            for hl in range(4):
                                sl = slice(hl * 128, (hl + 1) * 128)
                                p.op('pe', lambda e: e.matmul(pb[bqT[h4]][:, sl], Q[h4][q0][:, sl], QT[h4][q0][:, sl], start=True, stop=True),
                                     reads=[('D_Q', h4, q0), ('D_QT', h4, q0)], writes=[('D_pb', bqT[h4])])
                            if lvl < 5:
                                bq[h4] = nb()
                                for hl in range(4):
                                    sl = slice(hl * 128, (hl + 1) * 128)
                                    p.op('pe', lambda e: e.matmul(pb[bq[h4]][:, sl], QT[h4][q0][:, sl], Q[h4][q0][:, sl], start=True, stop=True),
                                         reads=[('D_Q', h4, q0), ('D_QT', h4, q0)], writes=[('D_pb', bq[h4])])
                        for h4 in H4:
                            qn = 1 - qi[h4]
                            p.op('act', lambda e: e.copy(QT[h4][qn][:], pb[bqT[h4]][:, :]), reads=[('D_pb', bqT[h4])], writes=[('D_QT', h4, qn)])
                            if lvl < 5:
                                p.op('act' if h4 % 2 else 'dve', (lambda e: e.copy(Q[h4][qn][:], pb[bq[h4]][:, :])) if h4 % 2 else (lambda e: e.tensor_copy(Q[h4][qn][:], pb[bq[h4]][:, :])),
                                     reads=[('D_pb', bq[h4])], writes=[('D_Q', h4, qn)])
                        for h4 in H4:
                            qn = 1 - qi[h4]
                            bp[h4] = nb()
                            for hl in range(4):
                                sl = slice(hl * 128, (hl + 1) * 128)
                                p.op('pe', lambda e: e.matmul(pb[bp[h4]][:, sl], QT[h4][qn][:, sl], Pr[h4][:, sl], start=True, stop=True),
                                     reads=[('D_QT', h4, qn), ('D_Pr', h4)], writes=[('D_pb', bp[h4])])
                        for h4 in H4:
                            p.op('dve', lambda e: e.tensor_tensor(P[h4][:], P[h4][:], pb[bp[h4]][:, :], ALU.add), reads=[('D_P', h4), ('D_pb', bp[h4])], writes=[('D_P', h4)])
                            p.op('act', lambda e: e.copy(Pr[h4][:], P[h4][:]), reads=[('D_P', h4)], writes=[('D_Pr', h4)])
                            qi[h4] = 1 - qi[h4]
                    bx = {}; bu = {}
                    for h4 in H4:
                        bx[h4] = nb()
                        for hl in range(4):
                            h = h4 * 4 + hl
                            p.op('pe', lambda e: e.matmul(pb[bx[h4]][:, hl * 64:(hl + 1) * 64], LakT[h4][:, hl * 128:(hl + 1) * 128], Vr[:, h * 64:(h + 1) * 64], start=True, stop=True),
                                 reads=[('D_LakT', h4), 'D_Vr'], writes=[('D_pb', bx[h4])])
                    for h4 in H4:
                        p.op('act', lambda e: e.copy(AXt[h4][:, :, 64:128], pb[bx[h4]][:, 0:256].rearrange("p (a b) -> p a b", a=4)), reads=[('D_pb', bx[h4])], writes=[('D_AX', h4)])
                        p.op('dve', lambda e: e.tensor_copy(AXt[h4][:, :, 0:64], Abr[:, h4 * 256:(h4 + 1) * 256].rearrange("p (a b) -> p a b", a=4)), reads=['D_Abr'], writes=[('D_AX', h4)])
                    for h4 in H4:
                        bu[h4] = nb()
                        for hl in range(4):
                            p.op('pe', lambda e: e.matmul(pb[bu[h4]][:, hl * 128:(hl + 1) * 128], Pr[h4][:, hl * 128:(hl + 1) * 128], AXt[h4][:, hl, :], start=True, stop=True),
                                 reads=[('D_Pr', h4), ('D_AX', h4)], writes=[('D_pb', bu[h4])])
                    for h4 in H4:
                        p.op('act', lambda e: e.copy(AU[h4][:].rearrange("p a b -> p (a b)"), pb[bu[h4]][:, :]), reads=[('D_pb', bu[h4])], writes=[('D_AU', h4)])
                    for h4 in H4:
                        br_, bg, bh, by = nb(), nb(), nb(), nb()
                        for hl in range(4):
                            h = h4 * 4 + hl
                            hc = slice(h * 64, (h + 1) * 64)
                            p.op('pe', lambda e: e.matmul(pb[br_][0:64, hl * 128:(hl + 1) * 128], AU[h4][:, hl, 0:64], MrbT[h4][:, hl * 128:(hl + 1) * 128], start=True, stop=True),
                                 reads=[('D_AU', h4), ('D_MrbT', h4)], writes=[('D_pb', br_)])
                            p.op('pe', lambda e: e.matmul(pb[bg][0:64, hl * 64:(hl + 1) * 64], AU[h4][:, hl, 0:64], Bt[:, hc], start=True, stop=False),
                                 reads=[('D_AU', h4), 'D_Bt'], writes=[('D_pb', bg)])
                            p.op('pe', lambda e: e.matmul(pb[bg][0:64, hl * 64:(hl + 1) * 64], identr[0:64, 0:64], ydg[:, hc], start=False, stop=True),
                                 reads=['D_identr', 'D_ydg'], writes=[('D_pb', bg)])
                            p.op('pe', lambda e: e.matmul(pb[bh][0:64, hl * 64:(hl + 1) * 64], Bt[:, hc], AU[h4][:, hl, 64:128], start=True, stop=False),
                                 reads=[('D_AU', h4), 'D_Bt'], writes=[('D_pb', bh)])
                            p.op('pe', lambda e: e.matmul(pb[bh][0:64, hl * 64:(hl + 1) * 64], Kt[:, hc], Vr[:, hc], start=False, stop=True),
                                 reads=['D_Kt', 'D_Vr'], writes=[('D_pb', bh)])
                            p.op('pe', lambda e: e.matmul(pb[by][:, hl * 64:(hl + 1) * 64], MrbT[h4][:, hl * 128:(hl + 1) * 128], AU[h4][:, hl, 64:128], start=True, stop=False),
                                 reads=[('D_AU', h4), ('D_MrbT', h4)], writes=[('D_pb', by)])
                            p.op('pe', lambda e: e.matmul(pb[by][:, hl * 64:(hl + 1) * 64], MrkT[h4][:, hl * 128:(hl + 1) * 128], Vr[:, hc], start=False, stop=True),
                                 reads=[('D_MrkT', h4), 'D_Vr'], writes=[('D_pb', by)])
                        p.op('dve', lambda e: e.tensor_tensor(RhT[:, h4 * 4:(h4 + 1) * 4, :].rearrange("p a b -> p (a b)"), pb[br_][0:64, :],
                                                             RbT[:, h4 * 4:(h4 + 1) * 4, :].rearrange("p a b -> p (a b)"), ALU.add),
                             reads=[('D_pb', br_), 'D_RbT'], writes=['D_RhT'])
                        p.op('act', lambda e: e.copy(GT[:, h4 * 256:(h4 + 1) * 256], pb[bg][0:64, 0:256]), reads=[('D_pb', bg)], writes=['D_GT'])
                        p.op('act', lambda e: e.copy(Hh[:, h4 * 256:(h4 + 1) * 256], pb[bh][0:64, 0:256]), reads=[('D_pb', bh)], writes=['D_H'])
                        p.op('dve', lambda e: e.tensor_copy(Yh[:, h4 * 256:(h4 + 1) * 256], pb[by][:, 0:256]), reads=[('D_pb', by)], writes=['D_Yh'])
                    for half in range(2):
                        bY = nb()
                        bS = nb()
                        for hh in range(8):
                            h = half * 8 + hh
                            hc = slice(h * 64, (h + 1) * 64)
                            p.op('pe', lambda e: e.matmul(pb[bY][:, hh * 64:(hh + 1) * 64], RhT[:, h, :], STr[:, hc], start=True, stop=True),
                                 reads=['D_RhT', 'D_STr'], writes=[('D_pb', bY)])
                            p.op('pe', lambda e: e.matmul(pb[bS][0:64, hh * 64:(hh + 1) * 64], GT[:, hc], STr[:, hc], start=True, stop=True),
                                 reads=['D_GT', 'D_STr'], writes=[('D_pb', bS)])
                        cs_ = slice(half * 512, (half + 1) * 512)
                        p.op('dve', lambda e: e.tensor_tensor(Yh[:, cs_], pb[bY][:, :], Yh[:, cs_], ALU.add), reads=[('D_pb', bY), 'D_Yh'], writes=['D_Yh'])
                        p.op('dve', lambda e: e.tensor_tensor(ST[:, cs_], pb[bS][0:64, :], Hh[:, cs_], ALU.add), reads=[('D_pb', bS), 'D_H'], writes=[('D_ST', half)])
                    p.op('act', lambda e: e.copy(STr[:], ST[:]), reads=[('D_ST', 0), ('D_ST', 1)], writes=['D_STr'])
                    p.dma('sp', ysc[d, t0:t0 + 128, :], Yh[:], reads=['D_Yh'], writes=[('ysc', d, c)])
        p.barrier()
        with ExitStack() as st:
            lnw = sb(st, "D2_lnw", [128, 1024]); lnb = sb(st, "D2_lnb", [128, 1024])
            p.dma('sp', lnw[:], rwkv_ln_w[l:l + 1, :].partition_broadcast(128), writes=['D2_lnw'])
            p.dma('sp', lnb[:], rwkv_ln_b[l:l + 1, :].partition_broadcast(128), writes=['D2_lnb'])
            y0 = sb(st, "D2_y0", [128, 1024]); y1 = sb(st, "D2_y1", [128, 1024]); vt = sb(st, "D2_v", [128, 1024]); sq = sb(st, "D2_sq", [128, 1024])
            b0t = sb(st, "D2_b0", [128, 16]); b1t = sb(st, "D2_b1", [128, 16]); mean = sb(st, "D2_mean", [128, 16]); var = sb(st, "D2_var", [128, 16])
            eps2 = sb(st, "D2_eps", [128, 1])
            p.op('dve', lambda e: e.memset(eps2[:], 64e-5), writes=['D2_eps'])
            v3 = lambda t: t[:].rearrange("p (h j) -> p h j", h=16)
            bc3 = lambda t: t[:].unsqueeze(2).broadcast_to([128, 16, 64])
            for i in range(NT):
                t0 = i * 128
                p.dma('sp', y0[:], ysc[0, t0:t0 + 128, :], reads=[('ysc', 0, i)], writes=['D2_y0'])
                p.dma('sp', y1[:], ysc[1, t0:t0 + 128, :], reads=[('ysc', 1, i)], writes=['D2_y1'])
                p.dma('sp', vt[:], rwc[t0:t0 + 128, 2048:3072], reads=[('rwc', i)], writes=['D2_v'])
                p.dma('sp', b0t[:], bon[0, t0:t0 + 128, :], reads=[('bon', 0, i)], writes=['D2_b0'])
                p.dma('sp', b1t[:], bon[1, t0:t0 + 128, :], reads=[('bon', 1, i)], writes=['D2_b1'])
                p.op('dve', lambda e: e.tensor_tensor(y0[:], y0[:], y1[:], ALU.add), reads=['D2_y0', 'D2_y1'], writes=['D2_y0'])
                p.op('dve', lambda e: e.tensor_reduce(mean[:], v3(y0), AX.X, ALU.add), reads=['D2_y0'], writes=['D2_mean'])
                p.op('dve', lambda e: e.tensor_scalar(mean[:], mean[:], 1.0 / 64, None, ALU.mult), reads=['D2_mean'], writes=['D2_mean'])
                p.op('dve', lambda e: e.tensor_tensor(v3(y0), v3(y0), bc3(mean), ALU.subtract), reads=['D2_y0', 'D2_mean'], writes=['D2_y0'])
                p.op('pool', lambda e: e.tensor_tensor(sq[:], y0[:], y0[:], ALU.mult), reads=['D2_y0'], writes=['D2_sq'])
                p.op('dve', lambda e: e.tensor_reduce(var[:], v3(sq), AX.X, ALU.add), reads=['D2_sq'], writes=['D2_var'])
                p.op('act', lambda e: e.activation(var[:], var[:], AF.Sqrt, bias=eps2[:], scale=1.0 / 64), reads=['D2_var', 'D2_eps'], writes=['D2_var'])
                p.op('dve', lambda e: e.reciprocal(var[:], var[:]), reads=['D2_var'], writes=['D2_var'])
                p.op('dve', lambda e: e.tensor_tensor(v3(y0), v3(y0), bc3(var), ALU.mult), reads=['D2_y0', 'D2_var'], writes=['D2_y0'])
                p.op('pool', lambda e: e.tensor_tensor(y0[:], y0[:], lnw[:], ALU.mult), reads=['D2_y0', 'D2_lnw'], writes=['D2_y0'])
                p.op('pool', lambda e: e.tensor_tensor(y0[:], y0[:], lnb[:], ALU.add), reads=['D2_y0', 'D2_lnb'], writes=['D2_y0'])
                p.op('dve', lambda e: e.tensor_tensor(b0t[:], b0t[:], b1t[:], ALU.add), reads=['D2_b0', 'D2_b1'], writes=['D2_b0'])
                p.op('dve', lambda e: e.tensor_tensor(v3(vt), v3(vt), bc3(b0t), ALU.mult), reads=['D2_v', 'D2_b0'], writes=['D2_v'])
                p.op('dve', lambda e: e.tensor_tensor(y0[:], y0[:], vt[:], ALU.add), reads=['D2_y0', 'D2_v'], writes=['D2_y0'])
                p.dma('sp', br[t0:t0 + 128, 2048:3072], y0[:], reads=['D2_y0'], writes=[('br', i, 2)])
        p.barrier()


    TWO_PI = float(2 * np.pi)

    def phase_C(l):
        with ExitStack() as st:
            TC = 512
            lr = sb(st, "C_lr", [128, 32]); li = sb(st, "C_li", [128, 32])
            for two in range(2):
                p.dma('sp', lr[two * 64:(two + 1) * 64, :], s5_lam_re[l, two::2, :].rearrange("q p -> p q"), writes=['C_lr'], allow_slow_non_contiguous=True)
                p.dma('sp', li[two * 64:(two + 1) * 64, :], s5_lam_im[l, two::2, :].rearrange("q p -> p q"), writes=['C_li'], allow_slow_non_contiguous=True)
            den = sb(st, "C_den", [128, 32]); t_a = sb(st, "C_ta", [128, 32]); t_b = sb(st, "C_tb", [128, 32]); t_c = sb(st, "C_tc", [128, 32])
            t_i = sb(st, "C_ti", [128, 32], I32)
            p.op('dve', lambda e: e.tensor_tensor(den[:], lr[:], lr[:], ALU.mult), reads=['C_lr'], writes=['C_den'])
            p.op('dve', lambda e: e.tensor_tensor(t_a[:], li[:], li[:], ALU.mult), reads=['C_li'], writes=['C_ta'])
            p.op('dve', lambda e: e.tensor_tensor(den[:], den[:], t_a[:], ALU.add), reads=['C_den', 'C_ta'], writes=['C_den'])
            p.op('dve', lambda e: e.reciprocal(den[:], den[:]), reads=['C_den'], writes=['C_den'])
            mag = [sb(st, f"C_mag{d}", [128, 32]) for d in range(2)]
            th = [sb(st, f"C_th{d}", [128, 32]) for d in range(2)]
            cre = [sb(st, f"C_cre{d}", [128, 32]) for d in range(2)]
            cim = [sb(st, f"C_cim{d}", [128, 32]) for d in range(2)]
            dtt = sb(st, "C_dt", [128, 32]); sn = sb(st, "C_sn", [128, 32]); cs = sb(st, "C_cs", [128, 32])

            def emit_sin(out, ang, n, key_out, key_ang, ti_, tf_, kti, ktf):
                p.op('dve', lambda e: e.tensor_scalar(ti_, ang, 1.0 / TWO_PI, None, ALU.mult), reads=[key_ang], writes=[kti])
                p.op('dve', lambda e: e.tensor_copy(tf_, ti_), reads=[kti], writes=[ktf])
                p.op('dve', lambda e: e.scalar_tensor_tensor(tf_, tf_, -TWO_PI, ang, ALU.mult, ALU.add), reads=[ktf, key_ang], writes=[ktf])
                p.op('dve', lambda e: e.tensor_scalar(tf_, tf_, float(np.pi), float(-np.pi), ALU.min, ALU.max), reads=[ktf], writes=[ktf])
                p.op('act', lambda e: e.activation(out, tf_, AF.Sin), reads=[ktf], writes=[key_out])

            for d in range(2):
                for two in range(2):
                    p.dma('sp', dtt[two * 64:(two + 1) * 64, :], s5_log_dt[l, d:d + 1, two::2].partition_broadcast(64), writes=['C_dt'],
                          allow_slow_non_contiguous=True)
                p.op('act', lambda e: e.activation(dtt[:], dtt[:], AF.Exp), reads=['C_dt'], writes=['C_dt'])
                p.op('dve', lambda e: e.tensor_tensor(t_a[:], lr[:], dtt[:], ALU.mult), reads=['C_lr', 'C_dt'], writes=['C_ta'])
                p.op('act', lambda e: e.activation(mag[d][:], t_a[:], AF.Exp), reads=['C_ta'], writes=[f'C_mag{d}'])
                p.op('dve', lambda e: e.tensor_tensor(th[d][:], li[:], dtt[:], ALU.mult), reads=['C_li', 'C_dt'], writes=[f'C_th{d}'])
                emit_sin(sn[:], th[d][:], 32, 'C_sn', f'C_th{d}', t_i[:], t_b[:], 'C_ti', 'C_tb')
                p.op('dve', lambda e: e.tensor_scalar(t_c[:], th[d][:], float(np.pi / 2), None, ALU.add), reads=[f'C_th{d}'], writes=['C_tc'])
                emit_sin(cs[:], t_c[:], 32, 'C_cs', 'C_tc', t_i[:], t_b[:], 'C_ti', 'C_tb')
                p.op('dve', lambda e: e.tensor_tensor(cs[:], cs[:], mag[d][:], ALU.mult), reads=['C_cs', f'C_mag{d}'], writes=['C_cs'])
                p.op('dve', lambda e: e.tensor_scalar(cs[:], cs[:], -1.0, None, ALU.add), reads=['C_cs'], writes=['C_cs'])
                p.op('dve', lambda e: e.tensor_tensor(sn[:], sn[:], mag[d][:], ALU.mult), reads=['C_sn', f'C_mag{d}'], writes=['C_sn'])
                p.op('dve', lambda e: e.tensor_tensor(t_a[:], cs[:], lr[:], ALU.mult), reads=['C_cs', 'C_lr'], writes=['C_ta'])
                p.op('dve', lambda e: e.tensor_tensor(t_b[:], sn[:], li[:], ALU.mult), reads=['C_sn', 'C_li'], writes=['C_tb'])
                p.op('dve', lambda e: e.tensor_tensor(t_a[:], t_a[:], t_b[:], ALU.add), reads=['C_ta', 'C_tb'], writes=['C_ta'])
                p.op('dve', lambda e: e.tensor_tensor(cre[d][:], t_a[:], den[:], ALU.mult), reads=['C_ta', 'C_den'], writes=[f'C_cre{d}'])
                p.op('dve', lambda e: e.tensor_tensor(t_a[:], sn[:], lr[:], ALU.mult), reads=['C_sn', 'C_lr'], writes=['C_ta'])
                p.op('dve', lambda e: e.tensor_tensor(t_b[:], cs[:], li[:], ALU.mult), reads=['C_cs', 'C_li'], writes=['C_tb'])
                p.op('dve', lambda e: e.tensor_tensor(t_a[:], t_a[:], t_b[:], ALU.subtract), reads=['C_ta', 'C_tb'], writes=['C_ta'])
                p.op('dve', lambda e: e.tensor_tensor(cim[d][:], t_a[:], den[:], ALU.mult), reads=['C_ta', 'C_den'], writes=[f'C_cim{d}'])
            WB = [[sb(st, f"C_WB{d}{ri}", [128, 16, 128]) for ri in range(2)] for d in range(2)]
            WC = [[sb(st, f"C_WC{d}{ri}", [128, 32, 64]) for ri in range(2)] for d in range(2)]
            pbs = [ps(st, f"C_pb{i}", [128, 512]) for i in range(8)]
            st2 = ExitStack()
            Bm = [sb(st2, f"C_Bm{ri}", [128, 32, 64]) for ri in range(2)]
            for ri, src in enumerate((s5_b_re, s5_b_im)):
                p.op('pool', lambda e: e.memset(Bm[ri][:], 0.0), writes=[f'C_Bm{ri}'])
                for two in range(2):
                    for qpar in range(2):
                        off = qpar * 32 + two * 16
                        p.dma('sp', Bm[ri][two * 64:(two + 1) * 64, qpar::2, off:off + 16],
                              src[l, (2 * qpar + two)::4, :, :].rearrange("m p c -> p m c"),
                              writes=[f'C_Bm{ri}'], allow_slow_non_contiguous=True)
            bbt = sb(st2, "C_bbt", [128, 32, 64]); bbt2 = sb(st2, "C_bbt2", [128, 32, 64])
            pbi = [0]

            def nb():
                i = pbi[0] % 8
                pbi[0] += 1
                return i
            b3 = lambda t: t[:].unsqueeze(2).broadcast_to([128, 32, 64])
            for d in range(2):
                for ri in range(2):
                    if ri == 0:
                        p.op('dve', lambda e: e.tensor_tensor(bbt[:], Bm[0][:], b3(cre[d]), ALU.mult), reads=['C_Bm0', f'C_cre{d}'], writes=['C_bbt'])
                        p.op('pool', lambda e: e.tensor_tensor(bbt2[:], Bm[1][:], b3(cim[d]), ALU.mult), reads=['C_Bm1', f'C_cim{d}'], writes=['C_bbt2'])
                        p.op('dve', lambda e: e.tensor_tensor(bbt[:], bbt[:], bbt2[:], ALU.subtract), reads=['C_bbt', 'C_bbt2'], writes=['C_bbt'])
                    else:
                        p.op('dve', lambda e: e.tensor_tensor(bbt[:], Bm[1][:], b3(cre[d]), ALU.mult), reads=['C_Bm1', f'C_cre{d}'], writes=['C_bbt'])
                        p.op('pool', lambda e: e.tensor_tensor(bbt2[:], Bm[0][:], b3(cim[d]), ALU.mult), reads=['C_Bm0', f'C_cim{d}'], writes=['C_bbt2'])
                        p.op('dve', lambda e: e.tensor_tensor(bbt[:], bbt[:], bbt2[:], ALU.add), reads=['C_bbt', 'C_bbt2'], writes=['C_bbt'])
                    for q in range(32):
                        bt = nb()
                        hb = (q % 4) // 2
                        qi_ = (q // 4) * 2 + q % 2
                        p.op('pe', lambda e: e.matmul(pbs[bt][hb * 64:(hb + 1) * 64, 0:128], bbt[:, q, :], ident_f[:], start=True, stop=True),
                             reads=['C_bbt', 'ident_f'], writes=[('C_pb', bt)])
                        p.op('act', lambda e: e.copy(WB[d][ri][hb * 64:(hb + 1) * 64, qi_, :], pbs[bt][hb * 64:(hb + 1) * 64, 0:128]),
                             reads=[('C_pb', bt)], writes=[f'C_WB{d}{ri}'])
            Cn = sb(st2, "C_Cn", [64, 32, 128])
            for d in range(2):
                for ri, src in enumerate((s5_c_re, s5_c_im)):
                    p.op('pool', lambda e: e.memset(Cn[:], 0.0), writes=['C_Cn'])
                    for two in range(2):
                        for qpar in range(2):
                            off = qpar * 32 + two * 16
                            p.dma('sp', Cn[off:off + 16, qpar::2, two * 64:(two + 1) * 64],
                                  src[l, d, (2 * qpar + two)::4, :, :].rearrange("m c p -> c m p"),
                                  writes=['C_Cn'], allow_slow_non_contiguous=True)
                    for q4 in range(16):
                        bt = nb()
                        for qq in range(2):
                            q = q4 * 2 + qq
                            p.op('pe', lambda e: e.transpose(pbs[bt][:, qq * 64:(qq + 1) * 64], Cn[:, q, :], ident_f[0:64, 0:64]),
                                 reads=['C_Cn', 'ident_f'], writes=[('C_pb', bt)])
                        dst = WC[d][ri][:, q4 * 2:(q4 + 1) * 2, :].rearrange("p a b -> p (a b)")
                        if ri == 0:
                            p.op('act', lambda e: e.copy(dst, pbs[bt][:, 0:128]), reads=[('C_pb', bt)], writes=[f'C_WC{d}{ri}'])
                        else:
                            p.op('act', lambda e: e.mul(dst, pbs[bt][:, 0:128], -1.0), reads=[('C_pb', bt)], writes=[f'C_WC{d}{ri}'])
            p.barrier()
            st2.close()
            ut = sb(st, "C_ut", [128, 128]); uT = sb(st, "C_uT", [128, S]); yacc = sb(st, "C_yacc", [128, S])
            iota1 = sb(st, "C_iota", [128, TC])
            p.dma('sp', iota1[:], c_iota[0:1, 0:TC].partition_broadcast(128), writes=['C_iota'])
            tfi = sb(st, "C_tfi", [128, TC], I32)
            mk4 = lambda nm, shp: [sb(st, f"{nm}{i}", shp) for i in range(4)]
            cosT = mk4("C_cosT", [128, TC]); sinT = mk4("C_sinT", [128, TC]); rtab = mk4("C_rtab", [128, TC])
            gre = mk4("C_gre", [128, TC]); gim = mk4("C_gim", [128, TC]); w1 = mk4("C_w1", [128, TC]); w2_ = mk4("C_w2", [128, TC])
            w3 = mk4("C_w3", [128, TC]); w4 = mk4("C_w4", [128, TC])
            hre = mk4("C_hre", [128, TC]); him = mk4("C_him", [128, TC]); carry = mk4("C_carry", [128, 2])
            ang = hre[0]; ang2 = hre[1]; tff = hre[2]; y3 = him[0]; y4 = him[1]
            dsk = sb(st, "C_dsk", [128, 8])
            p.dma('sp', dsk[:], s5_d[l, :].rearrange("(b c) -> c b", c=128), writes=['C_dsk'], allow_slow_non_contiguous=True)
            yo = sb(st, "C_yo", [128, 128])
            for cb in range(8):
                for i in range(NT):
                    p.dma('sp', ut[:], proj[i * 128:(i + 1) * 128, C_AU + cb * 128:C_AU + (cb + 1) * 128], reads=[('proj', i, 'all')], writes=['C_ut'])
                    bt = nb()
                    p.op('pe', lambda e: e.transpose(pbs[bt][:, 0:128], ut[:], ident_f[:]), reads=['C_ut', 'ident_f'], writes=[('C_pb', bt)])
                    p.op('act', lambda e: e.copy(uT[:, i * 128:(i + 1) * 128], pbs[bt][:, 0:128]), reads=[('C_pb', bt)], writes=[('C_uT', i // 4)])
                for d in range(2):
                    for qq in range(4):
                        q = cb * 4 + qq
                        p.op('dve', lambda e: e.tensor_scalar(ang[:], iota1[:], th[d][:, q:q + 1], None, ALU.mult), reads=['C_iota', f'C_th{d}'], writes=[('C_hre', 0)])
                        emit_sin(sinT[qq][:], ang[:], TC, ('C_sinT', qq), ('C_hre', 0), tfi[:], tff[:], 'C_tfi', ('C_hre', 2))
                        p.op('dve', lambda e: e.tensor_scalar(ang2[:], ang[:], float(np.pi / 2), None, ALU.add), reads=[('C_hre', 0)], writes=[('C_hre', 1)])
                        emit_sin(cosT[qq][:], ang2[:], TC, ('C_cosT', qq), ('C_hre', 1), tfi[:], tff[:], 'C_tfi', ('C_hre', 2))
                        p.op('act', lambda e: e.mul(rtab[qq][:], iota1[:], 0.0), reads=['C_iota'], writes=[('C_rtab', qq)])
                        p.op('dve', lambda e: e.tensor_scalar(rtab[qq][:], rtab[qq][:], mag[d][:, q:q + 1], None, ALU.add), reads=[('C_rtab', qq), f'C_mag{d}'], writes=[('C_rtab', qq)])
                        p.op('dve', lambda e: e.memset(carry[qq][:], 0.0), writes=[('C_carry', qq)])
                    chunks = range(S // TC) if d == 0 else range(S // TC - 1, -1, -1)
                    for ch in chunks:
                        tsl = slice(ch * TC, (ch + 1) * TC)
                        QS = range(4)
                        bre = {}; bim = {}; byq = {}
                        for qq in QS:
                            q = cb * 4 + qq
                            ps32 = slice((qq // 2) * 64, (qq // 2) * 64 + 64)
                            bre[qq], bim[qq] = nb(), nb()
                            p.op('pe', lambda e: e.matmul(pbs[bre[qq]][:, :], WB[d][0][ps32, (q // 4) * 2 + q % 2, :], uT[ps32, tsl], start=True, stop=True),
                                 reads=[f'C_WB{d}0', ('C_uT', ch)], writes=[('C_pb', bre[qq])])
                            p.op('pe', lambda e: e.matmul(pbs[bim[qq]][:, :], WB[d][1][ps32, (q // 4) * 2 + q % 2, :], uT[ps32, tsl], start=True, stop=True),
                                 reads=[f'C_WB{d}1', ('C_uT', ch)], writes=[('C_pb', bim[qq])])
                        Bre = lambda qq: pbs[bre[qq]][:, :] if d == 0 else pbs[bre[qq]][:, ::-1]
                        Bim = lambda qq: pbs[bim[qq]][:, :] if d == 0 else pbs[bim[qq]][:, ::-1]
                        K = lambda n, qq: (n, qq)
                        for qq in QS:
                            p.op('dve', lambda e: e.tensor_tensor(w1[qq][:], Bre(qq), cosT[qq][:], ALU.mult), reads=[('C_pb', bre[qq]), K('C_cosT', qq)], writes=[K('C_w1', qq)])
                            p.op('dve', lambda e: e.tensor_tensor(w2_[qq][:], Bim(qq), sinT[qq][:], ALU.mult), reads=[('C_pb', bim[qq]), K('C_sinT', qq)], writes=[K('C_w2', qq)])
                            p.op('dve', lambda e: e.tensor_tensor(w3[qq][:], Bim(qq), cosT[qq][:], ALU.mult), reads=[('C_pb', bim[qq]), K('C_cosT', qq)], writes=[K('C_w3', qq)])
                            p.op('dve', lambda e: e.tensor_tensor(w4[qq][:], Bre(qq), sinT[qq][:], ALU.mult), reads=[('C_pb', bre[qq]), K('C_sinT', qq)], writes=[K('C_w4', qq)])
                        for qq in QS:
                            p.op('dve', lambda e: e.tensor_tensor(w1[qq][:], w1[qq][:], w2_[qq][:], ALU.add), reads=[K('C_w1', qq), K('C_w2', qq)], writes=[K('C_w1', qq)])
                            p.op('dve', lambda e: e.tensor_tensor(w3[qq][:], w3[qq][:], w4[qq][:], ALU.subtract), reads=[K('C_w3', qq), K('C_w4', qq)], writes=[K('C_w3', qq)])
                        for qq in QS:
                            p.op('dve', lambda e: e.tensor_tensor_scan(gre[qq][:], rtab[qq][:], w1[qq][:], carry[qq][:, 0:1], ALU.mult, ALU.add),
                                 reads=[K('C_rtab', qq), K('C_w1', qq), K('C_carry', qq)], writes=[K('C_gre', qq)])
                        for qq in QS:
                            p.op('dve', lambda e: e.tensor_tensor_scan(gim[qq][:], rtab[qq][:], w3[qq][:], carry[qq][:, 1:2], ALU.mult, ALU.add),
                                 reads=[K('C_rtab', qq), K('C_w3', qq), K('C_carry', qq)], writes=[K('C_gim', qq)])
                            p.op('dve', lambda e: e.tensor_tensor(w1[qq][:], gre[qq][:], cosT[qq][:], ALU.mult), reads=[K('C_gre', qq), K('C_cosT', qq)], writes=[K('C_w1', qq)])
                            p.op('dve', lambda e: e.tensor_tensor(w4[qq][:], gre[qq][:], sinT[qq][:], ALU.mult), reads=[K('C_gre', qq), K('C_sinT', qq)], writes=[K('C_w4', qq)])
                        Hre = lambda qq: hre[qq][:] if d == 0 else hre[qq][:, ::-1]
                        Him = lambda qq: him[qq][:] if d == 0 else him[qq][:, ::-1]
                        for qq in QS:
                            p.op('dve', lambda e: e.tensor_tensor(w2_[qq][:], gim[qq][:], sinT[qq][:], ALU.mult), reads=[K('C_gim', qq), K('C_sinT', qq)], writes=[K('C_w2', qq)])
                            p.op('dve', lambda e: e.tensor_tensor(w3[qq][:], gim[qq][:], cosT[qq][:], ALU.mult), reads=[K('C_gim', qq), K('C_cosT', qq)], writes=[K('C_w3', qq)])
                        last = TC - 1 if d == 0 else 0
                        for qq in QS:
                            p.op('dve', lambda e: e.tensor_tensor(Hre(qq), w1[qq][:], w2_[qq][:], ALU.subtract), reads=[K('C_w1', qq), K('C_w2', qq)], writes=[K('C_hre', qq)])
                            p.op('dve', lambda e: e.tensor_tensor(Him(qq), w4[qq][:], w3[qq][:], ALU.add), reads=[K('C_w4', qq), K('C_w3', qq)], writes=[K('C_him', qq)])
                        for qq in QS:
                            q = cb * 4 + qq
                            ps32 = slice((qq // 2) * 64, (qq // 2) * 64 + 64)
                            p.op('act', lambda e: e.copy(carry[qq][:, 0:1], hre[qq][:, last:last + 1]), reads=[K('C_hre', qq)], writes=[K('C_carry', qq)])
                            p.op('act', lambda e: e.copy(carry[qq][:, 1:2], him[qq][:, last:last + 1]), reads=[K('C_him', qq)], writes=[K('C_carry', qq)])
                            by = nb()
                            byq[qq] = by
                            p.op('pe', lambda e: e.matmul(pbs[by][ps32, :], WC[d][0][:, q, :], hre[qq][:], start=True, stop=False),
                                 reads=[f'C_WC{d}0', K('C_hre', qq)], writes=[('C_pb', by)])
                            p.op('pe', lambda e: e.matmul(pbs[by][ps32, :], WC[d][1][:, q, :], him[qq][:], start=False, stop=True),
                                 reads=[f'C_WC{d}1', K('C_him', qq)], writes=[('C_pb', by)])
                        for qq in QS:
                            ps32 = slice((qq // 2) * 64, (qq // 2) * 64 + 64)
                            by = byq[qq]
                            if d == 0 and qq % 2 == 0:
                                p.op('act', lambda e: e.copy(yacc[ps32, tsl], pbs[by][ps32, :]), reads=[('C_pb', by)], writes=[('C_yacc', qq // 2, ch)])
                            else:
                                p.op('dve', lambda e: e.tensor_tensor(yacc[ps32, tsl], yacc[ps32, tsl], pbs[by][ps32, :], ALU.add),
                                     reads=[('C_pb', by), ('C_yacc', qq // 2, ch)], writes=[('C_yacc', qq // 2, ch)])
                for ch in range(S // TC):
                    tsl = slice(ch * TC, (ch + 1) * TC)
                    rk = [('C_yacc', qq, ch) for qq in range(2)]
                    p.op('dve', lambda e: e.scalar_tensor_tensor(y3[:], uT[:, tsl], dsk[:, cb:cb + 1], yacc[:, tsl], ALU.mult, ALU.add),
                         reads=rk + [('C_uT', ch), 'C_dsk'], writes=[('C_him', 0)])
                    p.op('pool', lambda e: e.tensor_tensor(y4[:], y3[:], y3[:], ALU.mult), reads=[('C_him', 0)], writes=[('C_him', 1)])
                    p.op('dve', lambda e: e.tensor_scalar(y4[:], y4[:], 0.044715, 1.0, ALU.mult, ALU.add), reads=[('C_him', 1)], writes=[('C_him', 1)])
                    p.op('dve', lambda e: e.tensor_tensor(y4[:], y4[:], y3[:], ALU.mult), reads=[('C_him', 1), ('C_him', 0)], writes=[('C_him', 1)])
                    p.op('act', lambda e: e.activation(y4[:], y4[:], AF.Sigmoid, scale=1.5957691216057308), reads=[('C_him', 1)], writes=[('C_him', 1)])
                    p.op('dve', lambda e: e.tensor_tensor(y3[:], y3[:], y4[:], ALU.mult), reads=[('C_him', 1), ('C_him', 0)], writes=[('C_him', 0)])
                    for i4_ in range(TC // 128):
                        i = ch * (TC // 128) + i4_
                        bt = nb()
                        p.op('pe', lambda e: e.transpose(pbs[bt][:, 0:128], y3[:, i4_ * 128:(i4_ + 1) * 128], ident_f[:]), reads=[('C_him', 0), 'ident_f'], writes=[('C_pb', bt)])
                        p.op('act', lambda e: e.copy(yo[:], pbs[bt][:, 0:128]), reads=[('C_pb', bt)], writes=['C_yo'])
                        p.dma('sp', ygd[i * 128:(i + 1) * 128, cb * 128:(cb + 1) * 128], yo[:], reads=['C_yo'], writes=[('ygd', i, cb)])
        p.barrier()
        with ExitStack() as st:
            gw = sb(st, "C2_gw", [128, 8, 1024], BF16)
            p.dma('pool', gw[:], s5_glu_w[l, :, :].rearrange("(k p) n -> p k n", p=128), writes=['C2_gw'])
            gb = sb(st, "C2_gb", [128, 1024])
            p.dma('sp', gb[:], s5_glu_b[l:l + 1, :].partition_broadcast(128), writes=['C2_gb'])
            yg = sb(st, "C2_yg", [128, 1024]); ygb = sb(st, "C2_ygb", [128, 1024], BF16); ygT = sb(st, "C2_ygT", [128, 8, 128], BF16)
            sg = sb(st, "C2_sg", [128, 1024])
            ptr = ps(st, "C2_pt", [128, 8, 128], BF16)
            pm = [ps(st, f"C2_pm{i}", [128, 512]) for i in range(2)]
            for i in range(NT):
                p.dma('sp', yg[:], ygd[i * 128:(i + 1) * 128, :], reads=[('ygd', i, cb) for cb in range(8)], writes=['C2_yg'])
                p.op('act', lambda e: e.copy(ygb[:], yg[:]), reads=['C2_yg'], writes=['C2_ygb'])
                for k in range(8):
                    p.op('pe', lambda e: e.transpose(ptr[:, k, :], ygb[:, k * 128:(k + 1) * 128], ident_b[:]), reads=['C2_ygb', 'ident_b'], writes=['C2_pt'])
                p.op('dve', lambda e: e.tensor_copy(ygT[:], ptr[:]), reads=['C2_pt'], writes=['C2_ygT'])
                for half in range(2):
                    cs_ = slice(half * 512, (half + 1) * 512)
                    for k in range(8):
                        p.op('pe', lambda e: e.matmul(pm[half][:, :], ygT[:, k, :], gw[:, k, cs_], start=(k == 0), stop=(k == 7)),
                             reads=['C2_ygT', 'C2_gw'], writes=[('C2_pm', half)])
                    p.op('dve', lambda e: e.tensor_tensor(sg[:, cs_], pm[half][:, :], gb[:, cs_], ALU.add), reads=[('C2_pm', half), 'C2_gb'], writes=['C2_sg'])
                p.op('act', lambda e: e.activation(sg[:], sg[:], AF.Sigmoid), reads=['C2_sg'], writes=['C2_sg'])
                p.op('dve', lambda e: e.tensor_tensor(sg[:], sg[:], yg[:], ALU.mult), reads=['C2_sg', 'C2_yg'], writes=['C2_sg'])
                p.dma('sp', br[i * 128:(i + 1) * 128, 0:1024], sg[:], reads=['C2_sg'], writes=[('br', i, 0)])
        p.barrier()


    def prologue_rope(ropec, ropes):
        with ExitStack() as st:
            pi_ = sb(st, "R_pi", [128, NT], I32); pf = sb(st, "R_pf", [128, NT]); ivf = sb(st, "R_ivf", [128, 32])
            ang = sb(st, "R_ang", [128, NT, 32]); ti_ = sb(st, "R_ti", [128, NT, 32], I32); tf_ = sb(st, "R_tf", [128, NT, 32])
            p.dma('sp', pi_[:], pos_in[0, :].rearrange("(i p) -> p i", p=128), writes=['R_pi'], allow_slow_non_contiguous=True)
            p.dma('sp', ivf[:], c_invfreq[0:1, :].partition_broadcast(128), writes=['R_ivf'])
            p.op('dve', lambda e: e.tensor_copy(pf[:], pi_[:]), reads=['R_pi'], writes=['R_pf'])
            p.op('dve', lambda e: e.tensor_tensor(ang[:], pf[:].unsqueeze(2).broadcast_to([128, NT, 32]),
                                                 ivf[:].unsqueeze(1).broadcast_to([128, NT, 32]), ALU.mult), reads=['R_pf', 'R_ivf'], writes=['R_ang'])
            for which, dst, key in ((0, ropes, 'rope_s'), (1, ropec, 'rope_c')):
                if which == 1:
                    p.op('dve', lambda e: e.tensor_scalar(ang[:], ang[:], float(np.pi / 2), None, ALU.add), reads=['R_ang'], writes=['R_ang'])
                p.op('dve', lambda e: e.tensor_scalar(ti_[:], ang[:], 1.0 / TWO_PI, None, ALU.mult), reads=['R_ang'], writes=['R_ti'])
                p.op('dve', lambda e: e.tensor_copy(tf_[:], ti_[:]), reads=['R_ti'], writes=['R_tf'])
                p.op('dve', lambda e: e.scalar_tensor_tensor(tf_[:], tf_[:], -TWO_PI, ang[:], ALU.mult, ALU.add), reads=['R_tf', 'R_ang'], writes=['R_tf'])
                p.op('dve', lambda e: e.tensor_scalar(tf_[:], tf_[:], float(np.pi), float(-np.pi), ALU.min, ALU.max), reads=['R_tf'], writes=['R_tf'])
                p.op('act', lambda e: e.activation(dst[:], tf_[:], AF.Sin), reads=['R_tf'], writes=[key])
        p.barrier()

    def phase_B(l):
        with ExitStack() as st:
            ropec = sb(st, "rope_c", [128, NT, 32]); ropes = sb(st, "rope_s", [128, NT, 32])
            prologue_rope(ropec, ropes)
            wuq = sb(st, "B_wuq", [128, 7, 1536], BF16); wukv = sb(st, "B_wukv", [128, 2, 2048], BF16)
            p.dma('pool', wuq[:], mla_w_uq[l, :, :].rearrange("(k p) n -> p k n", p=128), writes=['B_wuq'])
            p.dma('pool', wukv[:], mla_w_ukv[l, :, :].rearrange("(k p) n -> p k n", p=128), writes=['B_wukv'])
            gqa = sb(st, "B_gqa", [128, 896]); gkva = sb(st, "B_gkva", [128, 256]); gq = sb(st, "B_gq", [128, 192]); gk = sb(st, "B_gk", [128, 192])
            p.dma('sp', gqa[:], mla_q_a_norm[l:l + 1, :].partition_broadcast(128), writes=['B_gqa'])
            p.dma('sp', gkva[:], mla_kv_a_norm[l:l + 1, :].partition_broadcast(128), writes=['B_gkva'])
            p.dma('sp', gq[:], mla_q_norm[l:l + 1, :].partition_broadcast(128), writes=['B_gq'])
            p.dma('sp', gk[:], mla_k_norm[l:l + 1, :].partition_broadcast(128), writes=['B_gk'])
            lat = sb(st, "B_lat", [128, 1216]); latb = sb(st, "B_latb", [128, 1152], BF16); latT = sb(st, "B_latT", [128, 9, 128], BF16)
            ss = sb(st, "B_ss", [128, 2]); junk = sb(st, "B_junk", [128, 896], BF16)
            qk = [sb(st, f"B_qk{i}", [128, 8, 192]) for i in range(2)]
            sq = sb(st, "B_sq", [128, 8, 192]); hs = sb(st, "B_hs", [128, 8])
            r1 = sb(st, "B_r1", [128, 8, 32]); r2 = sb(st, "B_r2", [128, 8, 32]); r3 = sb(st, "B_r3", [128, 8, 32])
            qkb = sb(st, "B_qkb", [128, 8, 192], BF16); vb = sb(st, "B_vb", [128, 1024], BF16)
            tT = sb(st, "B_tT", [128, 16, 128], BF16)
            ptr = [ps(st, f"B_pt{i}", [128, 8, 128], BF16) for i in range(2)]
            pm = [ps(st, f"B_pm{i}", [128, 512]) for i in range(4)]
            pmi = [0]
            import os
            for i in range(int(os.environ.get("KNTB", NT))):
                t0 = i * 128
                p.dma('sp', lat[:], proj[t0:t0 + 128, C_CQ:C_CQ + 1216], reads=[('proj', i, 'all')], writes=['B_lat'])
                p.op('act', lambda e: e.activation(junk[:], lat[:, 0:896], AF.Square, accum_out=ss[:, 0:1]), reads=['B_lat'], writes=['B_junk', 'B_ss0'])
                p.op('act', lambda e: e.activation(junk[:, 0:256], lat[:, 896:1152], AF.Square, accum_out=ss[:, 1:2]), reads=['B_lat'], writes=['B_junk', 'B_ss1'])
                p.op('act', lambda e: e.activation(ss[:, 0:1], ss[:, 0:1], AF.Sqrt, bias=eps_t[:], scale=1.0 / 896), reads=['B_ss0', 'eps_t'], writes=['B_ss0'])
                p.op('act', lambda e: e.activation(ss[:, 1:2], ss[:, 1:2], AF.Sqrt, bias=eps_t[:], scale=1.0 / 256), reads=['B_ss1', 'eps_t'], writes=['B_ss1'])
                p.op('dve', lambda e: e.reciprocal(ss[:], ss[:]), reads=['B_ss0', 'B_ss1'], writes=['B_ss0', 'B_ss1'])
                p.op('dve', lambda e: e.scalar_tensor_tensor(latb[:, 0:896], lat[:, 0:896], ss[:, 0:1], gqa[:], ALU.mult, ALU.mult),
                     reads=['B_lat', 'B_ss0', 'B_gqa'], writes=['B_latb'])
                p.op('dve', lambda e: e.scalar_tensor_tensor(latb[:, 896:1152], lat[:, 896:1152], ss[:, 1:2], gkva[:], ALU.mult, ALU.mult),
                     reads=['B_lat', 'B_ss1', 'B_gkva'], writes=['B_latb'])
                BSTOP = int(os.environ.get("BSTOP", 9))
                if BSTOP <= 1:
                    continue
                for k in range(9):
                    pt = ptr[0] if k < 8 else ptr[1]
                    p.op('pe', lambda e: e.transpose(pt[:, k % 8, :], latb[:, k * 128:(k + 1) * 128], ident_b[:]), reads=['B_latb', 'ident_b'],
                         writes=[('B_pt', 0 if k < 8 else 1)])
                p.op('act', lambda e: e.copy(latT[:, 0:8, :], ptr[0][:]), reads=[('B_pt', 0)], writes=['B_latT'])
                p.op('dve', lambda e: e.tensor_copy(latT[:, 8, :], ptr[1][:, 0, :]), reads=[('B_pt', 1)], writes=['B_latT'])
                for c3 in range(3):
                    j = pmi[0] % 4
                    pmi[0] += 1
                    for k in range(7):
                        p.op('pe', lambda e: e.matmul(pm[j][:, :], latT[:, k, :], wuq[:, k, c3 * 512:(c3 + 1) * 512], start=(k == 0), stop=(k == 6)),
                             reads=['B_latT', 'B_wuq'], writes=[('B_pm', j)])
                    p.op('act', lambda e: e.copy(qk[0][:].rearrange("p h d -> p (h d)")[:, c3 * 512:(c3 + 1) * 512], pm[j][:, :]),
                         reads=[('B_pm', j)], writes=['B_qk0'])
                if BSTOP <= 2:
                    continue
                for c4 in range(4):
                    j = pmi[0] % 4
                    pmi[0] += 1
                    for k in range(2):
                        p.op('pe', lambda e: e.matmul(pm[j][:, :], latT[:, 7 + k, :], wukv[:, k, c4 * 512:(c4 + 1) * 512], start=(k == 0), stop=(k == 1)),
                             reads=['B_latT', 'B_wukv'], writes=[('B_pm', j)])
                    pv = pm[j][:, :].rearrange("p (h d) -> p h d", h=2)
                    BSKIP = os.environ.get("BSKIP", "")
                    if 'a' not in BSKIP:
                        p.op('act', lambda e: e.copy(qk[1][:, c4 * 2:(c4 + 1) * 2, 0:128], pv[:, :, 0:128]), reads=[('B_pm', j)], writes=['B_qk1'])
                    for hh in range(2):
                        hcol = (c4 * 2 + hh) * 128
                        p.op('act', lambda e: e.copy(vb[:, hcol:hcol + 128], pm[j][:, hh * 256 + 128:hh * 256 + 256]),
                             reads=[('B_pm', j)], writes=['B_vb'])
                if 'p' not in BSKIP:
                    p.op('pool', lambda e: e.tensor_copy(qk[1][:, :, 128:192], lat[:, 1152:1216].unsqueeze(1).broadcast_to([128, 8, 64])),
                         reads=['B_lat'], writes=['B_qk1'])
                if 'v' not in BSKIP:
                    p.dma('sp', v_d[t0:t0 + 128, :], vb[:], reads=['B_vb'], writes=[('v_d', i)])
                if BSTOP <= 3:
                    continue
                for which in range(2):
                    X = qk[which]
                    xk = f'B_qk{which}'
                    g = gq if which == 0 else gk
                    gk_ = 'B_gq' if which == 0 else 'B_gk'
                    p.op('pool', lambda e: e.tensor_tensor(sq[:], X[:], X[:], ALU.mult), reads=[xk], writes=['B_sq'])
                    p.op('dve', lambda e: e.tensor_reduce(hs[:], sq[:], AX.X, ALU.add), reads=['B_sq'], writes=['B_hs'])
                    p.op('act', lambda e: e.activation(hs[:], hs[:], AF.Sqrt, bias=eps_t[:], scale=1.0 / 192), reads=['B_hs', 'eps_t'], writes=['B_hs'])
                    p.op('dve', lambda e: e.reciprocal(hs[:], hs[:]), reads=['B_hs'], writes=['B_hs'])
                    p.op('dve', lambda e: e.tensor_tensor(X[:], X[:], hs[:].unsqueeze(2).broadcast_to([128, 8, 192]), ALU.mult), reads=[xk, 'B_hs'], writes=[xk])
                    p.op('pool', lambda e: e.tensor_tensor(X[:], X[:], g[:].unsqueeze(1).broadcast_to([128, 8, 192]), ALU.mult), reads=[xk, gk_], writes=[xk])
                    cb_ = ropec[:, i, :].unsqueeze(1).broadcast_to([128, 8, 32]); sb_ = ropes[:, i, :].unsqueeze(1).broadcast_to([128, 8, 32])
                    T1 = X[:, :, 128:160]; T2 = X[:, :, 160:192]
                    p.op('dve', lambda e: e.tensor_tensor(r1[:], T1, sb_, ALU.mult), reads=[xk, 'rope_s'], writes=['B_r1'])
                    p.op('dve', lambda e: e.tensor_tensor(r2[:], T2, sb_, ALU.mult), reads=[xk, 'rope_s'], writes=['B_r2'])
                    p.op('dve', lambda e: e.tensor_tensor(r3[:], T1, cb_, ALU.mult), reads=[xk, 'rope_c'], writes=['B_r3'])
                    p.op('dve', lambda e: e.tensor_tensor(T1, r3[:], r2[:], ALU.subtract), reads=['B_r3', 'B_r2'], writes=[xk])
                    p.op('dve', lambda e: e.tensor_tensor(r3[:], T2, cb_, ALU.mult), reads=[xk, 'rope_c'], writes=['B_r3'])
                    p.op('dve', lambda e: e.tensor_tensor(T2, r3[:], r1[:], ALU.add), reads=['B_r3', 'B_r1'], writes=[xk])
                    p.op('act', lambda e: e.copy(qkb[:], X[:]), reads=[xk], writes=['B_qkb'])
                    if BSTOP <= 4:
                        continue
                    for h in range(8):
                        pt = ptr[h % 2]
                        p.op('pe', lambda e: e.transpose(pt[:, 0, :], qkb[:, h, 0:128], ident_b[:]), reads=['B_qkb', 'ident_b'], writes=[('B_pt', h % 2)])
                        p.op('pe', lambda e: e.transpose(pt[0:64, 1, :], qkb[:, h, 128:192], ident_b[:]), reads=['B_qkb', 'ident_b'], writes=[('B_pt', h % 2)])
                        p.op('act', lambda e: e.copy(tT[:, 2 * h, :], pt[:, 0, :]), reads=[('B_pt', h % 2)], writes=[('B_tT', h)])
                        p.op('dve', lambda e: e.tensor_copy(tT[0:64, 2 * h + 1, :], pt[0:64, 1, :]), reads=[('B_pt', h % 2)], writes=[('B_tT', h)])
                        dstT = qT_d if which == 0 else kT_d
                        p.dma('sp', dstT[h, 0:128, t0:t0 + 128], tT[:, 2 * h, :], reads=[('B_tT', h)], writes=[('qkT', which, h, i)])
                        p.dma('sp', dstT[h, 128:192, t0:t0 + 128], tT[0:64, 2 * h + 1, :], reads=[('B_tT', h)], writes=[('qkT', which, h, i)])
        p.barrier()
        if 'b' in phases:
            return
        with ExitStack() as st:
            qT = sb(st, "B2_qT", [128, S], BF16); qTr = sb(st, "B2_qTr", [64, S], BF16)
            kT = sb(st, "B2_kT", [128, S], BF16); kTr = sb(st, "B2_kTr", [64, S], BF16)
            Va = sb(st, "B2_Va", [128, NT, 132], BF16)
            PT = [sb(st, f"B2_PT{i}", [128, 512], BF16) for i in range(2)]
            ob = sb(st, "B2_ob", [128, 128]); rs = sb(st, "B2_rs", [128, 1])
            psc = [ps(st, f"B2_ps{i}", [128, 512]) for i in range(2)]
            pac = [ps(st, f"B2_pa{i}", [128, 512]) for i in range(4)]
            p.op('dve', lambda e: e.memset(Va[:], 1.0), writes=['B2_Va'])
            SCALE = float(192 ** -0.5)
            it = 0
            for h in range(8):
                p.dma('sp', qT[:], qT_d[h, 0:128, :], writes=['B2_qT'])
                p.dma('sp', qTr[:], qT_d[h, 128:192, :], writes=['B2_qTr'])
                p.dma('sp', kT[:], kT_d[h, 0:128, :], writes=['B2_kT'])
                p.dma('sp', kTr[:], kT_d[h, 128:192, :], writes=['B2_kTr'])
                p.dma('sp', Va[:, :, 0:128], v_d[:, h * 128:(h + 1) * 128].rearrange("(i p) d -> p i d", p=128), writes=['B2_Va'])
                for qb in range(S // 512):
                    qs = slice(qb * 512, (qb + 1) * 512)
                    def scores(kt_):
                        ks_ = slice(kt_ * 128, (kt_ + 1) * 128)
                        j_ = kt_ % 2
                        p.op('pe', lambda e: e.matmul(psc[j_][:, :], kT[:, ks_], qT[:, qs], start=True, stop=False), reads=['B2_kT', 'B2_qT'], writes=[('B2_ps', j_)])
                        p.op('pe', lambda e: e.matmul(psc[j_][:, :], kTr[:, ks_], qTr[:, qs], start=False, stop=True), reads=['B2_kTr', 'B2_qTr'], writes=[('B2_ps', j_)])
                        p.op('act', lambda e: e.activation(PT[j_][:], psc[j_][:, :], AF.Exp, scale=SCALE), reads=[('B2_ps', j_)], writes=[('B2_PT', j_)])
                    scores(0)
                    for kt in range(NT):
                        j = kt % 2
                        if kt + 1 < NT:
                            scores(kt + 1)
                        for sub in range(4):
                            p.op('pe', lambda e: e.matmul(pac[sub][:, 0:129], PT[j][:, sub * 128:(sub + 1) * 128], Va[:, kt, 0:129],
                                                         start=(kt == 0), stop=(kt == NT - 1)), reads=[('B2_PT', j), 'B2_Va'], writes=[('B2_pa', sub)])
                    for sub in range(4):
                        t0 = qb * 512 + sub * 128
                        p.op('dve', lambda e: e.reciprocal(rs[:], pac[sub][:, 128:129]), reads=[('B2_pa', sub)], writes=['B2_rs'])
                        p.op('dve', lambda e: e.tensor_scalar(ob[:], pac[sub][:, 0:128], rs[:], None, ALU.mult), reads=[('B2_pa', sub), 'B2_rs'], writes=['B2_ob'])
                        p.dma('sp', br[t0:t0 + 128, 1024 + h * 128:1024 + (h + 1) * 128], ob[:], reads=['B2_ob'], writes=[('br', t0 // 128, 1, h)])
        p.barrier()

    def phase_M(l):
        with ExitStack() as st:
            wk = sb(st, "M_wk", [128, 32, 1024], BF16)
            gm = sb(st, "M_gm", [128, D]); mt_ = sb(st, "M_mt", [128, D]); mb = sb(st, "M_mb", [128, D], BF16)
            memT = sb(st, "M_memT", [128, 32, 256], BF16)
            ss = sb(st, "M_ss", [128, 1]); hs = sb(st, "M_hs", [128, 4]); gqn = sb(st, "M_gqn", [128, 256]); gkn = sb(st, "M_gkn", [128, 256])
            Kt = sb(st, "M_K", [128, 1024]); sq = sb(st, "M_sq", [128, 1024]); Kb = sb(st, "M_Kb", [128, 1024], BF16)
            KmT = sb(st, "M_KmT", [128, 8, 256], BF16)
            Vm = sb(st, "M_Vm", [128, 2, 4, 260], BF16)
            ptr = [ps(st, f"M_pt{i}", [128, 8, 128], BF16) for i in range(2)]
            pm = [ps(st, f"M_pm{i}", [128, 512]) for i in range(2)]
            psc = [ps(st, f"M_ps{i}", [128, 512]) for i in range(2)]
            pac = [ps(st, f"M_pa{i}", [128, 512]) for i in range(2)]
            p.dma('sp', gm[:], mem_norm_g[l:l + 1, :].partition_broadcast(128), writes=['M_gm'])
            p.dma('sp', gqn[:], mem_q_norm[l:l + 1, :].partition_broadcast(128), writes=['M_gqn'])
            p.dma('sp', gkn[:], mem_k_norm[l:l + 1, :].partition_broadcast(128), writes=['M_gkn'])
            p.op('dve', lambda e: e.memset(Vm[:], 1.0), writes=['M_Vm'])
            for mt in range(2):
                p.dma('sp', mt_[:], mem_in[mt * 128:(mt + 1) * 128, :], writes=['M_mt'])
                p.op('act', lambda e: e.activation(mb[:], mt_[:], AF.Square, accum_out=ss[:]), reads=['M_mt'], writes=['M_mb', 'M_ss'])
                p.op('act', lambda e: e.activation(ss[:], ss[:], AF.Sqrt, bias=eps_t[:], scale=1.0 / D), reads=['M_ss', 'eps_t'], writes=['M_ss'])
                p.op('dve', lambda e: e.reciprocal(ss[:], ss[:]), reads=['M_ss'], writes=['M_ss'])
                p.op('dve', lambda e: e.scalar_tensor_tensor(mb[:], mt_[:], ss[:], gm[:], ALU.mult, ALU.mult), reads=['M_mt', 'M_ss', 'M_gm'], writes=['M_mb'])
                for k8 in range(4):
                    pt = ptr[k8 % 2]
                    for kk in range(8):
                        k = k8 * 8 + kk
                        p.op('pe', lambda e: e.transpose(pt[:, kk, :], mb[:, k * 128:(k + 1) * 128], ident_b[:]), reads=['M_mb', 'ident_b'], writes=[('M_pt', k8 % 2)])
                    p.op('act', lambda e: e.copy(memT[:, k8 * 8:(k8 + 1) * 8, mt * 128:(mt + 1) * 128], pt[:]), reads=[('M_pt', k8 % 2)], writes=['M_memT'])
            for which, wsrc in ((0, mem_w_k), (1, mem_w_v)):
                for k4 in range(4):
                    p.dma('pool', wk[:, k4 * 8:(k4 + 1) * 8, :], wsrc[l, k4 * 1024:(k4 + 1) * 1024, :].rearrange("(k p) n -> p k n", p=128), writes=['M_wk'])
                for mt in range(2):
                    for half in range(2):
                        for k in range(32):
                            p.op('pe', lambda e: e.matmul(pm[half][:, :], memT[:, k, mt * 128:(mt + 1) * 128], wk[:, k, half * 512:(half + 1) * 512],
                                                         start=(k == 0), stop=(k == 31)), reads=['M_memT', 'M_wk'], writes=[('M_pm', half)])
                        if which == 0:
                            p.op('act', lambda e: e.copy(Kt[:, half * 512:(half + 1) * 512], pm[half][:, :]), reads=[('M_pm', half)], writes=['M_K'])
                        else:
                            p.op('act', lambda e: e.copy(Vm[:, mt, half * 2:(half + 1) * 2, 0:256], pm[half][:, :].rearrange("p (h d) -> p h d", h=2)),
                                 reads=[('M_pm', half)], writes=['M_Vm'])
                    if which == 0:
                        K3 = Kt[:].rearrange("p (h d) -> p h d", h=4)
                        p.op('pool', lambda e: e.tensor_tensor(sq[:], Kt[:], Kt[:], ALU.mult), reads=['M_K'], writes=['M_sq'])
                        p.op('dve', lambda e: e.tensor_reduce(hs[:], sq[:].rearrange("p (h d) -> p h d", h=4), AX.X, ALU.add), reads=['M_sq'], writes=['M_hs'])
                        p.op('act', lambda e: e.activation(hs[:], hs[:], AF.Sqrt, bias=eps_t[:], scale=1.0 / 256), reads=['M_hs', 'eps_t'], writes=['M_hs'])
                        p.op('dve', lambda e: e.reciprocal(hs[:], hs[:]), reads=['M_hs'], writes=['M_hs'])
                        p.op('dve', lambda e: e.tensor_tensor(K3, K3, hs[:].unsqueeze(2).broadcast_to([128, 4, 256]), ALU.mult), reads=['M_K', 'M_hs'], writes=['M_K'])
                        p.op('dve', lambda e: e.tensor_tensor(Kb[:].rearrange("p (h d) -> p h d", h=4), K3, gkn[:].unsqueeze(1).broadcast_to([128, 4, 256]), ALU.mult),
                             reads=['M_K', 'M_gkn'], writes=['M_Kb'])
                        for k in range(8):
                            p.op('pe', lambda e: e.transpose(ptr[0][:, k, :], Kb[:, k * 128:(k + 1) * 128], ident_b[:]), reads=['M_Kb', 'ident_b'], writes=[('M_pt', 0)])
                        p.op('act', lambda e: e.copy(KmT[:, :, mt * 128:(mt + 1) * 128], ptr[0][:]), reads=[('M_pt', 0)], writes=['M_KmT'])
            qt = sb(st, "M_q", [128, 1024]); qb_ = sb(st, "M_qb", [128, 1024], BF16); qT = sb(st, "M_qT", [128, 8, 512], BF16)
            PT = sb(st, "M_PT", [128, 2, 512], BF16); ob = sb(st, "M_ob", [128, 1024]); rs = sb(st, "M_rs", [128, 1])
            for g4 in range(NT // 4):
                for ti in range(4):
                    i = g4 * 4 + ti
                    p.dma('sp', qt[:], proj[i * 128:(i + 1) * 128, C_MQ:C_MQ + 1024], reads=[('proj', i, 'all')], writes=['M_q'])
                    Q3 = qt[:].rearrange("p (h d) -> p h d", h=4)
                    p.op('pool', lambda e: e.tensor_tensor(sq[:], qt[:], qt[:], ALU.mult), reads=['M_q'], writes=['M_sq'])
                    p.op('dve', lambda e: e.tensor_reduce(hs[:], sq[:].rearrange("p (h d) -> p h d", h=4), AX.X, ALU.add), reads=['M_sq'], writes=['M_hs'])
                    p.op('act', lambda e: e.activation(hs[:], hs[:], AF.Sqrt, bias=eps_t[:], scale=1.0 / 256), reads=['M_hs', 'eps_t'], writes=['M_hs'])
                    p.op('dve', lambda e: e.reciprocal(hs[:], hs[:]), reads=['M_hs'], writes=['M_hs'])
                    p.op('dve', lambda e: e.tensor_tensor(Q3, Q3, hs[:].unsqueeze(2).broadcast_to([128, 4, 256]), ALU.mult), reads=['M_q', 'M_hs'], writes=['M_q'])
                    p.op('dve', lambda e: e.tensor_tensor(qb_[:].rearrange("p (h d) -> p h d", h=4), Q3, gqn[:].unsqueeze(1).broadcast_to([128, 4, 256]), ALU.mult),
                         reads=['M_q', 'M_gqn'], writes=['M_qb'])
                    for k in range(8):
                        p.op('pe', lambda e: e.transpose(ptr[ti % 2][:, k, :], qb_[:, k * 128:(k + 1) * 128], ident_b[:]), reads=['M_qb', 'ident_b'], writes=[('M_pt', ti % 2)])
                    p.op('act', lambda e: e.copy(qT[:, :, ti * 128:(ti + 1) * 128], ptr[ti % 2][:]), reads=[('M_pt', ti % 2)], writes=['M_qT'])
                for h in range(4):
                    for mt in range(2):
                        for dc in range(2):
                            p.op('pe', lambda e: e.matmul(psc[mt][:, :], KmT[:, h * 2 + dc, mt * 128:(mt + 1) * 128], qT[:, h * 2 + dc, :],
                                                         start=(dc == 0), stop=(dc == 1)), reads=['M_KmT', 'M_qT'], writes=[('M_ps', mt)])
                        p.op('act', lambda e: e.activation(PT[:, mt, :], psc[mt][:, :], AF.Exp, scale=1.0 / 16), reads=[('M_ps', mt)], writes=['M_PT'])
                    for ti in range(4):
                        i = g4 * 4 + ti
                        j = ti % 2
                        for mt in range(2):
                            p.op('pe', lambda e: e.matmul(pac[j][:, 0:257], PT[:, mt, ti * 128:(ti + 1) * 128], Vm[:, mt, h, 0:257],
                                                         start=(mt == 0), stop=(mt == 1)), reads=['M_PT', 'M_Vm'], writes=[('M_pa', j)])
                        p.op('dve', lambda e: e.reciprocal(rs[:], pac[j][:, 256:257]), reads=[('M_pa', j)], writes=['M_rs'])
                        p.op('dve', lambda e: e.tensor_scalar(ob[:, 0:256], pac[j][:, 0:256], rs[:], None, ALU.mult), reads=[('M_pa', j), 'M_rs'], writes=['M_ob'])
                        p.dma('sp', br[i * 128:(i + 1) * 128, 3072 + h * 256:3072 + (h + 1) * 256], ob[:, 0:256], reads=['M_ob'], writes=[('br', i, 3, h)])
        p.barrier()

    def phase_E(l, xsrc):
        for r in range(0, D, 512):
            p.dma('pool', wbf_out[r:r + 512, :], w_out[l, r:r + 512, :], writes=[('wbf_out', r)])
        with ExitStack() as st:
            bt = sb(st, "E_b", [128, D]); G = sb(st, "E_G", [128, D]); mg = sb(st, "E_mg", [128, D], BF16)
            bg = sb(st, "E_bg", [128, 3, 1024]); ss = sb(st, "E_ss", [128, 1]); junk = sb(st, "E_junk", [128, 1024], BF16)
            mT = sb(st, "E_mT", [128, 32, 1024], BF16)
            W = [sb(st, f"E_W{i}", [128, 32, 512], BF16) for i in range(2)]
            xt = [sb(st, f"E_x{i}", [128, 512]) for i in range(2)]
            ptr = [ps(st, f"E_pt{i}", [128, 8, 128], BF16) for i in range(2)]
            pmm = [ps(st, f"E_pm{i}", [128, 512]) for i in range(4)]
            p.dma('sp', bg[:].rearrange("p a b -> p (a b)"), branch_g[l:l + 1, :].partition_broadcast(128), writes=['E_bg'])
            gates = (C_AG, C_BG, C_CG, C_MG)
            wi = 0
            for g in range(S // 1024):
                for ti in range(8):
                    i = g * 8 + ti
                    t0 = i * 128
                    p.dma('sp', bt[:], br[t0:t0 + 128, :], reads=[('br', i, 'all')], writes=['E_b'])
                    for bi in range(4):
                        p.dma('sp', G[:, bi * 1024:(bi + 1) * 1024], proj[t0:t0 + 128, gates[bi]:gates[bi] + 1024], reads=[('proj', i, 'all')], writes=['E_G'])
                    p.op('act', lambda e: e.activation(G[:], G[:], AF.Silu), reads=['E_G'], writes=['E_G'])
                    for bi in range(4):
                        cs_ = slice(bi * 1024, (bi + 1) * 1024)
                        if bi == 2:
                            p.op('pool', lambda e: e.tensor_tensor(mg[:, cs_], bt[:, cs_], G[:, cs_], ALU.mult), reads=['E_b', 'E_G'], writes=['E_mg'])
                            continue
                        gi = {0: 0, 1: 1, 3: 2}[bi]
                        p.op('act', lambda e: e.activation(junk[:], bt[:, cs_], AF.Square, accum_out=ss[:]), reads=['E_b'], writes=['E_junk', 'E_ss'])
                        p.op('act', lambda e: e.activation(ss[:], ss[:], AF.Sqrt, bias=eps_t[:], scale=1.0 / 1024), reads=['E_ss', 'eps_t'], writes=['E_ss'])
                        p.op('dve', lambda e: e.reciprocal(ss[:], ss[:]), reads=['E_ss'], writes=['E_ss'])
                        p.op('dve', lambda e: e.scalar_tensor_tensor(bt[:, cs_], bt[:, cs_], ss[:], bg[:, gi, :], ALU.mult, ALU.mult), reads=['E_b', 'E_ss', 'E_bg'], writes=['E_b'])
                        p.op('pool', lambda e: e.tensor_tensor(mg[:, cs_], bt[:, cs_], G[:, cs_], ALU.mult), reads=['E_b', 'E_G'], writes=['E_mg'])
                    for k8 in range(4):
                        pt = ptr[k8 % 2]
                        for kk in range(8):
                            k = k8 * 8 + kk
                            p.op('pe', lambda e: e.transpose(pt[:, kk, :], mg[:, k * 128:(k + 1) * 128], ident_b[:]), reads=['E_mg', 'ident_b'], writes=[('E_pt', k8 % 2)])
                        dst = mT[:, k8 * 8:(k8 + 1) * 8, ti * 128:(ti + 1) * 128]
                        if k8 % 2 == 0:
                            p.op('act', lambda e: e.copy(dst, pt[:]), reads=[('E_pt', k8 % 2)], writes=[('E_mT', ti)])
                        else:
                            p.op('dve', lambda e: e.tensor_copy(dst, pt[:]), reads=[('E_pt', k8 % 2)], writes=[('E_mT', ti)])
                def load_WE(ci_, wi_):
                    n0_ = ci_ * 512
                    for k4 in range(4):
                        p.dma('sp', W[wi_ % 2][:, k4 * 8:(k4 + 1) * 8, :], wbf_out[k4 * 1024:(k4 + 1) * 1024, n0_:n0_ + 512].rearrange("(k p) n -> p k n", p=128),
                              reads=[('wbf_out', k4 * 1024), ('wbf_out', k4 * 1024 + 512)], writes=[('E_W', wi_ % 2)])
                load_WE(0, wi)
                for ci in range(D // 512):
                    n0 = ci * 512
                    Wt = W[wi % 2]
                    if ci + 1 < D // 512:
                        load_WE(ci + 1, wi + 1)
                    for ti in range(8):
                        i = g * 8 + ti
                        t0 = i * 128
                        j = (ci * 8 + ti) % 4
                        pm = pmm[j]
                        X = xt[(ci * 8 + ti) % 2]
                        xk = ('E_x', (ci * 8 + ti) % 2)
                        p.dma('sp', X[:], xsrc[t0:t0 + 128, n0:n0 + 512], reads=[('y', i, ci)], writes=[xk])
                        for k in range(32):
                            p.op('pe', lambda e: e.matmul(pm[:, :], mT[:, k, ti * 128:(ti + 1) * 128], Wt[:, k, :], start=(k == 0), stop=(k == 31)),
                                 reads=[('E_mT', ti), ('E_W', wi % 2)], writes=[('E_pm', j)])
                        p.op('dve', lambda e: e.tensor_tensor(X[:], X[:], pm[:, :], ALU.add), reads=[('E_pm', j), xk], writes=[xk])
                        p.dma('sp', y_out[t0:t0 + 128, n0:n0 + 512], X[:], reads=[xk], writes=[('y', i, ci)])
                    wi += 1
        p.barrier()

    if dbg and 'A' not in phases:
        proj_in = din("proj_in", [S, NCOLS])
        for i in range(NT):
            p.dma('sp', proj[i * 128:(i + 1) * 128, :], proj_in[i * 128:(i + 1) * 128, :], writes=[('proj', i, 'all')])
        p.barrier()
    if dbg and 'E' in phases and len(phases) < 6:
        br_in = din("br_in", [S, 4096])
        for i in range(NT):
            p.dma('sp', br[i * 128:(i + 1) * 128, :], br_in[i * 128:(i + 1) * 128, :], writes=[('br', i, 'all')])
        p.barrier()
    for l in range(n_layers):
        xsrc = x_in if l == 0 else y_out
        if 'A' in phases:
            phase_A(l, xsrc)
        if 'D' in phases:
            phase_D(l)
        if 'C' in phases:
            phase_C(l)
        if 'M' in phases:
            phase_M(l)
        if 'B' in phases:
            phase_B(l)
        if 'E' in phases:
            phase_E(l, xsrc)
    p.barrier()
    es.close()
    print("instructions:", p.nins)
    nc.in_names = in_names
    return nc


def make_consts():
    r = np.arange(128)
    m = np.stack([r[:, None] < r[None, :], r[:, None] > r[None, :], r[:, None] <= r[None, :], r[:, None] >= r[None, :]]).astype(np.float32)
    return {"c_ident": np.eye(128, dtype=np.float32), "c_masks": m,
            "c_iota": np.arange(1, 513, dtype=np.float32)[None, :],
            "c_invfreq": (1.0 / (np.float32(10000.0) ** (np.arange(0, 64, 2, dtype=np.float32) / np.float32(64)))).astype(np.float32)[None, :]}


_NC_CACHE = {}


def kernel(**inputs):
    nb = 4
    if 'nc' not in _NC_CACHE:
        _NC_CACHE['nc'] = build()
    nc = _NC_CACHE['nc']
    cst = make_consts()
    shared = {}
    for n in nc.in_names:
        if n in cst:
            shared[n] = cst[n]
        elif n in ("x", "mem", "positions"):
            continue
        elif n == "rwkv_r_k":
            shared[n] = np.ascontiguousarray(np.asarray(inputs[n], dtype=np.float32).reshape(L, 1024))
        elif n == "branch_g":
            shared[n] = np.ascontiguousarray(np.asarray(inputs[n], dtype=np.float32).reshape(L, 3072))
        else:
            shared[n] = np.ascontiguousarray(np.asarray(inputs[n], dtype=np.float32))
    in_maps = []
    for b in range(nb):
        m = dict(shared)
        m["x"] = np.ascontiguousarray(np.asarray(inputs["x"][b], dtype=np.float32))
        m["mem"] = np.ascontiguousarray(np.asarray(inputs["mem"][b], dtype=np.float32))
        m["positions"] = np.ascontiguousarray(np.asarray(inputs["positions"][b:b + 1]).astype(np.int32))
        in_maps.append(m)
    res = run_bass_kernel_spmd(nc, in_maps, core_ids=list(range(nb)))
    return np.stack([np.asarray(r["y"], dtype=np.float32) for r in res.results], axis=0)
```

```python
import numpy as np
from contextlib import ExitStack
import concourse.bass as bass
import concourse.mybir as mybir
from concourse.bass_utils import run_bass_kernel_spmd

F32 = mybir.dt.float32
BF16 = mybir.dt.bfloat16
I32 = mybir.dt.int32
ALU = mybir.AluOpType
AF = mybir.ActivationFunctionType
AX = mybir.AxisListType

D = 4096
S = 4096
L = 4
NCOLS = 10688
NT = S // 128
EPS = 1e-6
C_AU, C_AG, C_CQ, C_CKV, C_KPE, C_BG, C_RW, C_CG, C_MQ, C_MG = 0, 1024, 2048, 2944, 3200, 3264, 4288, 7616, 8640, 9664
NDMA = 8


class Prog:
    def __init__(self, nc, es):
        self.nc = nc
        self.E = {'pe': nc.tensor, 'act': nc.scalar, 'dve': nc.vector, 'pool': nc.gpsimd, 'sp': nc.sync}
        self.sems = {}
        for e in ['pe', 'act', 'dve', 'pool']:
            self.sems[e] = es.enter_context(nc.semaphore('s_' + e))
        for q in ['sp', 'pool']:
            for i in range(NDMA):
                self.sems[('d', q, i)] = es.enter_context(nc.semaphore(f'd_{q}_{i}'))
        self.cnt = {e: 0 for e in ['pe', 'act', 'dve', 'pool']}
        self.dma_n = {'sp': 0, 'pool': 0}
        self.waited = {}
        self.bufs = {}
        self.nins = 0

    def _wait(self, eng, key, val):
        if self.waited.get((eng, key), 0) >= val:
            return
        self.E[eng].wait_ge(self.sems[key], val)
        self.waited[(eng, key)] = val

    def _deps(self, reads, writes):
        deps = {}
        for b in reads:
            st = self.bufs.get(b)
            if st and st[0] is not None:
                k, v = st[0]
                if deps.get(k, 0) < v:
                    deps[k] = v
        for b in writes:
            st = self.bufs.get(b)
            if st:
                if st[0] is not None:
                    k, v = st[0]
                    if deps.get(k, 0) < v:
                        deps[k] = v
                for k, v in st[1].items():
                    if deps.get(k, 0) < v:
                        deps[k] = v
        return deps

    def _commit(self, tk, reads, writes):
        k, v = tk
        for b in reads:
            st = self.bufs.get(b)
            if st is None:
                st = self.bufs[b] = [None, {}]
            if st[1].get(k, 0) < v:
                st[1][k] = v
        for b in writes:
            self.bufs[b] = [tk, {}]

    def op(self, eng, fn, reads=(), writes=()):
        deps = self._deps(reads, writes)
        for k, v in deps.items():
            if k == 'pe' and eng == 'pe':
                continue
            self._wait(eng, k, v)
        ins = fn(self.E[eng])
        self.cnt[eng] += 1
        ins.then_inc(self.sems[eng], 1)
        self._commit((eng, self.cnt[eng]), reads, writes)
        self.nins += 1

    def dma(self, q, out, in_, reads=(), writes=(), **kw):
        deps = self._deps(reads, writes)
        n = self.dma_n[q]
        self.dma_n[q] += 1
        key = ('d', q, n % NDMA)
        val = 16 * (n // NDMA + 1)
        if n >= NDMA:
            deps[key] = max(deps.get(key, 0), val - 16)
        for k, v in deps.items():
            self._wait(q, k, v)
        ins = self.E[q].dma_start(out=out, in_=in_, **kw)
        ins.then_inc(self.sems[key], 16)
        self._commit((key, val), reads, writes)
        self.nins += 1

    def barrier(self):
        for e in ['pe', 'act', 'dve', 'pool', 'sp']:
            for k in self.sems:
                if isinstance(k, tuple):
                    n = self.dma_n[k[1]]
                    v = 16 * ((n - k[2] + NDMA - 1) // NDMA) if n > k[2] else 0
                else:
                    v = self.cnt[k]
                if v > 0:
                    self._wait(e, k, v)
        self.bufs.clear()


def build(n_layers=L, phases="AMBCDE", dbg=False, RWDT=BF16):
    nc = bass.Bass("TRN2", target_bir_lowering=False)
    es = ExitStack()

    in_names = []

    def din(name, shape, dt=F32):
        in_names.append(name)
        return nc.dram_tensor(name, list(shape), dt, kind="ExternalInput").ap()

    def dscr(name, shape, dt=F32):
        return nc.dram_tensor(name, list(shape), dt, kind="ExternalOutput" if dbg else "Internal").ap()

    x_in = din("x", [S, D])
    mem_in = din("mem", [256, D])
    pos_in = din("positions", [1, S], I32)
    ln_g = din("ln_g", [L, D])
    w_in = din("w_in", [L, D, NCOLS]) if ('A' in phases or not dbg) else None
    w_out = din("w_out", [L, D, D]) if ('E' in phases or not dbg) else None
    cst_ident = din("c_ident", [128, 128])
    c_masks_t = din("c_masks", [4, 128, 128])
    c_masks = [c_masks_t[i, :, :] for i in range(4)]
    c_iota = din("c_iota", [1, 512])
    c_invfreq = din("c_invfreq", [1, 32])
    branch_g = din("branch_g", [L, 3072])
    mla_q_a_norm = din("mla_q_a_norm", [L, 896]); mla_kv_a_norm = din("mla_kv_a_norm", [L, 256])
    mla_w_uq = din("mla_w_uq", [L, 896, 1536]); mla_w_ukv = din("mla_w_ukv", [L, 256, 2048])
    mla_q_norm = din("mla_q_norm", [L, 192]); mla_k_norm = din("mla_k_norm", [L, 192])
    mem_norm_g = din("mem_norm_g", [L, D]); mem_w_k = din("mem_w_k", [L, D, 1024]) if ('M' in phases or not dbg) else None
    mem_w_v = din("mem_w_v", [L, D, 1024]) if ('M' in phases or not dbg) else None
    mem_q_norm = din("mem_q_norm", [L, 256]); mem_k_norm = din("mem_k_norm", [L, 256])
    s5_lam_re = din("s5_lam_re", [L, 64, 64]); s5_lam_im = din("s5_lam_im", [L, 64, 64])
    s5_b_re = din("s5_b_re", [L, 64, 64, 16]); s5_b_im = din("s5_b_im", [L, 64, 64, 16])
    s5_c_re = din("s5_c_re", [L, 2, 64, 16, 64]); s5_c_im = din("s5_c_im", [L, 2, 64, 16, 64])
    s5_log_dt = din("s5_log_dt", [L, 2, 64]); s5_d = din("s5_d", [L, 1024]); s5_glu_w = din("s5_glu_w", [L, 1024, 1024])
    s5_glu_b = din("s5_glu_b", [L, 1024])
    rwkv_mu = din("rwkv_mu", [L, 2, 3328]); rwkv_w0 = din("rwkv_w0", [L, 2, 1024]); rwkv_w2 = din("rwkv_w2", [L, 2, 64, 1024])
    rwkv_a0 = din("rwkv_a0", [L, 2, 1024]); rwkv_a2 = din("rwkv_a2", [L, 2, 64, 1024]); rwkv_k_k = din("rwkv_k_k", [L, 1024])
    rwkv_k_a = din("rwkv_k_a", [L, 1024]); rwkv_r_k = din("rwkv_r_k", [L, 1024]); rwkv_ln_w = din("rwkv_ln_w", [L, 1024])
    rwkv_ln_b = din("rwkv_ln_b", [L, 1024])
    y_out = nc.dram_tensor("y", [S, D], F32, kind="ExternalOutput").ap()
    proj = dscr("proj", [S, NCOLS])
    wbf_in = nc.dram_tensor("wbf_in", [D, NCOLS], BF16, kind="Internal").ap()
    rwc = dscr("rwc", [S, 3328])
    ysc = dscr("ysc", [2, S, 1024])
    bon = dscr("bon", [2, S, 16])
    br = dscr("br", [S, 4096])
    ygd = dscr("ygd", [S, 1024])
    wbf_out = nc.dram_tensor("wbf_out", [D, D], BF16, kind="Internal").ap()
    qT_d = nc.dram_tensor("qT_d", [8, 192, S], BF16, kind="Internal").ap()
    kT_d = nc.dram_tensor("kT_d", [8, 192, S], BF16, kind="Internal").ap()
    v_d = nc.dram_tensor("v_d", [S, 1024], BF16, kind="Internal").ap()

    p = Prog(nc, es)

    uniq = [0]

    def sb(stack, name, shape, dt=F32):
        uniq[0] += 1
        return stack.enter_context(nc.sbuf_tensor(f"{name}_{uniq[0]}", list(shape), dt))

    def ps(stack, name, shape, dt=F32):
        uniq[0] += 1
        return stack.enter_context(nc.psum_tensor(f"{name}_{uniq[0]}", list(shape), dt))

    ident_f = sb(es, "ident_f", [128, 128], F32)
    ident_b = sb(es, "ident_b", [128, 128], BF16)
    p.dma('sp', ident_f[:], cst_ident[:, :], writes=['ident_f'])
    p.op('dve', lambda e: e.tensor_copy(ident_b[:], ident_f[:]), reads=['ident_f'], writes=['ident_b'])

    eps_t = sb(es, "eps_t", [128, 1], F32)
    p.op('dve', lambda e: e.memset(eps_t[:], EPS), writes=['eps_t'])

    def phase_A(l, xsrc):
        for r in range(0, D, 512):
            p.dma('pool', wbf_in[r:r + 512, :], w_in[l, r:r + 512, :], writes=[('wbf_in', r)])
        with ExitStack() as st:
            gt = sb(st, "A_g", [128, D], F32)
            xt = sb(st, "A_x", [128, D], F32)
            hb = sb(st, "A_hb", [128, D], BF16)
            hT = sb(st, "A_hT", [128, 32, 1024], BF16)
            W = [sb(st, f"A_W{i}", [128, 32, 512], BF16) for i in range(2)]
            ob = [sb(st, f"A_ob{i}", [128, 512], F32) for i in range(4)]
            ss = sb(st, "A_ss", [128, 1], F32)
            rstd = sb(st, "A_rstd", [128, 1], F32)
            ptr = [ps(st, f"A_pt{i}", [128, 8, 128], BF16) for i in range(2)]
            pmm = [ps(st, f"A_pm{i}", [128, 512], F32) for i in range(4)]
            p.dma('sp', gt[:], ln_g[l:l + 1, :].partition_broadcast(128), writes=['A_g'])
            nch = (NCOLS + 511) // 512
            wi = 0
            for g in range(S // 1024):
                for ti in range(8):
                    t0 = g * 1024 + ti * 128
                    p.dma('sp', xt[:], xsrc[t0:t0 + 128, :], writes=['A_x'])
                    p.op('act', lambda e: e.activation(hb[:], xt[:], AF.Square, accum_out=ss[:]),
                         reads=['A_x'], writes=['A_hb', 'A_ss'])
                    p.op('act', lambda e: e.activation(rstd[:], ss[:], AF.Sqrt, bias=eps_t[:], scale=1.0 / D),
                         reads=['A_ss', 'eps_t'], writes=['A_rstd'])
                    p.op('dve', lambda e: e.reciprocal(rstd[:], rstd[:]), reads=['A_rstd'], writes=['A_rstd'])
                    p.op('dve', lambda e: e.scalar_tensor_tensor(hb[:], xt[:], rstd[:], gt[:], ALU.mult, ALU.mult),
                         reads=['A_x', 'A_rstd', 'A_g'], writes=['A_hb'])
                    for k8 in range(4):
                        pt = ptr[k8 % 2]
                        for kk in range(8):
                            k = k8 * 8 + kk
                            p.op('pe', lambda e: e.transpose(pt[:, kk, :], hb[:, k * 128:(k + 1) * 128], ident_b[:]),
                                 reads=['A_hb', 'ident_b'], writes=[('A_pt', k8 % 2)])
                        eng = 'act' if k8 % 2 == 0 else 'dve'
                        dst = hT[:, k8 * 8:(k8 + 1) * 8, ti * 128:(ti + 1) * 128]
                        if eng == 'act':
                            p.op('act', lambda e: e.copy(dst, pt[:]), reads=[('A_pt', k8 % 2)], writes=[('A_hT', ti)])
                        else:
                            p.op('dve', lambda e: e.tensor_copy(dst, pt[:]), reads=[('A_pt', k8 % 2)], writes=[('A_hT', ti)])
                def load_W(ci_, wi_):
                    n0_ = ci_ * 512
                    nw_ = min(512, NCOLS - n0_)
                    for k4 in range(4):
                        p.dma('sp', W[wi_ % 2][:, k4 * 8:(k4 + 1) * 8, 0:nw_],
                              wbf_in[k4 * 1024:(k4 + 1) * 1024, n0_:n0_ + nw_].rearrange("(k p) n -> p k n", p=128),
                              reads=[('wbf_in', (k4 * 1024) // 512 * 512), ('wbf_in', (k4 * 1024) // 512 * 512 + 512)],
                              writes=[('A_W', wi_ % 2)])
                load_W(0, wi)
                for ci in range(nch):
                    n0 = ci * 512
                    nw = min(512, NCOLS - n0)
                    Wt = W[wi % 2]
                    if ci + 1 < nch:
                        load_W(ci + 1, wi + 1)
                    for ti in range(8):
                        t0 = g * 1024 + ti * 128
                        j = (ci * 8 + ti) % 4
                        pm = pmm[j]
                        for k in range(32):
                            p.op('pe', lambda e: e.matmul(pm[:, 0:nw], hT[:, k, ti * 128:(ti + 1) * 128], Wt[:, k, 0:nw],
                                                         start=(k == 0), stop=(k == 31)),
                                 reads=[('A_hT', ti), ('A_W', wi % 2)], writes=[('A_pm', j)])
                        if j % 2 == 0:
                            p.op('act', lambda e: e.copy(ob[j][:, 0:nw], pm[:, 0:nw]), reads=[('A_pm', j)], writes=[('A_ob', j)])
                        else:
                            p.op('dve', lambda e: e.tensor_copy(ob[j][:, 0:nw], pm[:, 0:nw]), reads=[('A_pm', j)], writes=[('A_ob', j)])
                        p.dma('sp', proj[t0:t0 + 128, n0:n0 + nw], ob[j][:, 0:nw], reads=[('A_ob', j)],
                              writes=[('proj', t0 // 128, ci)])
                    wi += 1
        p.barrier()


    RW = 3328
    NEG_E = -float(np.exp(-0.5))
    RW_DT = RWDT

    def phase_D(l):
        with ExitStack() as st:
            mup = sb(st, "D0_mup", [128, RW]); mun = sb(st, "D0_mun", [128, RW]); m0 = sb(st, "D0_m0", [128, RW])
            ct = sb(st, "D0_c", [128, RW]); pt_ = sb(st, "D0_p", [128, RW]); nt = sb(st, "D0_n", [128, RW])
            p.dma('sp', mup[:], rwkv_mu[l, 0:1, :].partition_broadcast(128), writes=['D0_mup'])
            p.dma('sp', mun[:], rwkv_mu[l, 1:2, :].partition_broadcast(128), writes=['D0_mun'])
            p.op('dve', lambda e: e.tensor_tensor(m0[:], mup[:], mun[:], ALU.add), reads=['D0_mup', 'D0_mun'], writes=['D0_m0'])
            p.op('dve', lambda e: e.tensor_scalar(m0[:], m0[:], -1.0, 1.0, ALU.mult, ALU.add), reads=['D0_m0'], writes=['D0_m0'])
            for i in range(NT):
                t0 = i * 128
                p.dma('sp', ct[:], proj[t0:t0 + 128, C_RW:C_RW + RW], reads=[('proj', i, 'all')], writes=['D0_c'])
                if i == 0:
                    p.op('pool', lambda e: e.memset(pt_[:], 0.0), writes=['D0_p'])
                    p.dma('sp', pt_[1:128, :], proj[0:127, C_RW:C_RW + RW], reads=[('proj', 0, 'all')], writes=['D0_p'])
                else:
                    p.dma('sp', pt_[:], proj[t0 - 1:t0 + 127, C_RW:C_RW + RW], reads=[('proj', i, 'all'), ('proj', i - 1, 'all')], writes=['D0_p'])
                if i == NT - 1:
                    p.op('pool', lambda e: e.memset(nt[:], 0.0), writes=['D0_n'])
                    p.dma('sp', nt[0:127, :], proj[t0 + 1:t0 + 128, C_RW:C_RW + RW], reads=[('proj', i, 'all')], writes=['D0_n'])
                else:
                    p.dma('sp', nt[:], proj[t0 + 1:t0 + 129, C_RW:C_RW + RW], reads=[('proj', i, 'all'), ('proj', i + 1, 'all')], writes=['D0_n'])
                p.op('dve', lambda e: e.tensor_tensor(ct[:], ct[:], m0[:], ALU.mult), reads=['D0_c', 'D0_m0'], writes=['D0_c'])
                p.op('pool', lambda e: e.tensor_tensor(pt_[:], pt_[:], mup[:], ALU.mult), reads=['D0_p', 'D0_mup'], writes=['D0_p'])
                p.op('pool', lambda e: e.tensor_tensor(nt[:], nt[:], mun[:], ALU.mult), reads=['D0_n', 'D0_mun'], writes=['D0_n'])
                p.op('dve', lambda e: e.tensor_tensor(ct[:], ct[:], pt_[:], ALU.add), reads=['D0_c', 'D0_p'], writes=['D0_c'])
                p.op('dve', lambda e: e.tensor_tensor(ct[:], ct[:], nt[:], ALU.add), reads=['D0_c', 'D0_n'], writes=['D0_c'])
                p.dma('sp', rwc[t0:t0 + 128, :], ct[:], reads=['D0_c'], writes=[('rwc', i)])
        p.barrier()
        with ExitStack() as st:
            def bc(name, src):
                t = sb(st, name, [128, 1024])
                p.dma('sp', t[:], src.partition_broadcast(128), writes=[name])
                return t
            kk_c = bc("D_kk_c", rwkv_k_k[l:l + 1, :]); ka_c = bc("D_ka_c", rwkv_k_a[l:l + 1, :])
            rk_c = bc("D_rk_c", rwkv_r_k[l:l + 1, :])
            c1 = sb(st, "D_c1", [128, 1024])
            p.op('dve', lambda e: e.tensor_scalar(c1[:], ka_c[:], -1.0, 1.0, ALU.mult, ALU.add), reads=['D_ka_c'], writes=['D_c1'])
            w0_c = sb(st, "D_w0", [128, 1024]); a0_c = sb(st, "D_a0", [128, 1024])
            w2_t = sb(st, "D_w2", [64, 1024]); a2_t = sb(st, "D_a2", [64, 1024])
            mS = sb(st, "D_mS", [128, 128]); mI = sb(st, "D_mI", [128, 128]); mST = sb(st, "D_mST", [128, 128])
            imask = sb(st, "D_imask", [64, 1024])
            b4 = lambda t: t[:].unsqueeze(1).broadcast_to([128, 4, 128])
            v4 = lambda a: a.rearrange("p (a b) -> p a b", a=4)
            triI = sb(st, "D_triI", [128, 128]); triC = sb(st, "D_triC", [128, 128])
            identr = sb(st, "D_identr", [128, 128], RW_DT)
            p.op('dve', lambda e: e.tensor_copy(identr[:], ident_f[:]), reads=['ident_f'], writes=['D_identr'])
            for h in range(16):
                p.op('pool', lambda e: e.tensor_copy(imask[:, h * 64:(h + 1) * 64], ident_f[0:64, 0:64]), reads=['ident_f'], writes=['D_imask'])
            rw = sb(st, "D_rw", [128, RW])
            kk = sb(st, "D_kk", [128, 1024]); ld = sb(st, "D_ld", [128, 1024]); a_t = sb(st, "D_a", [128, 1024])
            kd = sb(st, "D_kd", [128, 1024]); ba = sb(st, "D_ba", [128, 1024]); tmp = sb(st, "D_tmp", [128, 1024])
            Ab = sb(st, "D_Ab", [128, 1024]); Rb = sb(st, "D_Rb", [128, 1024]); Bb = sb(st, "D_Bb", [128, 1024]); Kb = sb(st, "D_Kb", [128, 1024])
            Abr = sb(st, "D_Abr", [128, 1024], RW_DT)
            Bt = sb(st, "D_Bt", [128, 1024], RW_DT); Kt = sb(st, "D_Kt", [128, 1024], RW_DT); Vr = sb(st, "D_Vr", [128, 1024], RW_DT)
            ydg = sb(st, "D_ydg", [64, 1024], RW_DT)
            sm = sb(st, "D_sm", [128, 64]); smT = sb(st, "D_smT", [64, 2, 128])
            hs = sb(st, "D_hs", [128, 16]); hs2 = sb(st, "D_hs2", [128, 16])
            AbT = sb(st, "D_AbT", [64, 16, 128], RW_DT); RbT = sb(st, "D_RbT", [64, 16, 128], RW_DT)
            BbT = sb(st, "D_BbT", [64, 16, 128], RW_DT); KbT = sb(st, "D_KbT", [64, 16, 128], RW_DT)
            Q = [[sb(st, f"D_Q{g}{i}", [128, 512], RW_DT) for i in range(2)] for g in range(4)]
            QT = [[sb(st, f"D_QT{g}{i}", [128, 512], RW_DT) for i in range(2)] for g in range(4)]
            P = [sb(st, f"D_P{g}", [128, 512]) for g in range(4)]; Pr = [sb(st, f"D_Pr{g}", [128, 512], RW_DT) for g in range(4)]
            MrbT = [sb(st, f"D_MrbT{g}", [128, 512], RW_DT) for g in range(4)]; LakT = [sb(st, f"D_LakT{g}", [128, 512], RW_DT) for g in range(4)]
            MrkT = [sb(st, f"D_MrkT{g}", [128, 512], RW_DT) for g in range(4)]
            AXt = [sb(st, f"D_AX{g}", [128, 4, 128], RW_DT) for g in range(4)]; AU = [sb(st, f"D_AU{g}", [128, 4, 128], RW_DT) for g in range(4)]
            RhT = sb(st, "D_RhT", [64, 16, 128], RW_DT); GT = sb(st, "D_GT", [64, 1024], RW_DT)
            Hh = sb(st, "D_H", [64, 1024]); Yh = sb(st, "D_Yh", [128, 1024])
            ST = sb(st, "D_ST", [64, 1024]); STr = sb(st, "D_STr", [64, 1024], RW_DT)
            pb = [ps(st, f"D_pb{i}", [128, 512]) for i in range(8)]
            pbi = [0]

            def nb():
                i = pbi[0] % 8
                pbi[0] += 1
                return i

            for d in range(2):
                p.dma('sp', w0_c[:], rwkv_w0[l, d:d + 1, :].partition_broadcast(128), writes=['D_w0'])
                p.dma('sp', a0_c[:], rwkv_a0[l, d:d + 1, :].partition_broadcast(128), writes=['D_a0'])
                p.dma('sp', w2_t[:], rwkv_w2[l, d, :, :], writes=['D_w2'])
                p.dma('sp', a2_t[:], rwkv_a2[l, d, :, :], writes=['D_a2'])
                cm = c_masks
                p.dma('sp', mS[:], cm[0 if d == 0 else 1], writes=['D_mS'])
                p.dma('sp', mI[:], cm[2 if d == 0 else 3], writes=['D_mI'])
                p.dma('sp', mST[:], cm[1 if d == 0 else 0], writes=['D_mST'])
                p.dma('sp', triI[:], cm[2 if d == 0 else 3], writes=['D_triI'])
                p.dma('sp', triC[:], cm[1 if d == 0 else 0], writes=['D_triC'])
                p.op('dve', lambda e: e.memset(ST[:], 0.0), writes=['D_ST'])
                p.op('dve', lambda e: e.memset(STr[:], 0.0), writes=['D_STr'])
                order = range(NT) if d == 0 else range(NT - 1, -1, -1)
                for c in order:
                    t0 = c * 128
                    p.dma('sp', rw[:], rwc[t0:t0 + 128, :], reads=[('rwc', c)], writes=['D_rw'])
                    r_ = rw[:, 0:1024]; k_ = rw[:, 1024:2048]; v_ = rw[:, 2048:3072]
                    win = rw[:, 3072 + 64 * d:3136 + 64 * d]; ain = rw[:, 3200 + 64 * d:3264 + 64 * d]
                    p.op('dve', lambda e: e.tensor_tensor(kk[:], k_, kk_c[:], ALU.mult), reads=['D_rw', 'D_kk_c'], writes=['D_kk'])
                    p.op('pool', lambda e: e.tensor_tensor(tmp[:], kk[:], kk[:], ALU.mult), reads=['D_kk'], writes=['D_tmp'])
                    p.op('dve', lambda e: e.tensor_reduce(hs[:], tmp[:].rearrange("p (h j) -> p h j", h=16), AX.X, ALU.add), reads=['D_tmp'], writes=['D_hs'])
                    p.op('act', lambda e: e.activation(hs[:], hs[:], AF.Sqrt), reads=['D_hs'], writes=['D_hs'])
                    p.op('dve', lambda e: e.tensor_scalar(hs[:], hs[:], 1e-12, None, ALU.max), reads=['D_hs'], writes=['D_hs'])
                    p.op('dve', lambda e: e.reciprocal(hs[:], hs[:]), reads=['D_hs'], writes=['D_hs'])
                    p.op('dve', lambda e: e.tensor_tensor(kk[:].rearrange("p (h j) -> p h j", h=16), kk[:].rearrange("p (h j) -> p h j", h=16),
                                                         hs[:].unsqueeze(2).broadcast_to([128, 16, 64]), ALU.mult), reads=['D_kk', 'D_hs'], writes=['D_kk'])
                    p.op('act', lambda e: e.activation(sm[:], win, AF.Tanh), reads=['D_rw'], writes=['D_sm'])
                    b0 = nb()
                    p.op('pe', lambda e: e.transpose(pb[b0][0:64, 0:128], sm[:], ident_f[:]), reads=['D_sm', 'ident_f'], writes=[('D_pb', b0)])
                    p.op('pe', lambda e: e.transpose(pb[b0][0:64, 128:256], ain, ident_f[:]), reads=['D_rw', 'ident_f'], writes=[('D_pb', b0)])
                    p.op('act', lambda e: e.copy(smT[:].rearrange("p a b -> p (a b)"), pb[b0][0:64, 0:256]), reads=[('D_pb', b0)], writes=['D_smT'])
                    for half in range(2):
                        cs_ = slice(half * 512, (half + 1) * 512)
                        b1 = nb()
                        p.op('pe', lambda e: e.matmul(pb[b1][:, :], smT[:, 0, :], w2_t[:, cs_], start=True, stop=True),
                             reads=['D_smT', 'D_w2'], writes=[('D_pb', b1)])
                        p.op('dve', lambda e: e.tensor_tensor(ld[:, cs_], pb[b1][:, :], w0_c[:, cs_], ALU.add), reads=[('D_pb', b1), 'D_w0'], writes=['D_ld'])
                        b2 = nb()
                        p.op('pe', lambda e: e.matmul(pb[b2][:, :], smT[:, 1, :], a2_t[:, cs_], start=True, stop=True),
                             reads=['D_smT', 'D_a2'], writes=[('D_pb', b2)])
                        p.op('dve', lambda e: e.tensor_tensor(a_t[:, cs_], pb[b2][:, :], a0_c[:, cs_], ALU.add), reads=[('D_pb', b2), 'D_a0'], writes=['D_a'])
                    p.op('act', lambda e: e.activation(ld[:], ld[:], AF.Sigmoid), reads=['D_ld'], writes=['D_ld'])
                    p.op('act', lambda e: e.activation(a_t[:], a_t[:], AF.Sigmoid), reads=['D_a'], writes=['D_a'])
                    p.op('pool', lambda e: e.tensor_scalar(ld[:], ld[:], NEG_E, None, ALU.mult), reads=['D_ld'], writes=['D_ld'])
                    p.op('dve', lambda e: e.tensor_tensor(tmp[:], a_t[:], ka_c[:], ALU.mult), reads=['D_a', 'D_ka_c'], writes=['D_tmp'])
                    p.op('dve', lambda e: e.tensor_tensor(tmp[:], tmp[:], c1[:], ALU.add), reads=['D_tmp', 'D_c1'], writes=['D_tmp'])
                    p.op('dve', lambda e: e.tensor_tensor(kd[:], tmp[:], k_, ALU.mult), reads=['D_tmp', 'D_rw'], writes=['D_kd'])
                    p.op('pool', lambda e: e.tensor_tensor(ba[:], kk[:], a_t[:], ALU.mult), reads=['D_kk', 'D_a'], writes=['D_ba'])
                    p.op('pool', lambda e: e.tensor_tensor(tmp[:], kd[:], rk_c[:], ALU.mult), reads=['D_kd', 'D_rk_c'], writes=['D_tmp'])
                    p.op('pool', lambda e: e.tensor_tensor(tmp[:], tmp[:], r_, ALU.mult), reads=['D_tmp', 'D_rw'], writes=['D_tmp'])
                    p.op('dve', lambda e: e.tensor_reduce(hs2[:], tmp[:].rearrange("p (h j) -> p h j", h=16), AX.X, ALU.add), reads=['D_tmp'], writes=['D_hs2'])
                    p.dma('sp', bon[d, t0:t0 + 128, :], hs2[:], reads=['D_hs2'], writes=[('bon', d, c)])
                    for half in range(2):
                        cs_ = slice(half * 512, (half + 1) * 512)
                        bcs = nb()
                        p.op('pe', lambda e: e.matmul(pb[bcs][:, :], triI[:], ld[:, cs_], start=True, stop=True), reads=['D_triI', 'D_ld'], writes=[('D_pb', bcs)])
                        brm = nb()
                        p.op('pe', lambda e: e.matmul(pb[brm][:, :], triC[:], ld[:, cs_], start=True, stop=True), reads=['D_triC', 'D_ld'], writes=[('D_pb', brm)])
                        p.op('dve', lambda e: e.tensor_tensor(tmp[:, cs_], pb[bcs][:, :], ld[:, cs_], ALU.subtract), reads=[('D_pb', bcs), 'D_ld'], writes=['D_tmp'])
                        p.op('act', lambda e: e.activation(tmp[:, cs_], tmp[:, cs_], AF.Exp), reads=['D_tmp'], writes=['D_tmp'])
                        p.op('dve', lambda e: e.scalar_tensor_tensor(Ab[:, cs_], kk[:, cs_], -1.0, tmp[:, cs_], ALU.mult, ALU.mult), reads=['D_kk', 'D_tmp'], writes=['D_Ab'])
                        p.op('act', lambda e: e.activation(Rb[:, cs_], pb[bcs][:, :], AF.Exp), reads=[('D_pb', bcs)], writes=['D_Rb'])
                        p.op('act', lambda e: e.activation(Kt[0:64, cs_] if False else tmp[0:64, cs_], pb[brm][0:64, :], AF.Exp), reads=[('D_pb', brm), 'D_tmp'], writes=['D_tmp'])
                        p.op('dve', lambda e: e.tensor_tensor(tmp[0:64, cs_], tmp[0:64, cs_], Rb[0:64, cs_], ALU.mult), reads=['D_tmp', 'D_Rb'], writes=['D_tmp'])
                        p.op('dve', lambda e: e.tensor_tensor(ydg[:, cs_], tmp[0:64, cs_], imask[:, cs_], ALU.mult), reads=['D_tmp', 'D_imask'], writes=['D_ydg'])
                        p.op('pool', lambda e: e.tensor_tensor(Rb[:, cs_], Rb[:, cs_], r_[:, cs_] if False else rw[:, half * 512:(half + 1) * 512], ALU.mult), reads=['D_Rb', 'D_rw', 'D_tmp'], writes=['D_Rb'])
                        p.op('act', lambda e: e.activation(tmp[:, cs_], pb[bcs][:, :], AF.Exp, scale=-1.0), reads=[('D_pb', bcs), 'D_tmp', 'D_ydg'], writes=['D_tmp'])
                        p.op('dve', lambda e: e.tensor_tensor(Bb[:, cs_], ba[:, cs_], tmp[:, cs_], ALU.mult), reads=['D_ba', 'D_tmp'], writes=['D_Bb'])
                        p.op('pool', lambda e: e.tensor_tensor(Kb[:, cs_], kd[:, cs_], tmp[:, cs_], ALU.mult), reads=['D_kd', 'D_tmp'], writes=['D_Kb'])
                        p.op('act', lambda e: e.activation(tmp[:, cs_], pb[brm][:, :], AF.Exp), reads=[('D_pb', brm), 'D_tmp', 'D_Bb', 'D_Kb'], writes=['D_tmp'])
                        p.op('dve', lambda e: e.tensor_tensor(Bt[:, cs_], ba[:, cs_], tmp[:, cs_], ALU.mult), reads=['D_ba', 'D_tmp'], writes=['D_Bt'])
                        p.op('pool', lambda e: e.tensor_tensor(Kt[:, cs_], kd[:, cs_], tmp[:, cs_], ALU.mult), reads=['D_kd', 'D_tmp'], writes=['D_Kt'])
                    p.op('act', lambda e: e.copy(Vr[:], v_), reads=['D_rw'], writes=['D_Vr'])
                    p.op('act', lambda e: e.copy(Abr[:], Ab[:]), reads=['D_Ab'], writes=['D_Abr'])
                    for (src, dstT, nm) in ((Ab, AbT, 'D_AbT'), (Rb, RbT, 'D_RbT'), (Bb, BbT, 'D_BbT'), (Kb, KbT, 'D_KbT')):
                        srcn = {'D_AbT': 'D_Ab', 'D_RbT': 'D_Rb', 'D_BbT': 'D_Bb', 'D_KbT': 'D_Kb'}[nm]
                        for h4 in range(4):
                            bt = nb()
                            for hl in range(4):
                                h = h4 * 4 + hl
                                p.op('pe', lambda e: e.transpose(pb[bt][0:64, hl * 128:(hl + 1) * 128], src[:, h * 64:(h + 1) * 64], ident_f[:]),
                                     reads=[srcn, 'ident_f'], writes=[('D_pb', bt)])
                            dst = dstT[:, h4 * 4:(h4 + 1) * 4, :].rearrange("p a b -> p (a b)")
                            if h4 % 2 == 0:
                                p.op('act', lambda e: e.copy(dst, pb[bt][0:64, :]), reads=[('D_pb', bt)], writes=[nm])
                            else:
                                p.op('dve', lambda e: e.tensor_copy(dst, pb[bt][0:64, :]), reads=[('D_pb', bt)], writes=[nm])
                    H4 = range(4)
                    qi = {}
                    for h4 in H4:
                        bA, bB, bC, bD, bE = nb(), nb(), nb(), nb(), nb()
                        for hl in range(4):
                            h = h4 * 4 + hl
                            sl = slice(hl * 128, (hl + 1) * 128)
                            p.op('pe', lambda e: e.matmul(pb[bA][:, sl], BbT[:, h, :], AbT[:, h, :], start=True, stop=True), reads=['D_BbT', 'D_AbT'], writes=[('D_pb', bA)])
                            p.op('pe', lambda e: e.matmul(pb[bB][:, sl], BbT[:, h, :], RbT[:, h, :], start=True, stop=True), reads=['D_BbT', 'D_RbT'], writes=[('D_pb', bB)])
                            p.op('pe', lambda e: e.matmul(pb[bC][:, sl], KbT[:, h, :], AbT[:, h, :], start=True, stop=True), reads=['D_KbT', 'D_AbT'], writes=[('D_pb', bC)])
                            p.op('pe', lambda e: e.matmul(pb[bD][:, sl], KbT[:, h, :], RbT[:, h, :], start=True, stop=True), reads=['D_KbT', 'D_RbT'], writes=[('D_pb', bD)])
                            p.op('pe', lambda e: e.matmul(pb[bE][:, sl], AbT[:, h, :], BbT[:, h, :], start=True, stop=True), reads=['D_BbT', 'D_AbT'], writes=[('D_pb', bE)])
                        qi[h4] = 0
                        p.op('dve', lambda e: e.tensor_tensor(v4(Q[h4][0][:]), v4(pb[bA][:, :]), b4(mS), ALU.mult), reads=[('D_pb', bA), 'D_mS'], writes=[('D_Q', h4, 0)])
                        p.op('dve', lambda e: e.tensor_tensor(v4(MrbT[h4][:]), v4(pb[bB][:, :]), b4(mI), ALU.mult), reads=[('D_pb', bB), 'D_mI'], writes=[('D_MrbT', h4)])
                        p.op('dve', lambda e: e.tensor_tensor(v4(LakT[h4][:]), v4(pb[bC][:, :]), b4(mS), ALU.mult), reads=[('D_pb', bC), 'D_mS'], writes=[('D_LakT', h4)])
                        p.op('dve', lambda e: e.tensor_tensor(v4(MrkT[h4][:]), v4(pb[bD][:, :]), b4(mI), ALU.mult), reads=[('D_pb', bD), 'D_mI'], writes=[('D_MrkT', h4)])
                        p.op('dve', lambda e: e.tensor_tensor(v4(QT[h4][0][:]), v4(pb[bE][:, :]), b4(mST), ALU.mult), reads=[('D_pb', bE), 'D_mST'], writes=[('D_QT', h4, 0)])
                        p.op('dve', lambda e: e.tensor_tensor(v4(P[h4][:]), v4(Q[h4][0][:]), b4(ident_f), ALU.add), reads=[('D_Q', h4, 0), 'ident_f'], writes=[('D_P', h4)])
                        p.op('act', lambda e: e.copy(Pr[h4][:], P[h4][:]), reads=[('D_P', h4)], writes=[('D_Pr', h4)])
                    for lvl in range(6):
                        bqT = {}; bq = {}; bp = {}
                        for h4 in H4:
                            q0 = qi[h4]
                            bqT[h4] = nb()
                            for hl in range(4):
                                sl = slice(hl * 128, (hl + 1) * 128)
                                p.op('pe', lambda e: e.matmul(pb[bqT[h4]][:, sl], Q[h4][q0][:, sl], QT[h4][q0][:, sl], start=True, stop=True),
                                     reads=[('D_Q', h4, q0), ('D_QT', h4, q0)], writes=[('D_pb', bqT[h4])])
                            if lvl < 5:
                                bq[h4] = nb()
                                for hl in range(4):
                                    sl = slice(hl * 128, (hl + 1) * 128)
                                    p.op('pe', lambda e: e.matmul(pb[bq[h4]][:, sl], QT[h4][q0][:, sl], Q[h4][q0][:, sl], start=True, stop=True),
                                         reads=[('D_Q', h4, q0), ('D_QT', h4, q0)], writes=[('D_pb', bq[h4])])
                        for h4 in H4:
                            qn = 1 - qi[h4]
                            p.op('act', lambda e: e.copy(QT[h4][qn][:], pb[bqT[h4]][:, :]), reads=[('D_pb', bqT[h4])], writes=[('D_QT', h4, qn)])
                            if lvl < 5:
                                p.op('act' if h4 % 2 else 'dve', (lambda e: e.copy(Q[h4][qn][:], pb[bq[h4]][:, :])) if h4 % 2 else (lambda e: e.tensor_copy(Q[h4][qn][:], pb[bq[h4]][:, :])),
                                     reads=[('D_pb', bq[h4])], writes=[('D_Q', h4, qn)])
                        for h4 in H4:
                            qn = 1 - qi[h4]
                            bp[h4] = nb()
                            for hl in range(4):
                                sl = slice(hl * 128, (hl + 1) * 128)
                                p.op('pe', lambda e: e.matmul(pb[bp[h4]][:, sl], QT[h4][qn][:, sl], Pr[h4][:, sl], start=True, stop=True),
                                     reads=[('D_QT', h4, qn), ('D_Pr', h4)], writes=[('D_pb', bp[h4])])
                        for h4 in H4:
                            p.op('dve', lambda e: e.tensor_tensor(P[h4][:], P[h4][:], pb[bp[h4]][:, :], ALU.add), reads=[('D_P', h4), ('D_pb', bp[h4])], writes=[('D_P', h4)])
                            p.op('act', lambda e: e.copy(Pr[h4][:], P[h4][:]), reads=[('D_P', h4)], writes=[('D_Pr', h4)])
                            qi[h4] = 1 - qi[h4]
                    bx = {}; bu = {}
                    for h4 in H4:
                        bx[h4] = nb()
                        for hl in range(4):
                            h = h4 * 4 + hl
                            p.op('pe', lambda e: e.matmul(pb[bx[h4]][:, hl * 64:(hl + 1) * 64], LakT[h4][:, hl * 128:(hl + 1) * 128], Vr[:, h * 64:(h + 1) * 64], start=True, stop=True),
                                 reads=[('D_LakT', h4), 'D_Vr'], writes=[('D_pb', bx[h4])])
                    for h4 in H4:
                        p.op('act', lambda e: e.copy(AXt[h4][:, :, 64:128], pb[bx[h4]][:, 0:256].rearrange("p (a b) -> p a b", a=4)), reads=[('D_pb', bx[h4])], writes=[('D_AX', h4)])
                        p.op('dve', lambda e: e.tensor_copy(AXt[h4][:, :, 0:64], Abr[:, h4 * 256:(h4 + 1) * 256].rearrange("p (a b) -> p a b", a=4)), reads=['D_Abr'], writes=[('D_AX', h4)])
                    for h4 in H4:
                        bu[h4] = nb()
                        for hl in range(4):
                            p.op('pe', lambda e: e.matmul(pb[bu[h4]][:, hl * 128:(hl + 1) * 128], Pr[h4][:, hl * 128:(hl + 1) * 128], AXt[h4][:, hl, :], start=True, stop=True),
                                 reads=[('D_Pr', h4), ('D_AX', h4)], writes=[('D_pb', bu[h4])])
                    for h4 in H4:
                        p.op('act', lambda e: e.copy(AU[h4][:].rearrange("p a b -> p (a b)"), pb[bu[h4]][:, :]), reads=[('D_pb', bu[h4])], writes=[('D_AU', h4)])
                    for h4 in H4:
                        br_, bg, bh, by = nb(), nb(), nb(), nb()
                        for hl in range(4):
                            h = h4 * 4 + hl
                            hc = slice(h * 64, (h + 1) * 64)
                            p.op('pe', lambda e: e.matmul(pb[br_][0:64, hl * 128:(hl + 1) * 128], AU[h4][:, hl, 0:64], MrbT[h4][:, hl * 128:(hl + 1) * 128], start=True, stop=True),
                                 reads=[('D_AU', h4), ('D_MrbT', h4)], writes=[('D_pb', br_)])
                            p.op('pe', lambda e: e.matmul(pb[bg][0:64, hl * 64:(hl + 1) * 64], AU[h4][:, hl, 0:64], Bt[:, hc], start=True, stop=False),
                                 reads=[('D_AU', h4), 'D_Bt'], writes=[('D_pb', bg)])
                            p.op('pe', lambda e: e.matmul(pb[bg][0:64, hl * 64:(hl + 1) * 64], identr[0:64, 0:64], ydg[:, hc], start=False, stop=True),
                                 reads=['D_identr', 'D_ydg'], writes=[('D_pb', bg)])
                            p.op('pe', lambda e: e.matmul(pb[bh][0:64, hl * 64:(hl + 1) * 64], Bt[:, hc], AU[h4][:, hl, 64:128], start=True, stop=False),
                                 reads=[('D_AU', h4), 'D_Bt'], writes=[('D_pb', bh)])
                            p.op('pe', lambda e: e.matmul(pb[bh][0:64, hl * 64:(hl + 1) * 64], Kt[:, hc], Vr[:, hc], start=False, stop=True),
                                 reads=['D_Kt', 'D_Vr'], writes=[('D_pb', bh)])
                            p.op('pe', lambda e: e.matmul(pb[by][:, hl * 64:(hl + 1) * 64], MrbT[h4][:, hl * 128:(hl + 1) * 128], AU[h4][:, hl, 64:128], start=True, stop=False),
                                 reads=[('D_AU', h4), ('D_MrbT', h4)], writes=[('D_pb', by)])
                            p.op('pe', lambda e: e.matmul(pb[by][:, hl * 64:(hl + 1) * 64], MrkT[h4][:, hl * 128:(hl + 1) * 128], Vr[:, hc], start=False, stop=True),
                                 reads=[('D_MrkT', h4), 'D_Vr'], writes=[('D_pb', by)])
                        p.op('dve', lambda e: e.tensor_tensor(RhT[:, h4 * 4:(h4 + 1) * 4, :].rearrange("p a b -> p (a b)"), pb[br_][0:64, :],
                                                             RbT[:, h4 * 4:(h4 + 1) * 4, :].rearrange("p a b -> p (a b)"), ALU.add),
                             reads=[('D_pb', br_), 'D_RbT'], writes=['D_RhT'])
                        p.op('act', lambda e: e.copy(GT[:, h4 * 256:(h4 + 1) * 256], pb[bg][0:64, 0:256]), reads=[('D_pb', bg)], writes=['D_GT'])
                        p.op('act', lambda e: e.copy(Hh[:, h4 * 256:(h4 + 1) * 256], pb[bh][0:64, 0:256]), reads=[('D_pb', bh)], writes=['D_H'])
                        p.op('dve', lambda e: e.tensor_copy(Yh[:, h4 * 256:(h4 + 1) * 256], pb[by][:, 0:256]), reads=[('D_pb', by)], writes=['D_Yh'])
                    for half in range(2):
                        bY = nb()
                        bS = nb()
                        for hh in range(8):
                            h = half * 8 + hh
                            hc = slice(h * 64, (h + 1) * 64)
                            p.op('pe', lambda e: e.matmul(pb[bY][:, hh * 64:(hh + 1) * 64], RhT[:, h, :], STr[:, hc], start=True, stop=True),
                                 reads=['D_RhT', 'D_STr'], writes=[('D_pb', bY)])
                            p.op('pe', lambda e: e.matmul(pb[bS][0:64, hh * 64:(hh + 1) * 64], GT[:, hc], STr[:, hc], start=True, stop=True),
                                 reads=['D_GT', 'D_STr'], writes=[('D_pb', bS)])
                        cs_ = slice(half * 512, (half + 1) * 512)
                        p.op('dve', lambda e: e.tensor_tensor(Yh[:, cs_], pb[bY][:, :], Yh[:, cs_], ALU.add), reads=[('D_pb', bY), 'D_Yh'], writes=['D_Yh'])
                        p.op('dve', lambda e: e.tensor_tensor(ST[:, cs_], pb[bS][0:64, :], Hh[:, cs_], ALU.add), reads=[('D_pb', bS), 'D_H'], writes=[('D_ST', half)])
                    p.op('act', lambda e: e.copy(STr[:], ST[:]), reads=[('D_ST', 0), ('D_ST', 1)], writes=['D_STr'])
                    p.dma('sp', ysc[d, t0:t0 + 128, :], Yh[:], reads=['D_Yh'], writes=[('ysc', d, c)])
        p.barrier()
        with ExitStack() as st:
            lnw = sb(st, "D2_lnw", [128, 1024]); lnb = sb(st, "D2_lnb", [128, 1024])
            p.dma('sp', lnw[:], rwkv_ln_w[l:l + 1, :].partition_broadcast(128), writes=['D2_lnw'])
            p.dma('sp', lnb[:], rwkv_ln_b[l:l + 1, :].partition_broadcast(128), writes=['D2_lnb'])
            y0 = sb(st, "D2_y0", [128, 1024]); y1 = sb(st, "D2_y1", [128, 1024]); vt = sb(st, "D2_v", [128, 1024]); sq = sb(st, "D2_sq", [128, 1024])
            b0t = sb(st, "D2_b0", [128, 16]); b1t = sb(st, "D2_b1", [128, 16]); mean = sb(st, "D2_mean", [128, 16]); var = sb(st, "D2_var", [128, 16])
            eps2 = sb(st, "D2_eps", [128, 1])
            p.op('dve', lambda e: e.memset(eps2[:], 64e-5), writes=['D2_eps'])
            v3 = lambda t: t[:].rearrange("p (h j) -> p h j", h=16)
            bc3 = lambda t: t[:].unsqueeze(2).broadcast_to([128, 16, 64])
            for i in range(NT):
                t0 = i * 128
                p.dma('sp', y0[:], ysc[0, t0:t0 + 128, :], reads=[('ysc', 0, i)], writes=['D2_y0'])
                p.dma('sp', y1[:], ysc[1, t0:t0 + 128, :], reads=[('ysc', 1, i)], writes=['D2_y1'])
                p.dma('sp', vt[:], rwc[t0:t0 + 128, 2048:3072], reads=[('rwc', i)], writes=['D2_v'])
                p.dma('sp', b0t[:], bon[0, t0:t0 + 128, :], reads=[('bon', 0, i)], writes=['D2_b0'])
                p.dma('sp', b1t[:], bon[1, t0:t0 + 128, :], reads=[('bon', 1, i)], writes=['D2_b1'])
                p.op('dve', lambda e: e.tensor_tensor(y0[:], y0[:], y1[:], ALU.add), reads=['D2_y0', 'D2_y1'], writes=['D2_y0'])
                p.op('dve', lambda e: e.tensor_reduce(mean[:], v3(y0), AX.X, ALU.add), reads=['D2_y0'], writes=['D2_mean'])
                p.op('dve', lambda e: e.tensor_scalar(mean[:], mean[:], 1.0 / 64, None, ALU.mult), reads=['D2_mean'], writes=['D2_mean'])
                p.op('dve', lambda e: e.tensor_tensor(v3(y0), v3(y0), bc3(mean), ALU.subtract), reads=['D2_y0', 'D2_mean'], writes=['D2_y0'])
                p.op('pool', lambda e: e.tensor_tensor(sq[:], y0[:], y0[:], ALU.mult), reads=['D2_y0'], writes=['D2_sq'])
                p.op('dve', lambda e: e.tensor_reduce(var[:], v3(sq), AX.X, ALU.add), reads=['D2_sq'], writes=['D2_var'])
                p.op('act', lambda e: e.activation(var[:], var[:], AF.Sqrt, bias=eps2[:], scale=1.0 / 64), reads=['D2_var', 'D2_eps'], writes=['D2_var'])
                p.op('dve', lambda e: e.reciprocal(var[:], var[:]), reads=['D2_var'], writes=['D2_var'])
                p.op('dve', lambda e: e.tensor_tensor(v3(y0), v3(y0), bc3(var), ALU.mult), reads=['D2_y0', 'D2_var'], writes=['D2_y0'])
                p.op('pool', lambda e: e.tensor_tensor(y0[:], y0[:], lnw[:], ALU.mult), reads=['D2_y0', 'D2_lnw'], writes=['D2_y0'])
                p.op('pool', lambda e: e.tensor_tensor(y0[:], y0[:], lnb[:], ALU.add), reads=['D2_y0', 'D2_lnb'], writes=['D2_y0'])
                p.op('dve', lambda e: e.tensor_tensor(b0t[:], b0t[:], b1t[:], ALU.add), reads=['D2_b0', 'D2_b1'], writes=['D2_b0'])
                p.op('dve', lambda e: e.tensor_tensor(v3(vt), v3(vt), bc3(b0t), ALU.mult), reads=['D2_v', 'D2_b0'], writes=['D2_v'])
                p.op('dve', lambda e: e.tensor_tensor(y0[:], y0[:], vt[:], ALU.add), reads=['D2_y0', 'D2_v'], writes=['D2_y0'])
                p.dma('sp', br[t0:t0 + 128, 2048:3072], y0[:], reads=['D2_y0'], writes=[('br', i, 2)])
        p.barrier()


    TWO_PI = float(2 * np.pi)

    def phase_C(l):
        with ExitStack() as st:
            TC = 512
            lr = sb(st, "C_lr", [128, 32]); li = sb(st, "C_li", [128, 32])
            for two in range(2):
                p.dma('sp', lr[two * 64:(two + 1) * 64, :], s5_lam_re[l, two::2, :].rearrange("q p -> p q"), writes=['C_lr'], allow_slow_non_contiguous=True)
                p.dma('sp', li[two * 64:(two + 1) * 64, :], s5_lam_im[l, two::2, :].rearrange("q p -> p q"), writes=['C_li'], allow_slow_non_contiguous=True)
            den = sb(st, "C_den", [128, 32]); t_a = sb(st, "C_ta", [128, 32]); t_b = sb(st, "C_tb", [128, 32]); t_c = sb(st, "C_tc", [128, 32])
            t_i = sb(st, "C_ti", [128, 32], I32)
            p.op('dve', lambda e: e.tensor_tensor(den[:], lr[:], lr[:], ALU.mult), reads=['C_lr'], writes=['C_den'])
            p.op('dve', lambda e: e.tensor_tensor(t_a[:], li[:], li[:], ALU.mult), reads=['C_li'], writes=['C_ta'])
            p.op('dve', lambda e: e.tensor_tensor(den[:], den[:], t_a[:], ALU.add), reads=['C_den', 'C_ta'], writes=['C_den'])
            p.op('dve', lambda e: e.reciprocal(den[:], den[:]), reads=['C_den'], writes=['C_den'])
            mag = [sb(st, f"C_mag{d}", [128, 32]) for d in range(2)]
            th = [sb(st, f"C_th{d}", [128, 32]) for d in range(2)]
            cre = [sb(st, f"C_cre{d}", [128, 32]) for d in range(2)]
            cim = [sb(st, f"C_cim{d}", [128, 32]) for d in range(2)]
            dtt = sb(st, "C_dt", [128, 32]); sn = sb(st, "C_sn", [128, 32]); cs = sb(st, "C_cs", [128, 32])

            def emit_sin(out, ang, n, key_out, key_ang, ti_, tf_, kti, ktf):
                p.op('dve', lambda e: e.tensor_scalar(ti_, ang, 1.0 / TWO_PI, None, ALU.mult), reads=[key_ang], writes=[kti])
                p.op('dve', lambda e: e.tensor_copy(tf_, ti_), reads=[kti], writes=[ktf])
                p.op('dve', lambda e: e.scalar_tensor_tensor(tf_, tf_, -TWO_PI, ang, ALU.mult, ALU.add), reads=[ktf, key_ang], writes=[ktf])
                p.op('dve', lambda e: e.tensor_scalar(tf_, tf_, float(np.pi), float(-np.pi), ALU.min, ALU.max), reads=[ktf], writes=[ktf])
                p.op('act', lambda e: e.activation(out, tf_, AF.Sin), reads=[ktf], writes=[key_out])

            for d in range(2):
                for two in range(2):
                    p.dma('sp', dtt[two * 64:(two + 1) * 64, :], s5_log_dt[l, d:d + 1, two::2].partition_broadcast(64), writes=['C_dt'],
                          allow_slow_non_contiguous=True)
                p.op('act', lambda e: e.activation(dtt[:], dtt[:], AF.Exp), reads=['C_dt'], writes=['C_dt'])
                p.op('dve', lambda e: e.tensor_tensor(t_a[:], lr[:], dtt[:], ALU.mult), reads=['C_lr', 'C_dt'], writes=['C_ta'])
                p.op('act', lambda e: e.activation(mag[d][:], t_a[:], AF.Exp), reads=['C_ta'], writes=[f'C_mag{d}'])
                p.op('dve', lambda e: e.tensor_tensor(th[d][:], li[:], dtt[:], ALU.mult), reads=['C_li', 'C_dt'], writes=[f'C_th{d}'])
                emit_sin(sn[:], th[d][:], 32, 'C_sn', f'C_th{d}', t_i[:], t_b[:], 'C_ti', 'C_tb')
                p.op('dve', lambda e: e.tensor_scalar(t_c[:], th[d][:], float(np.pi / 2), None, ALU.add), reads=[f'C_th{d}'], writes=['C_tc'])
                emit_sin(cs[:], t_c[:], 32, 'C_cs', 'C_tc', t_i[:], t_b[:], 'C_ti', 'C_tb')
                p.op('dve', lambda e: e.tensor_tensor(cs[:], cs[:], mag[d][:], ALU.mult), reads=['C_cs', f'C_mag{d}'], writes=['C_cs'])
                p.op('dve', lambda e: e.tensor_scalar(cs[:], cs[:], -1.0, None, ALU.add), reads=['C_cs'], writes=['C_cs'])
                p.op('dve', lambda e: e.tensor_tensor(sn[:], sn[:], mag[d][:], ALU.mult), reads=['C_sn', f'C_mag{d}'], writes=['C_sn'])
                p.op('dve', lambda e: e.tensor_tensor(t_a[:], cs[:], lr[:], ALU.mult), reads=['C_cs', 'C_lr'], writes=['C_ta'])
                p.op('dve', lambda e: e.tensor_tensor(t_b[:], sn[:], li[:], ALU.mult), reads=['C_sn', 'C_li'], writes=['C_tb'])
                p.op('dve', lambda e: e.tensor_tensor(t_a[:], t_a[:], t_b[:], ALU.add), reads=['C_ta', 'C_tb'], writes=['C_ta'])
                p.op('dve', lambda e: e.tensor_tensor(cre[d][:], t_a[:], den[:], ALU.mult), reads=['C_ta', 'C_den'], writes=[f'C_cre{d}'])
                p.op('dve', lambda e: e.tensor_tensor(t_a[:], sn[:], lr[:], ALU.mult), reads=['C_sn', 'C_lr'], writes=['C_ta'])
                p.op('dve', lambda e: e.tensor_tensor(t_b[:], cs[:], li[:], ALU.mult), reads=['C_cs', 'C_li'], writes=['C_tb'])
                p.op('dve', lambda e: e.tensor_tensor(t_a[:], t_a[:], t_b[:], ALU.subtract), reads=['C_ta', 'C_tb'], writes=['C_ta'])
                p.op('dve', lambda e: e.tensor_tensor(cim[d][:], t_a[:], den[:], ALU.mult), reads=['C_ta', 'C_den'], writes=[f'C_cim{d}'])
            WB = [[sb(st, f"C_WB{d}{ri}", [128, 16, 128]) for ri in range(2)] for d in range(2)]
            WC = [[sb(st, f"C_WC{d}{ri}", [128, 32, 64]) for ri in range(2)] for d in range(2)]
            pbs = [ps(st, f"C_pb{i}", [128, 512]) for i in range(8)]
            st2 = ExitStack()
            Bm = [sb(st2, f"C_Bm{ri}", [128, 32, 64]) for ri in range(2)]
            for ri, src in enumerate((s5_b_re, s5_b_im)):
                p.op('pool', lambda e: e.memset(Bm[ri][:], 0.0), writes=[f'C_Bm{ri}'])
                for two in range(2):
                    for qpar in range(2):
                        off = qpar * 32 + two * 16
                        p.dma('sp', Bm[ri][two * 64:(two + 1) * 64, qpar::2, off:off + 16],
                              src[l, (2 * qpar + two)::4, :, :].rearrange("m p c -> p m c"),
                              writes=[f'C_Bm{ri}'], allow_slow_non_contiguous=True)
            bbt = sb(st2, "C_bbt", [128, 32, 64]); bbt2 = sb(st2, "C_bbt2", [128, 32, 64])
            pbi = [0]

            def nb():
                i = pbi[0] % 8
                pbi[0] += 1
                return i
            b3 = lambda t: t[:].unsqueeze(2).broadcast_to([128, 32, 64])
            for d in range(2):
                for ri in range(2):
                    if ri == 0:
                        p.op('dve', lambda e: e.tensor_tensor(bbt[:], Bm[0][:], b3(cre[d]), ALU.mult), reads=['C_Bm0', f'C_cre{d}'], writes=['C_bbt'])
                        p.op('pool', lambda e: e.tensor_tensor(bbt2[:], Bm[1][:], b3(cim[d]), ALU.mult), reads=['C_Bm1', f'C_cim{d}'], writes=['C_bbt2'])
                        p.op('dve', lambda e: e.tensor_tensor(bbt[:], bbt[:], bbt2[:], ALU.subtract), reads=['C_bbt', 'C_bbt2'], writes=['C_bbt'])
                    else:
                        p.op('dve', lambda e: e.tensor_tensor(bbt[:], Bm[1][:], b3(cre[d]), ALU.mult), reads=['C_Bm1', f'C_cre{d}'], writes=['C_bbt'])
                        p.op('pool', lambda e: e.tensor_tensor(bbt2[:], Bm[0][:], b3(cim[d]), ALU.mult), reads=['C_Bm0', f'C_cim{d}'], writes=['C_bbt2'])
                        p.op('dve', lambda e: e.tensor_tensor(bbt[:], bbt[:], bbt2[:], ALU.add), reads=['C_bbt', 'C_bbt2'], writes=['C_bbt'])
                    for q in range(32):
                        bt = nb()
                        hb = (q % 4) // 2
                        qi_ = (q // 4) * 2 + q % 2
                        p.op('pe', lambda e: e.matmul(pbs[bt][hb * 64:(hb + 1) * 64, 0:128], bbt[:, q, :], ident_f[:], start=True, stop=True),
                             reads=['C_bbt', 'ident_f'], writes=[('C_pb', bt)])
                        p.op('act', lambda e: e.copy(WB[d][ri][hb * 64:(hb + 1) * 64, qi_, :], pbs[bt][hb * 64:(hb + 1) * 64, 0:128]),
                             reads=[('C_pb', bt)], writes=[f'C_WB{d}{ri}'])
            Cn = sb(st2, "C_Cn", [64, 32, 128])
            for d in range(2):
                for ri, src in enumerate((s5_c_re, s5_c_im)):
                    p.op('pool', lambda e: e.memset(Cn[:], 0.0), writes=['C_Cn'])
                    for two in range(2):
                        for qpar in range(2):
                            off = qpar * 32 + two * 16
                            p.dma('sp', Cn[off:off + 16, qpar::2, two * 64:(two + 1) * 64],
                                  src[l, d, (2 * qpar + two)::4, :, :].rearrange("m c p -> c m p"),
                                  writes=['C_Cn'], allow_slow_non_contiguous=True)
                    for q4 in range(16):
                        bt = nb()
                        for qq in range(2):
                            q = q4 * 2 + qq
                            p.op('pe', lambda e: e.transpose(pbs[bt][:, qq * 64:(qq + 1) * 64], Cn[:, q, :], ident_f[0:64, 0:64]),
                                 reads=['C_Cn', 'ident_f'], writes=[('C_pb', bt)])
                        dst = WC[d][ri][:, q4 * 2:(q4 + 1) * 2, :].rearrange("p a b -> p (a b)")
                        if ri == 0:
                            p.op('act', lambda e: e.copy(dst, pbs[bt][:, 0:128]), reads=[('C_pb', bt)], writes=[f'C_WC{d}{ri}'])
                        else:
                            p.op('act', lambda e: e.mul(dst, pbs[bt][:, 0:128], -1.0), reads=[('C_pb', bt)], writes=[f'C_WC{d}{ri}'])
            p.barrier()
            st2.close()
            ut4 = [sb(st, f"C_ut{i}", [128, 128]) for i in range(4)]; uT = sb(st, "C_uT", [128, S]); yacc = sb(st, "C_yacc", [128, S])
            iota1 = sb(st, "C_iota", [128, TC])
            p.dma('sp', iota1[:], c_iota[0:1, 0:TC].partition_broadcast(128), writes=['C_iota'])
            tfi = sb(st, "C_tfi", [128, TC], I32)
            mk4 = lambda nm, shp: [sb(st, f"{nm}{i}", shp) for i in range(4)]
            cosT = mk4("C_cosT", [128, TC]); sinT = mk4("C_sinT", [128, TC]); rtab = mk4("C_rtab", [128, TC])
            gre = mk4("C_gre", [128, TC]); gim = mk4("C_gim", [128, TC]); w1 = mk4("C_w1", [128, TC]); w2_ = mk4("C_w2", [128, TC])
            w3 = mk4("C_w3", [128, TC]); w4 = mk4("C_w4", [128, TC])
            hre = mk4("C_hre", [128, TC]); him = mk4("C_him", [128, TC]); carry = mk4("C_carry", [128, 2])
            ang = hre[0]; ang2 = hre[1]; tff = hre[2]; y3 = him[0]; y4 = him[1]
            dsk = sb(st, "C_dsk", [128, 8])
            p.dma('sp', dsk[:], s5_d[l, :].rearrange("(b c) -> c b", c=128), writes=['C_dsk'], allow_slow_non_contiguous=True)
            yo4 = [sb(st, f"C_yo{i}", [128, 128]) for i in range(4)]
            for cb in range(8):
                for i in range(NT):
                    ut = ut4[i % 4]
                    p.dma('sp', ut[:], proj[i * 128:(i + 1) * 128, C_AU + cb * 128:C_AU + (cb + 1) * 128], reads=[('proj', i, 'all')], writes=[('C_ut', i % 4)])
                    bt = nb()
                    p.op('pe', lambda e: e.transpose(pbs[bt][:, 0:128], ut[:], ident_f[:]), reads=[('C_ut', i % 4), 'ident_f'], writes=[('C_pb', bt)])
                    p.op('act', lambda e: e.copy(uT[:, i * 128:(i + 1) * 128], pbs[bt][:, 0:128]), reads=[('C_pb', bt)], writes=[('C_uT', i // 4)])
                for d in range(2):
                    for qq in range(4):
                        q = cb * 4 + qq
                        p.op('dve', lambda e: e.tensor_scalar(ang[:], iota1[:], th[d][:, q:q + 1], None, ALU.mult), reads=['C_iota', f'C_th{d}'], writes=[('C_hre', 0)])
                        emit_sin(sinT[qq][:], ang[:], TC, ('C_sinT', qq), ('C_hre', 0), tfi[:], tff[:], 'C_tfi', ('C_hre', 2))
                        p.op('dve', lambda e: e.tensor_scalar(ang2[:], ang[:], float(np.pi / 2), None, ALU.add), reads=[('C_hre', 0)], writes=[('C_hre', 1)])
                        emit_sin(cosT[qq][:], ang2[:], TC, ('C_cosT', qq), ('C_hre', 1), tfi[:], tff[:], 'C_tfi', ('C_hre', 2))
                        p.op('act', lambda e: e.mul(rtab[qq][:], iota1[:], 0.0), reads=['C_iota'], writes=[('C_rtab', qq)])
                        p.op('dve', lambda e: e.tensor_scalar(rtab[qq][:], rtab[qq][:], mag[d][:, q:q + 1], None, ALU.add), reads=[('C_rtab', qq), f'C_mag{d}'], writes=[('C_rtab', qq)])
                        p.op('dve', lambda e: e.memset(carry[qq][:], 0.0), writes=[('C_carry', qq)])
                    chunks = range(S // TC) if d == 0 else range(S // TC - 1, -1, -1)
                    for ch in chunks:
                        tsl = slice(ch * TC, (ch + 1) * TC)
                        QS = range(4)
                        bre = {}; bim = {}; byq = {}
                        for qq in QS:
                            q = cb * 4 + qq
                            ps32 = slice((qq // 2) * 64, (qq // 2) * 64 + 64)
                            bre[qq], bim[qq] = nb(), nb()
                            p.op('pe', lambda e: e.matmul(pbs[bre[qq]][:, :], WB[d][0][ps32, (q // 4) * 2 + q % 2, :], uT[ps32, tsl], start=True, stop=True),
                                 reads=[f'C_WB{d}0', ('C_uT', ch)], writes=[('C_pb', bre[qq])])
                            p.op('pe', lambda e: e.matmul(pbs[bim[qq]][:, :], WB[d][1][ps32, (q // 4) * 2 + q % 2, :], uT[ps32, tsl], start=True, stop=True),
                                 reads=[f'C_WB{d}1', ('C_uT', ch)], writes=[('C_pb', bim[qq])])
                        Bre = lambda qq: pbs[bre[qq]][:, :] if d == 0 else pbs[bre[qq]][:, ::-1]
                        Bim = lambda qq: pbs[bim[qq]][:, :] if d == 0 else pbs[bim[qq]][:, ::-1]
                        K = lambda n, qq: (n, qq)
                        for qq in QS:
                            p.op('dve', lambda e: e.tensor_tensor(w1[qq][:], Bre(qq), cosT[qq][:], ALU.mult), reads=[('C_pb', bre[qq]), K('C_cosT', qq)], writes=[K('C_w1', qq)])
                            p.op('dve', lambda e: e.tensor_tensor(w2_[qq][:], Bim(qq), sinT[qq][:], ALU.mult), reads=[('C_pb', bim[qq]), K('C_sinT', qq)], writes=[K('C_w2', qq)])
                            p.op('dve', lambda e: e.tensor_tensor(w3[qq][:], Bim(qq), cosT[qq][:], ALU.mult), reads=[('C_pb', bim[qq]), K('C_cosT', qq)], writes=[K('C_w3', qq)])
                            p.op('dve', lambda e: e.tensor_tensor(w4[qq][:], Bre(qq), sinT[qq][:], ALU.mult), reads=[('C_pb', bre[qq]), K('C_sinT', qq)], writes=[K('C_w4', qq)])
                        for qq in QS:
                            p.op('dve', lambda e: e.tensor_tensor(w1[qq][:], w1[qq][:], w2_[qq][:], ALU.add), reads=[K('C_w1', qq), K('C_w2', qq)], writes=[K('C_w1', qq)])
                            p.op('dve', lambda e: e.tensor_tensor(w3[qq][:], w3[qq][:], w4[qq][:], ALU.subtract), reads=[K('C_w3', qq), K('C_w4', qq)], writes=[K('C_w3', qq)])
                        for qq in QS:
                            p.op('dve', lambda e: e.tensor_tensor_scan(gre[qq][:], rtab[qq][:], w1[qq][:], carry[qq][:, 0:1], ALU.mult, ALU.add),
                                 reads=[K('C_rtab', qq), K('C_w1', qq), K('C_carry', qq)], writes=[K('C_gre', qq)])
                        for qq in QS:
                            p.op('dve', lambda e: e.tensor_tensor_scan(gim[qq][:], rtab[qq][:], w3[qq][:], carry[qq][:, 1:2], ALU.mult, ALU.add),
                                 reads=[K('C_rtab', qq), K('C_w3', qq), K('C_carry', qq)], writes=[K('C_gim', qq)])
                            p.op('dve', lambda e: e.tensor_tensor(w1[qq][:], gre[qq][:], cosT[qq][:], ALU.mult), reads=[K('C_gre', qq), K('C_cosT', qq)], writes=[K('C_w1', qq)])
                            p.op('dve', lambda e: e.tensor_tensor(w4[qq][:], gre[qq][:], sinT[qq][:], ALU.mult), reads=[K('C_gre', qq), K('C_sinT', qq)], writes=[K('C_w4', qq)])
                        Hre = lambda qq: hre[qq][:] if d == 0 else hre[qq][:, ::-1]
                        Him = lambda qq: him[qq][:] if d == 0 else him[qq][:, ::-1]
                        for qq in QS:
                            p.op('dve', lambda e: e.tensor_tensor(w2_[qq][:], gim[qq][:], sinT[qq][:], ALU.mult), reads=[K('C_gim', qq), K('C_sinT', qq)], writes=[K('C_w2', qq)])
                            p.op('dve', lambda e: e.tensor_tensor(w3[qq][:], gim[qq][:], cosT[qq][:], ALU.mult), reads=[K('C_gim', qq), K('C_cosT', qq)], writes=[K('C_w3', qq)])
                        last = TC - 1 if d == 0 else 0
                        for qq in QS:
                            p.op('dve', lambda e: e.tensor_tensor(Hre(qq), w1[qq][:], w2_[qq][:], ALU.subtract), reads=[K('C_w1', qq), K('C_w2', qq)], writes=[K('C_hre', qq)])
                            p.op('dve', lambda e: e.tensor_tensor(Him(qq), w4[qq][:], w3[qq][:], ALU.add), reads=[K('C_w4', qq), K('C_w3', qq)], writes=[K('C_him', qq)])
                        for qq in QS:
                            q = cb * 4 + qq
                            ps32 = slice((qq // 2) * 64, (qq // 2) * 64 + 64)
                            p.op('act', lambda e: e.copy(carry[qq][:, 0:1], hre[qq][:, last:last + 1]), reads=[K('C_hre', qq)], writes=[K('C_carry', qq)])
                            p.op('act', lambda e: e.copy(carry[qq][:, 1:2], him[qq][:, last:last + 1]), reads=[K('C_him', qq)], writes=[K('C_carry', qq)])
                            by = nb()
                            byq[qq] = by
                            p.op('pe', lambda e: e.matmul(pbs[by][ps32, :], WC[d][0][:, q, :], hre[qq][:], start=True, stop=False),
                                 reads=[f'C_WC{d}0', K('C_hre', qq)], writes=[('C_pb', by)])
                            p.op('pe', lambda e: e.matmul(pbs[by][ps32, :], WC[d][1][:, q, :], him[qq][:], start=False, stop=True),
                                 reads=[f'C_WC{d}1', K('C_him', qq)], writes=[('C_pb', by)])
                        for qq in QS:
                            ps32 = slice((qq // 2) * 64, (qq // 2) * 64 + 64)
                            by = byq[qq]
                            if d == 0 and qq % 2 == 0:
                                p.op('act', lambda e: e.copy(yacc[ps32, tsl], pbs[by][ps32, :]), reads=[('C_pb', by)], writes=[('C_yacc', qq // 2, ch)])
                            else:
                                p.op('dve', lambda e: e.tensor_tensor(yacc[ps32, tsl], yacc[ps32, tsl], pbs[by][ps32, :], ALU.add),
                                     reads=[('C_pb', by), ('C_yacc', qq // 2, ch)], writes=[('C_yacc', qq // 2, ch)])
                for ch in range(S // TC):
                    tsl = slice(ch * TC, (ch + 1) * TC)
                    rk = [('C_yacc', qq, ch) for qq in range(2)]
                    p.op('dve', lambda e: e.scalar_tensor_tensor(y3[:], uT[:, tsl], dsk[:, cb:cb + 1], yacc[:, tsl], ALU.mult, ALU.add),
                         reads=rk + [('C_uT', ch), 'C_dsk'], writes=[('C_him', 0)])
                    p.op('pool', lambda e: e.tensor_tensor(y4[:], y3[:], y3[:], ALU.mult), reads=[('C_him', 0)], writes=[('C_him', 1)])
                    p.op('dve', lambda e: e.tensor_scalar(y4[:], y4[:], 0.044715, 1.0, ALU.mult, ALU.add), reads=[('C_him', 1)], writes=[('C_him', 1)])
                    p.op('dve', lambda e: e.tensor_tensor(y4[:], y4[:], y3[:], ALU.mult), reads=[('C_him', 1), ('C_him', 0)], writes=[('C_him', 1)])
                    p.op('act', lambda e: e.activation(y4[:], y4[:], AF.Sigmoid, scale=1.5957691216057308), reads=[('C_him', 1)], writes=[('C_him', 1)])
                    p.op('dve', lambda e: e.tensor_tensor(y3[:], y3[:], y4[:], ALU.mult), reads=[('C_him', 1), ('C_him', 0)], writes=[('C_him', 0)])
                    for i4_ in range(TC // 128):
                        i = ch * (TC // 128) + i4_
                        bt = nb()
                        p.op('pe', lambda e: e.transpose(pbs[bt][:, 0:128], y3[:, i4_ * 128:(i4_ + 1) * 128], ident_f[:]), reads=[('C_him', 0), 'ident_f'], writes=[('C_pb', bt)])
                        yo = yo4[i % 4]
                        p.op('act', lambda e: e.copy(yo[:], pbs[bt][:, 0:128]), reads=[('C_pb', bt)], writes=[('C_yo', i % 4)])
                        p.dma('sp', ygd[i * 128:(i + 1) * 128, cb * 128:(cb + 1) * 128], yo[:], reads=[('C_yo', i % 4)], writes=[('ygd', i, cb)])
        p.barrier()
        with ExitStack() as st:
            gw = sb(st, "C2_gw", [128, 8, 1024], BF16)
            p.dma('pool', gw[:], s5_glu_w[l, :, :].rearrange("(k p) n -> p k n", p=128), writes=['C2_gw'])
            gb = sb(st, "C2_gb", [128, 1024])
            p.dma('sp', gb[:], s5_glu_b[l:l + 1, :].partition_broadcast(128), writes=['C2_gb'])
            yg = sb(st, "C2_yg", [128, 1024]); ygb = sb(st, "C2_ygb", [128, 1024], BF16); ygT = sb(st, "C2_ygT", [128, 8, 128], BF16)
            sg = sb(st, "C2_sg", [128, 1024])
            ptr = ps(st, "C2_pt", [128, 8, 128], BF16)
            pm = [ps(st, f"C2_pm{i}", [128, 512]) for i in range(2)]
            for i in range(NT):
                p.dma('sp', yg[:], ygd[i * 128:(i + 1) * 128, :], reads=[('ygd', i, cb) for cb in range(8)], writes=['C2_yg'])
                p.op('act', lambda e: e.copy(ygb[:], yg[:]), reads=['C2_yg'], writes=['C2_ygb'])
                for k in range(8):
                    p.op('pe', lambda e: e.transpose(ptr[:, k, :], ygb[:, k * 128:(k + 1) * 128], ident_b[:]), reads=['C2_ygb', 'ident_b'], writes=['C2_pt'])
                p.op('dve', lambda e: e.tensor_copy(ygT[:], ptr[:]), reads=['C2_pt'], writes=['C2_ygT'])
                for half in range(2):
                    cs_ = slice(half * 512, (half + 1) * 512)
                    for k in range(8):
                        p.op('pe', lambda e: e.matmul(pm[half][:, :], ygT[:, k, :], gw[:, k, cs_], start=(k == 0), stop=(k == 7)),
                             reads=['C2_ygT', 'C2_gw'], writes=[('C2_pm', half)])
                    p.op('dve', lambda e: e.tensor_tensor(sg[:, cs_], pm[half][:, :], gb[:, cs_], ALU.add), reads=[('C2_pm', half), 'C2_gb'], writes=['C2_sg'])
                p.op('act', lambda e: e.activation(sg[:], sg[:], AF.Sigmoid), reads=['C2_sg'], writes=['C2_sg'])
                p.op('dve', lambda e: e.tensor_tensor(sg[:], sg[:], yg[:], ALU.mult), reads=['C2_sg', 'C2_yg'], writes=['C2_sg'])
                p.dma('sp', br[i * 128:(i + 1) * 128, 0:1024], sg[:], reads=['C2_sg'], writes=[('br', i, 0)])
        p.barrier()


    def prologue_rope(ropec, ropes):
        with ExitStack() as st:
            pi_ = sb(st, "R_pi", [128, NT], I32); pf = sb(st, "R_pf", [128, NT]); ivf = sb(st, "R_ivf", [128, 32])
            ang = sb(st, "R_ang", [128, NT, 32]); ti_ = sb(st, "R_ti", [128, NT, 32], I32); tf_ = sb(st, "R_tf", [128, NT, 32])
            p.dma('sp', pi_[:], pos_in[0, :].rearrange("(i p) -> p i", p=128), writes=['R_pi'], allow_slow_non_contiguous=True)
            p.dma('sp', ivf[:], c_invfreq[0:1, :].partition_broadcast(128), writes=['R_ivf'])
            p.op('dve', lambda e: e.tensor_copy(pf[:], pi_[:]), reads=['R_pi'], writes=['R_pf'])
            p.op('dve', lambda e: e.tensor_tensor(ang[:], pf[:].unsqueeze(2).broadcast_to([128, NT, 32]),
                                                 ivf[:].unsqueeze(1).broadcast_to([128, NT, 32]), ALU.mult), reads=['R_pf', 'R_ivf'], writes=['R_ang'])
            for which, dst, key in ((0, ropes, 'rope_s'), (1, ropec, 'rope_c')):
                if which == 1:
                    p.op('dve', lambda e: e.tensor_scalar(ang[:], ang[:], float(np.pi / 2), None, ALU.add), reads=['R_ang'], writes=['R_ang'])
                p.op('dve', lambda e: e.tensor_scalar(ti_[:], ang[:], 1.0 / TWO_PI, None, ALU.mult), reads=['R_ang'], writes=['R_ti'])
                p.op('dve', lambda e: e.tensor_copy(tf_[:], ti_[:]), reads=['R_ti'], writes=['R_tf'])
                p.op('dve', lambda e: e.scalar_tensor_tensor(tf_[:], tf_[:], -TWO_PI, ang[:], ALU.mult, ALU.add), reads=['R_tf', 'R_ang'], writes=['R_tf'])
                p.op('dve', lambda e: e.tensor_scalar(tf_[:], tf_[:], float(np.pi), float(-np.pi), ALU.min, ALU.max), reads=['R_tf'], writes=['R_tf'])
                p.op('act', lambda e: e.activation(dst[:], tf_[:], AF.Sin), reads=['R_tf'], writes=[key])
        p.barrier()

    def phase_B(l):
        with ExitStack() as st:
            ropec = sb(st, "rope_c", [128, NT, 32]); ropes = sb(st, "rope_s", [128, NT, 32])
            prologue_rope(ropec, ropes)
            wuq = sb(st, "B_wuq", [128, 7, 1536], BF16); wukv = sb(st, "B_wukv", [128, 2, 2048], BF16)
            p.dma('pool', wuq[:], mla_w_uq[l, :, :].rearrange("(k p) n -> p k n", p=128), writes=['B_wuq'])
            p.dma('pool', wukv[:], mla_w_ukv[l, :, :].rearrange("(k p) n -> p k n", p=128), writes=['B_wukv'])
            gqa = sb(st, "B_gqa", [128, 896]); gkva = sb(st, "B_gkva", [128, 256]); gq = sb(st, "B_gq", [128, 192]); gk = sb(st, "B_gk", [128, 192])
            p.dma('sp', gqa[:], mla_q_a_norm[l:l + 1, :].partition_broadcast(128), writes=['B_gqa'])
            p.dma('sp', gkva[:], mla_kv_a_norm[l:l + 1, :].partition_broadcast(128), writes=['B_gkva'])
            p.dma('sp', gq[:], mla_q_norm[l:l + 1, :].partition_broadcast(128), writes=['B_gq'])
            p.dma('sp', gk[:], mla_k_norm[l:l + 1, :].partition_broadcast(128), writes=['B_gk'])
            lat = sb(st, "B_lat", [128, 1216]); latb = sb(st, "B_latb", [128, 1152], BF16); latT = sb(st, "B_latT", [128, 9, 128], BF16)
            ss = sb(st, "B_ss", [128, 2]); junk = sb(st, "B_junk", [128, 896], BF16)
            qk = [sb(st, f"B_qk{i}", [128, 8, 192]) for i in range(2)]
            sq = sb(st, "B_sq", [128, 8, 192]); hs = sb(st, "B_hs", [128, 8])
            r1 = sb(st, "B_r1", [128, 8, 32]); r2 = sb(st, "B_r2", [128, 8, 32]); r3 = sb(st, "B_r3", [128, 8, 32])
            qkb = sb(st, "B_qkb", [128, 8, 192], BF16); vb = sb(st, "B_vb", [128, 1024], BF16)
            tT = sb(st, "B_tT", [128, 16, 128], BF16)
            ptr = [ps(st, f"B_pt{i}", [128, 8, 128], BF16) for i in range(2)]
            pm = [ps(st, f"B_pm{i}", [128, 512]) for i in range(4)]
            pmi = [0]
            import os
            for i in range(int(os.environ.get("KNTB", NT))):
                t0 = i * 128
                p.dma('sp', lat[:], proj[t0:t0 + 128, C_CQ:C_CQ + 1216], reads=[('proj', i, 'all')], writes=['B_lat'])
                p.op('act', lambda e: e.activation(junk[:], lat[:, 0:896], AF.Square, accum_out=ss[:, 0:1]), reads=['B_lat'], writes=['B_junk', 'B_ss0'])
                p.op('act', lambda e: e.activation(junk[:, 0:256], lat[:, 896:1152], AF.Square, accum_out=ss[:, 1:2]), reads=['B_lat'], writes=['B_junk', 'B_ss1'])
                p.op('act', lambda e: e.activation(ss[:, 0:1], ss[:, 0:1], AF.Sqrt, bias=eps_t[:], scale=1.0 / 896), reads=['B_ss0', 'eps_t'], writes=['B_ss0'])
                p.op('act', lambda e: e.activation(ss[:, 1:2], ss[:, 1:2], AF.Sqrt, bias=eps_t[:], scale=1.0 / 256), reads=['B_ss1', 'eps_t'], writes=['B_ss1'])
                p.op('dve', lambda e: e.reciprocal(ss[:], ss[:]), reads=['B_ss0', 'B_ss1'], writes=['B_ss0', 'B_ss1'])
                p.op('dve', lambda e: e.scalar_tensor_tensor(latb[:, 0:896], lat[:, 0:896], ss[:, 0:1], gqa[:], ALU.mult, ALU.mult),
                     reads=['B_lat', 'B_ss0', 'B_gqa'], writes=['B_latb'])
                p.op('dve', lambda e: e.scalar_tensor_tensor(latb[:, 896:1152], lat[:, 896:1152], ss[:, 1:2], gkva[:], ALU.mult, ALU.mult),
                     reads=['B_lat', 'B_ss1', 'B_gkva'], writes=['B_latb'])
                BSTOP = int(os.environ.get("BSTOP", 9))
                if BSTOP <= 1:
                    continue
                for k in range(9):
                    pt = ptr[0] if k < 8 else ptr[1]
                    p.op('pe', lambda e: e.transpose(pt[:, k % 8, :], latb[:, k * 128:(k + 1) * 128], ident_b[:]), reads=['B_latb', 'ident_b'],
                         writes=[('B_pt', 0 if k < 8 else 1)])
                p.op('act', lambda e: e.copy(latT[:, 0:8, :], ptr[0][:]), reads=[('B_pt', 0)], writes=['B_latT'])
                p.op('dve', lambda e: e.tensor_copy(latT[:, 8, :], ptr[1][:, 0, :]), reads=[('B_pt', 1)], writes=['B_latT'])
                for c3 in range(3):
                    j = pmi[0] % 4
                    pmi[0] += 1
                    for k in range(7):
                        p.op('pe', lambda e: e.matmul(pm[j][:, :], latT[:, k, :], wuq[:, k, c3 * 512:(c3 + 1) * 512], start=(k == 0), stop=(k == 6)),
                             reads=['B_latT', 'B_wuq'], writes=[('B_pm', j)])
                    p.op('act', lambda e: e.copy(qk[0][:].rearrange("p h d -> p (h d)")[:, c3 * 512:(c3 + 1) * 512], pm[j][:, :]),
                         reads=[('B_pm', j)], writes=['B_qk0'])
                if BSTOP <= 2:
                    continue
                for c4 in range(4):
                    j = pmi[0] % 4
                    pmi[0] += 1
                    for k in range(2):
                        p.op('pe', lambda e: e.matmul(pm[j][:, :], latT[:, 7 + k, :], wukv[:, k, c4 * 512:(c4 + 1) * 512], start=(k == 0), stop=(k == 1)),
                             reads=['B_latT', 'B_wukv'], writes=[('B_pm', j)])
                    pv = pm[j][:, :].rearrange("p (h d) -> p h d", h=2)
                    BSKIP = os.environ.get("BSKIP", "")
                    if 'a' not in BSKIP:
                        p.op('act', lambda e: e.copy(qk[1][:, c4 * 2:(c4 + 1) * 2, 0:128], pv[:, :, 0:128]), reads=[('B_pm', j)], writes=['B_qk1'])
                    for hh in range(2):
                        hcol = (c4 * 2 + hh) * 128
                        p.op('act', lambda e: e.copy(vb[:, hcol:hcol + 128], pm[j][:, hh * 256 + 128:hh * 256 + 256]),
                             reads=[('B_pm', j)], writes=['B_vb'])
                if 'p' not in BSKIP:
                    p.op('pool', lambda e: e.tensor_copy(qk[1][:, :, 128:192], lat[:, 1152:1216].unsqueeze(1).broadcast_to([128, 8, 64])),
                         reads=['B_lat'], writes=['B_qk1'])
                if 'v' not in BSKIP:
                    p.dma('sp', v_d[t0:t0 + 128, :], vb[:], reads=['B_vb'], writes=[('v_d', i)])
                if BSTOP <= 3:
                    continue
                for which in range(2):
                    X = qk[which]
                    xk = f'B_qk{which}'
                    g = gq if which == 0 else gk
                    gk_ = 'B_gq' if which == 0 else 'B_gk'
                    p.op('pool', lambda e: e.tensor_tensor(sq[:], X[:], X[:], ALU.mult), reads=[xk], writes=['B_sq'])
                    p.op('dve', lambda e: e.tensor_reduce(hs[:], sq[:], AX.X, ALU.add), reads=['B_sq'], writes=['B_hs'])
                    p.op('act', lambda e: e.activation(hs[:], hs[:], AF.Sqrt, bias=eps_t[:], scale=1.0 / 192), reads=['B_hs', 'eps_t'], writes=['B_hs'])
                    p.op('dve', lambda e: e.reciprocal(hs[:], hs[:]), reads=['B_hs'], writes=['B_hs'])
                    p.op('dve', lambda e: e.tensor_tensor(X[:], X[:], hs[:].unsqueeze(2).broadcast_to([128, 8, 192]), ALU.mult), reads=[xk, 'B_hs'], writes=[xk])
                    p.op('pool', lambda e: e.tensor_tensor(X[:], X[:], g[:].unsqueeze(1).broadcast_to([128, 8, 192]), ALU.mult), reads=[xk, gk_], writes=[xk])
                    cb_ = ropec[:, i, :].unsqueeze(1).broadcast_to([128, 8, 32]); sb_ = ropes[:, i, :].unsqueeze(1).broadcast_to([128, 8, 32])
                    T1 = X[:, :, 128:160]; T2 = X[:, :, 160:192]
                    p.op('dve', lambda e: e.tensor_tensor(r1[:], T1, sb_, ALU.mult), reads=[xk, 'rope_s'], writes=['B_r1'])
                    p.op('dve', lambda e: e.tensor_tensor(r2[:], T2, sb_, ALU.mult), reads=[xk, 'rope_s'], writes=['B_r2'])
                    p.op('dve', lambda e: e.tensor_tensor(r3[:], T1, cb_, ALU.mult), reads=[xk, 'rope_c'], writes=['B_r3'])
                    p.op('dve', lambda e: e.tensor_tensor(T1, r3[:], r2[:], ALU.subtract), reads=['B_r3', 'B_r2'], writes=[xk])
                    p.op('dve', lambda e: e.tensor_tensor(r3[:], T2, cb_, ALU.mult), reads=[xk, 'rope_c'], writes=['B_r3'])
                    p.op('dve', lambda e: e.tensor_tensor(T2, r3[:], r1[:], ALU.add), reads=['B_r3', 'B_r1'], writes=[xk])
                    p.op('act', lambda e: e.copy(qkb[:], X[:]), reads=[xk], writes=['B_qkb'])
                    if BSTOP <= 4:
                        continue
                    for h in range(8):
                        pt = ptr[h % 2]
                        p.op('pe', lambda e: e.transpose(pt[:, 0, :], qkb[:, h, 0:128], ident_b[:]), reads=['B_qkb', 'ident_b'], writes=[('B_pt', h % 2)])
                        p.op('pe', lambda e: e.transpose(pt[0:64, 1, :], qkb[:, h, 128:192], ident_b[:]), reads=['B_qkb', 'ident_b'], writes=[('B_pt', h % 2)])
                        p.op('act', lambda e: e.copy(tT[:, 2 * h, :], pt[:, 0, :]), reads=[('B_pt', h % 2)], writes=[('B_tT', h)])
                        p.op('dve', lambda e: e.tensor_copy(tT[0:64, 2 * h + 1, :], pt[0:64, 1, :]), reads=[('B_pt', h % 2)], writes=[('B_tT', h)])
                        dstT = qT_d if which == 0 else kT_d
                        p.dma('sp', dstT[h, 0:128, t0:t0 + 128], tT[:, 2 * h, :], reads=[('B_tT', h)], writes=[('qkT', which, h, i)])
                        p.dma('sp', dstT[h, 128:192, t0:t0 + 128], tT[0:64, 2 * h + 1, :], reads=[('B_tT', h)], writes=[('qkT', which, h, i)])
        p.barrier()
        if 'b' in phases:
            return
        with ExitStack() as st:
            qT = sb(st, "B2_qT", [128, S], BF16); qTr = sb(st, "B2_qTr", [64, S], BF16)
            kT = sb(st, "B2_kT", [128, S], BF16); kTr = sb(st, "B2_kTr", [64, S], BF16)
            Va = sb(st, "B2_Va", [128, NT, 132], BF16)
            PT = [sb(st, f"B2_PT{i}", [128, 512], BF16) for i in range(2)]
            ob = sb(st, "B2_ob", [128, 128]); rs = sb(st, "B2_rs", [128, 1])
            psc = [ps(st, f"B2_ps{i}", [128, 512]) for i in range(2)]
            pac = [ps(st, f"B2_pa{i}", [128, 512]) for i in range(4)]
            p.op('dve', lambda e: e.memset(Va[:], 1.0), writes=['B2_Va'])
            SCALE = float(192 ** -0.5)
            it = 0
            for h in range(8):
                p.dma('sp', qT[:], qT_d[h, 0:128, :], writes=['B2_qT'])
                p.dma('sp', qTr[:], qT_d[h, 128:192, :], writes=['B2_qTr'])
                p.dma('sp', kT[:], kT_d[h, 0:128, :], writes=['B2_kT'])
                p.dma('sp', kTr[:], kT_d[h, 128:192, :], writes=['B2_kTr'])
                p.dma('sp', Va[:, :, 0:128], v_d[:, h * 128:(h + 1) * 128].rearrange("(i p) d -> p i d", p=128), writes=['B2_Va'])
                for qb in range(S // 512):
                    qs = slice(qb * 512, (qb + 1) * 512)
                    def scores(kt_):
                        ks_ = slice(kt_ * 128, (kt_ + 1) * 128)
                        j_ = kt_ % 2
                        p.op('pe', lambda e: e.matmul(psc[j_][:, :], kT[:, ks_], qT[:, qs], start=True, stop=False), reads=['B2_kT', 'B2_qT'], writes=[('B2_ps', j_)])
                        p.op('pe', lambda e: e.matmul(psc[j_][:, :], kTr[:, ks_], qTr[:, qs], start=False, stop=True), reads=['B2_kTr', 'B2_qTr'], writes=[('B2_ps', j_)])
                        p.op('act', lambda e: e.activation(PT[j_][:], psc[j_][:, :], AF.Exp, scale=SCALE), reads=[('B2_ps', j_)], writes=[('B2_PT', j_)])
                    scores(0)
                    for kt in range(NT):
                        j = kt % 2
                        if kt + 1 < NT:
                            scores(kt + 1)
                        for sub in range(4):
                            p.op('pe', lambda e: e.matmul(pac[sub][:, 0:129], PT[j][:, sub * 128:(sub + 1) * 128], Va[:, kt, 0:129],
                                                         start=(kt == 0), stop=(kt == NT - 1)), reads=[('B2_PT', j), 'B2_Va'], writes=[('B2_pa', sub)])
                    for sub in range(4):
                        t0 = qb * 512 + sub * 128
                        p.op('dve', lambda e: e.reciprocal(rs[:], pac[sub][:, 128:129]), reads=[('B2_pa', sub)], writes=['B2_rs'])
                        p.op('dve', lambda e: e.tensor_scalar(ob[:], pac[sub][:, 0:128], rs[:], None, ALU.mult), reads=[('B2_pa', sub), 'B2_rs'], writes=['B2_ob'])
                        p.dma('sp', br[t0:t0 + 128, 1024 + h * 128:1024 + (h + 1) * 128], ob[:], reads=['B2_ob'], writes=[('br', t0 // 128, 1, h)])
        p.barrier()

    def phase_M(l):
        with ExitStack() as st:
            wk = sb(st, "M_wk", [128, 32, 1024], BF16)
            gm = sb(st, "M_gm", [128, D]); mt_ = sb(st, "M_mt", [128, D]); mb = sb(st, "M_mb", [128, D], BF16)
            memT = sb(st, "M_memT", [128, 32, 256], BF16)
            ss = sb(st, "M_ss", [128, 1]); hs = sb(st, "M_hs", [128, 4]); gqn = sb(st, "M_gqn", [128, 256]); gkn = sb(st, "M_gkn", [128, 256])
            Kt = sb(st, "M_K", [128, 1024]); sq = sb(st, "M_sq", [128, 1024]); Kb = sb(st, "M_Kb", [128, 1024], BF16)
            KmT = sb(st, "M_KmT", [128, 8, 256], BF16)
            Vm = sb(st, "M_Vm", [128, 2, 4, 260], BF16)
            ptr = [ps(st, f"M_pt{i}", [128, 8, 128], BF16) for i in range(2)]
            pm = [ps(st, f"M_pm{i}", [128, 512]) for i in range(2)]
            psc = [ps(st, f"M_ps{i}", [128, 512]) for i in range(2)]
            pac = [ps(st, f"M_pa{i}", [128, 512]) for i in range(2)]
            p.dma('sp', gm[:], mem_norm_g[l:l + 1, :].partition_broadcast(128), writes=['M_gm'])
            p.dma('sp', gqn[:], mem_q_norm[l:l + 1, :].partition_broadcast(128), writes=['M_gqn'])
            p.dma('sp', gkn[:], mem_k_norm[l:l + 1, :].partition_broadcast(128), writes=['M_gkn'])
            p.op('dve', lambda e: e.memset(Vm[:], 1.0), writes=['M_Vm'])
            for mt in range(2):
                p.dma('sp', mt_[:], mem_in[mt * 128:(mt + 1) * 128, :], writes=['M_mt'])
                p.op('act', lambda e: e.activation(mb[:], mt_[:], AF.Square, accum_out=ss[:]), reads=['M_mt'], writes=['M_mb', 'M_ss'])
                p.op('act', lambda e: e.activation(ss[:], ss[:], AF.Sqrt, bias=eps_t[:], scale=1.0 / D), reads=['M_ss', 'eps_t'], writes=['M_ss'])
                p.op('dve', lambda e: e.reciprocal(ss[:], ss[:]), reads=['M_ss'], writes=['M_ss'])
                p.op('dve', lambda e: e.scalar_tensor_tensor(mb[:], mt_[:], ss[:], gm[:], ALU.mult, ALU.mult), reads=['M_mt', 'M_ss', 'M_gm'], writes=['M_mb'])
                for k8 in range(4):
                    pt = ptr[k8 % 2]
                    for kk in range(8):
                        k = k8 * 8 + kk
                        p.op('pe', lambda e: e.transpose(pt[:, kk, :], mb[:, k * 128:(k + 1) * 128], ident_b[:]), reads=['M_mb', 'ident_b'], writes=[('M_pt', k8 % 2)])
                    p.op('act', lambda e: e.copy(memT[:, k8 * 8:(k8 + 1) * 8, mt * 128:(mt + 1) * 128], pt[:]), reads=[('M_pt', k8 % 2)], writes=['M_memT'])
            for which, wsrc in ((0, mem_w_k), (1, mem_w_v)):
                for k4 in range(4):
                    p.dma('pool', wk[:, k4 * 8:(k4 + 1) * 8, :], wsrc[l, k4 * 1024:(k4 + 1) * 1024, :].rearrange("(k p) n -> p k n", p=128), writes=['M_wk'])
                for mt in range(2):
                    for half in range(2):
                        for k in range(32):
                            p.op('pe', lambda e: e.matmul(pm[half][:, :], memT[:, k, mt * 128:(mt + 1) * 128], wk[:, k, half * 512:(half + 1) * 512],
                                                         start=(k == 0), stop=(k == 31)), reads=['M_memT', 'M_wk'], writes=[('M_pm', half)])
                        if which == 0:
                            p.op('act', lambda e: e.copy(Kt[:, half * 512:(half + 1) * 512], pm[half][:, :]), reads=[('M_pm', half)], writes=['M_K'])
                        else:
                            p.op('act', lambda e: e.copy(Vm[:, mt, half * 2:(half + 1) * 2, 0:256], pm[half][:, :].rearrange("p (h d) -> p h d", h=2)),
                                 reads=[('M_pm', half)], writes=['M_Vm'])
                    if which == 0:
                        K3 = Kt[:].rearrange("p (h d) -> p h d", h=4)
                        p.op('pool', lambda e: e.tensor_tensor(sq[:], Kt[:], Kt[:], ALU.mult), reads=['M_K'], writes=['M_sq'])
                        p.op('dve', lambda e: e.tensor_reduce(hs[:], sq[:].rearrange("p (h d) -> p h d", h=4), AX.X, ALU.add), reads=['M_sq'], writes=['M_hs'])
                        p.op('act', lambda e: e.activation(hs[:], hs[:], AF.Sqrt, bias=eps_t[:], scale=1.0 / 256), reads=['M_hs', 'eps_t'], writes=['M_hs'])
                        p.op('dve', lambda e: e.reciprocal(hs[:], hs[:]), reads=['M_hs'], writes=['M_hs'])
                        p.op('dve', lambda e: e.tensor_tensor(K3, K3, hs[:].unsqueeze(2).broadcast_to([128, 4, 256]), ALU.mult), reads=['M_K', 'M_hs'], writes=['M_K'])
                        p.op('dve', lambda e: e.tensor_tensor(Kb[:].rearrange("p (h d) -> p h d", h=4), K3, gkn[:].unsqueeze(1).broadcast_to([128, 4, 256]), ALU.mult),
                             reads=['M_K', 'M_gkn'], writes=['M_Kb'])
                        for k in range(8):
                            p.op('pe', lambda e: e.transpose(ptr[0][:, k, :], Kb[:, k * 128:(k + 1) * 128], ident_b[:]), reads=['M_Kb', 'ident_b'], writes=[('M_pt', 0)])
                        p.op('act', lambda e: e.copy(KmT[:, :, mt * 128:(mt + 1) * 128], ptr[0][:]), reads=[('M_pt', 0)], writes=['M_KmT'])
            qt = sb(st, "M_q", [128, 1024]); qb_ = sb(st, "M_qb", [128, 1024], BF16); qT = sb(st, "M_qT", [128, 8, 512], BF16)
            PT = sb(st, "M_PT", [128, 2, 512], BF16); ob = sb(st, "M_ob", [128, 1024]); rs = sb(st, "M_rs", [128, 1])
            for g4 in range(NT // 4):
                for ti in range(4):
                    i = g4 * 4 + ti
                    p.dma('sp', qt[:], proj[i * 128:(i + 1) * 128, C_MQ:C_MQ + 1024], reads=[('proj', i, 'all')], writes=['M_q'])
                    Q3 = qt[:].rearrange("p (h d) -> p h d", h=4)
                    p.op('pool', lambda e: e.tensor_tensor(sq[:], qt[:], qt[:], ALU.mult), reads=['M_q'], writes=['M_sq'])
                    p.op('dve', lambda e: e.tensor_reduce(hs[:], sq[:].rearrange("p (h d) -> p h d", h=4), AX.X, ALU.add), reads=['M_sq'], writes=['M_hs'])
                    p.op('act', lambda e: e.activation(hs[:], hs[:], AF.Sqrt, bias=eps_t[:], scale=1.0 / 256), reads=['M_hs', 'eps_t'], writes=['M_hs'])
                    p.op('dve', lambda e: e.reciprocal(hs[:], hs[:]), reads=['M_hs'], writes=['M_hs'])
                    p.op('dve', lambda e: e.tensor_tensor(Q3, Q3, hs[:].unsqueeze(2).broadcast_to([128, 4, 256]), ALU.mult), reads=['M_q', 'M_hs'], writes=['M_q'])
                    p.op('dve', lambda e: e.tensor_tensor(qb_[:].rearrange("p (h d) -> p h d", h=4), Q3, gqn[:].unsqueeze(1).broadcast_to([128, 4, 256]), ALU.mult),
                         reads=['M_q', 'M_gqn'], writes=['M_qb'])
                    for k in range(8):
                        p.op('pe', lambda e: e.transpose(ptr[ti % 2][:, k, :], qb_[:, k * 128:(k + 1) * 128], ident_b[:]), reads=['M_qb', 'ident_b'], writes=[('M_pt', ti % 2)])
                    p.op('act', lambda e: e.copy(qT[:, :, ti * 128:(ti + 1) * 128], ptr[ti % 2][:]), reads=[('M_pt', ti % 2)], writes=['M_qT'])
                for h in range(4):
                    for mt in range(2):
                        for dc in range(2):
                            p.op('pe', lambda e: e.matmul(psc[mt][:, :], KmT[:, h * 2 + dc, mt * 128:(mt + 1) * 128], qT[:, h * 2 + dc, :],
                                                         start=(dc == 0), stop=(dc == 1)), reads=['M_KmT', 'M_qT'], writes=[('M_ps', mt)])
                        p.op('act', lambda e: e.activation(PT[:, mt, :], psc[mt][:, :], AF.Exp, scale=1.0 / 16), reads=[('M_ps', mt)], writes=['M_PT'])
                    for ti in range(4):
                        i = g4 * 4 + ti
                        j = ti % 2
                        for mt in range(2):
                            p.op('pe', lambda e: e.matmul(pac[j][:, 0:257], PT[:, mt, ti * 128:(ti + 1) * 128], Vm[:, mt, h, 0:257],
                                                         start=(mt == 0), stop=(mt == 1)), reads=['M_PT', 'M_Vm'], writes=[('M_pa', j)])
                        p.op('dve', lambda e: e.reciprocal(rs[:], pac[j][:, 256:257]), reads=[('M_pa', j)], writes=['M_rs'])
                        p.op('dve', lambda e: e.tensor_scalar(ob[:, 0:256], pac[j][:, 0:256], rs[:], None, ALU.mult), reads=[('M_pa', j), 'M_rs'], writes=['M_ob'])
                        p.dma('sp', br[i * 128:(i + 1) * 128, 3072 + h * 256:3072 + (h + 1) * 256], ob[:, 0:256], reads=['M_ob'], writes=[('br', i, 3, h)])
        p.barrier()

    def phase_E(l, xsrc):
        for r in range(0, D, 512):
            p.dma('pool', wbf_out[r:r + 512, :], w_out[l, r:r + 512, :], writes=[('wbf_out', r)])
        with ExitStack() as st:
            bt = sb(st, "E_b", [128, D]); G = sb(st, "E_G", [128, D]); mg = sb(st, "E_mg", [128, D], BF16)
            bg = sb(st, "E_bg", [128, 3, 1024]); ss = sb(st, "E_ss", [128, 1]); junk = sb(st, "E_junk", [128, 1024], BF16)
            mT = sb(st, "E_mT", [128, 32, 1024], BF16)
            W = [sb(st, f"E_W{i}", [128, 32, 512], BF16) for i in range(2)]
            xt = [sb(st, f"E_x{i}", [128, 512]) for i in range(2)]
            ptr = [ps(st, f"E_pt{i}", [128, 8, 128], BF16) for i in range(2)]
            pmm = [ps(st, f"E_pm{i}", [128, 512]) for i in range(4)]
            p.dma('sp', bg[:].rearrange("p a b -> p (a b)"), branch_g[l:l + 1, :].partition_broadcast(128), writes=['E_bg'])
            gates = (C_AG, C_BG, C_CG, C_MG)
            wi = 0
            for g in range(S // 1024):
                for ti in range(8):
                    i = g * 8 + ti
                    t0 = i * 128
                    p.dma('sp', bt[:], br[t0:t0 + 128, :], reads=[('br', i, 'all')], writes=['E_b'])
                    for bi in range(4):
                        p.dma('sp', G[:, bi * 1024:(bi + 1) * 1024], proj[t0:t0 + 128, gates[bi]:gates[bi] + 1024], reads=[('proj', i, 'all')], writes=['E_G'])
                    p.op('act', lambda e: e.activation(G[:], G[:], AF.Silu), reads=['E_G'], writes=['E_G'])
                    for bi in range(4):
                        cs_ = slice(bi * 1024, (bi + 1) * 1024)
                        if bi == 2:
                            p.op('pool', lambda e: e.tensor_tensor(mg[:, cs_], bt[:, cs_], G[:, cs_], ALU.mult), reads=['E_b', 'E_G'], writes=['E_mg'])
                            continue
                        gi = {0: 0, 1: 1, 3: 2}[bi]
                        p.op('act', lambda e: e.activation(junk[:], bt[:, cs_], AF.Square, accum_out=ss[:]), reads=['E_b'], writes=['E_junk', 'E_ss'])
                        p.op('act', lambda e: e.activation(ss[:], ss[:], AF.Sqrt, bias=eps_t[:], scale=1.0 / 1024), reads=['E_ss', 'eps_t'], writes=['E_ss'])
                        p.op('dve', lambda e: e.reciprocal(ss[:], ss[:]), reads=['E_ss'], writes=['E_ss'])
                        p.op('dve', lambda e: e.scalar_tensor_tensor(bt[:, cs_], bt[:, cs_], ss[:], bg[:, gi, :], ALU.mult, ALU.mult), reads=['E_b', 'E_ss', 'E_bg'], writes=['E_b'])
                        p.op('pool', lambda e: e.tensor_tensor(mg[:, cs_], bt[:, cs_], G[:, cs_], ALU.mult), reads=['E_b', 'E_G'], writes=['E_mg'])
                    for k8 in range(4):
                        pt = ptr[k8 % 2]
                        for kk in range(8):
                            k = k8 * 8 + kk
                            p.op('pe', lambda e: e.transpose(pt[:, kk, :], mg[:, k * 128:(k + 1) * 128], ident_b[:]), reads=['E_mg', 'ident_b'], writes=[('E_pt', k8 % 2)])
                        dst = mT[:, k8 * 8:(k8 + 1) * 8, ti * 128:(ti + 1) * 128]
                        if k8 % 2 == 0:
                            p.op('act', lambda e: e.copy(dst, pt[:]), reads=[('E_pt', k8 % 2)], writes=[('E_mT', ti)])
                        else:
                            p.op('dve', lambda e: e.tensor_copy(dst, pt[:]), reads=[('E_pt', k8 % 2)], writes=[('E_mT', ti)])
                def load_WE(ci_, wi_):
                    n0_ = ci_ * 512
                    for k4 in range(4):
                        p.dma('sp', W[wi_ % 2][:, k4 * 8:(k4 + 1) * 8, :], wbf_out[k4 * 1024:(k4 + 1) * 1024, n0_:n0_ + 512].rearrange("(k p) n -> p k n", p=128),
                              reads=[('wbf_out', k4 * 1024), ('wbf_out', k4 * 1024 + 512)], writes=[('E_W', wi_ % 2)])
                load_WE(0, wi)
                for ci in range(D // 512):
                    n0 = ci * 512
                    Wt = W[wi % 2]
                    if ci + 1 < D // 512:
                        load_WE(ci + 1, wi + 1)
                    for ti in range(8):
                        i = g * 8 + ti
                        t0 = i * 128
                        j = (ci * 8 + ti) % 4
                        pm = pmm[j]
                        X = xt[(ci * 8 + ti) % 2]
                        xk = ('E_x', (ci * 8 + ti) % 2)
                        p.dma('sp', X[:], xsrc[t0:t0 + 128, n0:n0 + 512], reads=[('y', i, ci)], writes=[xk])
                        for k in range(32):
                            p.op('pe', lambda e: e.matmul(pm[:, :], mT[:, k, ti * 128:(ti + 1) * 128], Wt[:, k, :], start=(k == 0), stop=(k == 31)),
                                 reads=[('E_mT', ti), ('E_W', wi % 2)], writes=[('E_pm', j)])
                        p.op('dve', lambda e: e.tensor_tensor(X[:], X[:], pm[:, :], ALU.add), reads=[('E_pm', j), xk], writes=[xk])
                        p.dma('sp', y_out[t0:t0 + 128, n0:n0 + 512], X[:], reads=[xk], writes=[('y', i, ci)])
                    wi += 1
        p.barrier()

    if dbg and 'A' not in phases:
        proj_in = din("proj_in", [S, NCOLS])
        for i in range(NT):
            p.dma('sp', proj[i * 128:(i + 1) * 128, :], proj_in[i * 128:(i + 1) * 128, :], writes=[('proj', i, 'all')])
        p.barrier()
    if dbg and 'E' in phases and len(phases) < 6:
        br_in = din("br_in", [S, 4096])
        for i in range(NT):
            p.dma('sp', br[i * 128:(i + 1) * 128, :], br_in[i * 128:(i + 1) * 128, :], writes=[('br', i, 'all')])
        p.barrier()
    for l in range(n_layers):
        xsrc = x_in if l == 0 else y_out
        if 'A' in phases:
            phase_A(l, xsrc)
        if 'D' in phases:
            phase_D(l)
        if 'C' in phases:
            phase_C(l)
        if 'M' in phases:
            phase_M(l)
        if 'B' in phases:
            phase_B(l)
        if 'E' in phases:
            phase_E(l, xsrc)
    p.barrier()
    es.close()
    print("instructions:", p.nins)
    nc.in_names = in_names
    return nc


def make_consts():
    r = np.arange(128)
    m = np.stack([r[:, None] < r[None, :], r[:, None] > r[None, :], r[:, None] <= r[None, :], r[:, None] >= r[None, :]]).astype(np.float32)
    return {"c_ident": np.eye(128, dtype=np.float32), "c_masks": m,
            "c_iota": np.arange(1, 513, dtype=np.float32)[None, :],
            "c_invfreq": (1.0 / (np.float32(10000.0) ** (np.arange(0, 64, 2, dtype=np.float32) / np.float32(64)))).astype(np.float32)[None, :]}


_NC_CACHE = {}


def kernel(**inputs):
    nb = 4
    if 'nc' not in _NC_CACHE:
        _NC_CACHE['nc'] = build()
    nc = _NC_CACHE['nc']
    cst = make_consts()
    shared = {}
    for n in nc.in_names:
        if n in cst:
            shared[n] = cst[n]
        elif n in ("x", "mem", "positions"):
            continue
        elif n == "rwkv_r_k":
            shared[n] = np.ascontiguousarray(np.asarray(inputs[n], dtype=np.float32).reshape(L, 1024))
        elif n == "branch_g":
            shared[n] = np.ascontiguousarray(np.asarray(inputs[n], dtype=np.float32).reshape(L, 3072))
        else:
            shared[n] = np.ascontiguousarray(np.asarray(inputs[n], dtype=np.float32))
    in_maps = []
    for b in range(nb):
        m = dict(shared)
        m["x"] = np.ascontiguousarray(np.asarray(inputs["x"][b], dtype=np.float32))
        m["mem"] = np.ascontiguousarray(np.asarray(inputs["mem"][b], dtype=np.float32))
        m["positions"] = np.ascontiguousarray(np.asarray(inputs["positions"][b:b + 1]).astype(np.int32))
        in_maps.append(m)
    res = run_bass_kernel_spmd(nc, in_maps, core_ids=list(range(nb)))
    return np.stack([np.asarray(r["y"], dtype=np.float32) for r in res.results], axis=0)
```
